# Optimizing a Trainium2 kernel written in Bass

```python
import jax, jax.numpy as jnp
from jax import lax
import numpy as np

D_MODEL = 1024
BATCH = 8
SEQ = 2048
DEPTH = 4
DEC_BATCH = 128
DEC_SEQ = 1
PAST_LEN = 16384
PAGE_SIZE = 128

N_META = 16
N_EVEN = (DEPTH + 1) // 2
N_ODD = DEPTH // 2
H_A = 4
DK_A = 128
DV_A = 256
RET_CHUNK = 128
ROPE_BASE = 10000.0
W_B = 1024
NB_B = 8
BS_B = W_B // NB_B
CONV_B = 4
LRU_C = 8.0
HS_C = 64
H_C = D_MODEL // HS_C
LORA_W = 64
LORA_A = 64
LORA_V = 32
LORA_G = 128
GN_EPS_C = 64e-5
D_FF = 2816
CONV_F = 3
EPS = 1e-6
IN_WIDTH = 2 * H_A * DK_A + 2 * H_A * DV_A + 2 * W_B
MIX_WIDTH = H_A * DV_A + W_B

kernel_name = 'hybrid_retention_rglru_rwkv7_convffn_step'


def rms_norm(x, g):
    xf = x.astype(jnp.float32)
    y = xf * lax.rsqrt(jnp.mean(xf * xf, axis=-1, keepdims=True) + EPS)
    return (y * g.astype(jnp.float32)).astype(x.dtype)


def causal_dwconv(x, buf, w, b):
    K = w.shape[0]
    T = x.shape[1]
    xc = jnp.concatenate([buf.astype(x.dtype), x], axis=1)
    y = b + sum(w[j] * xc[:, j:j + T] for j in range(K))
    return y, xc[:, T:]


def rotary(x, pos):
    half = x.shape[-1] // 2
    inv = ROPE_BASE ** (-jnp.linspace(0.0, 1.0, half, dtype=jnp.float32))
    ang = pos.astype(jnp.float32)[:, None] * inv[None, :]
    cos = jnp.cos(ang)[None, :, None, :]
    sin = jnp.sin(ang)[None, :, None, :]
    x1, x2 = x[..., :half], x[..., half:]
    return jnp.concatenate([x1 * cos - x2 * sin, x1 * sin + x2 * cos], axis=-1)


def retention_chunk(S, qkv, log_g):
    q, k, v = qkv
    C = q.shape[1]
    idx = jnp.arange(C, dtype=jnp.float32)
    diff = idx[:, None] - idx[None, :]
    mask = jnp.where(diff >= 0, jnp.exp(log_g[:, None, None] * jnp.maximum(diff, 0.0)[None]), 0.0)
    scores = jnp.einsum('bchd,bshd->bhcs', q, k) * mask[None]
    o = jnp.einsum('bhcs,bshe->bche', scores, v)
    dec_in = jnp.exp((idx[:, None] + 1.0) * log_g[None, :])
    o = o + jnp.einsum('bchd,bhde->bche', q, S) * dec_in[None, :, :, None]
    dec_k = jnp.exp((C - 1.0 - idx)[:, None] * log_g[None, :])
    S_new = jnp.exp(C * log_g)[None, :, None, None] * S + jnp.einsum('bshd,bshe->bhde', k * dec_k[None, :, :, None], v)
    return S_new, o


def retention_seq(q, k, v, S0, log_g, lead):
    outs = []
    S = S0
    if lead > 0:
        S, o = retention_chunk(S, (q[:, :lead], k[:, :lead], v[:, :lead]), log_g)
        outs.append(o)
    q, k, v = q[:, lead:], k[:, lead:], v[:, lead:]
    B, T = q.shape[0], q.shape[1]
    C = RET_CHUNK if T % RET_CHUNK == 0 else T
    N = T // C
    split = lambda t: jnp.moveaxis(t.reshape((B, N, C) + t.shape[2:]), 1, 0)
    S, o = lax.scan(lambda s, c: retention_chunk(s, c, log_g), S, (split(q), split(k), split(v)))
    outs.append(jnp.moveaxis(o, 0, 1).reshape(B, T, H_A, DV_A))
    return jnp.concatenate(outs, axis=1), S


def linear_combine(left, right):
    a1, b1 = left
    a2, b2 = right
    return a1 * a2, a2 * b1 + b2


def retention_lru_mixer(h, pos, lead, s_ret, s_lru, s_conv, p):
    B, T, _ = h.shape
    f32 = jnp.float32
    qk, vg = H_A * DK_A, H_A * DV_A
    z = h @ p['w_in']
    q, k, v, g_a, x_b, g_b = jnp.split(z, [qk, 2 * qk, 2 * qk + vg, 2 * qk + 2 * vg, 2 * qk + 2 * vg + W_B], axis=-1)
    log_g = jnp.log1p(-jnp.exp2(-5.0 - jnp.arange(H_A, dtype=f32)))
    q = rotary(q.reshape(B, T, H_A, DK_A).astype(f32), pos)
    k = rotary(k.reshape(B, T, H_A, DK_A).astype(f32), pos) * (DK_A ** -0.5)
    v = v.reshape(B, T, H_A, DV_A).astype(f32)
    o, s_ret_new = retention_seq(q, k, v, s_ret.astype(f32), log_g, lead)
    mu = jnp.mean(o, axis=-1, keepdims=True)
    var = jnp.mean(jnp.square(o - mu), axis=-1, keepdims=True)
    o = ((o - mu) * lax.rsqrt(var + EPS)).reshape(B, T, vg) * p['ret_gn'].astype(f32)
    y_a = jax.nn.silu(g_a.astype(f32)) * o
    xc, s_conv_new = causal_dwconv(x_b, s_conv, p['conv_w'], p['conv_b'])
    xg = xc.reshape(B, T, NB_B, BS_B)
    r = jax.nn.sigmoid((jnp.einsum('btgc,gcd->btgd', xg, p['wa']).reshape(B, T, W_B) + p['ba']).astype(f32))
    i = jax.nn.sigmoid((jnp.einsum('btgc,gcd->btgd', xg, p['wx']).reshape(B, T, W_B) + p['bx']).astype(f32))
    log_a = -LRU_C * r * jax.nn.softplus(-p['lam'].astype(f32))
    a = jnp.exp(log_a)
    u = jnp.sqrt(-jnp.expm1(2.0 * log_a)) * (i * xc.astype(f32))
    u = u.at[:, 0].add(a[:, 0] * s_lru.astype(f32))
    _, hs = lax.associative_scan(linear_combine, (a, u), axis=1)
    y_b = hs * jax.nn.gelu(g_b.astype(f32))
    out = jnp.concatenate([y_a, y_b], axis=-1).astype(h.dtype) @ p['w_out']
    return out, s_ret_new, hs[:, -1], s_conv_new


def rwkv7_step(S, inp):
    r_t, w_t, k_t, v_t, a_t, b_t = inp
    sa = jnp.einsum('bhij,bhj->bhi', S, a_t)
    S = S * w_t[:, :, None, :] + sa[..., None] * b_t[:, :, None, :] + v_t[..., None] * k_t[:, :, None, :]
    y = jnp.einsum('bhij,bhj->bhi', S, r_t)
    return S, y


def rwkv7_mixer(h, s0, shift0, v_first, p, vp):
    B, T, D = h.shape
    f32 = jnp.float32
    hprev = jnp.concatenate([shift0.astype(h.dtype)[:, None], h[:, :-1]], axis=1)
    xx = hprev - h
    xr, xw, xk, xv, xa, xg = (h + xx * p['mix'][n] for n in range(6))
    r = xr @ p['w_r']
    k = xk @ p['w_k']
    v = xv @ p['w_v']
    w_log = -jax.nn.softplus(-(p['w0'] + jnp.tanh(xw @ p['w1']) @ p['w2']).astype(f32)) - 0.5
    decay = jnp.exp(-jnp.exp(w_log))
    if vp is None:
        v_first = v
    else:
        v = v + (v_first - v) * jax.nn.sigmoid(vp['v0'] + (xv @ vp['v1']) @ vp['v2'])
    a = jax.nn.sigmoid(p['a0'] + (xa @ p['a1']) @ p['a2'])
    g = jax.nn.sigmoid(xg @ p['g1']) @ p['g2']
    heads = lambda t: t.reshape(B, T, H_C, HS_C).astype(f32)
    kk = heads(k * p['k_k'])
    kk = kk * lax.rsqrt(jnp.maximum(jnp.sum(kk * kk, axis=-1, keepdims=True), 1e-24))
    k = k * (1 + (a - 1) * p['k_a'])
    rh, kh, vh, ah = heads(r), heads(k), heads(v), heads(a)
    wh = decay.reshape(B, T, H_C, HS_C)
    tm = lambda t: jnp.moveaxis(t, 1, 0)
    s_new, ys = lax.scan(rwkv7_step, s0.astype(f32), (tm(rh), tm(wh), tm(kh), tm(vh), tm(-kk), tm(kk * ah)))
    ys = jnp.moveaxis(ys, 0, 1)
    mu = jnp.mean(ys, axis=-1, keepdims=True)
    var = jnp.mean(jnp.square(ys - mu), axis=-1, keepdims=True)
    o = ((ys - mu) * lax.rsqrt(var + GN_EPS_C)).reshape(B, T, D) * p['gn_g'].astype(f32) + p['gn_b'].astype(f32)
    bonus = jnp.sum(rh * kh * p['r_k'].astype(f32), axis=-1, keepdims=True) * vh
    o = (o + bonus.reshape(B, T, D)) * g.astype(f32)
    return o.astype(h.dtype) @ p['w_o'], s_new, h[:, -1], v_first


def conv_ffn(h, buf, p):
    u = h @ p['w_up']
    ug, uv = u[..., :D_FF], u[..., D_FF:]
    c, buf_new = causal_dwconv(ug, buf, p['conv_w'], p['conv_b'])
    return (jax.nn.gelu(c) * uv) @ p['w_down'], buf_new


def trunk(x, pos, lead, states, weights):
    st_ret, st_lru, st_lconv, st_rwkv, st_shift, st_ffn = states
    ev, od, vps, ff, norm_mix_g, norm_ffn_g, norm_final_g = weights
    n_ret, n_lru, n_lconv, n_rwkv, n_shift, n_ffn = [], [], [], [], [], []
    v_first = None
    for li in range(DEPTH):
        j = li // 2
        h = rms_norm(x, norm_mix_g[li])
        if li % 2 == 0:
            mix, s_ret, s_lru, s_lconv = retention_lru_mixer(h, pos, lead, st_ret[j], st_lru[j], st_lconv[j], ev[j])
            n_ret.append(s_ret)
            n_lru.append(s_lru)
            n_lconv.append(s_lconv)
        else:
            mix, s_rwkv, s_shift, v_first = rwkv7_mixer(h, st_rwkv[j], st_shift[j], v_first, od[j], vps[j])
            n_rwkv.append(s_rwkv)
            n_shift.append(s_shift)
        x = x + mix
        f, s_ffn = conv_ffn(rms_norm(x, norm_ffn_g[li]), st_ffn[li], ff[li])
        x = x + f
        n_ffn.append(s_ffn)
    stack = lambda lst, ref: jnp.stack(lst).astype(ref.dtype)
    return (rms_norm(x, norm_final_g), stack(n_ret, st_ret), stack(n_lru, st_lru), stack(n_lconv, st_lconv),
            stack(n_rwkv, st_rwkv), stack(n_shift, st_shift), stack(n_ffn, st_ffn))


def setup_inputs(seed: int = 0) -> dict:
    key = jax.random.key(seed)
    keys = iter(jax.random.split(key, 96))
    f32 = jnp.float32
    D = D_MODEL

    def nrm(shape, scale=1.0):
        return jax.random.normal(next(keys), shape, f32) * scale

    def unif(shape, lo, hi):
        return jax.random.uniform(next(keys), shape, f32, lo, hi)

    lam_a = unif((N_EVEN, W_B), 0.9, 0.999)
    return {
        'x_prompt': nrm((BATCH, SEQ, D)),
        'x_sample': nrm((DEC_BATCH, DEC_SEQ, D)),
        'state_ret': nrm((N_EVEN, DEC_BATCH, H_A, DK_A, DV_A), 0.1),
        'state_lru': nrm((N_EVEN, DEC_BATCH, W_B), 0.5),
        'state_lru_conv': nrm((N_EVEN, DEC_BATCH, CONV_B - 1, W_B)),
        'state_rwkv': nrm((N_ODD, DEC_BATCH, H_C, HS_C, HS_C), 0.1),
        'state_shift': nrm((N_ODD, DEC_BATCH, D)),
        'state_ffn_conv': nrm((DEPTH, DEC_BATCH, CONV_F - 1, D_FF)),
        'meta_tokens': nrm((N_META, D)),
        'norm_mix_g': 1.0 + nrm((DEPTH, D), 0.02),
        'norm_ffn_g': 1.0 + nrm((DEPTH, D), 0.02),
        'norm_final_g': 1.0 + nrm((D,), 0.02),
        'ev_w_in': nrm((N_EVEN, D, IN_WIDTH), D ** -0.5),
        'ev_ret_gn_g': 1.0 + nrm((N_EVEN, H_A * DV_A), 0.02),
        'ev_lru_conv_w': nrm((N_EVEN, CONV_B, W_B), CONV_B ** -0.5),
        'ev_lru_conv_b': nrm((N_EVEN, W_B), 0.01),
        'ev_lru_wa': nrm((N_EVEN, NB_B, BS_B, BS_B), BS_B ** -0.5),
        'ev_lru_ba': nrm((N_EVEN, W_B), 0.01),
        'ev_lru_wx': nrm((N_EVEN, NB_B, BS_B, BS_B), BS_B ** -0.5),
        'ev_lru_bx': nrm((N_EVEN, W_B), 0.01),
        'ev_lru_lambda': jnp.log(lam_a) - jnp.log1p(-lam_a),
        'ev_w_out': nrm((N_EVEN, MIX_WIDTH, D), MIX_WIDTH ** -0.5),
        'od_mix': unif((N_ODD, 6, D), 0.0, 1.0),
        'od_w_r': nrm((N_ODD, D, D), D ** -0.5),
        'od_w_k': nrm((N_ODD, D, D), D ** -0.5),
        'od_w_v': nrm((N_ODD, D, D), D ** -0.5),
        'od_w0': unif((N_ODD, D), -6.0, 1.0),
        'od_w1': nrm((N_ODD, D, LORA_W), D ** -0.5),
        'od_w2': nrm((N_ODD, LORA_W, D), 0.1 * LORA_W ** -0.5),
        'od_a0': nrm((N_ODD, D), 0.1),
        'od_a1': nrm((N_ODD, D, LORA_A), D ** -0.5),
        'od_a2': nrm((N_ODD, LORA_A, D), 0.1 * LORA_A ** -0.5),
        'od_v0': nrm((N_ODD - 1, D), 0.1),
        'od_v1': nrm((N_ODD - 1, D, LORA_V), D ** -0.5),
        'od_v2': nrm((N_ODD - 1, LORA_V, D), 0.1 * LORA_V ** -0.5),
        'od_g1': nrm((N_ODD, D, LORA_G), D ** -0.5),
        'od_g2': nrm((N_ODD, LORA_G, D), LORA_G ** -0.5),
        'od_k_k': 0.85 + nrm((N_ODD, D), 0.02),
        'od_k_a': 1.0 + nrm((N_ODD, D), 0.02),
        'od_r_k': nrm((N_ODD, H_C, HS_C), 0.1),
        'od_gn_g': 1.0 + nrm((N_ODD, D), 0.02),
        'od_gn_b': nrm((N_ODD, D), 0.01),
        'od_w_o': nrm((N_ODD, D, D), D ** -0.5),
        'ff_w_up': nrm((DEPTH, D, 2 * D_FF), D ** -0.5),
        'ff_conv_w': nrm((DEPTH, CONV_F, D_FF), CONV_F ** -0.5),
        'ff_conv_b': nrm((DEPTH, D_FF), 0.01),
        'ff_w_down': nrm((DEPTH, D_FF, D), D_FF ** -0.5),
    }


def reference(x_prompt, x_sample, state_ret, state_lru, state_lru_conv, state_rwkv, state_shift, state_ffn_conv,
              meta_tokens, norm_mix_g, norm_ffn_g, norm_final_g,
              ev_w_in, ev_ret_gn_g, ev_lru_conv_w, ev_lru_conv_b, ev_lru_wa, ev_lru_ba, ev_lru_wx, ev_lru_bx,
              ev_lru_lambda, ev_w_out,
              od_mix, od_w_r, od_w_k, od_w_v, od_w0, od_w1, od_w2, od_a0, od_a1, od_a2, od_v0, od_v1, od_v2,
              od_g1, od_g2, od_k_k, od_k_a, od_r_k, od_gn_g, od_gn_b, od_w_o,
              ff_w_up, ff_conv_w, ff_conv_b, ff_w_down):
    ev = [dict(w_in=ev_w_in[j], ret_gn=ev_ret_gn_g[j], conv_w=ev_lru_conv_w[j], conv_b=ev_lru_conv_b[j],
               wa=ev_lru_wa[j], ba=ev_lru_ba[j], wx=ev_lru_wx[j], bx=ev_lru_bx[j], lam=ev_lru_lambda[j],
               w_out=ev_w_out[j]) for j in range(N_EVEN)]
    od = [dict(mix=od_mix[j], w_r=od_w_r[j], w_k=od_w_k[j], w_v=od_w_v[j], w0=od_w0[j], w1=od_w1[j], w2=od_w2[j],
               a0=od_a0[j], a1=od_a1[j], a2=od_a2[j], g1=od_g1[j], g2=od_g2[j], k_k=od_k_k[j], k_a=od_k_a[j],
               r_k=od_r_k[j], gn_g=od_gn_g[j], gn_b=od_gn_b[j], w_o=od_w_o[j]) for j in range(N_ODD)]
    vps = [None] + [dict(v0=od_v0[j], v1=od_v1[j], v2=od_v2[j]) for j in range(N_ODD - 1)]
    ff = [dict(w_up=ff_w_up[l], conv_w=ff_conv_w[l], conv_b=ff_conv_b[l], w_down=ff_w_down[l]) for l in range(DEPTH)]
    weights = (ev, od, vps, ff, norm_mix_g, norm_ffn_g, norm_final_g)
    sample_states = (state_ret, state_lru, state_lru_conv, state_rwkv, state_shift, state_ffn_conv)

    bp = x_prompt.shape[0]
    meta = jnp.broadcast_to(meta_tokens.astype(x_prompt.dtype)[None], (bp, N_META, D_MODEL))
    xp = jnp.concatenate([meta, x_prompt], axis=1)
    pos_p = jnp.arange(xp.shape[1], dtype=jnp.int32)
    prompt_states = tuple(jnp.zeros((s.shape[0], bp) + s.shape[2:], x_prompt.dtype) for s in sample_states)
    yp, ret_p, lru_p, lru_conv_p, rwkv_p, shift_p, ffn_conv_p = trunk(xp, pos_p, N_META, prompt_states, weights)

    pos_s = PAST_LEN + jnp.arange(x_sample.shape[1], dtype=jnp.int32)
    ys, ret_s, lru_s, lru_conv_s, rwkv_s, shift_s, ffn_conv_s = trunk(x_sample, pos_s, 0, sample_states, weights)

    return (yp[:, N_META:], ys, ret_p, lru_p, lru_conv_p, rwkv_p, shift_p, ffn_conv_p,
            ret_s, lru_s, lru_conv_s, rwkv_s, shift_s, ffn_conv_s)
```

```python
import math
import numpy as np
from contextlib import ExitStack
import concourse.bass as bass
import concourse.mybir as mybir
from concourse.bass_utils import run_bass_kernel_spmd

F32 = mybir.dt.float32
BF16 = mybir.dt.bfloat16
AF = mybir.ActivationFunctionType
ALU = mybir.AluOpType
AX = mybir.AxisListType

SEM_CAP = 30000
N_DMA_SEMS = 16

D = 1024
TC = 1040
NTOK = 2080
DFF = 2816
NFC = 22
EPS = 1e-6
GN_EPS_C = 64e-5
WDEC = math.exp(-0.5)
DK_SCALE = 128 ** -0.5
GAMMA = [1.0 - 2.0 ** (-5 - h) for h in range(4)]
TOK_TILES = [(0, 512), (512, 512), (1024, 16)]


class Buf:
    def __init__(self, t, name):
        self.t = t
        self.name = name
        self.st = {'_all': [[], []]}

    def __getitem__(self, idx):
        return self.t[idx]

    def states(self, key):
        if key is None:
            return list(self.st.values())
        if key not in self.st:
            a = self.st['_all']
            self.st[key] = [list(a[0]), list(a[1])]
        return [self.st[key]]

    def fence(self):
        w = []
        r = []
        for st in self.st.values():
            w.extend(st[0])
            r.extend(st[1])
        self.st = {'_all': [w, r]}


class Prog:
    def __init__(self, nc, es):
        self.nc = nc
        self.es = es
        self.engs = ['pe', 'act', 'dve', 'pool', 'sp']
        self.q = {e: [] for e in self.engs}
        self.cur_sem = {}
        self.cnt = {}
        self.nsem = 0
        for e in ['pe', 'act', 'dve', 'pool']:
            self._new_sem(e)
        self.waited = {e: {} for e in self.engs}
        self.dma_sems = [es.enter_context(nc.semaphore("dq%d" % i)) for i in range(2 * N_DMA_SEMS)]
        self.dma_cnt = [0] * (2 * N_DMA_SEMS)
        self.dma_n = 0
        self.dma_ne = {'sp': 0, 'pool': 0}
        self.nbuf = 0
        self.n_ops = 0
        self.n_waits = 0
        self.psi = 0

    def _new_sem(self, e):
        s = self.es.enter_context(self.nc.semaphore("tl_%s_%d" % (e, self.nsem)))
        self.nsem += 1
        self.cur_sem[e] = s
        self.cnt[e] = 0

    def sbuf(self, shape, dt, name=None):
        self.nbuf += 1
        name = "%s_%d" % (name or "sb", self.nbuf)
        t = self.es.enter_context(self.nc.sbuf_tensor(name, list(shape), dt))
        return Buf(t, name)

    def psum(self, shape, dt, name=None):
        self.nbuf += 1
        name = "%s_%d" % (name or "ps", self.nbuf)
        t = self.es.enter_context(self.nc.psum_tensor(name, list(shape), dt))
        return Buf(t, name)

    def _deps(self, eng, reads, writes):
        deps = []
        for (b, k) in reads:
            for st in b.states(k):
                deps.extend(st[0])
        for (b, k) in writes:
            for st in b.states(k):
                deps.extend(st[0])
                deps.extend(st[1])
        best = {}
        for (sem, val, seng) in deps:
            if eng == 'pe' and seng == 'pe':
                continue
            kk = id(sem)
            if kk not in best or best[kk][1] < val:
                best[kk] = (sem, val)
        waits = []
        w = self.waited[eng]
        for kk, (sem, val) in best.items():
            if w.get(kk, 0) >= val:
                continue
            w[kk] = val
            waits.append((sem, val))
        return waits

    @staticmethod
    def _compact(lst):
        best = {}
        for t in lst:
            kk = id(t[0])
            if kk not in best or best[kk][1] < t[1]:
                best[kk] = t
        return list(best.values())

    def _record(self, tok, reads, writes):
        for (b, k) in reads:
            for st in b.states(k):
                st[1].append(tok)
                if len(st[1]) > 48:
                    st[1] = self._compact(st[1])
        for (b, k) in writes:
            for st in b.states(k):
                st[0] = [tok]
                st[1] = []

    @staticmethod
    def _norm(lst):
        out = []
        for x in lst:
            if isinstance(x, tuple):
                out.append(x)
            else:
                out.append((x, None))
        return out

    def op(self, eng, fn, reads=(), writes=()):
        reads = self._norm(reads)
        writes = self._norm(writes)
        waits = self._deps(eng, reads, writes)
        if self.cnt[eng] >= SEM_CAP:
            self._new_sem(eng)
        self.cnt[eng] += 1
        tok = (self.cur_sem[eng], self.cnt[eng], eng)
        self.q[eng].append((waits, fn, (tok[0], 1)))
        self._record(tok, reads, writes)
        self.n_ops += 1
        self.n_waits += len(waits)
        return tok

    def dma(self, out_ap, in_ap, reads=(), writes=(), eng='sp', **kw):
        reads = self._norm(reads)
        writes = self._norm(writes)
        waits = self._deps(eng, reads, writes)
        i = (self.dma_ne[eng] % N_DMA_SEMS) + (N_DMA_SEMS if eng == 'pool' else 0)
        self.dma_ne[eng] += 1
        self.dma_n += 1
        sem = self.dma_sems[i]
        prev = self.dma_cnt[i]
        w = self.waited[eng]
        if prev > 0 and w.get(id(sem), 0) < prev:
            waits.append((sem, prev))
            w[id(sem)] = prev
        self.dma_cnt[i] += 16
        tok = (sem, self.dma_cnt[i], 'dma')

        def fn(e, out_ap=out_ap, in_ap=in_ap, kw=kw):
            return e.dma_start(out=out_ap, in_=in_ap, **kw)
        self.q[eng].append((waits, fn, (sem, 16)))
        self._record(tok, reads, writes)
        self.n_ops += 1
        self.n_waits += len(waits)
        return tok

    def finish(self):
        nc = self.nc
        waits = []
        for i, s in enumerate(self.dma_sems):
            if self.dma_cnt[i] > 0:
                waits.append((s, self.dma_cnt[i]))
        for en in ['pe', 'act', 'dve', 'pool']:
            if self.cnt[en] > 0:
                waits.append((self.cur_sem[en], self.cnt[en]))
        self.q['sp'].append((waits, None, None))
        qs = self.q
        with nc.Block() as block:
            def run(engine, lst):
                for (waits, fn, inc) in lst:
                    for (sem, val) in waits:
                        engine.wait_ge(sem, val)
                    if fn is not None:
                        ins = fn(engine)
                        if inc is not None:
                            ins.then_inc(inc[0], inc[1])

            @block.tensor
            def _(t):
                run(t, qs['pe'])

            @block.scalar
            def _(t):
                run(t, qs['act'])

            @block.vector
            def _(t):
                run(t, qs['dve'])

            @block.gpsimd
            def _(t):
                run(t, qs['pool'])

            @block.sync
            def _(t):
                run(t, qs['sp'])


V8_NAMES = []
for _li in range(4):
    V8_NAMES += ['nmix%d' % _li, 'nffn%d' % _li]
V8_NAMES += ['nfinal']
for _j in range(2):
    V8_NAMES += ['gn%d' % _j, 'cw0_%d' % _j, 'cw1_%d' % _j, 'cw2_%d' % _j, 'cw3_%d' % _j, 'cb%d' % _j,
                 'ba%d' % _j, 'bx%d' % _j, 'lam%d' % _j]
for _j in range(2):
    V8_NAMES += ['mix%d_%d' % (_j, m) for m in range(6)]
    V8_NAMES += ['w0_%d' % _j, 'a0_%d' % _j, 'kk%d' % _j, 'ka%d' % _j, 'rk%d' % _j, 'gng%d' % _j, 'gnb%d' % _j]
V8_NAMES += ['v0_0']
V8 = {n: i for i, n in enumerate(V8_NAMES)}
NV8 = len(V8_NAMES)
V22_NAMES = []
for _li in range(4):
    V22_NAMES += ['fw0_%d' % _li, 'fw1_%d' % _li, 'fw2_%d' % _li, 'fb%d' % _li]
V22 = {n: i for i, n in enumerate(V22_NAMES)}
NV22 = len(V22_NAMES)

CM_MEAN1024, CM_MEAN256, CM_BLKMEAN64, CM_IDENT, CM_BLKONES = range(5)
NCM = 5


def fm(v, nch):
    return np.ascontiguousarray(np.asarray(v, np.float32).reshape(nch, 128).T)


def build_consts():
    cm = np.zeros((128, NCM, 128), np.float32)
    cm[:, CM_MEAN1024, :] = 1.0 / 1024
    cm[:, CM_MEAN256, :] = 1.0 / 256
    blk = np.zeros((128, 128), np.float32)
    blk[:64, :64] = 1.0
    blk[64:, 64:] = 1.0
    cm[:, CM_BLKMEAN64, :] = blk / 64.0
    cm[:, CM_IDENT, :] = np.eye(128, dtype=np.float32)
    cm[:, CM_BLKONES, :] = blk
    inv = (10000.0 ** (-np.linspace(0.0, 1.0, 64, dtype=np.float32))).astype(np.float32)
    pos = np.concatenate([np.arange(2064, dtype=np.float32), np.full(16, 16384.0, np.float32)])
    ang = (pos[None, :] * inv[:, None]).astype(np.float32)
    cos = np.cos(ang.astype(np.float64)).astype(np.float32)
    sin = np.sin(ang.astype(np.float64)).astype(np.float32)
    rope = np.zeros((128, 2, NTOK), np.float32)
    rope[:64, 0] = cos
    rope[64:, 0] = cos
    rope[:64, 1] = -sin
    rope[64:, 1] = sin
    idx = np.arange(128)
    diff = idx[None, :] - idx[:, None]
    rmask = np.zeros((128, 4, 128), np.float32)
    decin = np.zeros((128, 4, 128), np.float32)
    deck = np.zeros((128, 2, 4), np.float32)
    for h in range(4):
        lg = math.log1p(-2.0 ** (-5 - h))
        rmask[:, h, :] = np.where(diff >= 0, DK_SCALE * np.exp(lg * np.maximum(diff, 0)), 0.0)
        decin[:, h, :] = np.exp(lg * (idx + 1.0))[None, :]
        deck[:, 0, h] = DK_SCALE * np.exp(lg * (127.0 - idx))
        deck[:16, 1, h] = DK_SCALE * np.exp(lg * (15.0 - idx[:16]))
    i64 = np.arange(64)
    strictT = (i64[:, None] < i64[None, :]).astype(np.float32)
    inclT = (i64[:, None] <= i64[None, :]).astype(np.float32)
    mask4 = np.zeros((128, 4, 64), np.float32)
    mask4[:64, 0] = strictT
    mask4[:64, 1] = inclT
    mask4[:64, 2] = strictT
    mask4[:64, 3] = inclT
    maska = np.zeros((128, 64), np.float32)
    maska[:64] = (i64[:, None] > i64[None, :]).astype(np.float32)
    reset = np.ones((128, 8, 64), np.float32)
    reset[:, :, 0] = 0.0
    sel = np.zeros((128, 16, 128), np.float32)
    for b in range(16):
        sel[b, b, :] = 1.0
    small = np.zeros((128, 8), np.float32)
    small[:, 0] = EPS
    small[:, 1] = GN_EPS_C
    small[:, 2] = 1.0
    small[:, 3] = 0.0
    return dict(cm=cm, rope=rope, rmask=rmask, decin=decin, deck=deck, mask4=mask4, maska=maska,
                reset=reset, sel=sel, small=small)


_CACHE = {}
DBG = {'layers': [0, 1, 2, 3], 'even': True, 'rwkv': True, 'ffn': True, 'segs': [0, 1], 'ncores': 8,
       'evp': ['qk', 'v', 'ga', 'chunks', 'sample', 'wouta', 'lru', 'woutb', 'post', 'supd']}


def build_program():
    nc = bass.Bass("TRN2", target_bir_lowering=False)

    def din(name, shape):
        return nc.dram_tensor(name, list(shape), F32, kind="ExternalInput").ap()

    def dout(name, shape):
        return nc.dram_tensor(name, list(shape), F32, kind="ExternalOutput").ap()

    xT = din("xT", [D, NTOK])
    d_cm = din("cm", [128, NCM, 128])
    d_rope = din("rope", [128, 2, NTOK])
    d_rmask = din("rmask", [128, 4, 128])
    d_decin = din("decin", [128, 4, 128])
    d_deck = din("deck", [128, 2, 4])
    d_mask4 = din("mask4", [128, 4, 64])
    d_maska = din("maska", [128, 64])
    d_reset = din("reset", [128, 8, 64])
    d_sel = din("sel", [128, 16, 128])
    d_small = din("small", [128, 8])
    d_selh = din("selh", [16, 16, 2, 128])
    d_cmask = din("cmask", [16, 768])
    d_vec8 = din("vec8", [128, NV8, 8])
    d_vec22 = din("vec22", [128, NV22, 22])
    w_in = din("w_in", [2, D, 5120])
    w_swap = din("w_swap", [2, D, 1024])
    w_out = din("w_out", [2, 2048, D])
    d_wax = din("wax", [2, 128, 2, 8, 128])
    w_r = din("w_r", [2, D, D])
    w_k = din("w_k", [2, D, D])
    w_v = din("w_v", [2, D, D])
    w_o = din("w_o", [2, D, D])
    d_w1 = din("w1", [2, D, 64])
    d_w2 = din("w2", [2, 64, D])
    d_a1 = din("a1", [2, D, 64])
    d_a2 = din("a2", [2, 64, D])
    d_v1 = din("v1", [1, D, 32])
    d_v2 = din("v2", [1, 32, D])
    d_g1 = din("g1", [2, D, 128])
    d_g2 = din("g2", [2, 128, D])
    w_up = din("w_up", [4, D, 2 * DFF])
    w_down = din("w_down", [4, DFF, D])
    s_ret = din("s_ret", [2, 16, 4, 128, 256])
    s_lru = din("s_lru", [2, 128, 8, 16])
    s_lconv = din("s_lconv", [2, 128, 8, 3, 16])
    s_rwkv = din("s_rwkv", [2, 16, 16, 64, 64])
    s_shift = din("s_shift", [2, 128, 8, 16])
    s_ffn = din("s_ffn", [4, 128, 22, 2, 16])

    yT = dout("yT", [D, NTOK])
    o_retp = dout("o_retp", [2, 4, 128, 256])
    o_lrup = dout("o_lrup", [128, 2, 8])
    o_lconvp = dout("o_lconvp", [128, 2, 8, 3])
    o_rwkvp = dout("o_rwkvp", [2, 128, 8, 64])
    o_shiftp = dout("o_shiftp", [128, 2, 8])
    o_ffnp = dout("o_ffnp", [128, 4, 22, 2])
    o_rets = dout("o_rets", [2, 16, 4, 128, 256])
    o_lrus = dout("o_lrus", [2, 128, 8, 16])
    o_lconvs = dout("o_lconvs", [2, 128, 8, 3, 16])
    o_rwkvs = dout("o_rwkvs", [2, 16, 16, 64, 64])
    o_shifts = dout("o_shifts", [2, 128, 8, 16])
    o_ffns = dout("o_ffns", [4, 128, 22, 2, 16])

    with ExitStack() as es:
        P = Prog(nc, es)
        X = P.sbuf([128, 8, TC], F32, "X")
        H = P.sbuf([128, 8 * TC], BF16, "H")
        VFD = Buf(None, "vfd")
        vf_dram = nc.dram_tensor("vf_scratch", [128, 8, NTOK], F32, kind="Internal").ap()
        WST0 = P.sbuf([128, 4096], F32, "WST0")
        WST = [WST0, WST0]
        WBF = [P.sbuf([128, 4096], BF16, "WBF%d" % i) for i in range(2)]
        BIG1 = P.sbuf([128, 22 * TC], BF16, "BIG1")
        BIG2 = P.sbuf([128, 9 * 1024], BF16, "BIG2")
        HBUF = P.sbuf([128, 1044], F32, "HBUF")
        SCR = [P.sbuf([128, 512], F32, "SCR%d" % i) for i in range(5)]
        SMP = [HBUF, HBUF]
        PS = [P.psum([128, 512], F32, "PS%d" % i) for i in range(8)]
        CM = P.sbuf([128, NCM, 128], F32, "CM")
        CMB = P.sbuf([128, NCM, 128], BF16, "CMB")
        ROPE = P.sbuf([128, 2, 512], F32, "ROPE")
        RMASK = P.sbuf([128, 4, 128], F32, "RMASK")
        DECIN = P.sbuf([128, 4, 128], F32, "DECIN")
        DECK = P.sbuf([128, 2, 4], F32, "DECK")
        SMALL = P.sbuf([128, 8], F32, "SMALL")
        VEC8 = P.sbuf([128, NV8, 8], F32, "VEC8")
        VEC22 = P.sbuf([128, NV22, 22], F32, "VEC22")
        RS = [P.sbuf([128, 4, 256], F32, "RS%d" % j) for j in range(2)]
        RSB = P.sbuf([128, 4, 256], BF16, "RSB")
        MST = [P.sbuf([128, 8, 64], F32, "MST%d" % j) for j in range(2)]
        CAR = P.sbuf([128, 160], F32, "CAR")
        FFH = P.sbuf([128, 4, 22, 2], F32, "FFH")
        FFO = P.sbuf([128, 4, 22, 2], F32, "FFO")
        SP8 = P.sbuf([128, 2, 8], F32, "SP8")
        QKS = P.sbuf([128, 8, 16], F32, "QKS")
        STS = P.sbuf([128, 22 * 2 * 16], F32, "STS")
        LCS = P.sbuf([128, 8, 3, 16], F32, "LCS")
        WAX = P.sbuf([128, 2, 8, 128], BF16, "WAX")
        SELH = P.sbuf([16, 16, 2, 128], BF16, "SELH")

        LHC = CAR.t[:, 0:16].rearrange("p (j g) -> p j g", g=8)
        LRO = CAR.t[:, 16:32].rearrange("p (j g) -> p j g", g=8)
        LCH = CAR.t[:, 32:80].rearrange("p (j g t) -> p j g t", g=8, t=3)
        LCO = CAR.t[:, 80:128].rearrange("p (j g t) -> p j g t", g=8, t=3)
        SHC = CAR.t[:, 128:144].rearrange("p (j g) -> p j g", g=8)
        SHO = CAR.t[:, 144:160].rearrange("p (j g) -> p j g", g=8)
        EPSC = SMALL.t[:, 0:1]
        GNEPS = SMALL.t[:, 1:2]

        state = {'ps': 0, 'ws': 0, 'scr': 0, 'ev': 0, 'reserved': set()}

        def ps():
            while True:
                i = state['ps'] % 8
                state['ps'] += 1
                if i not in state['reserved']:
                    return PS[i]

        def scr():
            i = state['scr'] % 5
            state['scr'] += 1
            return SCR[i]

        def mm(out_ap, lhsT, rhs, start, stop, reads, writes):
            P.op('pe', lambda e: e.matmul(out_ap, lhsT=lhsT, rhs=rhs, start=start, stop=stop), reads, writes)

        def cp(eng, out_ap, in_ap, reads, writes):
            if eng == 'act':
                P.op('act', lambda e: e.copy(out=out_ap, in_=in_ap), reads, writes)
            else:
                P.op(eng, lambda e: e.tensor_copy(out=out_ap, in_=in_ap), reads, writes)

        def evcp(out_ap, in_ap, reads, writes):
            state['ev'] += 1
            cp('act' if state['ev'] % 2 else 'dve', out_ap, in_ap, reads, writes)

        def tt(eng, out_ap, a, b, op, reads, writes):
            P.op(eng, lambda e: e.tensor_tensor(out=out_ap, in0=a, in1=b, op=op), reads, writes)

        def ts(eng, out_ap, a, s1, s2, op0, op1, reads, writes):
            if s2 is None:
                P.op(eng, lambda e: e.tensor_scalar(out=out_ap, in0=a, scalar1=s1, scalar2=None, op0=op0), reads, writes)
            else:
                P.op(eng, lambda e: e.tensor_scalar(out=out_ap, in0=a, scalar1=s1, scalar2=s2, op0=op0, op1=op1), reads, writes)

        def stt(out_ap, a, s, b, op0, op1, reads, writes):
            P.op('dve', lambda e: e.scalar_tensor_tensor(out=out_ap, in0=a, scalar=s, in1=b, op0=op0, op1=op1), reads, writes)

        def act(out_ap, in_ap, func, reads, writes, bias=None, scale=None):
            kw = {}
            if bias is not None:
                kw['bias'] = bias
            if scale is not None:
                kw['scale'] = scale
            P.op('act', lambda e: e.activation(out=out_ap, in_=in_ap, func=func, **kw), reads, writes)

        def v8(name):
            return VEC8.t[:, V8[name], :]

        def v8c(name, k):
            return VEC8.t[:, V8[name], k:k + 1]

        def v22c(name, k):
            return VEC22.t[:, V22[name], k:k + 1]

        def hview(k, c0, n):
            return H.t[:, k * TC + c0:k * TC + c0 + n]

        def b1(s, c0, n):
            return BIG1.t[:, s * TC + c0:s * TC + c0 + n]

        def b1f(r, c0, n):
            return BIG1.t[:, r * 2 * TC:(r + 1) * 2 * TC].bitcast(F32)[:, c0:c0 + n]

        def b2(s, c0, n):
            return BIG2.t[:, s * TC + c0:s * TC + c0 + n]

        def wload(parts, KC, NC):
            s = state['ws'] % 2
            state['ws'] += 1
            st, bf = WST[s], WBF[s]
            dstv = st.t[:, 0:KC * NC].rearrange("p (k n) -> p k n", n=NC)
            for (src, off, ncols) in parts:
                P.dma(dstv[:, :, off:off + ncols], src.rearrange("(k p) n -> p k n", p=128), writes=[st])
            P.op('pool', lambda e: e.tensor_copy(out=bf.t[:, 0:KC * NC], in_=st.t[:, 0:KC * NC]), reads=[st], writes=[bf])
            return bf, bf.t[:, 0:KC * NC].rearrange("p (k n) -> p k n", n=NC)

        def wload_to(dst_buf, dst_view, src, KC, NC, dkey=None):
            s = state['ws'] % 2
            state['ws'] += 1
            st = WST[s]
            dstv = st.t[:, 0:KC * NC].rearrange("p (k n) -> p k n", n=NC)
            P.dma(dstv, src.rearrange("(k p) n -> p k n", p=128), writes=[st])
            P.op('pool', lambda e: e.tensor_copy(out=dst_view, in_=dstv), reads=[st], writes=[(dst_buf, dkey)])

        P.dma(CM.t[:], d_cm, writes=[CM])
        cp('dve', CMB.t[:], CM.t[:], [CM], [CMB])
        for (b_, d_) in [(RMASK, d_rmask), (DECIN, d_decin), (DECK, d_deck), (SMALL, d_small), (VEC8, d_vec8), (VEC22, d_vec22)]:
            P.dma(b_.t[:], d_, writes=[b_])
        P.dma(WST[0].t[0:16, 0:4096].rearrange('p (a b c) -> p a b c', a=16, b=2), d_selh, writes=[WST[0]])
        cp('dve', SELH.t[:], WST[0].t[0:16, 0:4096].rearrange('p (a b c) -> p a b c', a=16, b=2), [WST[0]], [SELH])
        P.op('pool', lambda e: e.memset(CAR.t[:], 0.0), writes=[CAR])
        P.op('pool', lambda e: e.memset(FFH.t[:], 0.0), writes=[FFH])
        P.op('pool', lambda e: e.memset(FFO.t[:], 0.0), writes=[FFO])
        P.op('pool', lambda e: e.memset(STS.t[:], 0.0), writes=[STS])
        P.op('pool', lambda e: e.memset(LCS.t[:], 0.0), writes=[LCS])
        for j in range(2):
            P.op('pool', lambda e, j=j: e.memset(RS[j].t[:], 0.0), writes=[RS[j]])
            P.op('pool', lambda e, j=j: e.memset(MST[j].t[:], 0.0), writes=[MST[j]])
        for j in range(2):
            act(SP8.t[:, j, :], v8('lam%d' % j), AF.Exp, [VEC8], [SP8], scale=-1.0)
            act(SP8.t[:, j, :], SP8.t[:, j, :], AF.Ln, [SP8], [SP8], bias=1.0)
            ts('dve', SP8.t[:, j, :], SP8.t[:, j, :], -8.0, None, ALU.mult, None, [SP8], [SP8])

        def rmsnorm(gname, out_fn, out_buf, c_tiles=TOK_TILES):
            for (c0, n) in c_tiles:
                sq = WBF[state['ws'] % 2]
                sqv = sq.t[:, 0:8 * n].rearrange("p (k n) -> p k n", n=n)
                act(sqv, X.t[:, :, c0:c0 + n], AF.Square, [X], [sq])
                pb = ps()
                for k in range(8):
                    mm(pb.t[:, 0:n], CMB.t[:, CM_MEAN1024, :], sqv[:, k, :], k == 0, k == 7, [CMB, sq], [pb])
                rs = scr()
                act(rs.t[:, 0:n], pb.t[:, 0:n], AF.Sqrt, [pb, SMALL], [rs], bias=EPSC)
                P.op('dve', lambda e, rs=rs, n=n: e.reciprocal(out=rs.t[:, 0:n], in_=rs.t[:, 0:n]), [rs], [rs])
                for k in range(8):
                    stt(out_fn(k, c0, n), X.t[:, k, c0:c0 + n], v8c(gname, k), rs.t[:, 0:n], ALU.mult, ALU.mult,
                        [X, VEC8, rs], [out_buf])

        def proj_fm(wbuf, wview, KC, col0, src_fn, src_reads, evac):
            for (c0, n) in TOK_TILES:
                pb = ps()
                for k in range(KC):
                    mm(pb.t[:, 0:n], wview[:, k, col0:col0 + 128], src_fn(k, c0, n), k == 0, k == KC - 1,
                       [wbuf] + src_reads, [pb])
                evac(pb, c0, n)

        def x_accum(dc):
            def ev(pb, c0, n):
                tt('dve', X.t[:, dc, c0:c0 + n], X.t[:, dc, c0:c0 + n], pb.t[:, 0:n], ALU.add, [X, pb], [X])
            return ev

        def ffn(seg, li):
            rmsnorm('nffn%d' % li, hview, H)
            BIG1.fence()
            if seg == 1:
                P.dma(STS.t[:, 0:22 * 32].rearrange("p (f t b) -> p f t b", t=2, b=16), s_ffn[li], writes=[STS])
            stsv = STS.t[:, 0:22 * 32].rearrange("p (f t b) -> p f t b", t=2, b=16)
            NP = TC if seg == 0 else 1024
            for f2 in range(11):
                bf, wview = wload([(w_up[li][:, f2 * 256:(f2 + 1) * 256], 0, 256), (w_up[li][:, DFF + f2 * 256:DFF + (f2 + 1) * 256], 256, 256)], 8, 512)
                for fi in range(2):
                    fc = f2 * 2 + fi
                    UG = HBUF
                    cp('pool', UG.t[:, 0:2], FFH.t[:, li, fc, :], [FFH], [UG])
                    pbv = []
                    for (c0, n) in TOK_TILES:
                        pg = ps()
                        for k in range(8):
                            mm(pg.t[:, 0:n], wview[:, k, fi * 128:fi * 128 + 128], hview(k, c0, n), k == 0, k == 7, [bf, H], [pg])
                        cp('act', UG.t[:, 2 + c0:2 + c0 + n], pg.t[:, 0:n], [pg], [UG])
                        pv = ps()
                        for k in range(8):
                            mm(pv.t[:, 0:n], wview[:, k, 256 + fi * 128:256 + fi * 128 + 128], hview(k, c0, n), k == 0, k == 7, [bf, H], [pv])
                        pbv.append(pv)
                    if seg == 0:
                        cp('pool', FFH.t[:, li, fc, :], UG.t[:, 2 + TC - 2:2 + TC], [UG], [FFH])
                    else:
                        cp('pool', FFO.t[:, li, fc, :], UG.t[:, 2 + 1022:2 + 1024], [UG], [FFO])
                    T1 = (BIG2, None)
                    t1 = BIG2.t[:, 0:2 * TC].bitcast(F32)
                    ts('dve', t1[:, 0:TC], UG.t[:, 0:TC], v22c('fw0_%d' % li, fc), v22c('fb%d' % li, fc), ALU.mult, ALU.add,
                       [UG, VEC22], [BIG2])
                    stt(t1[:, 0:TC], UG.t[:, 1:TC + 1], v22c('fw1_%d' % li, fc), t1[:, 0:TC], ALU.mult, ALU.add, [UG, VEC22, BIG2], [BIG2])
                    stt(t1[:, 0:TC], UG.t[:, 2:TC + 2], v22c('fw2_%d' % li, fc), t1[:, 0:TC], ALU.mult, ALU.add, [UG, VEC22, BIG2], [BIG2])
                    if seg == 1:
                        sc = slice(1024, 1040)
                        ts('dve', t1[:, sc], stsv[:, fc, 0, :], v22c('fw0_%d' % li, fc), v22c('fb%d' % li, fc), ALU.mult, ALU.add,
                           [STS, VEC22], [BIG2])
                        stt(t1[:, sc], stsv[:, fc, 1, :], v22c('fw1_%d' % li, fc), t1[:, sc], ALU.mult, ALU.add, [STS, VEC22, BIG2], [BIG2])
                        stt(t1[:, sc], UG.t[:, 2 + 1024:2 + 1040], v22c('fw2_%d' % li, fc), t1[:, sc], ALU.mult, ALU.add, [UG, VEC22, BIG2], [BIG2])
                        P.dma(o_ffns[li, :, fc, 0, :], stsv[:, fc, 1, :], reads=[STS], eng='pool')
                        P.dma(o_ffns[li, :, fc, 1, :], UG.t[:, 2 + 1024:2 + 1040], reads=[UG], eng='pool')
                    act(t1[:, 0:TC], t1[:, 0:TC], AF.Gelu_apprx_tanh, [BIG2], [BIG2])
                    for ti, (c0, n) in enumerate(TOK_TILES):
                        tt('dve', b1(fc, c0, n), t1[:, c0:c0 + n], pbv[ti].t[:, 0:n], ALU.mult, [BIG2, pbv[ti]], [(BIG1, fc)])
            BIG1.fence()
            for dc in range(8):
                s = state['ws'] % 2
                state['ws'] += 1
                st, bf = WST[s], WBF[s]
                dstv = st.t[:, 0:22 * 128].rearrange("p (k n) -> p k n", n=128)
                P.dma(dstv, w_down[li][:, dc * 128:(dc + 1) * 128].rearrange("(k p) n -> p k n", p=128), writes=[st])
                P.op('pool', lambda e, bf=bf, st=st: e.tensor_copy(out=bf.t[:, 0:2816], in_=st.t[:, 0:2816]), reads=[st], writes=[bf])
                wview = bf.t[:, 0:2816].rearrange("p (k n) -> p k n", n=128)
                proj_fm(bf, wview, 22, 0, lambda k, c0, n: b1(k, c0, n), [BIG1], x_accum(dc))

        ctx = dict(locals())
        build_even(ctx)
        build_rwkv(ctx)
        even_layer = ctx['even_layer']
        rwkv_layer = ctx['rwkv_layer']

        for seg in DBG['segs']:
            P.dma(X.t[:], xT[:, seg * TC:(seg + 1) * TC].rearrange("(k p) t -> p k t", p=128), writes=[X])
            for li in DBG['layers']:
                if li % 2 == 0:
                    if DBG['even']:
                        even_layer(seg, li)
                else:
                    if DBG['rwkv']:
                        rwkv_layer(seg, li)
                if DBG['ffn']:
                    ffn(seg, li)
            for (c0, n) in TOK_TILES:
                s = state['ws'] % 2
                state['ws'] += 1
                st = WST[s]
                ov = st.t[:, 0:8 * n].rearrange("p (k n) -> p k n", n=n)
                rmsnorm('nfinal', lambda k, c0_, n_, ov=ov: ov[:, k, :], st, c_tiles=[(c0, n)])
                P.dma(yT[:, seg * TC + c0:seg * TC + c0 + n].rearrange("(k p) t -> p k t", p=128), ov, reads=[st], eng='pool')
        for j in range(2):
            P.dma(o_retp[j].rearrange("h d e -> d h e"), RS[j].t[:], reads=[RS[j]], eng='pool')
            P.dma(o_rwkvp[j], MST[j].t[:, :, :], reads=[MST[j]], eng='pool')
        P.dma(o_lrup, LRO, reads=[CAR], eng='pool')
        P.dma(o_lconvp, LCO, reads=[CAR], eng='pool')
        P.dma(o_shiftp, SHO, reads=[CAR], eng='pool')
        P.dma(o_ffnp, FFO.t[:], reads=[FFO], eng='pool')
        P.finish()
    return nc


def build_even(ctx):
    globals().update(ctx)

    def seg_chunks(seg):
        if seg == 0:
            return [(0, 16)] + [(16 + 128 * i, 128) for i in range(8)]
        return [(128 * i, 128) for i in range(8)]

    def vt(ci, n, c0, w):
        return BIG2.t[0:n, ci * 1024 + c0:ci * 1024 + c0 + w]

    def post(src3, src_reads, n, h, c0):
        OFb, OBb, OSb = scr(), scr(), scr()
        OF = OFb.t[:, 0:2 * n].rearrange("p (e c) -> p e c", c=n)
        OB = OBb.t[:, 0:256].bitcast(BF16)[:, 0:2 * n].rearrange("p (e c) -> p e c", c=n)
        OS = OSb.t[:, 0:256].bitcast(BF16)[:, 0:2 * n].rearrange("p (e c) -> p e c", c=n)
        cp('act', OF, src3, src_reads, [OFb])
        act(OS, src3, AF.Square, src_reads, [OSb])
        cp('pool', OB, OF, [OFb], [OBb])
        pst = ps()
        for ec in range(2):
            mm(pst.t[:, 0:n], CMB.t[:, CM_MEAN256, :], OB[:, ec, :], ec == 0, ec == 1, [CMB, OBb], [pst])
        for ec in range(2):
            mm(pst.t[:, 128:128 + n], CMB.t[:, CM_MEAN256, :], OS[:, ec, :], ec == 0, ec == 1, [CMB, OSb], [pst])
        MEb, VAb = scr(), scr()
        ME = MEb.t[:, 0:n]
        VA = VAb.t[:, 0:n]
        cp('act', ME, pst.t[:, 0:n], [pst], [MEb])
        tt('dve', VA, ME, ME, ALU.mult, [MEb], [VAb])
        tt('dve', VA, pst.t[:, 128:128 + n], VA, ALU.subtract, [pst, VAb], [VAb])
        ts('dve', VA, VA, 0.0, None, ALU.max, None, [VAb], [VAb])
        act(VA, VA, AF.Sqrt, [VAb, SMALL], [VAb], bias=EPSC)
        P.op('dve', lambda e: e.reciprocal(out=VA, in_=VA), [VAb], [VAb])
        for ec in range(2):
            tt('dve', OF[:, ec, :], OF[:, ec, :], ME, ALU.subtract, [OFb, MEb], [OFb])
            stt(OF[:, ec, :], OF[:, ec, :], VEC8.t[:, V8['gn%d' % cur['j']], 2 * h + ec:2 * h + ec + 1], VA, ALU.mult, ALU.mult,
                [OFb, VEC8, VAb], [OFb])
            tt('pool', b1(8 + 2 * h + ec, c0, n), OF[:, ec, :], b1(8 + 2 * h + ec, c0, n), ALU.mult,
               [OFb, (BIG1, 8 + 2 * h + ec)], [(BIG1, 8 + 2 * h + ec)])

    cur = {'j': 0}

    def even_layer(seg, li):
        j = li // 2
        cur['j'] = j
        rmsnorm('nmix%d' % li, hview, H)
        BIG1.fence()
        BIG2.fence()
        s = state['ws'] % 2
        state['ws'] += 1
        P.dma(WST[s].t[:, 0:2048].rearrange("p (a g d) -> p a g d", a=2, g=8), d_wax[j], writes=[WST[s]])
        cp('pool', WAX.t[:], WST[s].t[:, 0:2048].rearrange("p (a g d) -> p a g d", a=2, g=8), [WST[s]], [WAX])
        EVP = DBG['evp']
        for t in range(4 if 'qk' in EVP else 0):
            bf, wv = wload([(w_in[j][:, t * 256:(t + 1) * 256], 0, 256), (w_swap[j][:, t * 256:(t + 1) * 256], 256, 256)], 8, 512)
            for hi in range(2):
                hc = 2 * t + hi
                for (c0, n) in TOK_TILES:
                    po_, ps_ = ps(), ps()
                    for k in range(8):
                        mm(po_.t[:, 0:n], wv[:, k, hi * 128:hi * 128 + 128], hview(k, c0, n), k == 0, k == 7, [bf, H], [po_])
                    for k in range(8):
                        mm(ps_.t[:, 0:n], wv[:, k, 256 + hi * 128:256 + hi * 128 + 128], hview(k, c0, n), k == 0, k == 7, [bf, H], [ps_])
                    T1, T2 = scr(), scr()
                    P.dma(ROPE.t[:, :, 0:n], d_rope[:, :, seg * TC + c0:seg * TC + c0 + n], writes=[ROPE])
                    tt('dve', T1.t[:, 0:n], po_.t[:, 0:n], ROPE.t[:, 0, 0:n], ALU.mult, [po_, ROPE], [T1])
                    tt('dve', T2.t[:, 0:n], ps_.t[:, 0:n], ROPE.t[:, 1, 0:n], ALU.mult, [ps_, ROPE], [T2])
                    tt('pool', b1(hc, c0, n), T1.t[:, 0:n], T2.t[:, 0:n], ALU.add, [T1, T2], [(BIG1, hc)])
                    if seg == 1 and c0 == 1024:
                        tt('pool', QKS.t[:, hc, :], T1.t[:, 0:n], T2.t[:, 0:n], ALU.add, [T1, T2], [QKS])
        chunks = seg_chunks(seg)
        allch = chunks + ([(1024, 16)] if seg == 1 else [])
        for wt in range(2 if 'v' in EVP else 0):
            bf, wv = wload([(w_in[j][:, 1024 + wt * 512:1024 + (wt + 1) * 512], 0, 512)], 8, 512)
            for ci, (c0, n) in enumerate(allch):
                pb = ps()
                for k in range(8):
                    mm(pb.t[0:n, 0:512], hview(k, c0, n), wv[:, k, 0:512], k == 0, k == 7, [bf, H], [pb])
                vm = DBG.get('vmode', '')
                if 'dveonly' in vm:
                    cp('dve', vt(ci, n, wt * 512, 512), pb.t[0:n, 0:512], [pb], [(BIG2, ci)])
                elif 'actonly' in vm:
                    cp('act', vt(ci, n, wt * 512, 512), pb.t[0:n, 0:512], [pb], [(BIG2, ci)])
                elif 'noevac' in vm:
                    pass
                else:
                    evcp(vt(ci, n, wt * 512, 512), pb.t[0:n, 0:512], [pb], [(BIG2, ci)])
        for wt in range(2 if 'ga' in EVP else 0):
            bf, wv = wload([(w_in[j][:, 2048 + wt * 512:2048 + (wt + 1) * 512], 0, 512)], 8, 512)
            for q in range(4):
                oc = wt * 4 + q

                def ev(pb, c0, n, oc=oc):
                    act(b1(8 + oc, c0, n), pb.t[:, 0:n], AF.Silu, [pb], [(BIG1, 8 + oc)])
                proj_fm(bf, wv, 8, q * 128, hview, [H], ev)
        cp('act', RSB.t[:], RS[j].t[:], [RS[j]], [RSB])
        for ci, (c0, n) in enumerate(chunks if 'chunks' in EVP else []):
            di = 0 if n == 128 else 1
            for h in range(4):
                pb = ps()
                mm(pb.t[0:n, 0:n], b1(4 + h, c0, n), b1(h, c0, n), True, True, [(BIG1, 4 + h), (BIG1, h)], [pb])
                SCb, QDb, KDb = scr(), scr(), scr()
                SC = SCb.t[:, 0:64].bitcast(BF16)
                QD = QDb.t[:, 0:64].bitcast(BF16)
                KD = KDb.t[:, 0:64].bitcast(BF16)
                tt('dve', SC[0:n, 0:n], pb.t[0:n, 0:n], RMASK.t[0:n, h, 0:n], ALU.mult, [pb, RMASK], [SCb])
                tt('pool', QD[:, 0:n], b1(h, c0, n), DECIN.t[:, h, 0:n], ALU.mult, [(BIG1, h), DECIN], [QDb])
                po = ps()
                for ec in range(2):
                    mm(po.t[:, ec * 128:ec * 128 + n], vt(ci, n, h * 256 + ec * 128, 128), SC[0:n, 0:n], True, False,
                       [(BIG2, ci), SCb], [po])
                    mm(po.t[:, ec * 128:ec * 128 + n], RSB.t[:, h, ec * 128:ec * 128 + 128], QD[:, 0:n], False, True,
                       [RSB, QDb], [po])
                src3 = po.t[:, 0:256].rearrange("p (e c) -> p e c", c=128)[:, :, 0:n]
                if 'post' in EVP:
                    post(src3, [po], n, h, c0)
                if 'supd' not in EVP:
                    continue
                pt = ps()
                ptb = pt.t[:, :].bitcast(BF16)
                P.op('pe', lambda e, ptb=ptb, n=n, h=h, c0=c0: e.transpose(ptb[0:n, 0:128], b1(4 + h, c0, n), CMB.t[:, CM_IDENT, :]),
                     [(BIG1, 4 + h), CMB], [pt])
                ts('dve', KD[0:n, 0:128], ptb[0:n, 0:128], DECK.t[0:n, di, h:h + 1], None, ALU.mult, None, [pt, DECK], [KDb])
                pu = ps()
                mm(pu.t[:, 0:256], KD[0:n, 0:128], vt(ci, n, h * 256, 256), True, True, [KDb, (BIG2, ci)], [pu])
                stt(RS[j].t[:, h, :], RS[j].t[:, h, :], float(GAMMA[h] ** n), pu.t[:, 0:256], ALU.mult, ALU.add, [RS[j], pu], [RS[j]])
                cp('act', RSB.t[:, h, :], RS[j].t[:, h, :], [RS[j]], [RSB])
        if seg == 1 and 'sample' in EVP:
            c0 = 1024
            KFb = scr()
            KF = KFb.t[:, 0:64].rearrange("p (h b) -> p h b", b=16)
            ts('dve', KF, QKS.t[:, 4:8, :], DK_SCALE, None, ALU.mult, None, [QKS], [KFb])
            POS = ps()
            state['reserved'].add(PS.index(POS))
            for b in range(16):
                RSSb = SMP[b % 2]
                RSS = RSSb.t[:, 0:1024].rearrange("p (h e) -> p h e", e=256)
                P.dma(RSS, s_ret[j, b].rearrange("h d e -> d h e"), writes=[RSSb])
                pbv = [ps(), ps()]
                for half in range(2):
                    for hh in range(2):
                        mm(pbv[half].t[:, 0:512], SELH.t[0:16, b, hh, :], BIG2.t[0:16, 8 * 1024 + half * 512:8 * 1024 + (half + 1) * 512], hh == 0, hh == 1,
                           [SELH, (BIG2, 8)], [pbv[half]])
                for h in range(4):
                    ts('dve', RSS[:, h, :], RSS[:, h, :], float(GAMMA[h]), None, ALU.mult, None, [RSSb], [RSSb])
                    stt(RSS[:, h, :], pbv[h // 2].t[:, (h % 2) * 256:(h % 2) * 256 + 256], KF[:, h, b:b + 1], RSS[:, h, :], ALU.mult, ALU.add,
                        [pbv[h // 2], KFb, RSSb], [RSSb])
                P.dma(o_rets[j, b].rearrange("h d e -> d h e"), RSS, reads=[RSSb], eng='pool')
                for h in range(4):
                    for ec in range(2):
                        col = (h * 2 + ec) * 16 + b
                        mm(POS.t[:, col:col + 1], RSS[:, h, ec * 128:(ec + 1) * 128], QKS.t[:, h, b:b + 1], True, True, [RSSb, QKS], [POS])
            for h in range(4):
                src3 = POS.t[:, h * 32:h * 32 + 32].rearrange("p (e c) -> p e c", c=16)
                post(src3, [POS], 16, h, c0)
            state['reserved'].discard(PS.index(POS))
        for wt in range(2 if 'wouta' in EVP else 0):
            bf, wv = wload([(w_out[j][0:1024, wt * 512:(wt + 1) * 512], 0, 512)], 8, 512)
            for q in range(4):
                proj_fm(bf, wv, 8, q * 128, lambda k, c0, n: b1(8 + k, c0, n), [BIG1], x_accum(wt * 4 + q))
        BIG1.fence()
        BIG2.fence()
        NP = TC if seg == 0 else 1024
        if seg == 1:
            P.dma(STS.t[:, 0:128].rearrange("p (g b) -> p g b", b=16), s_lru[j], writes=[STS])
            P.dma(STS.t[:, 128:512].rearrange("p (g t b) -> p g t b", t=3, b=16), s_lconv[j], writes=[STS])
        SLR = STS.t[:, 0:128].rearrange("p (g b) -> p g b", b=16)
        SLC = STS.t[:, 128:512].rearrange("p (g t b) -> p g t b", t=3, b=16)
        XB = HBUF
        XC, RR, II, AA, TT_, HS, GB = [(lambda c0, n, r=r: b1f(r, c0, n)) for r in range(7)]
        keys = [(BIG1, ('f', r)) for r in range(7)]
        kXC, kRR, kII, kAA, kTT, kHS, kGB = keys
        kXCB = (BIG1, ('f', 7))

        def XCB(c0, n):
            return BIG1.t[:, 7 * 2 * TC + c0:7 * 2 * TC + c0 + n]
        for gp in range(4 if 'lru' in EVP else 0):
            bf, wv = wload([(w_in[j][:, 3072 + gp * 256:3072 + (gp + 1) * 256], 0, 256),
                            (w_in[j][:, 4096 + gp * 256:4096 + (gp + 1) * 256], 256, 256)], 8, 512)
            for gi in range(2):
                g = gp * 2 + gi
                cw = [VEC8.t[:, V8['cw%d_%d' % (m, j)], g:g + 1] for m in range(4)]
                cb = VEC8.t[:, V8['cb%d' % j], g:g + 1]
                cp('pool', XB.t[:, 0:3], LCH[:, j, g, :], [CAR], [XB])

                def evx(pb, c0, n):
                    cp('act', XB.t[:, 3 + c0:3 + c0 + n], pb.t[:, 0:n], [pb], [XB])
                proj_fm(bf, wv, 8, gi * 128, hview, [H], evx)

                def evg(pb, c0, n):
                    act(GB(c0, n), pb.t[:, 0:n], AF.Gelu_apprx_tanh, [pb], [kGB])
                proj_fm(bf, wv, 8, 256 + gi * 128, hview, [H], evg)
                if seg == 0:
                    cp('pool', LCH[:, j, g, :], XB.t[:, 3 + TC - 3:3 + TC], [XB], [CAR])
                else:
                    cp('pool', LCO[:, j, g, :], XB.t[:, 3 + 1021:3 + 1024], [XB], [CAR])
                ts('dve', XC(0, TC), XB.t[:, 0:TC], cw[0], cb, ALU.mult, ALU.add, [XB, VEC8], [kXC])
                for m in range(1, 4):
                    stt(XC(0, TC), XB.t[:, m:m + TC], cw[m], XC(0, TC), ALU.mult, ALU.add, [XB, VEC8, kXC], [kXC])
                if seg == 1:
                    ts('dve', XC(1024, 16), SLC[:, g, 0, :], cw[0], cb, ALU.mult, ALU.add, [STS, VEC8], [kXC])
                    stt(XC(1024, 16), SLC[:, g, 1, :], cw[1], XC(1024, 16), ALU.mult, ALU.add, [STS, VEC8, kXC], [kXC])
                    stt(XC(1024, 16), SLC[:, g, 2, :], cw[2], XC(1024, 16), ALU.mult, ALU.add, [STS, VEC8, kXC], [kXC])
                    stt(XC(1024, 16), XB.t[:, 3 + 1024:3 + 1040], cw[3], XC(1024, 16), ALU.mult, ALU.add, [XB, VEC8, kXC], [kXC])
                    cp('pool', LCS.t[:, g, 0, :], SLC[:, g, 1, :], [STS], [LCS])
                    cp('pool', LCS.t[:, g, 1, :], SLC[:, g, 2, :], [STS], [LCS])
                    cp('pool', LCS.t[:, g, 2, :], XB.t[:, 3 + 1024:3 + 1040], [XB], [LCS])
                cp('pool', XCB(0, TC), XC(0, TC), [kXC], [kXCB])
                for which, dst, kd, bname in ((0, RR, kRR, 'ba%d' % j), (1, II, kII, 'bx%d' % j)):
                    for (c0, n) in TOK_TILES:
                        pb = ps()
                        mm(pb.t[:, 0:n], WAX.t[:, which, g, :], XCB(c0, n), True, True, [WAX, kXCB], [pb])
                        act(dst(c0, n), pb.t[:, 0:n], AF.Sigmoid, [pb, VEC8], [kd], bias=VEC8.t[:, V8[bname], g:g + 1])
                act(AA(0, TC), RR(0, TC), AF.Exp, [kRR, SP8], [kAA], scale=SP8.t[:, j, g:g + 1])
                tt('pool', TT_(0, TC), AA(0, TC), AA(0, TC), ALU.mult, [kAA], [kTT])
                ts('pool', TT_(0, TC), TT_(0, TC), -1.0, 1.0, ALU.mult, ALU.add, [kTT], [kTT])
                ts('pool', TT_(0, TC), TT_(0, TC), 0.0, None, ALU.max, None, [kTT], [kTT])
                act(TT_(0, TC), TT_(0, TC), AF.Sqrt, [kTT], [kTT])
                tt('pool', II(0, TC), II(0, TC), XC(0, TC), ALU.mult, [kII, kXC], [kII])
                tt('dve', II(0, TC), II(0, TC), TT_(0, TC), ALU.mult, [kII, kTT], [kII])
                init = 0.0 if seg == 0 else LHC[:, j, g:g + 1]
                P.op('dve', lambda e, init=init: e.tensor_tensor_scan(out=HS(0, NP), data0=AA(0, NP), data1=II(0, NP), initial=init,
                                                                      op0=ALU.mult, op1=ALU.add), [kAA, kII, CAR], [kHS])
                if seg == 0:
                    cp('pool', LHC[:, j, g:g + 1], HS(NP - 1, 1), [kHS], [CAR])
                else:
                    cp('pool', LRO[:, j, g:g + 1], HS(1023, 1), [kHS], [CAR])
                    tt('pool', HS(1024, 16), AA(1024, 16), SLR[:, g, :], ALU.mult, [kAA, STS], [kHS])
                    tt('pool', HS(1024, 16), HS(1024, 16), II(1024, 16), ALU.add, [kHS, kII], [kHS])
                    cp('pool', LCS.t[:, 0, 0, 0:1] if False else STS.t[:, 512 + g * 16:512 + g * 16 + 16], HS(1024, 16), [kHS], [STS])
                tt('dve', b2(g, 0, TC), HS(0, TC), GB(0, TC), ALU.mult, [kHS, kGB], [(BIG2, ('y', g))])
        if seg == 1:
            P.dma(o_lrus[j], STS.t[:, 512:640].rearrange("p (g b) -> p g b", b=16), reads=[STS], eng='pool')
            P.dma(o_lconvs[j], LCS.t[:], reads=[LCS], eng='pool')
        BIG2.fence()
        for wt in range(2 if 'woutb' in EVP else 0):
            bf, wv = wload([(w_out[j][1024:2048, wt * 512:(wt + 1) * 512], 0, 512)], 8, 512)
            for q in range(4):
                proj_fm(bf, wv, 8, q * 128, lambda k, c0, n: b2(k, c0, n), [BIG2], x_accum(wt * 4 + q))
        BIG1.fence()
        BIG2.fence()

    ctx['even_layer'] = even_layer


def build_rwkv(ctx):
    globals().update(ctx)
    W0 = WST[0]
    B0 = WBF[0]
    B1_ = WBF[1]

    def Q(idx):
        return W0.t[:, idx * 136:idx * 136 + 128].rearrange("p (k n) -> p k n", n=16)

    def kq(idx):
        return (W0, ('q', idx))

    def bc(name):
        return VEC8.t[:, V8[name], :].rearrange("p (k o) -> p k o", o=1).to_broadcast([128, 8, 16])

    def p3(pb):
        return pb.t[:, 0:128].rearrange("p (k n) -> p k n", n=16)

    WRv = BIG1.t[:, 0:8192].rearrange("p (k n) -> p k n", n=1024)
    WKv = BIG1.t[:, 8192:16384].rearrange("p (k n) -> p k n", n=1024)
    WVv = BIG2.t[:, 0:8192].rearrange("p (k n) -> p k n", n=1024)
    WOv = H.t[:, 0:8192].rearrange("p (k n) -> p k n", n=1024)
    o_ = 16384
    W1v = BIG1.t[:, o_:o_ + 512].rearrange("p (k n) -> p k n", n=64)
    A1v = BIG1.t[:, o_ + 512:o_ + 1024].rearrange("p (k n) -> p k n", n=64)
    V1v = BIG1.t[:, o_ + 1024:o_ + 1280].rearrange("p (k n) -> p k n", n=32)
    G1v = BIG1.t[:, o_ + 1280:o_ + 2304].rearrange("p (k n) -> p k n", n=128)
    W2v = BIG1.t[:, o_ + 2304:o_ + 3328]
    A2v = BIG1.t[:, o_ + 3328:o_ + 4352]
    V2v = BIG1.t[:, o_ + 4352:o_ + 5376]
    G2v = BIG1.t[:, o_ + 5376:o_ + 6400]

    def load_small(dst, src, rows):
        s = state['ws'] % 2
        state['ws'] += 1
        st = WST[s]
        P.dma(st.t[0:rows, 0:1024], src, writes=[st])
        cp('pool', dst[0:rows, :], st.t[0:rows, 0:1024], [st], [BIG1])

    def rwkv_layer(seg, li):
        j = li // 2
        BIG1.fence()
        BIG2.fence()
        H.fence()
        for half in range(2):
            cs = slice(half * 512, (half + 1) * 512)
            wload_to(BIG1, WRv[:, :, cs], w_r[j][:, cs], 8, 512)
            wload_to(BIG1, WKv[:, :, cs], w_k[j][:, cs], 8, 512)
            wload_to(BIG2, WVv[:, :, cs], w_v[j][:, cs], 8, 512)
            wload_to(H, WOv[:, :, cs], w_o[j][:, cs], 8, 512)
        wload_to(BIG1, W1v, d_w1[j], 8, 64)
        wload_to(BIG1, A1v, d_a1[j], 8, 64)
        if li == 3:
            wload_to(BIG1, V1v, d_v1[0], 8, 32)
            load_small(V2v, d_v2[0], 32)
        wload_to(BIG1, G1v, d_g1[j], 8, 128)
        load_small(W2v, d_w2[j], 64)
        load_small(A2v, d_a2[j], 64)
        load_small(G2v, d_g2[j], 128)
        CHUNK = DBG.get('chunk', True)
        if CHUNK:
            P.dma(HBUF.t[0:16, 0:768], d_cmask, writes=[HBUF])
        W0.fence()
        B0.fence()
        B1_.fence()
        if seg == 0:
            groups = [(16 * i, False) for i in range(65)]
        else:
            groups = [(16 * i, False) for i in range(64)] + [(1024, True)]
        if seg == 1:
            P.dma(STS.t[:, 0:128].rearrange("p (g b) -> p g b", b=16), s_shift[j], writes=[STS])
        SSH = STS.t[:, 0:128].rearrange("p (g b) -> p g b", b=16)
        MS = MST[j].t[:, :, :]
        XM = B0.t[:, 0:768].rearrange("p (m k n) -> p m k n", m=6, k=8)
        for gi_, (c0, sample) in enumerate(groups):
            HFb = W0.t[:, 29 * 136:30 * 136].rearrange("p (k n) -> p k n", n=17)
            kHF = (W0, ('q', 29))
            sqv = B1_.t[:, 3072:3200].rearrange("p (k n) -> p k n", n=16)
            kSQ = (B1_, 'sq')
            act(sqv, X.t[:, :, c0:c0 + 16], AF.Square, [X], [kSQ])
            pb = ps()
            for k in range(8):
                mm(pb.t[:, 0:16], CMB.t[:, CM_MEAN1024, :], sqv[:, k, :], k == 0, k == 7, [CMB, kSQ], [pb])
            rs = scr()
            act(rs.t[:, 0:16], pb.t[:, 0:16], AF.Sqrt, [pb, SMALL], [rs], bias=EPSC)
            P.op('dve', lambda e, rs=rs: e.reciprocal(out=rs.t[:, 0:16], in_=rs.t[:, 0:16]), [rs], [rs])
            if not sample:
                cp('pool', HFb[:, :, 0], SHC[:, j, :], [CAR], [kHF])
            for k in range(8):
                stt(HFb[:, k, 1:17], X.t[:, k, c0:c0 + 16], v8c('nmix%d' % li, k), rs.t[:, 0:16], ALU.mult, ALU.mult,
                    [X, VEC8, rs], [kHF])
            cur = HFb[:, :, 1:17]
            XX = Q(0)
            if sample:
                tt('dve', XX, SSH, cur, ALU.subtract, [STS, kHF], [kq(0)])
                P.dma(o_shifts[j], cur, reads=[kHF], eng='pool')
            else:
                tt('dve', XX, HFb[:, :, 0:16], cur, ALU.subtract, [kHF], [kq(0)])
                cp('pool', SHC[:, j, :], HFb[:, :, 16], [kHF], [CAR])
                if seg == 1 and c0 == 1008:
                    cp('pool', SHO[:, j, :], HFb[:, :, 16], [kHF], [CAR])
            for m in range(6):
                tt('dve', Q(1), XX, bc('mix%d_%d' % (j, m)), ALU.mult, [kq(0), VEC8], [kq(1)])
                tt('pool', XM[:, m], Q(1), cur, ALU.add, [kq(1), kHF], [(B0, ('xm', m))])

            def projN(wv, m, wbuf):
                pb = ps()
                for oc in range(8):
                    for k in range(8):
                        mm(pb.t[:, oc * 16:oc * 16 + 16], wv[:, k, oc * 128:oc * 128 + 128], XM[:, m, k, :], k == 0, k == 7,
                           [wbuf, (B0, ('xm', m))], [pb])
                return pb

            def lora(w1v, r, m, func, w2v):
                pl = ps()
                for k in range(8):
                    mm(pl.t[0:r, 0:16], w1v[:, k, 0:r], XM[:, m, k, :], k == 0, k == 7, [BIG1, (B0, ('xm', m))], [pl])
                tl = B1_.t[0:r, 3200 + m * 16:3200 + m * 16 + 16]
                kt = (B1_, ('tl', m))
                if func is None:
                    cp('act', tl, pl.t[0:r, 0:16], [pl], [kt])
                else:
                    act(tl, pl.t[0:r, 0:16], func, [pl], [kt])
                po2 = ps()
                for oc in range(8):
                    mm(po2.t[:, oc * 16:oc * 16 + 16], w2v[0:r, oc * 128:oc * 128 + 128], tl, True, True, [BIG1, kt], [po2])
                return po2
            pr = projN(WRv, 0, BIG1)
            RF = Q(2)
            cp('act', RF, p3(pr), [pr], [kq(2)])
            pk = projN(WKv, 2, BIG1)
            KF = Q(3)
            cp('act', KF, p3(pk), [pk], [kq(3)])
            pv = projN(WVv, 3, BIG2)
            VV = Q(4)
            cp('act', VV, p3(pv), [pv], [kq(4)])
            pw = lora(W1v, 64, 1, AF.Tanh, W2v)
            DEC = Q(5)
            tt('dve', DEC, p3(pw), bc('w0_%d' % j), ALU.add, [pw, VEC8], [kq(5)])
            act(DEC, DEC, AF.Sigmoid, [kq(5)], [kq(5)])
            use_chunk = CHUNK and not sample
            if not use_chunk:
                act(DEC, DEC, AF.Exp, [kq(5)], [kq(5)], scale=-WDEC)
            pa = lora(A1v, 64, 4, None, A2v)
            AS = Q(6)
            tt('dve', AS, p3(pa), bc('a0_%d' % j), ALU.add, [pa, VEC8], [kq(6)])
            act(AS, AS, AF.Sigmoid, [kq(6)], [kq(6)])
            pg = lora(G1v, 128, 5, AF.Sigmoid, G2v)
            GG = Q(7)
            cp('act', GG, p3(pg), [pg], [kq(7)])
            if li == 3:
                pvl = lora(V1v, 32, 3, None, V2v)
                VG = Q(8)
                tt('dve', VG, p3(pvl), bc('v0_0'), ALU.add, [pvl, VEC8], [kq(8)])
                act(VG, VG, AF.Sigmoid, [kq(8)], [kq(8)])
                P.dma(Q(23), vf_dram[:, :, seg * TC + c0:seg * TC + c0 + 16], reads=[(VFD, seg * TC + c0)], writes=[kq(23)])
                tt('dve', Q(9), Q(23), VV, ALU.subtract, [kq(23), kq(4)], [kq(9)])
                tt('dve', Q(9), Q(9), VG, ALU.mult, [kq(9), kq(8)], [kq(9)])
                tt('dve', VV, VV, Q(9), ALU.add, [kq(4), kq(9)], [kq(4)])
            else:
                P.dma(vf_dram[:, :, seg * TC + c0:seg * TC + c0 + 16], VV, reads=[kq(4)], writes=[(VFD, seg * TC + c0)], eng='pool')
            KK = Q(10)
            tt('dve', KK, KF, bc('kk%d' % j), ALU.mult, [kq(3), VEC8], [kq(10)])
            tt('pool', Q(11), KK, KK, ALU.mult, [kq(10)], [kq(11)])
            pss = ps()
            mm(pss.t[:, 0:128], CM.t[:, CM_BLKONES, :], W0.t[:, 11 * 136:11 * 136 + 128], True, True, [CM, kq(11)], [pss])
            INV = Q(12)
            ts('dve', INV, p3(pss), 1e-24, None, ALU.max, None, [pss], [kq(12)])
            act(INV, INV, AF.Sqrt, [kq(12)], [kq(12)])
            P.op('dve', lambda e, INV=INV: e.reciprocal(out=INV, in_=INV), [kq(12)], [kq(12)])
            tt('dve', KK, KK, INV, ALU.mult, [kq(10), kq(12)], [kq(10)])
            AT = Q(13)
            ts('pool', AT, KK, -1.0, None, ALU.mult, None, [kq(10)], [kq(13)])
            BT = Q(14)
            tt('pool', BT, KK, AS, ALU.mult, [kq(10), kq(6)], [kq(14)])
            K2 = Q(15)
            stt(K2, AS, -1.0, bc('ka%d' % j), ALU.add, ALU.mult, [kq(6), VEC8], [kq(15)])
            stt(K2, K2, 1.0, KF, ALU.add, ALU.mult, [kq(15), kq(3)], [kq(15)])
            RK = Q(16)
            tt('dve', RK, RF, K2, ALU.mult, [kq(2), kq(15)], [kq(16)])
            tt('pool', RK, RK, bc('rk%d' % j), ALU.mult, [kq(16), VEC8], [kq(16)])
            pbo = ps()
            mm(pbo.t[:, 0:128], CM.t[:, CM_BLKONES, :], W0.t[:, 16 * 136:16 * 136 + 128], True, True, [CM, kq(16)], [pbo])
            BON = Q(17)
            tt('dve', BON, p3(pbo), VV, ALU.mult, [pbo, kq(4)], [kq(17)])
            YS = Q(18)
            for _once in ([0] if use_chunk else []):
                SG = DEC
                CS = Q(24)
                ones16 = SMALL.t[:, 2:3].to_broadcast([128, 16])
                for k in range(8):
                    P.op('dve', lambda e, k=k: e.tensor_tensor_scan(out=CS[:, k, :], data0=ones16, data1=SG[:, k, :], initial=0.0,
                                                                op0=ALU.mult, op1=ALU.add), [kq(5), SMALL], [kq(24)])
                PEXC, PINC, PINV, BP, KP = Q(0), Q(1), Q(11), Q(27), Q(28)
                tt('dve', PEXC, CS, SG, ALU.subtract, [kq(24), kq(5)], [kq(0)])
                act(PEXC, PEXC, AF.Exp, [kq(0)], [kq(0)], scale=-WDEC)
                act(PINC, CS, AF.Exp, [kq(24)], [kq(1)], scale=-WDEC)
                act(PINV, CS, AF.Exp, [kq(24)], [kq(11)], scale=WDEC)
                PLb = PINC[:, :, 15:16].to_broadcast([128, 8, 16])
                ARv = B1_.t[:, 3328:3584].rearrange("p (k a n) -> p k a n", a=2, n=16)
                kAR = (B1_, ('qb', 0))
                BTq = B1_.t[:, 3584:3712].rearrange("p (k n) -> p k n", n=16)
                kBT = (B1_, ('qb', 2))
                KTq = B1_.t[:, 3712:3840].rearrange("p (k n) -> p k n", n=16)
                kKT = (B1_, ('qb', 3))
                BDq = B1_.t[:, 3840:3968].rearrange("p (k n) -> p k n", n=16)
                kBD = (B1_, ('qb', 4))
                KDq = B0.t[:, 768:896].rearrange("p (k n) -> p k n", n=16)
                kKD = (B0, ('qb', 5))
                VBq = B0.t[:, 896:1024].rearrange("p (k n) -> p k n", n=16)
                kVB = (B0, ('qb', 6))
                tt('dve', ARv[:, :, 0, :], AT, PEXC, ALU.mult, [kq(13), kq(0)], [kAR])
                tt('dve', ARv[:, :, 1, :], RF, PINC, ALU.mult, [kq(2), kq(1)], [kAR])
                tt('dve', BP, BT, PINV, ALU.mult, [kq(14), kq(11)], [kq(27)])
                tt('dve', KP, K2, PINV, ALU.mult, [kq(15), kq(11)], [kq(28)])
                cp('pool', BTq, BP, [kq(27)], [kBT])
                cp('pool', KTq, KP, [kq(28)], [kKT])
                tt('pool', BDq, BP, PLb, ALU.mult, [kq(27), kq(1)], [kBD])
                tt('pool', KDq, KP, PLb, ALU.mult, [kq(28), kq(1)], [kKD])
                cp('act', VBq, VV, [kq(4)], [kVB])
                ARm = [B0.t[:, 3584:3840].rearrange("p (k a n) -> p k a n", a=2, n=16),
                       B0.t[:, 3840:4096].rearrange("p (k a n) -> p k a n", a=2, n=16)]
                kARm = (B0, ('arm', 0))
                for hh_ in range(2):
                    ts('dve' if hh_ == 0 else 'pool', ARm[hh_], ARv, CM.t[:, CM_BLKONES, 64 * hh_:64 * hh_ + 1], None, ALU.mult, None, [kAR, CM], [kARm])
                if DBG.get('cstop', 99) < 2:
                    break
                VTM = B1_.t[0:16, 0:1024]
                BDTM = B1_.t[0:16, 1024:2048]
                KDTM = B1_.t[0:16, 2048:3072]
                kVTM, kBDTM, kKDTM = (B1_, ('tm', 0)), (B1_, ('tm', 1)), (B1_, ('tm', 2))
                for (qb, kqb, tm, ktm) in ((VBq, kVB, VTM, kVTM), (BDq, kBD, BDTM, kBDTM), (KDq, kKD, KDTM, kKDTM)):
                    pt = ps()
                    ptb = pt.t[:, :].bitcast(BF16)
                    for k in range(8):
                        P.op('pe', lambda e, ptb=ptb, qb=qb, k=k: e.transpose(ptb[0:16, k * 128:(k + 1) * 128], qb[:, k, :], CMB.t[:, CM_IDENT, :]),
                             [kqb, CMB], [pt])
                    evcp(tm, ptb[0:16, 0:1024], [pt], [ktm])
                MB = W0.t[:, 25 * 136:25 * 136 + 256].bitcast(BF16).rearrange("p (k n) -> p k n", n=64)
                kMB = (W0, ('q', 25))
                cp('act', MB, MS, [MST[j]], [kMB])
                if DBG.get('cstop', 99) < 3:
                    break
                G1 = B0.t[0:16, 1024:1536].rearrange("s (h a t) -> s h a t", a=2, t=16)
                G2 = B0.t[0:16, 1536:2048].rearrange("s (h a t) -> s h a t", a=2, t=16)
                kG1, kG2 = (B0, ('g', 1)), (B0, ('g', 2))
                ATn = B0.t[0:16, 3072:3328].rearrange("s (h t) -> s h t", t=16)
                An = B0.t[0:16, 3328:3584].rearrange("s (h t) -> s h t", t=16)
                kATn = [(B0, ('a', 0, 0)), (B0, ('a', 0, 1))]
                kAn = [(B0, ('a', 1, 0)), (B0, ('a', 1, 1))]
                ZB = B0.t[0:16, 2048:3072]
                kZB = [(B0, ('z', 0)), (B0, ('z', 1))]
                MG = HBUF.t[0:16, 0:512]
                MA = HBUF.t[0:16, 512:768]
                pg1, pg2, pga = ps(), ps(), ps()
                for h in range(16):
                    p_, hh = h // 2, h % 2
                    rows = slice(64 * hh, 64 * hh + 64)
                    arr = ARm[hh][:, p_, :, :]
                    mm(pg1.t[0:16, h * 32:h * 32 + 32], BTq[:, p_, :], arr, True, True, [kBT, kARm], [pg1])
                    mm(pg2.t[0:16, h * 32:h * 32 + 32], KTq[:, p_, :], arr, True, True, [kKT, kARm], [pg2])
                    mm(pga.t[0:16, h * 16:h * 16 + 16], ARm[hh][:, p_, 0, :], BTq[:, p_, :], True, True, [kARm, kBT], [pga])
                tt('dve', B0.t[0:16, 1024:1536], pg1.t[0:16, 0:512], MG, ALU.mult, [pg1, HBUF], [kG1])
                tt('dve', B0.t[0:16, 1536:2048], pg2.t[0:16, 0:512], MG, ALU.mult, [pg2, HBUF], [kG2])
                for hf in range(2):
                    tt('dve', B0.t[0:16, 3328 + hf * 128:3328 + hf * 128 + 128], pga.t[0:16, hf * 128:hf * 128 + 128], MA[:, hf * 128:hf * 128 + 128], ALU.mult,
                       [pga, HBUF], [kAn[hf]])
                    cp('pool', ATn[:, hf * 8:hf * 8 + 8, :], G1[:, hf * 8:hf * 8 + 8, 0, :], [kG1], [kATn[hf]])
                if DBG.get('cstop', 99) < 4:
                    break
                pw = [ps(), ps()]
                for h in range(16):
                    p_, hh = h // 2, h % 2
                    rows = slice(64 * hh, 64 * hh + 64)
                    o_ap = pw[h // 8].t[0:16, (h % 8) * 64:(h % 8) * 64 + 64]
                    mm(o_ap, ARm[hh][:, p_, 0, :], MB[:, p_, :], True, False, [kARm, kMB], [pw[h // 8]])
                    mm(o_ap, G2[:, h, 0, :], VTM[:, h * 64:h * 64 + 64], False, True, [kG2, kVTM], [pw[h // 8]])
                for half in range(2):
                    evcp(ZB[:, half * 512:(half + 1) * 512], pw[half].t[0:16, 0:512], [pw[half]], [kZB[half]])
                if DBG.get('cstop', 99) < 5:
                    break
                I16 = CMB.t[0:16, CM_IDENT, 0:16]
                for lvl in range(4):
                    pz = [ps(), ps()]
                    pa = [ps(), ps()] if lvl < 3 else None
                    for hf in range(2):
                        for h in range(hf * 8, hf * 8 + 8):
                            o_ap = pz[hf].t[0:16, (h % 8) * 64:(h % 8) * 64 + 64]
                            zsl = ZB[:, h * 64:h * 64 + 64]
                            mm(o_ap, ATn[:, h, :], zsl, True, True, [kATn[hf], kZB[hf]], [pz[hf]])
                        if lvl < 3:
                            for h in range(hf * 8, hf * 8 + 8):
                                hl = h % 8
                                mm(pa[hf].t[0:16, hl * 16:hl * 16 + 16], An[:, h, :], ATn[:, h, :], True, True, [kAn[hf], kATn[hf]], [pa[hf]])
                                mm(pa[hf].t[0:16, 128 + hl * 16:128 + hl * 16 + 16], ATn[:, h, :], An[:, h, :], True, True, [kAn[hf], kATn[hf]], [pa[hf]])
                    for hf in range(2):
                        tt('dve', ZB[:, hf * 512:(hf + 1) * 512], pz[hf].t[0:16, 0:512], ZB[:, hf * 512:(hf + 1) * 512], ALU.add,
                           [pz[hf], kZB[hf]], [kZB[hf]])
                        if lvl < 3:
                            cp('act', B0.t[0:16, 3072 + hf * 128:3072 + hf * 128 + 128], pa[hf].t[0:16, 0:128], [pa[hf]], [kATn[hf]])
                            cp('act', B0.t[0:16, 3328 + hf * 128:3328 + hf * 128 + 128], pa[hf].t[0:16, 128:256], [pa[hf]], [kAn[hf]])
                if DBG.get('cstop', 99) < 6:
                    break
                py = [ps(), ps()]
                for h in range(16):
                    p_, hh = h // 2, h % 2
                    rows = slice(64 * hh, 64 * hh + 64)
                    o_ap = py[h // 8].t[0:16, (h % 8) * 64:(h % 8) * 64 + 64]
                    mm(o_ap, ARm[hh][:, p_, 1, :], MB[:, p_, :], True, False, [kARm, kMB], [py[h // 8]])
                    mm(o_ap, G1[:, h, 1, :], ZB[:, h * 64:h * 64 + 64], False, False, [kG1, kZB[h // 8]], [py[h // 8]])
                    mm(o_ap, G2[:, h, 1, :], VTM[:, h * 64:h * 64 + 64], False, True, [kG2, kVTM], [py[h // 8]])
                YTB = B0.t[0:16, 3072:4096]
                kYTB = (B0, ('ytb', 0))
                for half in range(2):
                    evcp(YTB[:, half * 512:(half + 1) * 512], py[half].t[0:16, 0:512], [py[half]], [kYTB, kAn[0], kAn[1], kATn[0], kATn[1], kARm])
                pyt = ps()
                pytb = pyt.t[:, :].bitcast(BF16)
                for k in range(8):
                    P.op('pe', lambda e, pytb=pytb, k=k: e.transpose(pytb[:, k * 16:(k + 1) * 16], YTB[:, k * 128:(k + 1) * 128], CMB.t[0:16, CM_IDENT, 0:16]),
                         [kYTB, kAn[0], kAn[1], kATn[0], kATn[1], kARm, CMB], [pyt])
                cp('act', YS, pytb[:, 0:128].rearrange("p (k n) -> p k n", n=16), [pyt], [kq(18)])
                if DBG.get('cstop', 99) < 7:
                    break
                psu = [ps(), ps()]
                for p_ in range(8):
                    o_ap = psu[p_ // 4].t[:, (p_ % 4) * 128:(p_ % 4) * 128 + 128]
                    mm(o_ap, BDTM[:, p_ * 128:(p_ + 1) * 128], ZB[:, p_ * 128:(p_ + 1) * 128], True, False, [kBDTM, kZB[p_ // 4]], [psu[p_ // 4]])
                    mm(o_ap, KDTM[:, p_ * 128:(p_ + 1) * 128], VTM[:, p_ * 128:(p_ + 1) * 128], False, True, [kKDTM, kVTM], [psu[p_ // 4]])
                tt('dve', MS, MS, PINC[:, :, 15:16].to_broadcast([128, 8, 64]), ALU.mult, [MST[j], kq(1)], [MST[j]])
                for g2 in range(2):
                    for hh in range(2):
                        rows = slice(64 * hh, 64 * hh + 64)
                        src = psu[g2].t[rows, 0:512].rearrange("p (q c) -> p q c", c=128)[:, :, hh * 64:hh * 64 + 64]
                        tt('dve', MS[rows, 4 * g2:4 * g2 + 4, :], MS[rows, 4 * g2:4 * g2 + 4, :], src, ALU.add, [MST[j], psu[g2]], [MST[j]])
            if not use_chunk:
                TM = []
                qlist = [(AT, kq(13)), (DEC, kq(5)), (BT, kq(14)), (K2, kq(15)), (RF, kq(2)), (None, None)]
                qbs = []
                for qi, (src, ksrc) in enumerate(qlist):
                    if qi < 5:
                        qb = B1_.t[:, 3328 + qi * 128:3328 + qi * 128 + 128].rearrange("p (k n) -> p k n", n=16)
                        kqb = (B1_, ('qb', qi))
                    else:
                        qb = B0.t[:, 768:896].rearrange("p (k n) -> p k n", n=16)
                        kqb = (B0, ('qb', qi))
                    qbs.append((qb, kqb))
                    if qi == 5:
                        tt('dve', Q(24), DEC, qbs[1][0], ALU.subtract, [kq(5), qbs[1][1]], [kq(24)])
                        cp('pool', qb, Q(24), [kq(24)], [kqb])
                    else:
                        cp('pool' if qi % 2 else 'act', qb, src, [ksrc], [kqb])
                    pt = ps()
                    ptb = pt.t[:, :].bitcast(BF16)
                    for k in range(8):
                        P.op('pe', lambda e, ptb=ptb, qb=qb, k=k: e.transpose(ptb[0:16, k * 128:(k + 1) * 128], qb[:, k, :], CMB.t[:, CM_IDENT, :]),
                             [kqb, CMB], [pt])
                    if qi < 3:
                        tm = B1_.t[0:16, qi * 1024:(qi + 1) * 1024]
                        ktm = (B1_, ('tm', qi))
                    else:
                        tm = B0.t[0:16, 1024 + (qi - 3) * 1024:1024 + (qi - 2) * 1024]
                        ktm = (B0, ('tm', qi))
                    evcp(tm, ptb[0:16, 0:1024], [pt], [ktm])
                    TM.append((tm, ktm))
                for t in range(16):
                    if sample:
                        Sb = SMP[0]
                        S3 = Sb.t[:, (t % 2) * 512:(t % 2) * 512 + 512].rearrange("p (k n) -> p k n", n=64)
                        for hh in range(2):
                            P.dma(S3[hh * 64:(hh + 1) * 64], s_rwkv[j, t].rearrange("(p hh) i jj -> hh i p jj", hh=2)[hh], writes=[Sb])
                        kS = Sb
                    else:
                        S3 = MS
                        kS = MST[j]
                    pq = []
                    for qi in range(5):
                        pbq = ps()
                        srcs = [TM[qi]] + ([TM[5]] if qi == 1 else [])
                        nmm = 2 * len(srcs)
                        im = 0
                        for (tm, ktm) in srcs:
                            tm4 = tm.rearrange("s (p hh jj) -> s p hh jj", hh=2, jj=64)
                            for hh in range(2):
                                mm(pbq.t[:, 0:512], SELH.t[0:16, t, hh, :], tm4[:, :, hh, :], im == 0, im == nmm - 1, [SELH, ktm], [pbq])
                                im += 1
                        pq.append(pbq)

                    def q3(i):
                        return pq[i].t[:, 0:512].rearrange("p (k n) -> p k n", n=64)
                    T1b = scr()
                    T1 = T1b.t[:, 0:512].rearrange("p (k n) -> p k n", n=64)
                    SAb = scr()
                    SA = SAb.t[:, 0:8]
                    tt('dve', T1, S3, q3(0), ALU.mult, [kS, pq[0]], [T1b])
                    P.op('dve', lambda e, SA=SA, T1=T1: e.tensor_reduce(out=SA, in_=T1, axis=AX.X, op=ALU.add), [T1b], [SAb])
                    tt('dve', S3, S3, q3(1), ALU.mult, [kS, pq[1]], [kS])
                    T2b = scr()
                    T2 = T2b.t[:, 0:512].rearrange("p (k n) -> p k n", n=64)
                    tt('dve', T2, q3(2), SAb.t[:, 0:8].rearrange("p (k o) -> p k o", o=1).to_broadcast([128, 8, 64]), ALU.mult, [pq[2], SAb], [T2b])
                    tt('pool', S3, S3, T2, ALU.add, [kS, T2b], [kS])
                    T3b = scr()
                    T3 = T3b.t[:, 0:512].rearrange("p (k n) -> p k n", n=64)
                    tt('dve', T3, q3(3), VV[:, :, t:t + 1].to_broadcast([128, 8, 64]), ALU.mult, [pq[3], kq(4)], [T3b])
                    tt('pool', S3, S3, T3, ALU.add, [kS, T3b], [kS])
                    T4b = scr()
                    T4 = T4b.t[:, 0:512].rearrange("p (k n) -> p k n", n=64)
                    tt('dve', T4, S3, q3(4), ALU.mult, [kS, pq[4]], [T4b])
                    P.op('dve', lambda e, T4=T4, t=t: e.tensor_reduce(out=YS[:, :, t], in_=T4, axis=AX.X, op=ALU.add), [T4b], [kq(18)])
                    if sample:
                        for hh in range(2):
                            P.dma(o_rwkvs[j, t].rearrange("(p hh) i jj -> hh i p jj", hh=2)[hh], S3[hh * 64:(hh + 1) * 64], reads=[Sb], eng='pool')
            tt('pool', Q(19), YS, YS, ALU.mult, [kq(18)], [kq(19)])
            pm = ps()
            mm(pm.t[:, 0:128], CM.t[:, CM_BLKMEAN64, :], W0.t[:, 18 * 136:18 * 136 + 128], True, True, [CM, kq(18)], [pm])
            mm(pm.t[:, 128:256], CM.t[:, CM_BLKMEAN64, :], W0.t[:, 19 * 136:19 * 136 + 128], True, True, [CM, kq(19)], [pm])
            ME = Q(20)
            VA = Q(21)
            cp('act', ME, p3(pm), [pm], [kq(20)])
            tt('dve', VA, ME, ME, ALU.mult, [kq(20)], [kq(21)])
            tt('dve', VA, pm.t[:, 128:256].rearrange("p (k n) -> p k n", n=16), VA, ALU.subtract, [pm, kq(21)], [kq(21)])
            ts('dve', VA, VA, 0.0, None, ALU.max, None, [kq(21)], [kq(21)])
            act(VA, VA, AF.Sqrt, [kq(21), SMALL], [kq(21)], bias=GNEPS)
            P.op('dve', lambda e, VA=VA: e.reciprocal(out=VA, in_=VA), [kq(21)], [kq(21)])
            OO = Q(22)
            tt('dve', OO, YS, ME, ALU.subtract, [kq(18), kq(20)], [kq(22)])
            tt('dve', OO, OO, VA, ALU.mult, [kq(22), kq(21)], [kq(22)])
            tt('pool', OO, OO, bc('gng%d' % j), ALU.mult, [kq(22), VEC8], [kq(22)])
            tt('pool', OO, OO, bc('gnb%d' % j), ALU.add, [kq(22), VEC8], [kq(22)])
            tt('pool', OO, OO, BON, ALU.add, [kq(22), kq(17)], [kq(22)])
            OB = B1_.t[:, 3968:4096].rearrange("p (k n) -> p k n", n=16)
            kOB = (B1_, 'ob')
            tt('dve', OB, OO, GG, ALU.mult, [kq(22), kq(7)], [kOB])
            po_ = ps()
            for oc in range(8):
                for k in range(8):
                    mm(po_.t[:, oc * 16:oc * 16 + 16], WOv[:, k, oc * 128:oc * 128 + 128], OB[:, k, :], k == 0, k == 7, [H, kOB], [po_])
            tt('dve', X.t[:, :, c0:c0 + 16], X.t[:, :, c0:c0 + 16], p3(po_), ALU.add, [X, po_], [X])
        W0.fence()
        B0.fence()
        B1_.fence()
        BIG1.fence()
        BIG2.fence()
        H.fence()

    ctx['rwkv_layer'] = rwkv_layer


def _prep_shared(inp):
    c = build_consts()
    sh = dict(cm=c['cm'], rope=c['rope'], rmask=c['rmask'], decin=c['decin'], deck=c['deck'], mask4=c['mask4'],
              maska=c['maska'], reset=c['reset'], sel=c['sel'], small=c['small'])
    selh = np.zeros((16, 16, 2, 128), np.float32)
    for t in range(16):
        selh[t, t, 0, :64] = 1.0
        selh[t, t, 1, 64:] = 1.0
    sh['selh'] = selh
    i16 = np.arange(16)
    strictT = (i16[:, None] < i16[None, :]).astype(np.float32)
    inclT = (i16[:, None] <= i16[None, :]).astype(np.float32)
    cm_ = np.zeros((16, 768), np.float32)
    cm_[:, 0:512] = np.tile(np.concatenate([strictT, inclT], axis=1), (1, 16))
    cm_[:, 512:768] = np.tile((i16[:, None] > i16[None, :]).astype(np.float32), (1, 16))
    sh['cmask'] = cm_
    f = lambda a: np.ascontiguousarray(np.asarray(a, np.float32))
    vec8 = np.zeros((128, NV8, 8), np.float32)

    def put8(name, v):
        vec8[:, V8[name], :] = fm(v, 8)
    for li in range(4):
        put8('nmix%d' % li, inp['norm_mix_g'][li])
        put8('nffn%d' % li, inp['norm_ffn_g'][li])
    put8('nfinal', inp['norm_final_g'])
    for j in range(2):
        put8('gn%d' % j, inp['ev_ret_gn_g'][j])
        for m in range(4):
            put8('cw%d_%d' % (m, j), inp['ev_lru_conv_w'][j][m])
        put8('cb%d' % j, inp['ev_lru_conv_b'][j])
        put8('ba%d' % j, inp['ev_lru_ba'][j])
        put8('bx%d' % j, inp['ev_lru_bx'][j])
        put8('lam%d' % j, inp['ev_lru_lambda'][j])
        for m in range(6):
            put8('mix%d_%d' % (j, m), inp['od_mix'][j][m])
        put8('w0_%d' % j, inp['od_w0'][j])
        put8('a0_%d' % j, inp['od_a0'][j])
        put8('kk%d' % j, inp['od_k_k'][j])
        put8('ka%d' % j, inp['od_k_a'][j])
        put8('rk%d' % j, np.asarray(inp['od_r_k'][j]).reshape(-1))
        put8('gng%d' % j, inp['od_gn_g'][j])
        put8('gnb%d' % j, inp['od_gn_b'][j])
    put8('v0_0', inp['od_v0'][0])
    vec22 = np.zeros((128, NV22, 22), np.float32)
    for li in range(4):
        for m in range(3):
            vec22[:, V22['fw%d_%d' % (m, li)], :] = fm(inp['ff_conv_w'][li][m], 22)
        vec22[:, V22['fb%d' % li], :] = fm(inp['ff_conv_b'][li], 22)
    sh['vec8'] = vec8
    sh['vec22'] = vec22
    w_in = f(inp['ev_w_in'])
    perm = np.array([(c // 128) * 128 + ((c % 128) + 64) % 128 for c in range(1024)])
    sh['w_in'] = w_in
    sh['w_swap'] = np.ascontiguousarray(w_in[:, :, perm])
    sh['w_out'] = f(inp['ev_w_out'])
    wa = np.asarray(inp['ev_lru_wa'], np.float32).transpose(0, 2, 1, 3)
    wx = np.asarray(inp['ev_lru_wx'], np.float32).transpose(0, 2, 1, 3)
    sh['wax'] = np.ascontiguousarray(np.stack([wa, wx], axis=2))
    sh['w_r'] = f(inp['od_w_r'])
    sh['w_k'] = f(inp['od_w_k'])
    sh['w_v'] = f(inp['od_w_v'])
    sh['w_o'] = f(inp['od_w_o'])
    sh['w1'] = f(inp['od_w1'])
    sh['w2'] = f(inp['od_w2'])
    sh['a1'] = f(inp['od_a1'])
    sh['a2'] = f(inp['od_a2'])
    sh['v1'] = f(inp['od_v1'])
    sh['v2'] = f(inp['od_v2'])
    sh['g1'] = f(inp['od_g1'])
    sh['g2'] = f(inp['od_g2'])
    sh['w_up'] = f(inp['ff_w_up'])
    sh['w_down'] = f(inp['ff_w_down'])
    return sh


def kernel(**inp):
    if 'nc' not in _CACHE:
        _CACHE['nc'] = build_program()
    nc = _CACHE['nc']
    sh = _prep_shared(inp)
    xp = np.asarray(inp['x_prompt'], np.float32)
    xs = np.asarray(inp['x_sample'], np.float32)
    meta = np.asarray(inp['meta_tokens'], np.float32)
    in_maps = []
    NCORES = DBG['ncores']
    for c in range(NCORES):
        sl = slice(16 * c, 16 * c + 16)
        m = dict(sh)
        xall = np.concatenate([meta, xp[c], xs[sl, 0, :]], axis=0)
        m['xT'] = np.ascontiguousarray(xall.T)
        m['s_ret'] = np.ascontiguousarray(np.asarray(inp['state_ret'], np.float32)[:, sl])
        m['s_lru'] = np.ascontiguousarray(np.asarray(inp['state_lru'], np.float32)[:, sl].reshape(2, 16, 8, 128).transpose(0, 3, 2, 1))
        m['s_lconv'] = np.ascontiguousarray(np.asarray(inp['state_lru_conv'], np.float32)[:, sl].reshape(2, 16, 3, 8, 128).transpose(0, 4, 3, 2, 1))
        m['s_rwkv'] = np.ascontiguousarray(np.asarray(inp['state_rwkv'], np.float32)[:, sl])
        m['s_shift'] = np.ascontiguousarray(np.asarray(inp['state_shift'], np.float32)[:, sl].reshape(2, 16, 8, 128).transpose(0, 3, 2, 1))
        m['s_ffn'] = np.ascontiguousarray(np.asarray(inp['state_ffn_conv'], np.float32)[:, sl].reshape(4, 16, 2, 22, 128).transpose(0, 4, 3, 2, 1))
        in_maps.append(m)
    res = run_bass_kernel_spmd(nc, in_maps, core_ids=list(range(NCORES)))
    R = res.results
    y_p = np.zeros((8, 2048, 1024), np.float32)
    y_s = np.zeros((128, 1, 1024), np.float32)
    ret_p = np.zeros((2, 8, 4, 128, 256), np.float32)
    lru_p = np.zeros((2, 8, 1024), np.float32)
    lconv_p = np.zeros((2, 8, 3, 1024), np.float32)
    rwkv_p = np.zeros((2, 8, 16, 64, 64), np.float32)
    shift_p = np.zeros((2, 8, 1024), np.float32)
    ffn_p = np.zeros((4, 8, 2, 2816), np.float32)
    ret_s = np.zeros((2, 128, 4, 128, 256), np.float32)
    lru_s = np.zeros((2, 128, 1024), np.float32)
    lconv_s = np.zeros((2, 128, 3, 1024), np.float32)
    rwkv_s = np.zeros((2, 128, 16, 64, 64), np.float32)
    shift_s = np.zeros((2, 128, 1024), np.float32)
    ffn_s = np.zeros((4, 128, 2, 2816), np.float32)
    for c in range(NCORES):
        r = R[c]
        sl = slice(16 * c, 16 * c + 16)
        yT = np.asarray(r['yT'])
        y_p[c] = yT[:, 16:2064].T
        y_s[sl, 0, :] = yT[:, 2064:].T
        ret_p[:, c] = np.asarray(r['o_retp'])
        lru_p[:, c] = np.asarray(r['o_lrup']).transpose(1, 2, 0).reshape(2, 1024)
        lconv_p[:, c] = np.asarray(r['o_lconvp']).transpose(1, 3, 2, 0).reshape(2, 3, 1024)
        if DBG.get('chunk', True):
            rwkv_p[:, c] = np.asarray(r['o_rwkvp']).reshape(2, 2, 64, 8, 64).transpose(0, 3, 1, 4, 2).reshape(2, 16, 64, 64)
        else:
            rwkv_p[:, c] = np.asarray(r['o_rwkvp']).reshape(2, 2, 64, 8, 64).transpose(0, 3, 1, 2, 4).reshape(2, 16, 64, 64)
        shift_p[:, c] = np.asarray(r['o_shiftp']).transpose(1, 2, 0).reshape(2, 1024)
        ffn_p[:, c] = np.asarray(r['o_ffnp']).transpose(1, 3, 2, 0).reshape(4, 2, 2816)
        ret_s[:, sl] = np.asarray(r['o_rets'])
        lru_s[:, sl] = np.asarray(r['o_lrus']).transpose(0, 3, 2, 1).reshape(2, 16, 1024)
        lconv_s[:, sl] = np.asarray(r['o_lconvs']).transpose(0, 4, 3, 2, 1).reshape(2, 16, 3, 1024)
        rwkv_s[:, sl] = np.asarray(r['o_rwkvs'])
        shift_s[:, sl] = np.asarray(r['o_shifts']).transpose(0, 3, 2, 1).reshape(2, 16, 1024)
        ffn_s[:, sl] = np.asarray(r['o_ffns']).transpose(0, 4, 3, 2, 1).reshape(4, 16, 2, 2816)
    return (y_p, y_s, ret_p, lru_p, lconv_p, rwkv_p, shift_p, ffn_p, ret_s, lru_s, lconv_s, rwkv_s, shift_s, ffn_s)
```

```python
import math
import numpy as np
from contextlib import ExitStack
import concourse.bass as bass
import concourse.mybir as mybir
from concourse.bass_utils import run_bass_kernel_spmd

F32 = mybir.dt.float32
BF16 = mybir.dt.bfloat16
AF = mybir.ActivationFunctionType
ALU = mybir.AluOpType
AX = mybir.AxisListType

SEM_CAP = 30000
N_DMA_SEMS = 16

D = 1024
TC = 1040
NTOK = 2080
DFF = 2816
NFC = 22
EPS = 1e-6
GN_EPS_C = 64e-5
WDEC = math.exp(-0.5)
DK_SCALE = 128 ** -0.5
GAMMA = [1.0 - 2.0 ** (-5 - h) for h in range(4)]
TOK_TILES = [(0, 512), (512, 512), (1024, 16)]


class Buf:
    def __init__(self, t, name):
        self.t = t
        self.name = name
        self.st = {'_all': [[], []]}

    def __getitem__(self, idx):
        return self.t[idx]

    def states(self, key):
        if key is None:
            return list(self.st.values())
        if key not in self.st:
            a = self.st['_all']
            self.st[key] = [list(a[0]), list(a[1])]
        return [self.st[key]]

    def fence(self):
        w = []
        r = []
        for st in self.st.values():
            w.extend(st[0])
            r.extend(st[1])
        self.st = {'_all': [w, r]}


class Prog:
    def __init__(self, nc, es):
        self.nc = nc
        self.es = es
        self.engs = ['pe', 'act', 'dve', 'pool', 'sp']
        self.q = {e: [] for e in self.engs}
        self.cur_sem = {}
        self.cnt = {}
        self.nsem = 0
        for e in ['pe', 'act', 'dve', 'pool']:
            self._new_sem(e)
        self.waited = {e: {} for e in self.engs}
        self.dma_sems = [es.enter_context(nc.semaphore("dq%d" % i)) for i in range(2 * N_DMA_SEMS)]
        self.dma_cnt = [0] * (2 * N_DMA_SEMS)
        self.dma_n = 0
        self.dma_ne = {'sp': 0, 'pool': 0}
        self.nbuf = 0
        self.n_ops = 0
        self.n_waits = 0
        self.psi = 0

    def _new_sem(self, e):
        s = self.es.enter_context(self.nc.semaphore("tl_%s_%d" % (e, self.nsem)))
        self.nsem += 1
        self.cur_sem[e] = s
        self.cnt[e] = 0

    def sbuf(self, shape, dt, name=None):
        self.nbuf += 1
        name = "%s_%d" % (name or "sb", self.nbuf)
        t = self.es.enter_context(self.nc.sbuf_tensor(name, list(shape), dt))
        return Buf(t, name)

    def psum(self, shape, dt, name=None):
        self.nbuf += 1
        name = "%s_%d" % (name or "ps", self.nbuf)
        t = self.es.enter_context(self.nc.psum_tensor(name, list(shape), dt))
        return Buf(t, name)

    def _deps(self, eng, reads, writes):
        deps = []
        for (b, k) in reads:
            for st in b.states(k):
                deps.extend(st[0])
        for (b, k) in writes:
            for st in b.states(k):
                deps.extend(st[0])
                deps.extend(st[1])
        best = {}
        for (sem, val, seng) in deps:
            if eng == 'pe' and seng == 'pe':
                continue
            kk = id(sem)
            if kk not in best or best[kk][1] < val:
                best[kk] = (sem, val)
        waits = []
        w = self.waited[eng]
        for kk, (sem, val) in best.items():
            if w.get(kk, 0) >= val:
                continue
            w[kk] = val
            waits.append((sem, val))
        return waits

    @staticmethod
    def _compact(lst):
        best = {}
        for t in lst:
            kk = id(t[0])
            if kk not in best or best[kk][1] < t[1]:
                best[kk] = t
        return list(best.values())

    def _record(self, tok, reads, writes):
        for (b, k) in reads:
            for st in b.states(k):
                st[1].append(tok)
                if len(st[1]) > 48:
                    st[1] = self._compact(st[1])
        for (b, k) in writes:
            for st in b.states(k):
                st[0] = [tok]
                st[1] = []

    @staticmethod
    def _norm(lst):
        out = []
        for x in lst:
            if isinstance(x, tuple):
                out.append(x)
            else:
                out.append((x, None))
        return out

    def op(self, eng, fn, reads=(), writes=()):
        reads = self._norm(reads)
        writes = self._norm(writes)
        waits = self._deps(eng, reads, writes)
        if self.cnt[eng] >= SEM_CAP:
            self._new_sem(eng)
        self.cnt[eng] += 1
        tok = (self.cur_sem[eng], self.cnt[eng], eng)
        self.q[eng].append((waits, fn, (tok[0], 1)))
        self._record(tok, reads, writes)
        self.n_ops += 1
        self.n_waits += len(waits)
        return tok

    def dma(self, out_ap, in_ap, reads=(), writes=(), eng='sp', **kw):
        reads = self._norm(reads)
        writes = self._norm(writes)
        waits = self._deps(eng, reads, writes)
        i = (self.dma_ne[eng] % N_DMA_SEMS) + (N_DMA_SEMS if eng == 'pool' else 0)
        self.dma_ne[eng] += 1
        self.dma_n += 1
        sem = self.dma_sems[i]
        prev = self.dma_cnt[i]
        w = self.waited[eng]
        if prev > 0 and w.get(id(sem), 0) < prev:
            waits.append((sem, prev))
            w[id(sem)] = prev
        self.dma_cnt[i] += 16
        tok = (sem, self.dma_cnt[i], 'dma')

        def fn(e, out_ap=out_ap, in_ap=in_ap, kw=kw):
            return e.dma_start(out=out_ap, in_=in_ap, **kw)
        self.q[eng].append((waits, fn, (sem, 16)))
        self._record(tok, reads, writes)
        self.n_ops += 1
        self.n_waits += len(waits)
        return tok

    def finish(self):
        nc = self.nc
        waits = []
        for i, s in enumerate(self.dma_sems):
            if self.dma_cnt[i] > 0:
                waits.append((s, self.dma_cnt[i]))
        for en in ['pe', 'act', 'dve', 'pool']:
            if self.cnt[en] > 0:
                waits.append((self.cur_sem[en], self.cnt[en]))
        self.q['sp'].append((waits, None, None))
        qs = self.q
        with nc.Block() as block:
            def run(engine, lst):
                for (waits, fn, inc) in lst:
                    for (sem, val) in waits:
                        engine.wait_ge(sem, val)
                    if fn is not None:
                        ins = fn(engine)
                        if inc is not None:
                            ins.then_inc(inc[0], inc[1])

            @block.tensor
            def _(t):
                run(t, qs['pe'])

            @block.scalar
            def _(t):
                run(t, qs['act'])

            @block.vector
            def _(t):
                run(t, qs['dve'])

            @block.gpsimd
            def _(t):
                run(t, qs['pool'])

            @block.sync
            def _(t):
                run(t, qs['sp'])


V8_NAMES = []
for _li in range(4):
    V8_NAMES += ['nmix%d' % _li, 'nffn%d' % _li]
V8_NAMES += ['nfinal']
for _j in range(2):
    V8_NAMES += ['gn%d' % _j, 'cw0_%d' % _j, 'cw1_%d' % _j, 'cw2_%d' % _j, 'cw3_%d' % _j, 'cb%d' % _j,
                 'ba%d' % _j, 'bx%d' % _j, 'lam%d' % _j]
for _j in range(2):
    V8_NAMES += ['mix%d_%d' % (_j, m) for m in range(6)]
    V8_NAMES += ['w0_%d' % _j, 'a0_%d' % _j, 'kk%d' % _j, 'ka%d' % _j, 'rk%d' % _j, 'gng%d' % _j, 'gnb%d' % _j]
V8_NAMES += ['v0_0']
V8 = {n: i for i, n in enumerate(V8_NAMES)}
NV8 = len(V8_NAMES)
V22_NAMES = []
for _li in range(4):
    V22_NAMES += ['fw0_%d' % _li, 'fw1_%d' % _li, 'fw2_%d' % _li, 'fb%d' % _li]
V22 = {n: i for i, n in enumerate(V22_NAMES)}
NV22 = len(V22_NAMES)

CM_MEAN1024, CM_MEAN256, CM_BLKMEAN64, CM_IDENT, CM_BLKONES = range(5)
NCM = 5


def fm(v, nch):
    return np.ascontiguousarray(np.asarray(v, np.float32).reshape(nch, 128).T)


def build_consts():
    cm = np.zeros((128, NCM, 128), np.float32)
    cm[:, CM_MEAN1024, :] = 1.0 / 1024
    cm[:, CM_MEAN256, :] = 1.0 / 256
    blk = np.zeros((128, 128), np.float32)
    blk[:64, :64] = 1.0
    blk[64:, 64:] = 1.0
    cm[:, CM_BLKMEAN64, :] = blk / 64.0
    cm[:, CM_IDENT, :] = np.eye(128, dtype=np.float32)
    cm[:, CM_BLKONES, :] = blk
    inv = (10000.0 ** (-np.linspace(0.0, 1.0, 64, dtype=np.float32))).astype(np.float32)
    pos = np.concatenate([np.arange(2064, dtype=np.float32), np.full(16, 16384.0, np.float32)])
    ang = (pos[None, :] * inv[:, None]).astype(np.float32)
    cos = np.cos(ang.astype(np.float64)).astype(np.float32)
    sin = np.sin(ang.astype(np.float64)).astype(np.float32)
    rope = np.zeros((128, 2, NTOK), np.float32)
    rope[:64, 0] = cos
    rope[64:, 0] = cos
    rope[:64, 1] = -sin
    rope[64:, 1] = sin
    idx = np.arange(128)
    diff = idx[None, :] - idx[:, None]
    rmask = np.zeros((128, 4, 128), np.float32)
    decin = np.zeros((128, 4, 128), np.float32)
    deck = np.zeros((128, 2, 4), np.float32)
    for h in range(4):
        lg = math.log1p(-2.0 ** (-5 - h))
        rmask[:, h, :] = np.where(diff >= 0, DK_SCALE * np.exp(lg * np.maximum(diff, 0)), 0.0)
        decin[:, h, :] = np.exp(lg * (idx + 1.0))[None, :]
        deck[:, 0, h] = DK_SCALE * np.exp(lg * (127.0 - idx))
        deck[:16, 1, h] = DK_SCALE * np.exp(lg * (15.0 - idx[:16]))
    i64 = np.arange(64)
    strictT = (i64[:, None] < i64[None, :]).astype(np.float32)
    inclT = (i64[:, None] <= i64[None, :]).astype(np.float32)
    mask4 = np.zeros((128, 4, 64), np.float32)
    mask4[:64, 0] = strictT
    mask4[:64, 1] = inclT
    mask4[:64, 2] = strictT
    mask4[:64, 3] = inclT
    maska = np.zeros((128, 64), np.float32)
    maska[:64] = (i64[:, None] > i64[None, :]).astype(np.float32)
    reset = np.ones((128, 8, 64), np.float32)
    reset[:, :, 0] = 0.0
    sel = np.zeros((128, 16, 128), np.float32)
    for b in range(16):
        sel[b, b, :] = 1.0
    small = np.zeros((128, 8), np.float32)
    small[:, 0] = EPS
    small[:, 1] = GN_EPS_C
    small[:, 2] = 1.0
    small[:, 3] = 0.0
    return dict(cm=cm, rope=rope, rmask=rmask, decin=decin, deck=deck, mask4=mask4, maska=maska,
                reset=reset, sel=sel, small=small)


_CACHE = {}
DBG = {'layers': [0, 1, 2, 3], 'even': True, 'rwkv': True, 'ffn': True, 'segs': [0, 1], 'ncores': 8,
       'evp': ['qk', 'v', 'ga', 'chunks', 'sample', 'wouta', 'lru', 'woutb', 'post', 'supd']}


def build_program():
    nc = bass.Bass("TRN2", target_bir_lowering=False)

    def din(name, shape):
        return nc.dram_tensor(name, list(shape), F32, kind="ExternalInput").ap()

    def dout(name, shape):
        return nc.dram_tensor(name, list(shape), F32, kind="ExternalOutput").ap()

    xT = din("xT", [D, NTOK])
    d_cm = din("cm", [128, NCM, 128])
    d_rope = din("rope", [128, 2, NTOK])
    d_rmask = din("rmask", [128, 4, 128])
    d_decin = din("decin", [128, 4, 128])
    d_deck = din("deck", [128, 2, 4])
    d_mask4 = din("mask4", [128, 4, 64])
    d_maska = din("maska", [128, 64])
    d_reset = din("reset", [128, 8, 64])
    d_sel = din("sel", [128, 16, 128])
    d_small = din("small", [128, 8])
    d_selh = din("selh", [16, 16, 2, 128])
    d_cmask = din("cmask", [16, 768])
    d_vec8 = din("vec8", [128, NV8, 8])
    d_vec22 = din("vec22", [128, NV22, 22])
    w_in = din("w_in", [2, D, 5120])
    w_swap = din("w_swap", [2, D, 1024])
    w_out = din("w_out", [2, 2048, D])
    d_wax = din("wax", [2, 128, 2, 8, 128])
    w_r = din("w_r", [2, D, D])
    w_k = din("w_k", [2, D, D])
    w_v = din("w_v", [2, D, D])
    w_o = din("w_o", [2, D, D])
    d_w1 = din("w1", [2, D, 64])
    d_w2 = din("w2", [2, 64, D])
    d_a1 = din("a1", [2, D, 64])
    d_a2 = din("a2", [2, 64, D])
    d_v1 = din("v1", [1, D, 32])
    d_v2 = din("v2", [1, 32, D])
    d_g1 = din("g1", [2, D, 128])
    d_g2 = din("g2", [2, 128, D])
    w_up = din("w_up", [4, D, 2 * DFF])
    w_down = din("w_down", [4, 8, 128, NFC * 128])
    s_ret = din("s_ret", [2, 16, 4, 128, 256])
    s_lru = din("s_lru", [2, 128, 8, 16])
    s_lconv = din("s_lconv", [2, 128, 8, 3, 16])
    s_rwkv = din("s_rwkv", [2, 16, 16, 64, 64])
    s_shift = din("s_shift", [2, 128, 8, 16])
    s_ffn = din("s_ffn", [4, 128, 22, 2, 16])

    yT = dout("yT", [D, NTOK])
    o_retp = dout("o_retp", [2, 4, 128, 256])
    o_lrup = dout("o_lrup", [128, 2, 8])
    o_lconvp = dout("o_lconvp", [128, 2, 8, 3])
    o_rwkvp = dout("o_rwkvp", [2, 128, 8, 64])
    o_shiftp = dout("o_shiftp", [128, 2, 8])
    o_ffnp = dout("o_ffnp", [128, 4, 22, 2])
    o_rets = dout("o_rets", [2, 16, 4, 128, 256])
    o_lrus = dout("o_lrus", [2, 128, 8, 16])
    o_lconvs = dout("o_lconvs", [2, 128, 8, 3, 16])
    o_rwkvs = dout("o_rwkvs", [2, 16, 16, 64, 64])
    o_shifts = dout("o_shifts", [2, 128, 8, 16])
    o_ffns = dout("o_ffns", [4, 128, 22, 2, 16])

    with ExitStack() as es:
        P = Prog(nc, es)
        X = P.sbuf([128, 8, TC], F32, "X")
        H = P.sbuf([128, 8 * TC], BF16, "H")
        VFD = Buf(None, "vfd")
        vf_dram = nc.dram_tensor("vf_scratch", [128, 8, NTOK], F32, kind="Internal").ap()
        WST0 = P.sbuf([128, 4096], F32, "WST0")
        WST = [WST0, WST0]
        WBF = [P.sbuf([128, 4096], BF16, "WBF%d" % i) for i in range(2)]
        BIG1 = P.sbuf([128, 22 * TC], BF16, "BIG1")
        BIG2 = P.sbuf([128, 9 * 1024], BF16, "BIG2")
        HBUF = P.sbuf([128, 1044], F32, "HBUF")
        SCR = [P.sbuf([128, 512], F32, "SCR%d" % i) for i in range(5)]
        SMP = [HBUF, HBUF]
        PS = [P.psum([128, 512], F32, "PS%d" % i) for i in range(8)]
        CM = P.sbuf([128, NCM, 128], F32, "CM")
        CMB = P.sbuf([128, NCM, 128], BF16, "CMB")
        ROPE = P.sbuf([128, 2, 512], F32, "ROPE")
        RMASK = P.sbuf([128, 4, 128], F32, "RMASK")
        DECIN = P.sbuf([128, 4, 128], F32, "DECIN")
        DECK = P.sbuf([128, 2, 4], F32, "DECK")
        SMALL = P.sbuf([128, 8], F32, "SMALL")
        VEC8 = P.sbuf([128, NV8, 8], F32, "VEC8")
        VEC22 = P.sbuf([128, NV22, 22], F32, "VEC22")
        RS = [P.sbuf([128, 4, 256], F32, "RS%d" % j) for j in range(2)]
        RSB = P.sbuf([128, 4, 256], BF16, "RSB")
        MST = [P.sbuf([128, 8, 64], F32, "MST%d" % j) for j in range(2)]
        CAR = P.sbuf([128, 160], F32, "CAR")
        FFH = P.sbuf([128, 4, 22, 2], F32, "FFH")
        FFO = P.sbuf([128, 4, 22, 2], F32, "FFO")
        SP8 = P.sbuf([128, 2, 8], F32, "SP8")
        QKS = P.sbuf([128, 8, 16], F32, "QKS")
        STS = P.sbuf([128, 22 * 2 * 16], F32, "STS")
        LCS = P.sbuf([128, 8, 3, 16], F32, "LCS")
        WAX = P.sbuf([128, 2, 8, 128], BF16, "WAX")
        SELH = P.sbuf([16, 16, 2, 128], BF16, "SELH")

        LHC = CAR.t[:, 0:16].rearrange("p (j g) -> p j g", g=8)
        LRO = CAR.t[:, 16:32].rearrange("p (j g) -> p j g", g=8)
        LCH = CAR.t[:, 32:80].rearrange("p (j g t) -> p j g t", g=8, t=3)
        LCO = CAR.t[:, 80:128].rearrange("p (j g t) -> p j g t", g=8, t=3)
        SHC = CAR.t[:, 128:144].rearrange("p (j g) -> p j g", g=8)
        SHO = CAR.t[:, 144:160].rearrange("p (j g) -> p j g", g=8)
        EPSC = SMALL.t[:, 0:1]
        GNEPS = SMALL.t[:, 1:2]

        state = {'ps': 0, 'ws': 0, 'scr': 0, 'ev': 0, 'reserved': set()}

        def ps():
            while True:
                i = state['ps'] % 8
                state['ps'] += 1
                if i not in state['reserved']:
                    return PS[i]

        def scr():
            i = state['scr'] % 5
            state['scr'] += 1
            return SCR[i]

        def mm(out_ap, lhsT, rhs, start, stop, reads, writes):
            P.op('pe', lambda e: e.matmul(out_ap, lhsT=lhsT, rhs=rhs, start=start, stop=stop), reads, writes)

        def cp(eng, out_ap, in_ap, reads, writes):
            if eng == 'act':
                P.op('act', lambda e: e.copy(out=out_ap, in_=in_ap), reads, writes)
            else:
                P.op(eng, lambda e: e.tensor_copy(out=out_ap, in_=in_ap), reads, writes)

        def evcp(out_ap, in_ap, reads, writes):
            state['ev'] += 1
            cp('act' if state['ev'] % 2 else 'dve', out_ap, in_ap, reads, writes)

        def tt(eng, out_ap, a, b, op, reads, writes):
            P.op(eng, lambda e: e.tensor_tensor(out=out_ap, in0=a, in1=b, op=op), reads, writes)

        def ts(eng, out_ap, a, s1, s2, op0, op1, reads, writes):
            if s2 is None:
                P.op(eng, lambda e: e.tensor_scalar(out=out_ap, in0=a, scalar1=s1, scalar2=None, op0=op0), reads, writes)
            else:
                P.op(eng, lambda e: e.tensor_scalar(out=out_ap, in0=a, scalar1=s1, scalar2=s2, op0=op0, op1=op1), reads, writes)

        def stt(out_ap, a, s, b, op0, op1, reads, writes):
            P.op('dve', lambda e: e.scalar_tensor_tensor(out=out_ap, in0=a, scalar=s, in1=b, op0=op0, op1=op1), reads, writes)

        def act(out_ap, in_ap, func, reads, writes, bias=None, scale=None):
            kw = {}
            if bias is not None:
                kw['bias'] = bias
            if scale is not None:
                kw['scale'] = scale
            P.op('act', lambda e: e.activation(out=out_ap, in_=in_ap, func=func, **kw), reads, writes)

        def v8(name):
            return VEC8.t[:, V8[name], :]

        def v8c(name, k):
            return VEC8.t[:, V8[name], k:k + 1]

        def v22c(name, k):
            return VEC22.t[:, V22[name], k:k + 1]

        def hview(k, c0, n):
            return H.t[:, k * TC + c0:k * TC + c0 + n]

        def b1(s, c0, n):
            return BIG1.t[:, s * TC + c0:s * TC + c0 + n]

        def b1f(r, c0, n):
            return BIG1.t[:, r * 2 * TC:(r + 1) * 2 * TC].bitcast(F32)[:, c0:c0 + n]

        def b2(s, c0, n):
            return BIG2.t[:, s * TC + c0:s * TC + c0 + n]

        def wload(parts, KC, NC):
            s = state['ws'] % 2
            state['ws'] += 1
            st, bf = WST[s], WBF[s]
            dstv = st.t[:, 0:KC * NC].rearrange("p (k n) -> p k n", n=NC)
            for (src, off, ncols) in parts:
                P.dma(dstv[:, :, off:off + ncols], src.rearrange("(k p) n -> p k n", p=128), writes=[st])
            P.op('pool', lambda e: e.tensor_copy(out=bf.t[:, 0:KC * NC], in_=st.t[:, 0:KC * NC]), reads=[st], writes=[bf])
            return bf, bf.t[:, 0:KC * NC].rearrange("p (k n) -> p k n", n=NC)

        def wload_to(dst_buf, dst_view, src, KC, NC, dkey=None):
            s = state['ws'] % 2
            state['ws'] += 1
            st = WST[s]
            dstv = st.t[:, 0:KC * NC].rearrange("p (k n) -> p k n", n=NC)
            P.dma(dstv, src.rearrange("(k p) n -> p k n", p=128), writes=[st])
            P.op('pool', lambda e: e.tensor_copy(out=dst_view, in_=dstv), reads=[st], writes=[(dst_buf, dkey)])

        P.dma(CM.t[:], d_cm, writes=[CM])
        cp('dve', CMB.t[:], CM.t[:], [CM], [CMB])
        for (b_, d_) in [(RMASK, d_rmask), (DECIN, d_decin), (DECK, d_deck), (SMALL, d_small), (VEC8, d_vec8), (VEC22, d_vec22)]:
            P.dma(b_.t[:], d_, writes=[b_])
        P.dma(WST[0].t[0:16, 0:4096].rearrange('p (a b c) -> p a b c', a=16, b=2), d_selh, writes=[WST[0]])
        cp('dve', SELH.t[:], WST[0].t[0:16, 0:4096].rearrange('p (a b c) -> p a b c', a=16, b=2), [WST[0]], [SELH])
        P.op('pool', lambda e: e.memset(CAR.t[:], 0.0), writes=[CAR])
        P.op('pool', lambda e: e.memset(FFH.t[:], 0.0), writes=[FFH])
        P.op('pool', lambda e: e.memset(FFO.t[:], 0.0), writes=[FFO])
        P.op('pool', lambda e: e.memset(STS.t[:], 0.0), writes=[STS])
        P.op('pool', lambda e: e.memset(LCS.t[:], 0.0), writes=[LCS])
        for j in range(2):
            P.op('pool', lambda e, j=j: e.memset(RS[j].t[:], 0.0), writes=[RS[j]])
            P.op('pool', lambda e, j=j: e.memset(MST[j].t[:], 0.0), writes=[MST[j]])
        for j in range(2):
            act(SP8.t[:, j, :], v8('lam%d' % j), AF.Exp, [VEC8], [SP8], scale=-1.0)
            act(SP8.t[:, j, :], SP8.t[:, j, :], AF.Ln, [SP8], [SP8], bias=1.0)
            ts('dve', SP8.t[:, j, :], SP8.t[:, j, :], -8.0, None, ALU.mult, None, [SP8], [SP8])

        def rmsnorm(gname, out_fn, out_buf, c_tiles=TOK_TILES):
            for (c0, n) in c_tiles:
                sq = WBF[state['ws'] % 2]
                sqv = sq.t[:, 0:8 * n].rearrange("p (k n) -> p k n", n=n)
                act(sqv, X.t[:, :, c0:c0 + n], AF.Square, [X], [sq])
                pb = ps()
                for k in range(8):
                    mm(pb.t[:, 0:n], CMB.t[:, CM_MEAN1024, :], sqv[:, k, :], k == 0, k == 7, [CMB, sq], [pb])
                rs = scr()
                act(rs.t[:, 0:n], pb.t[:, 0:n], AF.Sqrt, [pb, SMALL], [rs], bias=EPSC)
                P.op('dve', lambda e, rs=rs, n=n: e.reciprocal(out=rs.t[:, 0:n], in_=rs.t[:, 0:n]), [rs], [rs])
                for k in range(8):
                    stt(out_fn(k, c0, n), X.t[:, k, c0:c0 + n], v8c(gname, k), rs.t[:, 0:n], ALU.mult, ALU.mult,
                        [X, VEC8, rs], [out_buf])

        def proj_fm(wbuf, wview, KC, col0, src_fn, src_reads, evac):
            for (c0, n) in TOK_TILES:
                pb = ps()
                for k in range(KC):
                    mm(pb.t[:, 0:n], wview[:, k, col0:col0 + 128], src_fn(k, c0, n), k == 0, k == KC - 1,
                       [wbuf] + src_reads, [pb])
                evac(pb, c0, n)

        def x_accum(dc):
            def ev(pb, c0, n):
                tt('dve', X.t[:, dc, c0:c0 + n], X.t[:, dc, c0:c0 + n], pb.t[:, 0:n], ALU.add, [X, pb], [X])
            return ev

        def ffn(seg, li):
            rmsnorm('nffn%d' % li, hview, H)
            BIG1.fence()
            if seg == 1:
                P.dma(STS.t[:, 0:22 * 32].rearrange("p (f t b) -> p f t b", t=2, b=16), s_ffn[li], writes=[STS])
            stsv = STS.t[:, 0:22 * 32].rearrange("p (f t b) -> p f t b", t=2, b=16)
            NP = TC if seg == 0 else 1024
            for f2 in range(11):
                bf, wview = wload([(w_up[li][:, f2 * 256:(f2 + 1) * 256], 0, 256), (w_up[li][:, DFF + f2 * 256:DFF + (f2 + 1) * 256], 256, 256)], 8, 512)
                for fi in range(2):
                    fc = f2 * 2 + fi
                    UG = HBUF
                    cp('pool', UG.t[:, 0:2], FFH.t[:, li, fc, :], [FFH], [UG])
                    pbv = []
                    for (c0, n) in TOK_TILES:
                        pg = ps()
                        for k in range(8):
                            mm(pg.t[:, 0:n], wview[:, k, fi * 128:fi * 128 + 128], hview(k, c0, n), k == 0, k == 7, [bf, H], [pg])
                        cp('act', UG.t[:, 2 + c0:2 + c0 + n], pg.t[:, 0:n], [pg], [UG])
                        pv = ps()
                        for k in range(8):
                            mm(pv.t[:, 0:n], wview[:, k, 256 + fi * 128:256 + fi * 128 + 128], hview(k, c0, n), k == 0, k == 7, [bf, H], [pv])
                        pbv.append(pv)
                    if seg == 0:
                        cp('pool', FFH.t[:, li, fc, :], UG.t[:, 2 + TC - 2:2 + TC], [UG], [FFH])
                    else:
                        cp('pool', FFO.t[:, li, fc, :], UG.t[:, 2 + 1022:2 + 1024], [UG], [FFO])
                    T1 = (BIG2, None)
                    t1 = BIG2.t[:, 0:2 * TC].bitcast(F32)
                    ts('dve', t1[:, 0:TC], UG.t[:, 0:TC], v22c('fw0_%d' % li, fc), v22c('fb%d' % li, fc), ALU.mult, ALU.add,
                       [UG, VEC22], [BIG2])
                    stt(t1[:, 0:TC], UG.t[:, 1:TC + 1], v22c('fw1_%d' % li, fc), t1[:, 0:TC], ALU.mult, ALU.add, [UG, VEC22, BIG2], [BIG2])
                    stt(t1[:, 0:TC], UG.t[:, 2:TC + 2], v22c('fw2_%d' % li, fc), t1[:, 0:TC], ALU.mult, ALU.add, [UG, VEC22, BIG2], [BIG2])
                    if seg == 1:
                        sc = slice(1024, 1040)
                        ts('dve', t1[:, sc], stsv[:, fc, 0, :], v22c('fw0_%d' % li, fc), v22c('fb%d' % li, fc), ALU.mult, ALU.add,
                           [STS, VEC22], [BIG2])
                        stt(t1[:, sc], stsv[:, fc, 1, :], v22c('fw1_%d' % li, fc), t1[:, sc], ALU.mult, ALU.add, [STS, VEC22, BIG2], [BIG2])
                        stt(t1[:, sc], UG.t[:, 2 + 1024:2 + 1040], v22c('fw2_%d' % li, fc), t1[:, sc], ALU.mult, ALU.add, [UG, VEC22, BIG2], [BIG2])
                        P.dma(o_ffns[li, :, fc, 0, :], stsv[:, fc, 1, :], reads=[STS], eng='pool')
                        P.dma(o_ffns[li, :, fc, 1, :], UG.t[:, 2 + 1024:2 + 1040], reads=[UG], eng='pool')
                    act(t1[:, 0:TC], t1[:, 0:TC], AF.Gelu_apprx_tanh, [BIG2], [BIG2])
                    for ti, (c0, n) in enumerate(TOK_TILES):
                        tt('dve', b1(fc, c0, n), t1[:, c0:c0 + n], pbv[ti].t[:, 0:n], ALU.mult, [BIG2, pbv[ti]], [(BIG1, fc)])
            BIG1.fence()
            for dc in range(8):
                s = state['ws'] % 2
                state['ws'] += 1
                st, bf = WST[s], WBF[s]
                dstv = st.t[:, 0:22 * 128].rearrange("p (k n) -> p k n", n=128)
                P.dma(st.t[:, 0:22 * 128], w_down[li, dc], writes=[st])
                P.op('pool', lambda e, bf=bf, st=st: e.tensor_copy(out=bf.t[:, 0:2816], in_=st.t[:, 0:2816]), reads=[st], writes=[bf])
                wview = bf.t[:, 0:2816].rearrange("p (k n) -> p k n", n=128)
                proj_fm(bf, wview, 22, 0, lambda k, c0, n: b1(k, c0, n), [BIG1], x_accum(dc))

        ctx = dict(locals())
        build_even(ctx)
        build_rwkv(ctx)
        even_layer = ctx['even_layer']
        rwkv_layer = ctx['rwkv_layer']

        for seg in DBG['segs']:
            P.dma(X.t[:], xT[:, seg * TC:(seg + 1) * TC].rearrange("(k p) t -> p k t", p=128), writes=[X])
            for li in DBG['layers']:
                if li % 2 == 0:
                    if DBG['even']:
                        even_layer(seg, li)
                else:
                    if DBG['rwkv']:
                        rwkv_layer(seg, li)
                if DBG['ffn']:
                    ffn(seg, li)
            for (c0, n) in TOK_TILES:
                s = state['ws'] % 2
                state['ws'] += 1
                st = WST[s]
                ov = st.t[:, 0:8 * n].rearrange("p (k n) -> p k n", n=n)
                rmsnorm('nfinal', lambda k, c0_, n_, ov=ov: ov[:, k, :], st, c_tiles=[(c0, n)])
                P.dma(yT[:, seg * TC + c0:seg * TC + c0 + n].rearrange("(k p) t -> p k t", p=128), ov, reads=[st], eng='pool')
        for j in range(2):
            P.dma(o_retp[j].rearrange("h d e -> d h e"), RS[j].t[:], reads=[RS[j]], eng='pool')
            P.dma(o_rwkvp[j], MST[j].t[:, :, :], reads=[MST[j]], eng='pool')
        P.dma(o_lrup, LRO, reads=[CAR], eng='pool')
        P.dma(o_lconvp, LCO, reads=[CAR], eng='pool')
        P.dma(o_shiftp, SHO, reads=[CAR], eng='pool')
        P.dma(o_ffnp, FFO.t[:], reads=[FFO], eng='pool')
        P.finish()
    return nc


def build_even(ctx):
    globals().update(ctx)

    def seg_chunks(seg):
        if seg == 0:
            return [(0, 16)] + [(16 + 128 * i, 128) for i in range(8)]
        return [(128 * i, 128) for i in range(8)]

    def vt(ci, n, c0, w):
        return BIG2.t[0:n, ci * 1024 + c0:ci * 1024 + c0 + w]

    def post(src3, src_reads, n, h, c0):
        OFb, OBb, OSb = scr(), scr(), scr()
        OF = OFb.t[:, 0:2 * n].rearrange("p (e c) -> p e c", c=n)
        OB = OBb.t[:, 0:256].bitcast(BF16)[:, 0:2 * n].rearrange("p (e c) -> p e c", c=n)
        OS = OSb.t[:, 0:256].bitcast(BF16)[:, 0:2 * n].rearrange("p (e c) -> p e c", c=n)
        cp('act', OF, src3, src_reads, [OFb])
        act(OS, src3, AF.Square, src_reads, [OSb])
        cp('pool', OB, OF, [OFb], [OBb])
        pst = ps()
        for ec in range(2):
            mm(pst.t[:, 0:n], CMB.t[:, CM_MEAN256, :], OB[:, ec, :], ec == 0, ec == 1, [CMB, OBb], [pst])
        for ec in range(2):
            mm(pst.t[:, 128:128 + n], CMB.t[:, CM_MEAN256, :], OS[:, ec, :], ec == 0, ec == 1, [CMB, OSb], [pst])
        MEb, VAb = scr(), scr()
        ME = MEb.t[:, 0:n]
        VA = VAb.t[:, 0:n]
        cp('act', ME, pst.t[:, 0:n], [pst], [MEb])
        tt('dve', VA, ME, ME, ALU.mult, [MEb], [VAb])
        tt('dve', VA, pst.t[:, 128:128 + n], VA, ALU.subtract, [pst, VAb], [VAb])
        ts('dve', VA, VA, 0.0, None, ALU.max, None, [VAb], [VAb])
        act(VA, VA, AF.Sqrt, [VAb, SMALL], [VAb], bias=EPSC)
        P.op('dve', lambda e: e.reciprocal(out=VA, in_=VA), [VAb], [VAb])
        for ec in range(2):
            tt('dve', OF[:, ec, :], OF[:, ec, :], ME, ALU.subtract, [OFb, MEb], [OFb])
            stt(OF[:, ec, :], OF[:, ec, :], VEC8.t[:, V8['gn%d' % cur['j']], 2 * h + ec:2 * h + ec + 1], VA, ALU.mult, ALU.mult,
                [OFb, VEC8, VAb], [OFb])
            tt('pool', b1(8 + 2 * h + ec, c0, n), OF[:, ec, :], b1(8 + 2 * h + ec, c0, n), ALU.mult,
               [OFb, (BIG1, 8 + 2 * h + ec)], [(BIG1, 8 + 2 * h + ec)])

    cur = {'j': 0}

    def even_layer(seg, li):
        j = li // 2
        cur['j'] = j
        rmsnorm('nmix%d' % li, hview, H)
        BIG1.fence()
        BIG2.fence()
        s = state['ws'] % 2
        state['ws'] += 1
        P.dma(WST[s].t[:, 0:2048].rearrange("p (a g d) -> p a g d", a=2, g=8), d_wax[j], writes=[WST[s]])
        cp('pool', WAX.t[:], WST[s].t[:, 0:2048].rearrange("p (a g d) -> p a g d", a=2, g=8), [WST[s]], [WAX])
        EVP = DBG['evp']
        ropes = {0: (ROPE, ROPE.t[:, :, 0:512]),
                 512: (HBUF, HBUF.t[:, 0:1024].rearrange("p (a n) -> p a n", a=2)),
                 1024: (LCS, LCS.t[:, 0:2, 0, :])}
        for c0_, (rb_, rv_) in ropes.items():
            n_ = 16 if c0_ == 1024 else 512
            P.dma(rv_, d_rope[:, :, seg * TC + c0_:seg * TC + c0_ + n_], writes=[rb_])
        for t in range(4 if 'qk' in EVP else 0):
            bf, wv = wload([(w_in[j][:, t * 256:(t + 1) * 256], 0, 256), (w_swap[j][:, t * 256:(t + 1) * 256], 256, 256)], 8, 512)
            for hi in range(2):
                hc = 2 * t + hi
                for (c0, n) in TOK_TILES:
                    po_, ps_ = ps(), ps()
                    for k in range(8):
                        mm(po_.t[:, 0:n], wv[:, k, hi * 128:hi * 128 + 128], hview(k, c0, n), k == 0, k == 7, [bf, H], [po_])
                    for k in range(8):
                        mm(ps_.t[:, 0:n], wv[:, k, 256 + hi * 128:256 + hi * 128 + 128], hview(k, c0, n), k == 0, k == 7, [bf, H], [ps_])
                    T1, T2 = scr(), scr()
                    rb, rv = ropes[c0]
                    tt('dve', T1.t[:, 0:n], po_.t[:, 0:n], rv[:, 0, :], ALU.mult, [po_, rb], [T1])
                    tt('dve', T2.t[:, 0:n], ps_.t[:, 0:n], rv[:, 1, :], ALU.mult, [ps_, rb], [T2])
                    tt('pool', b1(hc, c0, n), T1.t[:, 0:n], T2.t[:, 0:n], ALU.add, [T1, T2], [(BIG1, hc)])
                    if seg == 1 and c0 == 1024:
                        tt('pool', QKS.t[:, hc, :], T1.t[:, 0:n], T2.t[:, 0:n], ALU.add, [T1, T2], [QKS])
        chunks = seg_chunks(seg)
        allch = chunks + ([(1024, 16)] if seg == 1 else [])
        for wt in range(2 if 'v' in EVP else 0):
            bf, wv = wload([(w_in[j][:, 1024 + wt * 512:1024 + (wt + 1) * 512], 0, 512)], 8, 512)
            for ci, (c0, n) in enumerate(allch):
                pb = ps()
                for k in range(8):
                    mm(pb.t[0:n, 0:512], hview(k, c0, n), wv[:, k, 0:512], k == 0, k == 7, [bf, H], [pb])
                vm = DBG.get('vmode', '')
                if 'dveonly' in vm:
                    cp('dve', vt(ci, n, wt * 512, 512), pb.t[0:n, 0:512], [pb], [(BIG2, ci)])
                elif 'actonly' in vm:
                    cp('act', vt(ci, n, wt * 512, 512), pb.t[0:n, 0:512], [pb], [(BIG2, ci)])
                elif 'noevac' in vm:
                    pass
                else:
                    evcp(vt(ci, n, wt * 512, 512), pb.t[0:n, 0:512], [pb], [(BIG2, ci)])
        for wt in range(2 if 'ga' in EVP else 0):
            bf, wv = wload([(w_in[j][:, 2048 + wt * 512:2048 + (wt + 1) * 512], 0, 512)], 8, 512)
            for q in range(4):
                oc = wt * 4 + q

                def ev(pb, c0, n, oc=oc):
                    act(b1(8 + oc, c0, n), pb.t[:, 0:n], AF.Silu, [pb], [(BIG1, 8 + oc)])
                proj_fm(bf, wv, 8, q * 128, hview, [H], ev)
        cp('act', RSB.t[:], RS[j].t[:], [RS[j]], [RSB])
        for ci, (c0, n) in enumerate(chunks if 'chunks' in EVP else []):
            di = 0 if n == 128 else 1
            for h in range(4):
                pb = ps()
                mm(pb.t[0:n, 0:n], b1(4 + h, c0, n), b1(h, c0, n), True, True, [(BIG1, 4 + h), (BIG1, h)], [pb])
                SCb, QDb, KDb = scr(), scr(), scr()
                SC = SCb.t[:, 0:64].bitcast(BF16)
                QD = QDb.t[:, 0:64].bitcast(BF16)
                KD = KDb.t[:, 0:64].bitcast(BF16)
                tt('dve', SC[0:n, 0:n], pb.t[0:n, 0:n], RMASK.t[0:n, h, 0:n], ALU.mult, [pb, RMASK], [SCb])
                tt('pool', QD[:, 0:n], b1(h, c0, n), DECIN.t[:, h, 0:n], ALU.mult, [(BIG1, h), DECIN], [QDb])
                po = ps()
                for ec in range(2):
                    mm(po.t[:, ec * 128:ec * 128 + n], vt(ci, n, h * 256 + ec * 128, 128), SC[0:n, 0:n], True, False,
                       [(BIG2, ci), SCb], [po])
                    mm(po.t[:, ec * 128:ec * 128 + n], RSB.t[:, h, ec * 128:ec * 128 + 128], QD[:, 0:n], False, True,
                       [RSB, QDb], [po])
                src3 = po.t[:, 0:256].rearrange("p (e c) -> p e c", c=128)[:, :, 0:n]
                if 'post' in EVP:
                    post(src3, [po], n, h, c0)
                if 'supd' not in EVP:
                    continue
                pt = ps()
                ptb = pt.t[:, :].bitcast(BF16)
                P.op('pe', lambda e, ptb=ptb, n=n, h=h, c0=c0: e.transpose(ptb[0:n, 0:128], b1(4 + h, c0, n), CMB.t[:, CM_IDENT, :]),
                     [(BIG1, 4 + h), CMB], [pt])
                ts('dve', KD[0:n, 0:128], ptb[0:n, 0:128], DECK.t[0:n, di, h:h + 1], None, ALU.mult, None, [pt, DECK], [KDb])
                pu = ps()
                mm(pu.t[:, 0:256], KD[0:n, 0:128], vt(ci, n, h * 256, 256), True, True, [KDb, (BIG2, ci)], [pu])
                stt(RS[j].t[:, h, :], RS[j].t[:, h, :], float(GAMMA[h] ** n), pu.t[:, 0:256], ALU.mult, ALU.add, [RS[j], pu], [RS[j]])
                cp('act', RSB.t[:, h, :], RS[j].t[:, h, :], [RS[j]], [RSB])
        if seg == 1 and 'sample' in EVP:
            c0 = 1024
            KFb = scr()
            KF = KFb.t[:, 0:64].rearrange("p (h b) -> p h b", b=16)
            ts('dve', KF, QKS.t[:, 4:8, :], DK_SCALE, None, ALU.mult, None, [QKS], [KFb])
            POS = ps()
            state['reserved'].add(PS.index(POS))
            for b in range(16):
                RSSb = SMP[b % 2]
                RSS = RSSb.t[:, 0:1024].rearrange("p (h e) -> p h e", e=256)
                P.dma(RSS, s_ret[j, b].rearrange("h d e -> d h e"), writes=[RSSb])
                pbv = [ps(), ps()]
                for half in range(2):
                    for hh in range(2):
                        mm(pbv[half].t[:, 0:512], SELH.t[0:16, b, hh, :], BIG2.t[0:16, 8 * 1024 + half * 512:8 * 1024 + (half + 1) * 512], hh == 0, hh == 1,
                           [SELH, (BIG2, 8)], [pbv[half]])
                for h in range(4):
                    ts('dve', RSS[:, h, :], RSS[:, h, :], float(GAMMA[h]), None, ALU.mult, None, [RSSb], [RSSb])
                    stt(RSS[:, h, :], pbv[h // 2].t[:, (h % 2) * 256:(h % 2) * 256 + 256], KF[:, h, b:b + 1], RSS[:, h, :], ALU.mult, ALU.add,
                        [pbv[h // 2], KFb, RSSb], [RSSb])
                P.dma(o_rets[j, b].rearrange("h d e -> d h e"), RSS, reads=[RSSb], eng='pool')
                for h in range(4):
                    for ec in range(2):
                        col = (h * 2 + ec) * 16 + b
                        mm(POS.t[:, col:col + 1], RSS[:, h, ec * 128:(ec + 1) * 128], QKS.t[:, h, b:b + 1], True, True, [RSSb, QKS], [POS])
            for h in range(4):
                src3 = POS.t[:, h * 32:h * 32 + 32].rearrange("p (e c) -> p e c", c=16)
                post(src3, [POS], 16, h, c0)
            state['reserved'].discard(PS.index(POS))
        for wt in range(2 if 'wouta' in EVP else 0):
            bf, wv = wload([(w_out[j][0:1024, wt * 512:(wt + 1) * 512], 0, 512)], 8, 512)
            for q in range(4):
                proj_fm(bf, wv, 8, q * 128, lambda k, c0, n: b1(8 + k, c0, n), [BIG1], x_accum(wt * 4 + q))
        BIG1.fence()
        BIG2.fence()
        NP = TC if seg == 0 else 1024
        if seg == 1:
            P.dma(STS.t[:, 0:128].rearrange("p (g b) -> p g b", b=16), s_lru[j], writes=[STS])
            P.dma(STS.t[:, 128:512].rearrange("p (g t b) -> p g t b", t=3, b=16), s_lconv[j], writes=[STS])
        SLR = STS.t[:, 0:128].rearrange("p (g b) -> p g b", b=16)
        SLC = STS.t[:, 128:512].rearrange("p (g t b) -> p g t b", t=3, b=16)
        XB = HBUF
        XC, RR, II, AA, TT_, HS, GB = [(lambda c0, n, r=r: b1f(r, c0, n)) for r in range(7)]
        keys = [(BIG1, ('f', r)) for r in range(7)]
        kXC, kRR, kII, kAA, kTT, kHS, kGB = keys
        kXCB = (BIG1, ('f', 7))

        def XCB(c0, n):
            return BIG1.t[:, 7 * 2 * TC + c0:7 * 2 * TC + c0 + n]
        for gp in range(4 if 'lru' in EVP else 0):
            bf, wv = wload([(w_in[j][:, 3072 + gp * 256:3072 + (gp + 1) * 256], 0, 256),
                            (w_in[j][:, 4096 + gp * 256:4096 + (gp + 1) * 256], 256, 256)], 8, 512)
            for gi in range(2):
                g = gp * 2 + gi
                cw = [VEC8.t[:, V8['cw%d_%d' % (m, j)], g:g + 1] for m in range(4)]
                cb = VEC8.t[:, V8['cb%d' % j], g:g + 1]
                cp('pool', XB.t[:, 0:3], LCH[:, j, g, :], [CAR], [XB])

                def evx(pb, c0, n):
                    cp('act', XB.t[:, 3 + c0:3 + c0 + n], pb.t[:, 0:n], [pb], [XB])
                proj_fm(bf, wv, 8, gi * 128, hview, [H], evx)

                def evg(pb, c0, n):
                    act(GB(c0, n), pb.t[:, 0:n], AF.Gelu_apprx_tanh, [pb], [kGB])
                proj_fm(bf, wv, 8, 256 + gi * 128, hview, [H], evg)
                if seg == 0:
                    cp('pool', LCH[:, j, g, :], XB.t[:, 3 + TC - 3:3 + TC], [XB], [CAR])
                else:
                    cp('pool', LCO[:, j, g, :], XB.t[:, 3 + 1021:3 + 1024], [XB], [CAR])
                ts('dve', XC(0, TC), XB.t[:, 0:TC], cw[0], cb, ALU.mult, ALU.add, [XB, VEC8], [kXC])
                for m in range(1, 4):
                    stt(XC(0, TC), XB.t[:, m:m + TC], cw[m], XC(0, TC), ALU.mult, ALU.add, [XB, VEC8, kXC], [kXC])
                if seg == 1:
                    ts('dve', XC(1024, 16), SLC[:, g, 0, :], cw[0], cb, ALU.mult, ALU.add, [STS, VEC8], [kXC])
                    stt(XC(1024, 16), SLC[:, g, 1, :], cw[1], XC(1024, 16), ALU.mult, ALU.add, [STS, VEC8, kXC], [kXC])
                    stt(XC(1024, 16), SLC[:, g, 2, :], cw[2], XC(1024, 16), ALU.mult, ALU.add, [STS, VEC8, kXC], [kXC])
                    stt(XC(1024, 16), XB.t[:, 3 + 1024:3 + 1040], cw[3], XC(1024, 16), ALU.mult, ALU.add, [XB, VEC8, kXC], [kXC])
                    cp('pool', LCS.t[:, g, 0, :], SLC[:, g, 1, :], [STS], [LCS])
                    cp('pool', LCS.t[:, g, 1, :], SLC[:, g, 2, :], [STS], [LCS])
                    cp('pool', LCS.t[:, g, 2, :], XB.t[:, 3 + 1024:3 + 1040], [XB], [LCS])
                cp('pool', XCB(0, TC), XC(0, TC), [kXC], [kXCB])
                for which, dst, kd, bname in ((0, RR, kRR, 'ba%d' % j), (1, II, kII, 'bx%d' % j)):
                    for (c0, n) in TOK_TILES:
                        pb = ps()
                        mm(pb.t[:, 0:n], WAX.t[:, which, g, :], XCB(c0, n), True, True, [WAX, kXCB], [pb])
                        act(dst(c0, n), pb.t[:, 0:n], AF.Sigmoid, [pb, VEC8], [kd], bias=VEC8.t[:, V8[bname], g:g + 1])
                act(AA(0, TC), RR(0, TC), AF.Exp, [kRR, SP8], [kAA], scale=SP8.t[:, j, g:g + 1])
                tt('pool', TT_(0, TC), AA(0, TC), AA(0, TC), ALU.mult, [kAA], [kTT])
                ts('pool', TT_(0, TC), TT_(0, TC), -1.0, 1.0, ALU.mult, ALU.add, [kTT], [kTT])
                ts('pool', TT_(0, TC), TT_(0, TC), 0.0, None, ALU.max, None, [kTT], [kTT])
                act(TT_(0, TC), TT_(0, TC), AF.Sqrt, [kTT], [kTT])
                tt('pool', II(0, TC), II(0, TC), XC(0, TC), ALU.mult, [kII, kXC], [kII])
                tt('dve', II(0, TC), II(0, TC), TT_(0, TC), ALU.mult, [kII, kTT], [kII])
                init = 0.0 if seg == 0 else LHC[:, j, g:g + 1]
                P.op('dve', lambda e, init=init: e.tensor_tensor_scan(out=HS(0, NP), data0=AA(0, NP), data1=II(0, NP), initial=init,
                                                                      op0=ALU.mult, op1=ALU.add), [kAA, kII, CAR], [kHS])
                if seg == 0:
                    cp('pool', LHC[:, j, g:g + 1], HS(NP - 1, 1), [kHS], [CAR])
                else:
                    cp('pool', LRO[:, j, g:g + 1], HS(1023, 1), [kHS], [CAR])
                    tt('pool', HS(1024, 16), AA(1024, 16), SLR[:, g, :], ALU.mult, [kAA, STS], [kHS])
                    tt('pool', HS(1024, 16), HS(1024, 16), II(1024, 16), ALU.add, [kHS, kII], [kHS])
                    cp('pool', LCS.t[:, 0, 0, 0:1] if False else STS.t[:, 512 + g * 16:512 + g * 16 + 16], HS(1024, 16), [kHS], [STS])
                tt('dve', b2(g, 0, TC), HS(0, TC), GB(0, TC), ALU.mult, [kHS, kGB], [(BIG2, ('y', g))])
        if seg == 1:
            P.dma(o_lrus[j], STS.t[:, 512:640].rearrange("p (g b) -> p g b", b=16), reads=[STS], eng='pool')
            P.dma(o_lconvs[j], LCS.t[:], reads=[LCS], eng='pool')
        BIG2.fence()
        for wt in range(2 if 'woutb' in EVP else 0):
            bf, wv = wload([(w_out[j][1024:2048, wt * 512:(wt + 1) * 512], 0, 512)], 8, 512)
            for q in range(4):
                proj_fm(bf, wv, 8, q * 128, lambda k, c0, n: b2(k, c0, n), [BIG2], x_accum(wt * 4 + q))
        BIG1.fence()
        BIG2.fence()

    ctx['even_layer'] = even_layer


def build_rwkv(ctx):
    globals().update(ctx)
    W0 = WST[0]
    B0 = WBF[0]
    B1_ = WBF[1]

    def Q(idx):
        return W0.t[:, idx * 136:idx * 136 + 128].rearrange("p (k n) -> p k n", n=16)

    def kq(idx):
        return (W0, ('q', idx))

    def bc(name):
        return VEC8.t[:, V8[name], :].rearrange("p (k o) -> p k o", o=1).to_broadcast([128, 8, 16])

    def p3(pb):
        return pb.t[:, 0:128].rearrange("p (k n) -> p k n", n=16)

    WRv = BIG1.t[:, 0:8192].rearrange("p (k n) -> p k n", n=1024)
    WKv = BIG1.t[:, 8192:16384].rearrange("p (k n) -> p k n", n=1024)
    WVv = BIG2.t[:, 0:8192].rearrange("p (k n) -> p k n", n=1024)
    WOv = H.t[:, 0:8192].rearrange("p (k n) -> p k n", n=1024)
    o_ = 16384
    W1v = BIG1.t[:, o_:o_ + 512].rearrange("p (k n) -> p k n", n=64)
    A1v = BIG1.t[:, o_ + 512:o_ + 1024].rearrange("p (k n) -> p k n", n=64)
    V1v = BIG1.t[:, o_ + 1024:o_ + 1280].rearrange("p (k n) -> p k n", n=32)
    G1v = BIG1.t[:, o_ + 1280:o_ + 2304].rearrange("p (k n) -> p k n", n=128)
    W2v = BIG1.t[:, o_ + 2304:o_ + 3328]
    A2v = BIG1.t[:, o_ + 3328:o_ + 4352]
    V2v = BIG1.t[:, o_ + 4352:o_ + 5376]
    G2v = BIG1.t[:, o_ + 5376:o_ + 6400]

    def load_small(dst, src, rows):
        s = state['ws'] % 2
        state['ws'] += 1
        st = WST[s]
        P.dma(st.t[0:rows, 0:1024], src, writes=[st])
        cp('pool', dst[0:rows, :], st.t[0:rows, 0:1024], [st], [BIG1])

    def rwkv_layer(seg, li):
        j = li // 2
        BIG1.fence()
        BIG2.fence()
        H.fence()
        for half in range(2):
            cs = slice(half * 512, (half + 1) * 512)
            wload_to(BIG1, WRv[:, :, cs], w_r[j][:, cs], 8, 512)
            wload_to(BIG1, WKv[:, :, cs], w_k[j][:, cs], 8, 512)
            wload_to(BIG2, WVv[:, :, cs], w_v[j][:, cs], 8, 512)
            wload_to(H, WOv[:, :, cs], w_o[j][:, cs], 8, 512)
        wload_to(BIG1, W1v, d_w1[j], 8, 64)
        wload_to(BIG1, A1v, d_a1[j], 8, 64)
        if li == 3:
            wload_to(BIG1, V1v, d_v1[0], 8, 32)
            load_small(V2v, d_v2[0], 32)
        wload_to(BIG1, G1v, d_g1[j], 8, 128)
        load_small(W2v, d_w2[j], 64)
        load_small(A2v, d_a2[j], 64)
        load_small(G2v, d_g2[j], 128)
        CHUNK = DBG.get('chunk', True)
        if CHUNK:
            P.dma(HBUF.t[0:16, 0:768], d_cmask, writes=[HBUF])
        W0.fence()
        B0.fence()
        B1_.fence()
        if seg == 0:
            groups = [(16 * i, False) for i in range(65)]
        else:
            groups = [(16 * i, False) for i in range(64)] + [(1024, True)]
        if seg == 1:
            P.dma(STS.t[:, 0:128].rearrange("p (g b) -> p g b", b=16), s_shift[j], writes=[STS])
        SSH = STS.t[:, 0:128].rearrange("p (g b) -> p g b", b=16)
        MS = MST[j].t[:, :, :]
        XM = B0.t[:, 0:768].rearrange("p (m k n) -> p m k n", m=6, k=8)
        for gi_, (c0, sample) in enumerate(groups):
            HFb = W0.t[:, 29 * 136:30 * 136].rearrange("p (k n) -> p k n", n=17)
            kHF = (W0, ('q', 29))
            sqv = B1_.t[:, 3072:3200].rearrange("p (k n) -> p k n", n=16)
            kSQ = (B1_, 'sq')
            act(sqv, X.t[:, :, c0:c0 + 16], AF.Square, [X], [kSQ])
            pb = ps()
            for k in range(8):
                mm(pb.t[:, 0:16], CMB.t[:, CM_MEAN1024, :], sqv[:, k, :], k == 0, k == 7, [CMB, kSQ], [pb])
            rs = scr()
            act(rs.t[:, 0:16], pb.t[:, 0:16], AF.Sqrt, [pb, SMALL], [rs], bias=EPSC)
            P.op('dve', lambda e, rs=rs: e.reciprocal(out=rs.t[:, 0:16], in_=rs.t[:, 0:16]), [rs], [rs])
            if not sample:
                cp('pool', HFb[:, :, 0], SHC[:, j, :], [CAR], [kHF])
            for k in range(8):
                stt(HFb[:, k, 1:17], X.t[:, k, c0:c0 + 16], v8c('nmix%d' % li, k), rs.t[:, 0:16], ALU.mult, ALU.mult,
                    [X, VEC8, rs], [kHF])
            cur = HFb[:, :, 1:17]
            XX = Q(0)
            if sample:
                tt('dve', XX, SSH, cur, ALU.subtract, [STS, kHF], [kq(0)])
                P.dma(o_shifts[j], cur, reads=[kHF], eng='pool')
            else:
                tt('dve', XX, HFb[:, :, 0:16], cur, ALU.subtract, [kHF], [kq(0)])
                cp('pool', SHC[:, j, :], HFb[:, :, 16], [kHF], [CAR])
                if seg == 1 and c0 == 1008:
                    cp('pool', SHO[:, j, :], HFb[:, :, 16], [kHF], [CAR])
            for m in range(6):
                tt('dve', Q(1), XX, bc('mix%d_%d' % (j, m)), ALU.mult, [kq(0), VEC8], [kq(1)])
                tt('pool', XM[:, m], Q(1), cur, ALU.add, [kq(1), kHF], [(B0, ('xm', m))])

            def projN(wv, m, wbuf):
                pb = ps()
                for oc in range(8):
                    for k in range(8):
                        mm(pb.t[:, oc * 16:oc * 16 + 16], wv[:, k, oc * 128:oc * 128 + 128], XM[:, m, k, :], k == 0, k == 7,
                           [wbuf, (B0, ('xm', m))], [pb])
                return pb

            def lora(w1v, r, m, func, w2v):
                pl = ps()
                for k in range(8):
                    mm(pl.t[0:r, 0:16], w1v[:, k, 0:r], XM[:, m, k, :], k == 0, k == 7, [BIG1, (B0, ('xm', m))], [pl])
                tl = B1_.t[0:r, 3200 + m * 16:3200 + m * 16 + 16]
                kt = (B1_, ('tl', m))
                if func is None:
                    cp('act', tl, pl.t[0:r, 0:16], [pl], [kt])
                else:
                    act(tl, pl.t[0:r, 0:16], func, [pl], [kt])
                po2 = ps()
                for oc in range(8):
                    mm(po2.t[:, oc * 16:oc * 16 + 16], w2v[0:r, oc * 128:oc * 128 + 128], tl, True, True, [BIG1, kt], [po2])
                return po2
            pr = projN(WRv, 0, BIG1)
            RF = Q(2)
            cp('act', RF, p3(pr), [pr], [kq(2)])
            pk = projN(WKv, 2, BIG1)
            KF = Q(3)
            cp('act', KF, p3(pk), [pk], [kq(3)])
            pv = projN(WVv, 3, BIG2)
            VV = Q(4)
            cp('act', VV, p3(pv), [pv], [kq(4)])
            pw = lora(W1v, 64, 1, AF.Tanh, W2v)
            DEC = Q(5)
            tt('dve', DEC, p3(pw), bc('w0_%d' % j), ALU.add, [pw, VEC8], [kq(5)])
            act(DEC, DEC, AF.Sigmoid, [kq(5)], [kq(5)])
            use_chunk = CHUNK and not sample
            if not use_chunk:
                act(DEC, DEC, AF.Exp, [kq(5)], [kq(5)], scale=-WDEC)
            pa = lora(A1v, 64, 4, None, A2v)
            AS = Q(6)
            tt('dve', AS, p3(pa), bc('a0_%d' % j), ALU.add, [pa, VEC8], [kq(6)])
            act(AS, AS, AF.Sigmoid, [kq(6)], [kq(6)])
            pg = lora(G1v, 128, 5, AF.Sigmoid, G2v)
            GG = Q(7)
            cp('act', GG, p3(pg), [pg], [kq(7)])
            if li == 3:
                pvl = lora(V1v, 32, 3, None, V2v)
                VG = Q(8)
                tt('dve', VG, p3(pvl), bc('v0_0'), ALU.add, [pvl, VEC8], [kq(8)])
                act(VG, VG, AF.Sigmoid, [kq(8)], [kq(8)])
                P.dma(Q(23), vf_dram[:, :, seg * TC + c0:seg * TC + c0 + 16], reads=[(VFD, seg * TC + c0)], writes=[kq(23)])
                tt('dve', Q(9), Q(23), VV, ALU.subtract, [kq(23), kq(4)], [kq(9)])
                tt('dve', Q(9), Q(9), VG, ALU.mult, [kq(9), kq(8)], [kq(9)])
                tt('dve', VV, VV, Q(9), ALU.add, [kq(4), kq(9)], [kq(4)])
            else:
                P.dma(vf_dram[:, :, seg * TC + c0:seg * TC + c0 + 16], VV, reads=[kq(4)], writes=[(VFD, seg * TC + c0)], eng='pool')
            KK = Q(10)
            tt('dve', KK, KF, bc('kk%d' % j), ALU.mult, [kq(3), VEC8], [kq(10)])
            tt('pool', Q(11), KK, KK, ALU.mult, [kq(10)], [kq(11)])
            pss = ps()
            mm(pss.t[:, 0:128], CM.t[:, CM_BLKONES, :], W0.t[:, 11 * 136:11 * 136 + 128], True, True, [CM, kq(11)], [pss])
            INV = Q(12)
            ts('dve', INV, p3(pss), 1e-24, None, ALU.max, None, [pss], [kq(12)])
            act(INV, INV, AF.Sqrt, [kq(12)], [kq(12)])
            P.op('dve', lambda e, INV=INV: e.reciprocal(out=INV, in_=INV), [kq(12)], [kq(12)])
            tt('dve', KK, KK, INV, ALU.mult, [kq(10), kq(12)], [kq(10)])
            AT = Q(13)
            ts('pool', AT, KK, -1.0, None, ALU.mult, None, [kq(10)], [kq(13)])
            BT = Q(14)
            tt('pool', BT, KK, AS, ALU.mult, [kq(10), kq(6)], [kq(14)])
            K2 = Q(15)
            stt(K2, AS, -1.0, bc('ka%d' % j), ALU.add, ALU.mult, [kq(6), VEC8], [kq(15)])
            stt(K2, K2, 1.0, KF, ALU.add, ALU.mult, [kq(15), kq(3)], [kq(15)])
            RK = Q(16)
            tt('dve', RK, RF, K2, ALU.mult, [kq(2), kq(15)], [kq(16)])
            tt('pool', RK, RK, bc('rk%d' % j), ALU.mult, [kq(16), VEC8], [kq(16)])
            pbo = ps()
            mm(pbo.t[:, 0:128], CM.t[:, CM_BLKONES, :], W0.t[:, 16 * 136:16 * 136 + 128], True, True, [CM, kq(16)], [pbo])
            BON = Q(17)
            tt('dve', BON, p3(pbo), VV, ALU.mult, [pbo, kq(4)], [kq(17)])
            YS = Q(18)
            for _once in ([0] if use_chunk else []):
                SG = DEC
                CS = Q(24)
                ones16 = SMALL.t[:, 2:3].to_broadcast([128, 16])
                for k in range(8):
                    P.op('dve', lambda e, k=k: e.tensor_tensor_scan(out=CS[:, k, :], data0=ones16, data1=SG[:, k, :], initial=0.0,
                                                                op0=ALU.mult, op1=ALU.add), [kq(5), SMALL], [kq(24)])
                PEXC, PINC, PINV, BP, KP = Q(0), Q(1), Q(11), Q(27), Q(28)
                tt('dve', PEXC, CS, SG, ALU.subtract, [kq(24), kq(5)], [kq(0)])
                act(PEXC, PEXC, AF.Exp, [kq(0)], [kq(0)], scale=-WDEC)
                act(PINC, CS, AF.Exp, [kq(24)], [kq(1)], scale=-WDEC)
                act(PINV, CS, AF.Exp, [kq(24)], [kq(11)], scale=WDEC)
                PLb = PINC[:, :, 15:16].to_broadcast([128, 8, 16])
                ARv = B1_.t[:, 3328:3584].rearrange("p (k a n) -> p k a n", a=2, n=16)
                kAR = (B1_, ('qb', 0))
                BTq = B1_.t[:, 3584:3712].rearrange("p (k n) -> p k n", n=16)
                kBT = (B1_, ('qb', 2))
                KTq = B1_.t[:, 3712:3840].rearrange("p (k n) -> p k n", n=16)
                kKT = (B1_, ('qb', 3))
                BDq = B1_.t[:, 3840:3968].rearrange("p (k n) -> p k n", n=16)
                kBD = (B1_, ('qb', 4))
                KDq = B0.t[:, 768:896].rearrange("p (k n) -> p k n", n=16)
                kKD = (B0, ('qb', 5))
                VBq = B0.t[:, 896:1024].rearrange("p (k n) -> p k n", n=16)
                kVB = (B0, ('qb', 6))
                tt('dve', ARv[:, :, 0, :], AT, PEXC, ALU.mult, [kq(13), kq(0)], [kAR])
                tt('dve', ARv[:, :, 1, :], RF, PINC, ALU.mult, [kq(2), kq(1)], [kAR])
                tt('dve', BP, BT, PINV, ALU.mult, [kq(14), kq(11)], [kq(27)])
                tt('dve', KP, K2, PINV, ALU.mult, [kq(15), kq(11)], [kq(28)])
                cp('pool', BTq, BP, [kq(27)], [kBT])
                cp('pool', KTq, KP, [kq(28)], [kKT])
                tt('pool', BDq, BP, PLb, ALU.mult, [kq(27), kq(1)], [kBD])
                tt('pool', KDq, KP, PLb, ALU.mult, [kq(28), kq(1)], [kKD])
                cp('act', VBq, VV, [kq(4)], [kVB])
                ARm = [B0.t[:, 3584:3840].rearrange("p (k a n) -> p k a n", a=2, n=16),
                       B0.t[:, 3840:4096].rearrange("p (k a n) -> p k a n", a=2, n=16)]
                kARm = (B0, ('arm', 0))
                for hh_ in range(2):
                    ts('dve' if hh_ == 0 else 'pool', ARm[hh_], ARv, CM.t[:, CM_BLKONES, 64 * hh_:64 * hh_ + 1], None, ALU.mult, None, [kAR, CM], [kARm])
                if DBG.get('cstop', 99) < 2:
                    break
                VTM = B1_.t[0:16, 0:1024]
                BDTM = B1_.t[0:16, 1024:2048]
                KDTM = B1_.t[0:16, 2048:3072]
                kVTM, kBDTM, kKDTM = (B1_, ('tm', 0)), (B1_, ('tm', 1)), (B1_, ('tm', 2))
                for (qb, kqb, tm, ktm) in ((VBq, kVB, VTM, kVTM), (BDq, kBD, BDTM, kBDTM), (KDq, kKD, KDTM, kKDTM)):
                    pt = ps()
                    ptb = pt.t[:, :].bitcast(BF16)
                    for k in range(8):
                        P.op('pe', lambda e, ptb=ptb, qb=qb, k=k: e.transpose(ptb[0:16, k * 128:(k + 1) * 128], qb[:, k, :], CMB.t[:, CM_IDENT, :]),
                             [kqb, CMB], [pt])
                    evcp(tm, ptb[0:16, 0:1024], [pt], [ktm])
                MB = W0.t[:, 25 * 136:25 * 136 + 256].bitcast(BF16).rearrange("p (k n) -> p k n", n=64)
                kMB = (W0, ('q', 25))
                cp('act', MB, MS, [MST[j]], [kMB])
                if DBG.get('cstop', 99) < 3:
                    break
                G1 = B0.t[0:16, 1024:1536].rearrange("s (h a t) -> s h a t", a=2, t=16)
                G2 = B0.t[0:16, 1536:2048].rearrange("s (h a t) -> s h a t", a=2, t=16)
                kG1, kG2 = (B0, ('g', 1)), (B0, ('g', 2))
                ATn = B0.t[0:16, 3072:3328].rearrange("s (h t) -> s h t", t=16)
                An = B0.t[0:16, 3328:3584].rearrange("s (h t) -> s h t", t=16)
                kATn = [(B0, ('a', 0, 0)), (B0, ('a', 0, 1))]
                kAn = [(B0, ('a', 1, 0)), (B0, ('a', 1, 1))]
                ZB = B0.t[0:16, 2048:3072]
                kZB = [(B0, ('z', 0)), (B0, ('z', 1))]
                MG = HBUF.t[0:16, 0:512]
                MA = HBUF.t[0:16, 512:768]
                pg1, pg2, pga = ps(), ps(), ps()
                for h in range(16):
                    p_, hh = h // 2, h % 2
                    rows = slice(64 * hh, 64 * hh + 64)
                    arr = ARm[hh][:, p_, :, :]
                    mm(pg1.t[0:16, h * 32:h * 32 + 32], BTq[:, p_, :], arr, True, True, [kBT, kARm], [pg1])
                    mm(pg2.t[0:16, h * 32:h * 32 + 32], KTq[:, p_, :], arr, True, True, [kKT, kARm], [pg2])
                    mm(pga.t[0:16, h * 16:h * 16 + 16], ARm[hh][:, p_, 0, :], BTq[:, p_, :], True, True, [kARm, kBT], [pga])
                tt('dve', B0.t[0:16, 1024:1536], pg1.t[0:16, 0:512], MG, ALU.mult, [pg1, HBUF], [kG1])
                tt('dve', B0.t[0:16, 1536:2048], pg2.t[0:16, 0:512], MG, ALU.mult, [pg2, HBUF], [kG2])
                for hf in range(2):
                    tt('dve', B0.t[0:16, 3328 + hf * 128:3328 + hf * 128 + 128], pga.t[0:16, hf * 128:hf * 128 + 128], MA[:, hf * 128:hf * 128 + 128], ALU.mult,
                       [pga, HBUF], [kAn[hf]])
                    cp('pool', ATn[:, hf * 8:hf * 8 + 8, :], G1[:, hf * 8:hf * 8 + 8, 0, :], [kG1], [kATn[hf]])
                if DBG.get('cstop', 99) < 4:
                    break
                pw = [ps(), ps()]
                for h in range(16):
                    p_, hh = h // 2, h % 2
                    rows = slice(64 * hh, 64 * hh + 64)
                    o_ap = pw[h // 8].t[0:16, (h % 8) * 64:(h % 8) * 64 + 64]
                    mm(o_ap, ARm[hh][:, p_, 0, :], MB[:, p_, :], True, False, [kARm, kMB], [pw[h // 8]])
                    mm(o_ap, G2[:, h, 0, :], VTM[:, h * 64:h * 64 + 64], False, True, [kG2, kVTM], [pw[h // 8]])
                for half in range(2):
                    evcp(ZB[:, half * 512:(half + 1) * 512], pw[half].t[0:16, 0:512], [pw[half]], [kZB[half]])
                if DBG.get('cstop', 99) < 5:
                    break
                I16 = CMB.t[0:16, CM_IDENT, 0:16]
                for lvl in range(4):
                    pz = [ps(), ps()]
                    pa = [ps(), ps()] if lvl < 3 else None
                    for hf in range(2):
                        for h in range(hf * 8, hf * 8 + 8):
                            o_ap = pz[hf].t[0:16, (h % 8) * 64:(h % 8) * 64 + 64]
                            zsl = ZB[:, h * 64:h * 64 + 64]
                            mm(o_ap, ATn[:, h, :], zsl, True, True, [kATn[hf], kZB[hf]], [pz[hf]])
                        if lvl < 3:
                            for h in range(hf * 8, hf * 8 + 8):
                                hl = h % 8
                                mm(pa[hf].t[0:16, hl * 16:hl * 16 + 16], An[:, h, :], ATn[:, h, :], True, True, [kAn[hf], kATn[hf]], [pa[hf]])
                                mm(pa[hf].t[0:16, 128 + hl * 16:128 + hl * 16 + 16], ATn[:, h, :], An[:, h, :], True, True, [kAn[hf], kATn[hf]], [pa[hf]])
                    for hf in range(2):
                        tt('dve', ZB[:, hf * 512:(hf + 1) * 512], pz[hf].t[0:16, 0:512], ZB[:, hf * 512:(hf + 1) * 512], ALU.add,
                           [pz[hf], kZB[hf]], [kZB[hf]])
                        if lvl < 3:
                            cp('act', B0.t[0:16, 3072 + hf * 128:3072 + hf * 128 + 128], pa[hf].t[0:16, 0:128], [pa[hf]], [kATn[hf]])
                            cp('act', B0.t[0:16, 3328 + hf * 128:3328 + hf * 128 + 128], pa[hf].t[0:16, 128:256], [pa[hf]], [kAn[hf]])
                if DBG.get('cstop', 99) < 6:
                    break
                py = [ps(), ps()]
                for h in range(16):
                    p_, hh = h // 2, h % 2
                    rows = slice(64 * hh, 64 * hh + 64)
                    o_ap = py[h // 8].t[0:16, (h % 8) * 64:(h % 8) * 64 + 64]
                    mm(o_ap, ARm[hh][:, p_, 1, :], MB[:, p_, :], True, False, [kARm, kMB], [py[h // 8]])
                    mm(o_ap, G1[:, h, 1, :], ZB[:, h * 64:h * 64 + 64], False, False, [kG1, kZB[h // 8]], [py[h // 8]])
                    mm(o_ap, G2[:, h, 1, :], VTM[:, h * 64:h * 64 + 64], False, True, [kG2, kVTM], [py[h // 8]])
                YTB = B0.t[0:16, 3072:4096]
                kYTB = (B0, ('ytb', 0))
                for half in range(2):
                    evcp(YTB[:, half * 512:(half + 1) * 512], py[half].t[0:16, 0:512], [py[half]], [kYTB, kAn[0], kAn[1], kATn[0], kATn[1], kARm])
                pyt = ps()
                pytb = pyt.t[:, :].bitcast(BF16)
                for k in range(8):
                    P.op('pe', lambda e, pytb=pytb, k=k: e.transpose(pytb[:, k * 16:(k + 1) * 16], YTB[:, k * 128:(k + 1) * 128], CMB.t[0:16, CM_IDENT, 0:16]),
                         [kYTB, kAn[0], kAn[1], kATn[0], kATn[1], kARm, CMB], [pyt])
                cp('act', YS, pytb[:, 0:128].rearrange("p (k n) -> p k n", n=16), [pyt], [kq(18)])
                if DBG.get('cstop', 99) < 7:
                    break
                psu = [ps(), ps()]
                for p_ in range(8):
                    o_ap = psu[p_ // 4].t[:, (p_ % 4) * 128:(p_ % 4) * 128 + 128]
                    mm(o_ap, BDTM[:, p_ * 128:(p_ + 1) * 128], ZB[:, p_ * 128:(p_ + 1) * 128], True, False, [kBDTM, kZB[p_ // 4]], [psu[p_ // 4]])
                    mm(o_ap, KDTM[:, p_ * 128:(p_ + 1) * 128], VTM[:, p_ * 128:(p_ + 1) * 128], False, True, [kKDTM, kVTM], [psu[p_ // 4]])
                tt('dve', MS, MS, PINC[:, :, 15:16].to_broadcast([128, 8, 64]), ALU.mult, [MST[j], kq(1)], [MST[j]])
                for g2 in range(2):
                    for hh in range(2):
                        rows = slice(64 * hh, 64 * hh + 64)
                        src = psu[g2].t[rows, 0:512].rearrange("p (q c) -> p q c", c=128)[:, :, hh * 64:hh * 64 + 64]
                        tt('dve', MS[rows, 4 * g2:4 * g2 + 4, :], MS[rows, 4 * g2:4 * g2 + 4, :], src, ALU.add, [MST[j], psu[g2]], [MST[j]])
            if not use_chunk:
                TM = []
                qlist = [(AT, kq(13)), (DEC, kq(5)), (BT, kq(14)), (K2, kq(15)), (RF, kq(2)), (None, None)]
                qbs = []
                for qi, (src, ksrc) in enumerate(qlist):
                    if qi < 5:
                        qb = B1_.t[:, 3328 + qi * 128:3328 + qi * 128 + 128].rearrange("p (k n) -> p k n", n=16)
                        kqb = (B1_, ('qb', qi))
                    else:
                        qb = B0.t[:, 768:896].rearrange("p (k n) -> p k n", n=16)
                        kqb = (B0, ('qb', qi))
                    qbs.append((qb, kqb))
                    if qi == 5:
                        tt('dve', Q(24), DEC, qbs[1][0], ALU.subtract, [kq(5), qbs[1][1]], [kq(24)])
                        cp('pool', qb, Q(24), [kq(24)], [kqb])
                    else:
                        cp('pool' if qi % 2 else 'act', qb, src, [ksrc], [kqb])
                    pt = ps()
                    ptb = pt.t[:, :].bitcast(BF16)
                    for k in range(8):
                        P.op('pe', lambda e, ptb=ptb, qb=qb, k=k: e.transpose(ptb[0:16, k * 128:(k + 1) * 128], qb[:, k, :], CMB.t[:, CM_IDENT, :]),
                             [kqb, CMB], [pt])
                    if qi < 3:
                        tm = B1_.t[0:16, qi * 1024:(qi + 1) * 1024]
                        ktm = (B1_, ('tm', qi))
                    else:
                        tm = B0.t[0:16, 1024 + (qi - 3) * 1024:1024 + (qi - 2) * 1024]
                        ktm = (B0, ('tm', qi))
                    evcp(tm, ptb[0:16, 0:1024], [pt], [ktm])
                    TM.append((tm, ktm))
                for t in range(16):
                    if sample:
                        Sb = SMP[0]
                        S3 = Sb.t[:, (t % 2) * 512:(t % 2) * 512 + 512].rearrange("p (k n) -> p k n", n=64)
                        for hh in range(2):
                            P.dma(S3[hh * 64:(hh + 1) * 64], s_rwkv[j, t].rearrange("(p hh) i jj -> hh i p jj", hh=2)[hh], writes=[Sb])
                        kS = Sb
                    else:
                        S3 = MS
                        kS = MST[j]
                    pq = []
                    for qi in range(5):
                        pbq = ps()
                        srcs = [TM[qi]] + ([TM[5]] if qi == 1 else [])
                        nmm = 2 * len(srcs)
                        im = 0
                        for (tm, ktm) in srcs:
                            tm4 = tm.rearrange("s (p hh jj) -> s p hh jj", hh=2, jj=64)
                            for hh in range(2):
                                mm(pbq.t[:, 0:512], SELH.t[0:16, t, hh, :], tm4[:, :, hh, :], im == 0, im == nmm - 1, [SELH, ktm], [pbq])
                                im += 1
                        pq.append(pbq)

                    def q3(i):
                        return pq[i].t[:, 0:512].rearrange("p (k n) -> p k n", n=64)
                    T1b = scr()
                    T1 = T1b.t[:, 0:512].rearrange("p (k n) -> p k n", n=64)
                    SAb = scr()
                    SA = SAb.t[:, 0:8]
                    tt('dve', T1, S3, q3(0), ALU.mult, [kS, pq[0]], [T1b])
                    P.op('dve', lambda e, SA=SA, T1=T1: e.tensor_reduce(out=SA, in_=T1, axis=AX.X, op=ALU.add), [T1b], [SAb])
                    tt('dve', S3, S3, q3(1), ALU.mult, [kS, pq[1]], [kS])
                    T2b = scr()
                    T2 = T2b.t[:, 0:512].rearrange("p (k n) -> p k n", n=64)
                    tt('dve', T2, q3(2), SAb.t[:, 0:8].rearrange("p (k o) -> p k o", o=1).to_broadcast([128, 8, 64]), ALU.mult, [pq[2], SAb], [T2b])
                    tt('pool', S3, S3, T2, ALU.add, [kS, T2b], [kS])
                    T3b = scr()
                    T3 = T3b.t[:, 0:512].rearrange("p (k n) -> p k n", n=64)
                    tt('dve', T3, q3(3), VV[:, :, t:t + 1].to_broadcast([128, 8, 64]), ALU.mult, [pq[3], kq(4)], [T3b])
                    tt('pool', S3, S3, T3, ALU.add, [kS, T3b], [kS])
                    T4b = scr()
                    T4 = T4b.t[:, 0:512].rearrange("p (k n) -> p k n", n=64)
                    tt('dve', T4, S3, q3(4), ALU.mult, [kS, pq[4]], [T4b])
                    P.op('dve', lambda e, T4=T4, t=t: e.tensor_reduce(out=YS[:, :, t], in_=T4, axis=AX.X, op=ALU.add), [T4b], [kq(18)])
                    if sample:
                        for hh in range(2):
                            P.dma(o_rwkvs[j, t].rearrange("(p hh) i jj -> hh i p jj", hh=2)[hh], S3[hh * 64:(hh + 1) * 64], reads=[Sb], eng='pool')
            tt('pool', Q(19), YS, YS, ALU.mult, [kq(18)], [kq(19)])
            pm = ps()
            mm(pm.t[:, 0:128], CM.t[:, CM_BLKMEAN64, :], W0.t[:, 18 * 136:18 * 136 + 128], True, True, [CM, kq(18)], [pm])
            mm(pm.t[:, 128:256], CM.t[:, CM_BLKMEAN64, :], W0.t[:, 19 * 136:19 * 136 + 128], True, True, [CM, kq(19)], [pm])
            ME = Q(20)
            VA = Q(21)
            cp('act', ME, p3(pm), [pm], [kq(20)])
            tt('dve', VA, ME, ME, ALU.mult, [kq(20)], [kq(21)])
            tt('dve', VA, pm.t[:, 128:256].rearrange("p (k n) -> p k n", n=16), VA, ALU.subtract, [pm, kq(21)], [kq(21)])
            ts('dve', VA, VA, 0.0, None, ALU.max, None, [kq(21)], [kq(21)])
            act(VA, VA, AF.Sqrt, [kq(21), SMALL], [kq(21)], bias=GNEPS)
            P.op('dve', lambda e, VA=VA: e.reciprocal(out=VA, in_=VA), [kq(21)], [kq(21)])
            OO = Q(22)
            tt('dve', OO, YS, ME, ALU.subtract, [kq(18), kq(20)], [kq(22)])
            tt('dve', OO, OO, VA, ALU.mult, [kq(22), kq(21)], [kq(22)])
            tt('pool', OO, OO, bc('gng%d' % j), ALU.mult, [kq(22), VEC8], [kq(22)])
            tt('pool', OO, OO, bc('gnb%d' % j), ALU.add, [kq(22), VEC8], [kq(22)])
            tt('pool', OO, OO, BON, ALU.add, [kq(22), kq(17)], [kq(22)])
            OB = B1_.t[:, 3968:4096].rearrange("p (k n) -> p k n", n=16)
            kOB = (B1_, 'ob')
            tt('dve', OB, OO, GG, ALU.mult, [kq(22), kq(7)], [kOB])
            po_ = ps()
            for oc in range(8):
                for k in range(8):
                    mm(po_.t[:, oc * 16:oc * 16 + 16], WOv[:, k, oc * 128:oc * 128 + 128], OB[:, k, :], k == 0, k == 7, [H, kOB], [po_])
            tt('dve', X.t[:, :, c0:c0 + 16], X.t[:, :, c0:c0 + 16], p3(po_), ALU.add, [X, po_], [X])
        W0.fence()
        B0.fence()
        B1_.fence()
        BIG1.fence()
        BIG2.fence()
        H.fence()

    ctx['rwkv_layer'] = rwkv_layer


def _prep_shared(inp):
    c = build_consts()
    sh = dict(cm=c['cm'], rope=c['rope'], rmask=c['rmask'], decin=c['decin'], deck=c['deck'], mask4=c['mask4'],
              maska=c['maska'], reset=c['reset'], sel=c['sel'], small=c['small'])
    selh = np.zeros((16, 16, 2, 128), np.float32)
    for t in range(16):
        selh[t, t, 0, :64] = 1.0
        selh[t, t, 1, 64:] = 1.0
    sh['selh'] = selh
    i16 = np.arange(16)
    strictT = (i16[:, None] < i16[None, :]).astype(np.float32)
    inclT = (i16[:, None] <= i16[None, :]).astype(np.float32)
    cm_ = np.zeros((16, 768), np.float32)
    cm_[:, 0:512] = np.tile(np.concatenate([strictT, inclT], axis=1), (1, 16))
    cm_[:, 512:768] = np.tile((i16[:, None] > i16[None, :]).astype(np.float32), (1, 16))
    sh['cmask'] = cm_
    f = lambda a: np.ascontiguousarray(np.asarray(a, np.float32))
    vec8 = np.zeros((128, NV8, 8), np.float32)

    def put8(name, v):
        vec8[:, V8[name], :] = fm(v, 8)
    for li in range(4):
        put8('nmix%d' % li, inp['norm_mix_g'][li])
        put8('nffn%d' % li, inp['norm_ffn_g'][li])
    put8('nfinal', inp['norm_final_g'])
    for j in range(2):
        put8('gn%d' % j, inp['ev_ret_gn_g'][j])
        for m in range(4):
            put8('cw%d_%d' % (m, j), inp['ev_lru_conv_w'][j][m])
        put8('cb%d' % j, inp['ev_lru_conv_b'][j])
        put8('ba%d' % j, inp['ev_lru_ba'][j])
        put8('bx%d' % j, inp['ev_lru_bx'][j])
        put8('lam%d' % j, inp['ev_lru_lambda'][j])
        for m in range(6):
            put8('mix%d_%d' % (j, m), inp['od_mix'][j][m])
        put8('w0_%d' % j, inp['od_w0'][j])
        put8('a0_%d' % j, inp['od_a0'][j])
        put8('kk%d' % j, inp['od_k_k'][j])
        put8('ka%d' % j, inp['od_k_a'][j])
        put8('rk%d' % j, np.asarray(inp['od_r_k'][j]).reshape(-1))
        put8('gng%d' % j, inp['od_gn_g'][j])
        put8('gnb%d' % j, inp['od_gn_b'][j])
    put8('v0_0', inp['od_v0'][0])
    vec22 = np.zeros((128, NV22, 22), np.float32)
    for li in range(4):
        for m in range(3):
            vec22[:, V22['fw%d_%d' % (m, li)], :] = fm(inp['ff_conv_w'][li][m], 22)
        vec22[:, V22['fb%d' % li], :] = fm(inp['ff_conv_b'][li], 22)
    sh['vec8'] = vec8
    sh['vec22'] = vec22
    w_in = f(inp['ev_w_in'])
    perm = np.array([(c // 128) * 128 + ((c % 128) + 64) % 128 for c in range(1024)])
    sh['w_in'] = w_in
    sh['w_swap'] = np.ascontiguousarray(w_in[:, :, perm])
    sh['w_out'] = f(inp['ev_w_out'])
    wa = np.asarray(inp['ev_lru_wa'], np.float32).transpose(0, 2, 1, 3)
    wx = np.asarray(inp['ev_lru_wx'], np.float32).transpose(0, 2, 1, 3)
    sh['wax'] = np.ascontiguousarray(np.stack([wa, wx], axis=2))
    sh['w_r'] = f(inp['od_w_r'])
    sh['w_k'] = f(inp['od_w_k'])
    sh['w_v'] = f(inp['od_w_v'])
    sh['w_o'] = f(inp['od_w_o'])
    sh['w1'] = f(inp['od_w1'])
    sh['w2'] = f(inp['od_w2'])
    sh['a1'] = f(inp['od_a1'])
    sh['a2'] = f(inp['od_a2'])
    sh['v1'] = f(inp['od_v1'])
    sh['v2'] = f(inp['od_v2'])
    sh['g1'] = f(inp['od_g1'])
    sh['g2'] = f(inp['od_g2'])
    sh['w_up'] = f(inp['ff_w_up'])
    sh['w_down'] = np.ascontiguousarray(f(inp['ff_w_down']).reshape(4, NFC, 128, 8, 128).transpose(0, 3, 2, 1, 4).reshape(4, 8, 128, NFC * 128))
    return sh


def kernel(**inp):
    if 'nc' not in _CACHE:
        _CACHE['nc'] = build_program()
    nc = _CACHE['nc']
    sh = _prep_shared(inp)
    xp = np.asarray(inp['x_prompt'], np.float32)
    xs = np.asarray(inp['x_sample'], np.float32)
    meta = np.asarray(inp['meta_tokens'], np.float32)
    in_maps = []
    NCORES = DBG['ncores']
    for c in range(NCORES):
        sl = slice(16 * c, 16 * c + 16)
        m = dict(sh)
        xall = np.concatenate([meta, xp[c], xs[sl, 0, :]], axis=0)
        m['xT'] = np.ascontiguousarray(xall.T)
        m['s_ret'] = np.ascontiguousarray(np.asarray(inp['state_ret'], np.float32)[:, sl])
        m['s_lru'] = np.ascontiguousarray(np.asarray(inp['state_lru'], np.float32)[:, sl].reshape(2, 16, 8, 128).transpose(0, 3, 2, 1))
        m['s_lconv'] = np.ascontiguousarray(np.asarray(inp['state_lru_conv'], np.float32)[:, sl].reshape(2, 16, 3, 8, 128).transpose(0, 4, 3, 2, 1))
        m['s_rwkv'] = np.ascontiguousarray(np.asarray(inp['state_rwkv'], np.float32)[:, sl])
        m['s_shift'] = np.ascontiguousarray(np.asarray(inp['state_shift'], np.float32)[:, sl].reshape(2, 16, 8, 128).transpose(0, 3, 2, 1))
        m['s_ffn'] = np.ascontiguousarray(np.asarray(inp['state_ffn_conv'], np.float32)[:, sl].reshape(4, 16, 2, 22, 128).transpose(0, 4, 3, 2, 1))
        in_maps.append(m)
    res = run_bass_kernel_spmd(nc, in_maps, core_ids=list(range(NCORES)))
    R = res.results
    y_p = np.zeros((8, 2048, 1024), np.float32)
    y_s = np.zeros((128, 1, 1024), np.float32)
    ret_p = np.zeros((2, 8, 4, 128, 256), np.float32)
    lru_p = np.zeros((2, 8, 1024), np.float32)
    lconv_p = np.zeros((2, 8, 3, 1024), np.float32)
    rwkv_p = np.zeros((2, 8, 16, 64, 64), np.float32)
    shift_p = np.zeros((2, 8, 1024), np.float32)
    ffn_p = np.zeros((4, 8, 2, 2816), np.float32)
    ret_s = np.zeros((2, 128, 4, 128, 256), np.float32)
    lru_s = np.zeros((2, 128, 1024), np.float32)
    lconv_s = np.zeros((2, 128, 3, 1024), np.float32)
    rwkv_s = np.zeros((2, 128, 16, 64, 64), np.float32)
    shift_s = np.zeros((2, 128, 1024), np.float32)
    ffn_s = np.zeros((4, 128, 2, 2816), np.float32)
    for c in range(NCORES):
        r = R[c]
        sl = slice(16 * c, 16 * c + 16)
        yT = np.asarray(r['yT'])
        y_p[c] = yT[:, 16:2064].T
        y_s[sl, 0, :] = yT[:, 2064:].T
        ret_p[:, c] = np.asarray(r['o_retp'])
        lru_p[:, c] = np.asarray(r['o_lrup']).transpose(1, 2, 0).reshape(2, 1024)
        lconv_p[:, c] = np.asarray(r['o_lconvp']).transpose(1, 3, 2, 0).reshape(2, 3, 1024)
        if DBG.get('chunk', True):
            rwkv_p[:, c] = np.asarray(r['o_rwkvp']).reshape(2, 2, 64, 8, 64).transpose(0, 3, 1, 4, 2).reshape(2, 16, 64, 64)
        else:
            rwkv_p[:, c] = np.asarray(r['o_rwkvp']).reshape(2, 2, 64, 8, 64).transpose(0, 3, 1, 2, 4).reshape(2, 16, 64, 64)
        shift_p[:, c] = np.asarray(r['o_shiftp']).transpose(1, 2, 0).reshape(2, 1024)
        ffn_p[:, c] = np.asarray(r['o_ffnp']).transpose(1, 3, 2, 0).reshape(4, 2, 2816)
        ret_s[:, sl] = np.asarray(r['o_rets'])
        lru_s[:, sl] = np.asarray(r['o_lrus']).transpose(0, 3, 2, 1).reshape(2, 16, 1024)
        lconv_s[:, sl] = np.asarray(r['o_lconvs']).transpose(0, 4, 3, 2, 1).reshape(2, 16, 3, 1024)
        rwkv_s[:, sl] = np.asarray(r['o_rwkvs'])
        shift_s[:, sl] = np.asarray(r['o_shifts']).transpose(0, 3, 2, 1).reshape(2, 16, 1024)
        ffn_s[:, sl] = np.asarray(r['o_ffns']).transpose(0, 4, 3, 2, 1).reshape(4, 16, 2, 2816)
    return (y_p, y_s, ret_p, lru_p, lconv_p, rwkv_p, shift_p, ffn_p, ret_s, lru_s, lconv_s, rwkv_s, shift_s, ffn_s)
```

```python
import math
import numpy as np
from contextlib import ExitStack
import concourse.bass as bass
import concourse.mybir as mybir
from concourse.bass_utils import run_bass_kernel_spmd

F32 = mybir.dt.float32
BF16 = mybir.dt.bfloat16
AF = mybir.ActivationFunctionType
ALU = mybir.AluOpType
AX = mybir.AxisListType

SEM_CAP = 30000
N_DMA_SEMS = 16

D = 1024
TC = 1040
NTOK = 2080
DFF = 2816
NFC = 22
EPS = 1e-6
GN_EPS_C = 64e-5
WDEC = math.exp(-0.5)
DK_SCALE = 128 ** -0.5
GAMMA = [1.0 - 2.0 ** (-5 - h) for h in range(4)]
TOK_TILES = [(0, 512), (512, 512), (1024, 16)]


class Buf:
    def __init__(self, t, name):
        self.t = t
        self.name = name
        self.st = {'_all': [[], []]}

    def __getitem__(self, idx):
        return self.t[idx]

    def states(self, key):
        if key is None:
            return list(self.st.values())
        if key not in self.st:
            a = self.st['_all']
            self.st[key] = [list(a[0]), list(a[1])]
        return [self.st[key]]

    def fence(self):
        w = []
        r = []
        for st in self.st.values():
            w.extend(st[0])
            r.extend(st[1])
        self.st = {'_all': [w, r]}


class Prog:
    def __init__(self, nc, es):
        self.nc = nc
        self.es = es
        self.engs = ['pe', 'act', 'dve', 'pool', 'sp']
        self.q = {e: [] for e in self.engs}
        self.cur_sem = {}
        self.cnt = {}
        self.nsem = 0
        for e in ['pe', 'act', 'dve', 'pool']:
            self._new_sem(e)
        self.waited = {e: {} for e in self.engs}
        self.dma_sems = [es.enter_context(nc.semaphore("dq%d" % i)) for i in range(2 * N_DMA_SEMS)]
        self.dma_cnt = [0] * (2 * N_DMA_SEMS)
        self.dma_n = 0
        self.dma_ne = {'sp': 0, 'pool': 0}
        self.nbuf = 0
        self.n_ops = 0
        self.n_waits = 0
        self.psi = 0

    def _new_sem(self, e):
        s = self.es.enter_context(self.nc.semaphore("tl_%s_%d" % (e, self.nsem)))
        self.nsem += 1
        self.cur_sem[e] = s
        self.cnt[e] = 0

    def sbuf(self, shape, dt, name=None):
        self.nbuf += 1
        name = "%s_%d" % (name or "sb", self.nbuf)
        t = self.es.enter_context(self.nc.sbuf_tensor(name, list(shape), dt))
        return Buf(t, name)

    def psum(self, shape, dt, name=None):
        self.nbuf += 1
        name = "%s_%d" % (name or "ps", self.nbuf)
        t = self.es.enter_context(self.nc.psum_tensor(name, list(shape), dt))
        return Buf(t, name)

    def _deps(self, eng, reads, writes):
        deps = []
        for (b, k) in reads:
            for st in b.states(k):
                deps.extend(st[0])
        for (b, k) in writes:
            for st in b.states(k):
                deps.extend(st[0])
                deps.extend(st[1])
        best = {}
        for (sem, val, seng) in deps:
            if eng == 'pe' and seng == 'pe':
                continue
            kk = id(sem)
            if kk not in best or best[kk][1] < val:
                best[kk] = (sem, val)
        waits = []
        w = self.waited[eng]
        for kk, (sem, val) in best.items():
            if w.get(kk, 0) >= val:
                continue
            w[kk] = val
            waits.append((sem, val))
        return waits

    @staticmethod
    def _compact(lst):
        best = {}
        for t in lst:
            kk = id(t[0])
            if kk not in best or best[kk][1] < t[1]:
                best[kk] = t
        return list(best.values())

    def _record(self, tok, reads, writes):
        for (b, k) in reads:
            for st in b.states(k):
                st[1].append(tok)
                if len(st[1]) > 48:
                    st[1] = self._compact(st[1])
        for (b, k) in writes:
            for st in b.states(k):
                st[0] = [tok]
                st[1] = []

    @staticmethod
    def _norm(lst):
        out = []
        for x in lst:
            if isinstance(x, tuple):
                out.append(x)
            else:
                out.append((x, None))
        return out

    def op(self, eng, fn, reads=(), writes=()):
        reads = self._norm(reads)
        writes = self._norm(writes)
        waits = self._deps(eng, reads, writes)
        if self.cnt[eng] >= SEM_CAP:
            self._new_sem(eng)
        self.cnt[eng] += 1
        tok = (self.cur_sem[eng], self.cnt[eng], eng)
        self.q[eng].append((waits, fn, (tok[0], 1)))
        self._record(tok, reads, writes)
        self.n_ops += 1
        self.n_waits += len(waits)
        return tok

    def dma(self, out_ap, in_ap, reads=(), writes=(), eng='sp', **kw):
        reads = self._norm(reads)
        writes = self._norm(writes)
        waits = self._deps(eng, reads, writes)
        i = (self.dma_ne[eng] % N_DMA_SEMS) + (N_DMA_SEMS if eng == 'pool' else 0)
        self.dma_ne[eng] += 1
        self.dma_n += 1
        sem = self.dma_sems[i]
        prev = self.dma_cnt[i]
        w = self.waited[eng]
        if prev > 0 and w.get(id(sem), 0) < prev:
            waits.append((sem, prev))
            w[id(sem)] = prev
        self.dma_cnt[i] += 16
        tok = (sem, self.dma_cnt[i], 'dma')

        def fn(e, out_ap=out_ap, in_ap=in_ap, kw=kw):
            return e.dma_start(out=out_ap, in_=in_ap, **kw)
        self.q[eng].append((waits, fn, (sem, 16)))
        self._record(tok, reads, writes)
        self.n_ops += 1
        self.n_waits += len(waits)
        return tok

    def finish(self):
        nc = self.nc
        waits = []
        for i, s in enumerate(self.dma_sems):
            if self.dma_cnt[i] > 0:
                waits.append((s, self.dma_cnt[i]))
        for en in ['pe', 'act', 'dve', 'pool']:
            if self.cnt[en] > 0:
                waits.append((self.cur_sem[en], self.cnt[en]))
        self.q['sp'].append((waits, None, None))
        qs = self.q
        with nc.Block() as block:
            def run(engine, lst):
                for (waits, fn, inc) in lst:
                    for (sem, val) in waits:
                        engine.wait_ge(sem, val)
                    if fn is not None:
                        ins = fn(engine)
                        if inc is not None:
                            ins.then_inc(inc[0], inc[1])

            @block.tensor
            def _(t):
                run(t, qs['pe'])

            @block.scalar
            def _(t):
                run(t, qs['act'])

            @block.vector
            def _(t):
                run(t, qs['dve'])

            @block.gpsimd
            def _(t):
                run(t, qs['pool'])

            @block.sync
            def _(t):
                run(t, qs['sp'])


V8_NAMES = []
for _li in range(4):
    V8_NAMES += ['nmix%d' % _li, 'nffn%d' % _li]
V8_NAMES += ['nfinal']
for _j in range(2):
    V8_NAMES += ['gn%d' % _j, 'cw0_%d' % _j, 'cw1_%d' % _j, 'cw2_%d' % _j, 'cw3_%d' % _j, 'cb%d' % _j,
                 'ba%d' % _j, 'bx%d' % _j, 'lam%d' % _j]
for _j in range(2):
    V8_NAMES += ['mix%d_%d' % (_j, m) for m in range(6)]
    V8_NAMES += ['w0_%d' % _j, 'a0_%d' % _j, 'kk%d' % _j, 'ka%d' % _j, 'rk%d' % _j, 'gng%d' % _j, 'gnb%d' % _j]
V8_NAMES += ['v0_0']
V8 = {n: i for i, n in enumerate(V8_NAMES)}
NV8 = len(V8_NAMES)
V22_NAMES = []
for _li in range(4):
    V22_NAMES += ['fw0_%d' % _li, 'fw1_%d' % _li, 'fw2_%d' % _li, 'fb%d' % _li]
V22 = {n: i for i, n in enumerate(V22_NAMES)}
NV22 = len(V22_NAMES)

CM_MEAN1024, CM_MEAN256, CM_BLKMEAN64, CM_IDENT, CM_BLKONES = range(5)
NCM = 5


def fm(v, nch):
    return np.ascontiguousarray(np.asarray(v, np.float32).reshape(nch, 128).T)


def build_consts():
    cm = np.zeros((128, NCM, 128), np.float32)
    cm[:, CM_MEAN1024, :] = 1.0 / 1024
    cm[:, CM_MEAN256, :] = 1.0 / 256
    blk = np.zeros((128, 128), np.float32)
    blk[:64, :64] = 1.0
    blk[64:, 64:] = 1.0
    cm[:, CM_BLKMEAN64, :] = blk / 64.0
    cm[:, CM_IDENT, :] = np.eye(128, dtype=np.float32)
    cm[:, CM_BLKONES, :] = blk
    inv = (10000.0 ** (-np.linspace(0.0, 1.0, 64, dtype=np.float32))).astype(np.float32)
    pos = np.concatenate([np.arange(2064, dtype=np.float32), np.full(16, 16384.0, np.float32)])
    ang = (pos[None, :] * inv[:, None]).astype(np.float32)
    cos = np.cos(ang.astype(np.float64)).astype(np.float32)
    sin = np.sin(ang.astype(np.float64)).astype(np.float32)
    rope = np.zeros((128, 2, NTOK), np.float32)
    rope[:64, 0] = cos
    rope[64:, 0] = cos
    rope[:64, 1] = -sin
    rope[64:, 1] = sin
    idx = np.arange(128)
    diff = idx[None, :] - idx[:, None]
    rmask = np.zeros((128, 4, 128), np.float32)
    decin = np.zeros((128, 4, 128), np.float32)
    deck = np.zeros((128, 2, 4), np.float32)
    for h in range(4):
        lg = math.log1p(-2.0 ** (-5 - h))
        rmask[:, h, :] = np.where(diff >= 0, DK_SCALE * np.exp(lg * np.maximum(diff, 0)), 0.0)
        decin[:, h, :] = np.exp(lg * (idx + 1.0))[None, :]
        deck[:, 0, h] = DK_SCALE * np.exp(lg * (127.0 - idx))
        deck[:16, 1, h] = DK_SCALE * np.exp(lg * (15.0 - idx[:16]))
    i64 = np.arange(64)
    strictT = (i64[:, None] < i64[None, :]).astype(np.float32)
    inclT = (i64[:, None] <= i64[None, :]).astype(np.float32)
    mask4 = np.zeros((128, 4, 64), np.float32)
    mask4[:64, 0] = strictT
    mask4[:64, 1] = inclT
    mask4[:64, 2] = strictT
    mask4[:64, 3] = inclT
    maska = np.zeros((128, 64), np.float32)
    maska[:64] = (i64[:, None] > i64[None, :]).astype(np.float32)
    reset = np.ones((128, 8, 64), np.float32)
    reset[:, :, 0] = 0.0
    sel = np.zeros((128, 16, 128), np.float32)
    for b in range(16):
        sel[b, b, :] = 1.0
    small = np.zeros((128, 8), np.float32)
    small[:, 0] = EPS
    small[:, 1] = GN_EPS_C
    small[:, 2] = 1.0
    small[:, 3] = 0.0
    return dict(cm=cm, rope=rope, rmask=rmask, decin=decin, deck=deck, mask4=mask4, maska=maska,
                reset=reset, sel=sel, small=small)


_CACHE = {}
DBG = {'layers': [0, 1, 2, 3], 'even': True, 'rwkv': True, 'ffn': True, 'segs': [0, 1], 'ncores': 8,
       'evp': ['qk', 'v', 'ga', 'chunks', 'sample', 'wouta', 'lru', 'woutb', 'post', 'supd']}


def build_program():
    nc = bass.Bass("TRN2", target_bir_lowering=False)

    def din(name, shape):
        return nc.dram_tensor(name, list(shape), F32, kind="ExternalInput").ap()

    def dout(name, shape):
        return nc.dram_tensor(name, list(shape), F32, kind="ExternalOutput").ap()

    xT = din("xT", [D, NTOK])
    d_cm = din("cm", [128, NCM, 128])
    d_rope = din("rope", [128, 2, NTOK])
    d_rmask = din("rmask", [128, 4, 128])
    d_decin = din("decin", [128, 4, 128])
    d_deck = din("deck", [128, 2, 4])
    d_mask4 = din("mask4", [128, 4, 64])
    d_maska = din("maska", [128, 64])
    d_reset = din("reset", [128, 8, 64])
    d_sel = din("sel", [128, 16, 128])
    d_small = din("small", [128, 8])
    d_selh = din("selh", [16, 16, 2, 128])
    d_cmask = din("cmask", [16, 768])
    d_vec8 = din("vec8", [128, NV8, 8])
    d_vec22 = din("vec22", [128, NV22, 22])
    w_in = din("w_in", [2, D, 5120])
    w_swap = din("w_swap", [2, D, 1024])
    w_out = din("w_out", [2, 2048, D])
    d_wax = din("wax", [2, 128, 2, 8, 128])
    w_r = din("w_r", [2, D, D])
    w_k = din("w_k", [2, D, D])
    w_v = din("w_v", [2, D, D])
    w_o = din("w_o", [2, D, D])
    d_w1 = din("w1", [2, D, 64])
    d_w2 = din("w2", [2, 64, D])
    d_a1 = din("a1", [2, D, 64])
    d_a2 = din("a2", [2, 64, D])
    d_v1 = din("v1", [1, D, 32])
    d_v2 = din("v2", [1, 32, D])
    d_g1 = din("g1", [2, D, 128])
    d_g2 = din("g2", [2, 128, D])
    w_up = din("w_up", [4, D, 2 * DFF])
    w_down = din("w_down", [4, 8, 128, NFC * 128])
    s_ret = din("s_ret", [2, 16, 4, 128, 256])
    s_lru = din("s_lru", [2, 128, 8, 16])
    s_lconv = din("s_lconv", [2, 128, 8, 3, 16])
    s_rwkv = din("s_rwkv", [2, 16, 16, 64, 64])
    s_shift = din("s_shift", [2, 128, 8, 16])
    s_ffn = din("s_ffn", [4, 128, 22, 2, 16])

    yT = dout("yT", [D, NTOK])
    o_retp = dout("o_retp", [2, 4, 128, 256])
    o_lrup = dout("o_lrup", [128, 2, 8])
    o_lconvp = dout("o_lconvp", [128, 2, 8, 3])
    o_rwkvp = dout("o_rwkvp", [2, 128, 8, 64])
    o_shiftp = dout("o_shiftp", [128, 2, 8])
    o_ffnp = dout("o_ffnp", [128, 4, 22, 2])
    o_rets = dout("o_rets", [2, 16, 4, 128, 256])
    o_lrus = dout("o_lrus", [2, 128, 8, 16])
    o_lconvs = dout("o_lconvs", [2, 128, 8, 3, 16])
    o_rwkvs = dout("o_rwkvs", [2, 16, 16, 64, 64])
    o_shifts = dout("o_shifts", [2, 128, 8, 16])
    o_ffns = dout("o_ffns", [4, 128, 22, 2, 16])

    with ExitStack() as es:
        P = Prog(nc, es)
        X = P.sbuf([128, 8, TC], F32, "X")
        H = P.sbuf([128, 8 * TC], BF16, "H")
        VFD = Buf(None, "vfd")
        vf_dram = nc.dram_tensor("vf_scratch", [128, 8, NTOK], F32, kind="Internal").ap()
        WST0 = P.sbuf([128, 4096], F32, "WST0")
        WST = [WST0, WST0]
        WBF = [P.sbuf([128, 4096], BF16, "WBF%d" % i) for i in range(2)]
        BIG1 = P.sbuf([128, 22 * TC], BF16, "BIG1")
        BIG2 = P.sbuf([128, 9 * 1024], BF16, "BIG2")
        HBUF = P.sbuf([128, 1044], F32, "HBUF")
        SCR = [P.sbuf([128, 512], F32, "SCR%d" % i) for i in range(5)]
        SMP = [HBUF, HBUF]
        PS = [P.psum([128, 512], F32, "PS%d" % i) for i in range(8)]
        CM = P.sbuf([128, NCM, 128], F32, "CM")
        CMB = P.sbuf([128, NCM, 128], BF16, "CMB")
        ROPE = P.sbuf([128, 2, 512], F32, "ROPE")
        RMASK = P.sbuf([128, 4, 128], F32, "RMASK")
        DECIN = P.sbuf([128, 4, 128], F32, "DECIN")
        DECK = P.sbuf([128, 2, 4], F32, "DECK")
        SMALL = P.sbuf([128, 8], F32, "SMALL")
        VEC8 = P.sbuf([128, NV8, 8], F32, "VEC8")
        VEC22 = P.sbuf([128, NV22, 22], F32, "VEC22")
        RS = [P.sbuf([128, 4, 256], F32, "RS%d" % j) for j in range(2)]
        RSB = P.sbuf([128, 4, 256], BF16, "RSB")
        MST = [P.sbuf([128, 8, 64], F32, "MST%d" % j) for j in range(2)]
        CAR = P.sbuf([128, 160], F32, "CAR")
        FFH = P.sbuf([128, 4, 22, 2], F32, "FFH")
        FFO = P.sbuf([128, 4, 22, 2], F32, "FFO")
        SP8 = P.sbuf([128, 2, 8], F32, "SP8")
        QKS = P.sbuf([128, 8, 16], F32, "QKS")
        STS = P.sbuf([128, 22 * 2 * 16], F32, "STS")
        LCS = P.sbuf([128, 8, 3, 16], F32, "LCS")
        WAX = P.sbuf([128, 2, 8, 128], BF16, "WAX")
        SELH = P.sbuf([16, 16, 2, 128], BF16, "SELH")

        LHC = CAR.t[:, 0:16].rearrange("p (j g) -> p j g", g=8)
        LRO = CAR.t[:, 16:32].rearrange("p (j g) -> p j g", g=8)
        LCH = CAR.t[:, 32:80].rearrange("p (j g t) -> p j g t", g=8, t=3)
        LCO = CAR.t[:, 80:128].rearrange("p (j g t) -> p j g t", g=8, t=3)
        SHC = CAR.t[:, 128:144].rearrange("p (j g) -> p j g", g=8)
        SHO = CAR.t[:, 144:160].rearrange("p (j g) -> p j g", g=8)
        EPSC = SMALL.t[:, 0:1]
        GNEPS = SMALL.t[:, 1:2]

        state = {'ps': 0, 'ws': 0, 'scr': 0, 'ev': 0, 'reserved': set()}

        def ps():
            while True:
                i = state['ps'] % 8
                state['ps'] += 1
                if i not in state['reserved']:
                    return PS[i]

        def scr():
            i = state['scr'] % 5
            state['scr'] += 1
            return SCR[i]

        def mm(out_ap, lhsT, rhs, start, stop, reads, writes):
            P.op('pe', lambda e: e.matmul(out_ap, lhsT=lhsT, rhs=rhs, start=start, stop=stop), reads, writes)

        def cp(eng, out_ap, in_ap, reads, writes):
            if eng == 'act':
                P.op('act', lambda e: e.copy(out=out_ap, in_=in_ap), reads, writes)
            else:
                P.op(eng, lambda e: e.tensor_copy(out=out_ap, in_=in_ap), reads, writes)

        def evcp(out_ap, in_ap, reads, writes):
            state['ev'] += 1
            cp('act' if state['ev'] % 2 else 'dve', out_ap, in_ap, reads, writes)

        def tt(eng, out_ap, a, b, op, reads, writes):
            P.op(eng, lambda e: e.tensor_tensor(out=out_ap, in0=a, in1=b, op=op), reads, writes)

        def ts(eng, out_ap, a, s1, s2, op0, op1, reads, writes):
            if s2 is None:
                P.op(eng, lambda e: e.tensor_scalar(out=out_ap, in0=a, scalar1=s1, scalar2=None, op0=op0), reads, writes)
            else:
                P.op(eng, lambda e: e.tensor_scalar(out=out_ap, in0=a, scalar1=s1, scalar2=s2, op0=op0, op1=op1), reads, writes)

        def stt(out_ap, a, s, b, op0, op1, reads, writes):
            P.op('dve', lambda e: e.scalar_tensor_tensor(out=out_ap, in0=a, scalar=s, in1=b, op0=op0, op1=op1), reads, writes)

        def act(out_ap, in_ap, func, reads, writes, bias=None, scale=None):
            kw = {}
            if bias is not None:
                kw['bias'] = bias
            if scale is not None:
                kw['scale'] = scale
            P.op('act', lambda e: e.activation(out=out_ap, in_=in_ap, func=func, **kw), reads, writes)

        def v8(name):
            return VEC8.t[:, V8[name], :]

        def v8c(name, k):
            return VEC8.t[:, V8[name], k:k + 1]

        def v22c(name, k):
            return VEC22.t[:, V22[name], k:k + 1]

        def hview(k, c0, n):
            return H.t[:, k * TC + c0:k * TC + c0 + n]

        def b1(s, c0, n):
            return BIG1.t[:, s * TC + c0:s * TC + c0 + n]

        def b1f(r, c0, n):
            return BIG1.t[:, r * 2 * TC:(r + 1) * 2 * TC].bitcast(F32)[:, c0:c0 + n]

        def b2(s, c0, n):
            return BIG2.t[:, s * TC + c0:s * TC + c0 + n]

        def wload(parts, KC, NC):
            s = state['ws'] % 2
            state['ws'] += 1
            st, bf = WST[s], WBF[s]
            dstv = st.t[:, 0:KC * NC].rearrange("p (k n) -> p k n", n=NC)
            for (src, off, ncols) in parts:
                P.dma(dstv[:, :, off:off + ncols], src.rearrange("(k p) n -> p k n", p=128), writes=[st])
            P.op('pool', lambda e: e.tensor_copy(out=bf.t[:, 0:KC * NC], in_=st.t[:, 0:KC * NC]), reads=[st], writes=[bf])
            return bf, bf.t[:, 0:KC * NC].rearrange("p (k n) -> p k n", n=NC)

        def wload_to(dst_buf, dst_view, src, KC, NC, dkey=None):
            s = state['ws'] % 2
            state['ws'] += 1
            st = WST[s]
            dstv = st.t[:, 0:KC * NC].rearrange("p (k n) -> p k n", n=NC)
            P.dma(dstv, src.rearrange("(k p) n -> p k n", p=128), writes=[st])
            P.op('pool', lambda e: e.tensor_copy(out=dst_view, in_=dstv), reads=[st], writes=[(dst_buf, dkey)])

        P.dma(CM.t[:], d_cm, writes=[CM])
        cp('dve', CMB.t[:], CM.t[:], [CM], [CMB])
        for (b_, d_) in [(RMASK, d_rmask), (DECIN, d_decin), (DECK, d_deck), (SMALL, d_small), (VEC8, d_vec8), (VEC22, d_vec22)]:
            P.dma(b_.t[:], d_, writes=[b_])
        P.dma(WST[0].t[0:16, 0:4096].rearrange('p (a b c) -> p a b c', a=16, b=2), d_selh, writes=[WST[0]])
        cp('dve', SELH.t[:], WST[0].t[0:16, 0:4096].rearrange('p (a b c) -> p a b c', a=16, b=2), [WST[0]], [SELH])
        P.op('pool', lambda e: e.memset(CAR.t[:], 0.0), writes=[CAR])
        P.op('pool', lambda e: e.memset(FFH.t[:], 0.0), writes=[FFH])
        P.op('pool', lambda e: e.memset(FFO.t[:], 0.0), writes=[FFO])
        P.op('pool', lambda e: e.memset(STS.t[:], 0.0), writes=[STS])
        P.op('pool', lambda e: e.memset(LCS.t[:], 0.0), writes=[LCS])
        for j in range(2):
            P.op('pool', lambda e, j=j: e.memset(RS[j].t[:], 0.0), writes=[RS[j]])
            P.op('pool', lambda e, j=j: e.memset(MST[j].t[:], 0.0), writes=[MST[j]])
        for j in range(2):
            act(SP8.t[:, j, :], v8('lam%d' % j), AF.Exp, [VEC8], [SP8], scale=-1.0)
            act(SP8.t[:, j, :], SP8.t[:, j, :], AF.Ln, [SP8], [SP8], bias=1.0)
            ts('dve', SP8.t[:, j, :], SP8.t[:, j, :], -8.0, None, ALU.mult, None, [SP8], [SP8])

        def rmsnorm(gname, out_fn, out_buf, c_tiles=TOK_TILES):
            for (c0, n) in c_tiles:
                sq = WBF[state['ws'] % 2]
                sqv = sq.t[:, 0:8 * n].rearrange("p (k n) -> p k n", n=n)
                act(sqv, X.t[:, :, c0:c0 + n], AF.Square, [X], [sq])
                pb = ps()
                for k in range(8):
                    mm(pb.t[:, 0:n], CMB.t[:, CM_MEAN1024, :], sqv[:, k, :], k == 0, k == 7, [CMB, sq], [pb])
                rs = scr()
                act(rs.t[:, 0:n], pb.t[:, 0:n], AF.Sqrt, [pb, SMALL], [rs], bias=EPSC)
                P.op('dve', lambda e, rs=rs, n=n: e.reciprocal(out=rs.t[:, 0:n], in_=rs.t[:, 0:n]), [rs], [rs])
                for k in range(8):
                    stt(out_fn(k, c0, n), X.t[:, k, c0:c0 + n], v8c(gname, k), rs.t[:, 0:n], ALU.mult, ALU.mult,
                        [X, VEC8, rs], [out_buf])

        def proj_fm(wbuf, wview, KC, col0, src_fn, src_reads, evac):
            for (c0, n) in TOK_TILES:
                pb = ps()
                for k in range(KC):
                    mm(pb.t[:, 0:n], wview[:, k, col0:col0 + 128], src_fn(k, c0, n), k == 0, k == KC - 1,
                       [wbuf] + src_reads, [pb])
                evac(pb, c0, n)

        def x_accum(dc):
            def ev(pb, c0, n):
                tt('dve', X.t[:, dc, c0:c0 + n], X.t[:, dc, c0:c0 + n], pb.t[:, 0:n], ALU.add, [X, pb], [X])
            return ev

        def ffn(seg, li):
            rmsnorm('nffn%d' % li, hview, H)
            BIG1.fence()
            if seg == 1:
                P.dma(STS.t[:, 0:22 * 32].rearrange("p (f t b) -> p f t b", t=2, b=16), s_ffn[li], writes=[STS])
            stsv = STS.t[:, 0:22 * 32].rearrange("p (f t b) -> p f t b", t=2, b=16)
            NP = TC if seg == 0 else 1024
            for f2 in range(11):
                bf, wview = wload([(w_up[li][:, f2 * 256:(f2 + 1) * 256], 0, 256), (w_up[li][:, DFF + f2 * 256:DFF + (f2 + 1) * 256], 256, 256)], 8, 512)
                for fi in range(2):
                    fc = f2 * 2 + fi
                    UG = HBUF
                    cp('pool', UG.t[:, 0:2], FFH.t[:, li, fc, :], [FFH], [UG])
                    pbv = []
                    for (c0, n) in TOK_TILES:
                        pg = ps()
                        for k in range(8):
                            mm(pg.t[:, 0:n], wview[:, k, fi * 128:fi * 128 + 128], hview(k, c0, n), k == 0, k == 7, [bf, H], [pg])
                        cp('act', UG.t[:, 2 + c0:2 + c0 + n], pg.t[:, 0:n], [pg], [UG])
                        pv = ps()
                        for k in range(8):
                            mm(pv.t[:, 0:n], wview[:, k, 256 + fi * 128:256 + fi * 128 + 128], hview(k, c0, n), k == 0, k == 7, [bf, H], [pv])
                        pbv.append(pv)
                    if seg == 0:
                        cp('pool', FFH.t[:, li, fc, :], UG.t[:, 2 + TC - 2:2 + TC], [UG], [FFH])
                    else:
                        cp('pool', FFO.t[:, li, fc, :], UG.t[:, 2 + 1022:2 + 1024], [UG], [FFO])
                    T1 = (BIG2, None)
                    par = fc % 2
                    t1 = BIG2.t[:, par * 2 * TC:(par + 1) * 2 * TC].bitcast(F32)
                    kT1 = (BIG2, ('t1', par))
                    ts('dve', t1[:, 0:TC], UG.t[:, 0:TC], v22c('fw0_%d' % li, fc), v22c('fb%d' % li, fc), ALU.mult, ALU.add,
                       [UG, VEC22], [kT1])
                    stt(t1[:, 0:TC], UG.t[:, 1:TC + 1], v22c('fw1_%d' % li, fc), t1[:, 0:TC], ALU.mult, ALU.add, [UG, VEC22, kT1], [kT1])
                    stt(t1[:, 0:TC], UG.t[:, 2:TC + 2], v22c('fw2_%d' % li, fc), t1[:, 0:TC], ALU.mult, ALU.add, [UG, VEC22, kT1], [kT1])
                    if seg == 1:
                        sc = slice(1024, 1040)
                        ts('dve', t1[:, sc], stsv[:, fc, 0, :], v22c('fw0_%d' % li, fc), v22c('fb%d' % li, fc), ALU.mult, ALU.add,
                           [STS, VEC22], [kT1])
                        stt(t1[:, sc], stsv[:, fc, 1, :], v22c('fw1_%d' % li, fc), t1[:, sc], ALU.mult, ALU.add, [STS, VEC22, kT1], [kT1])
                        stt(t1[:, sc], UG.t[:, 2 + 1024:2 + 1040], v22c('fw2_%d' % li, fc), t1[:, sc], ALU.mult, ALU.add, [UG, VEC22, kT1], [kT1])
                        P.dma(o_ffns[li, :, fc, 0, :], stsv[:, fc, 1, :], reads=[STS], eng='pool')
                        P.dma(o_ffns[li, :, fc, 1, :], UG.t[:, 2 + 1024:2 + 1040], reads=[UG], eng='pool')
                    act(t1[:, 0:TC], t1[:, 0:TC], AF.Gelu_apprx_tanh, [kT1], [kT1])
                    for ti, (c0, n) in enumerate(TOK_TILES):
                        tt('dve', b1(fc, c0, n), t1[:, c0:c0 + n], pbv[ti].t[:, 0:n], ALU.mult, [kT1, pbv[ti]], [(BIG1, fc)])
            BIG1.fence()
            for dc in range(8):
                s = state['ws'] % 2
                state['ws'] += 1
                st, bf = WST[s], WBF[s]
                dstv = st.t[:, 0:22 * 128].rearrange("p (k n) -> p k n", n=128)
                P.dma(st.t[:, 0:22 * 128], w_down[li, dc], writes=[st])
                P.op('pool', lambda e, bf=bf, st=st: e.tensor_copy(out=bf.t[:, 0:2816], in_=st.t[:, 0:2816]), reads=[st], writes=[bf])
                wview = bf.t[:, 0:2816].rearrange("p (k n) -> p k n", n=128)
                proj_fm(bf, wview, 22, 0, lambda k, c0, n: b1(k, c0, n), [BIG1], x_accum(dc))

        ctx = dict(locals())
        build_even(ctx)
        build_rwkv(ctx)
        even_layer = ctx['even_layer']
        rwkv_layer = ctx['rwkv_layer']

        for seg in DBG['segs']:
            P.dma(X.t[:], xT[:, seg * TC:(seg + 1) * TC].rearrange("(k p) t -> p k t", p=128), writes=[X])
            for li in DBG['layers']:
                if li % 2 == 0:
                    if DBG['even']:
                        even_layer(seg, li)
                else:
                    if DBG['rwkv']:
                        rwkv_layer(seg, li)
                if DBG['ffn']:
                    ffn(seg, li)
            for (c0, n) in TOK_TILES:
                s = state['ws'] % 2
                state['ws'] += 1
                st = WST[s]
                ov = st.t[:, 0:8 * n].rearrange("p (k n) -> p k n", n=n)
                rmsnorm('nfinal', lambda k, c0_, n_, ov=ov: ov[:, k, :], st, c_tiles=[(c0, n)])
                P.dma(yT[:, seg * TC + c0:seg * TC + c0 + n].rearrange("(k p) t -> p k t", p=128), ov, reads=[st], eng='pool')
        for j in range(2):
            P.dma(o_retp[j].rearrange("h d e -> d h e"), RS[j].t[:], reads=[RS[j]], eng='pool')
            P.dma(o_rwkvp[j], MST[j].t[:, :, :], reads=[MST[j]], eng='pool')
        P.dma(o_lrup, LRO, reads=[CAR], eng='pool')
        P.dma(o_lconvp, LCO, reads=[CAR], eng='pool')
        P.dma(o_shiftp, SHO, reads=[CAR], eng='pool')
        P.dma(o_ffnp, FFO.t[:], reads=[FFO], eng='pool')
        P.finish()
    return nc


def build_even(ctx):
    globals().update(ctx)

    def seg_chunks(seg):
        if seg == 0:
            return [(0, 16)] + [(16 + 128 * i, 128) for i in range(8)]
        return [(128 * i, 128) for i in range(8)]

    def vt(ci, n, c0, w):
        return BIG2.t[0:n, ci * 1024 + c0:ci * 1024 + c0 + w]

    def post(src3, src_reads, n, h, c0):
        OFb, OBb, OSb = scr(), scr(), scr()
        OF = OFb.t[:, 0:2 * n].rearrange("p (e c) -> p e c", c=n)
        OB = OBb.t[:, 0:256].bitcast(BF16)[:, 0:2 * n].rearrange("p (e c) -> p e c", c=n)
        OS = OSb.t[:, 0:256].bitcast(BF16)[:, 0:2 * n].rearrange("p (e c) -> p e c", c=n)
        cp('act', OF, src3, src_reads, [OFb])
        act(OS, src3, AF.Square, src_reads, [OSb])
        cp('pool', OB, OF, [OFb], [OBb])
        pst = ps()
        for ec in range(2):
            mm(pst.t[:, 0:n], CMB.t[:, CM_MEAN256, :], OB[:, ec, :], ec == 0, ec == 1, [CMB, OBb], [pst])
        for ec in range(2):
            mm(pst.t[:, 128:128 + n], CMB.t[:, CM_MEAN256, :], OS[:, ec, :], ec == 0, ec == 1, [CMB, OSb], [pst])
        MEb, VAb = scr(), scr()
        ME = MEb.t[:, 0:n]
        VA = VAb.t[:, 0:n]
        cp('act', ME, pst.t[:, 0:n], [pst], [MEb])
        tt('dve', VA, ME, ME, ALU.mult, [MEb], [VAb])
        tt('dve', VA, pst.t[:, 128:128 + n], VA, ALU.subtract, [pst, VAb], [VAb])
        ts('dve', VA, VA, 0.0, None, ALU.max, None, [VAb], [VAb])
        act(VA, VA, AF.Sqrt, [VAb, SMALL], [VAb], bias=EPSC)
        P.op('dve', lambda e: e.reciprocal(out=VA, in_=VA), [VAb], [VAb])
        for ec in range(2):
            tt('dve', OF[:, ec, :], OF[:, ec, :], ME, ALU.subtract, [OFb, MEb], [OFb])
            stt(OF[:, ec, :], OF[:, ec, :], VEC8.t[:, V8['gn%d' % cur['j']], 2 * h + ec:2 * h + ec + 1], VA, ALU.mult, ALU.mult,
                [OFb, VEC8, VAb], [OFb])
            tt('pool', b1(8 + 2 * h + ec, c0, n), OF[:, ec, :], b1(8 + 2 * h + ec, c0, n), ALU.mult,
               [OFb, (BIG1, 8 + 2 * h + ec)], [(BIG1, 8 + 2 * h + ec)])

    cur = {'j': 0}

    def even_layer(seg, li):
        j = li // 2
        cur['j'] = j
        rmsnorm('nmix%d' % li, hview, H)
        BIG1.fence()
        BIG2.fence()
        s = state['ws'] % 2
        state['ws'] += 1
        P.dma(WST[s].t[:, 0:2048].rearrange("p (a g d) -> p a g d", a=2, g=8), d_wax[j], writes=[WST[s]])
        cp('pool', WAX.t[:], WST[s].t[:, 0:2048].rearrange("p (a g d) -> p a g d", a=2, g=8), [WST[s]], [WAX])
        EVP = DBG['evp']
        ropes = {0: (ROPE, ROPE.t[:, :, 0:512]),
                 512: (HBUF, HBUF.t[:, 0:1024].rearrange("p (a n) -> p a n", a=2)),
                 1024: (LCS, LCS.t[:, 0:2, 0, :])}
        for c0_, (rb_, rv_) in ropes.items():
            n_ = 16 if c0_ == 1024 else 512
            P.dma(rv_, d_rope[:, :, seg * TC + c0_:seg * TC + c0_ + n_], writes=[rb_])
        for t in range(4 if 'qk' in EVP else 0):
            bf, wv = wload([(w_in[j][:, t * 256:(t + 1) * 256], 0, 256), (w_swap[j][:, t * 256:(t + 1) * 256], 256, 256)], 8, 512)
            for hi in range(2):
                hc = 2 * t + hi
                for (c0, n) in TOK_TILES:
                    po_, ps_ = ps(), ps()
                    for k in range(8):
                        mm(po_.t[:, 0:n], wv[:, k, hi * 128:hi * 128 + 128], hview(k, c0, n), k == 0, k == 7, [bf, H], [po_])
                    for k in range(8):
                        mm(ps_.t[:, 0:n], wv[:, k, 256 + hi * 128:256 + hi * 128 + 128], hview(k, c0, n), k == 0, k == 7, [bf, H], [ps_])
                    T1, T2 = scr(), scr()
                    rb, rv = ropes[c0]
                    tt('dve', T1.t[:, 0:n], po_.t[:, 0:n], rv[:, 0, :], ALU.mult, [po_, rb], [T1])
                    tt('dve', T2.t[:, 0:n], ps_.t[:, 0:n], rv[:, 1, :], ALU.mult, [ps_, rb], [T2])
                    tt('pool', b1(hc, c0, n), T1.t[:, 0:n], T2.t[:, 0:n], ALU.add, [T1, T2], [(BIG1, hc)])
                    if seg == 1 and c0 == 1024:
                        tt('pool', QKS.t[:, hc, :], T1.t[:, 0:n], T2.t[:, 0:n], ALU.add, [T1, T2], [QKS])
        chunks = seg_chunks(seg)
        allch = chunks + ([(1024, 16)] if seg == 1 else [])
        for wt in range(2 if 'v' in EVP else 0):
            bf, wv = wload([(w_in[j][:, 1024 + wt * 512:1024 + (wt + 1) * 512], 0, 512)], 8, 512)
            for ci, (c0, n) in enumerate(allch):
                pb = ps()
                for k in range(8):
                    mm(pb.t[0:n, 0:512], hview(k, c0, n), wv[:, k, 0:512], k == 0, k == 7, [bf, H], [pb])
                vm = DBG.get('vmode', '')
                if 'dveonly' in vm:
                    cp('dve', vt(ci, n, wt * 512, 512), pb.t[0:n, 0:512], [pb], [(BIG2, ci)])
                elif 'actonly' in vm:
                    cp('act', vt(ci, n, wt * 512, 512), pb.t[0:n, 0:512], [pb], [(BIG2, ci)])
                elif 'noevac' in vm:
                    pass
                else:
                    evcp(vt(ci, n, wt * 512, 512), pb.t[0:n, 0:512], [pb], [(BIG2, ci)])
        for wt in range(2 if 'ga' in EVP else 0):
            bf, wv = wload([(w_in[j][:, 2048 + wt * 512:2048 + (wt + 1) * 512], 0, 512)], 8, 512)
            for q in range(4):
                oc = wt * 4 + q

                def ev(pb, c0, n, oc=oc):
                    act(b1(8 + oc, c0, n), pb.t[:, 0:n], AF.Silu, [pb], [(BIG1, 8 + oc)])
                proj_fm(bf, wv, 8, q * 128, hview, [H], ev)
        cp('act', RSB.t[:], RS[j].t[:], [RS[j]], [RSB])
        for ci, (c0, n) in enumerate(chunks if 'chunks' in EVP else []):
            di = 0 if n == 128 else 1
            for h in range(4):
                pb = ps()
                mm(pb.t[0:n, 0:n], b1(4 + h, c0, n), b1(h, c0, n), True, True, [(BIG1, 4 + h), (BIG1, h)], [pb])
                SCb, QDb, KDb = scr(), scr(), scr()
                SC = SCb.t[:, 0:64].bitcast(BF16)
                QD = QDb.t[:, 0:64].bitcast(BF16)
                KD = KDb.t[:, 0:64].bitcast(BF16)
                tt('dve', SC[0:n, 0:n], pb.t[0:n, 0:n], RMASK.t[0:n, h, 0:n], ALU.mult, [pb, RMASK], [SCb])
                tt('pool', QD[:, 0:n], b1(h, c0, n), DECIN.t[:, h, 0:n], ALU.mult, [(BIG1, h), DECIN], [QDb])
                po = ps()
                for ec in range(2):
                    mm(po.t[:, ec * 128:ec * 128 + n], vt(ci, n, h * 256 + ec * 128, 128), SC[0:n, 0:n], True, False,
                       [(BIG2, ci), SCb], [po])
                    mm(po.t[:, ec * 128:ec * 128 + n], RSB.t[:, h, ec * 128:ec * 128 + 128], QD[:, 0:n], False, True,
                       [RSB, QDb], [po])
                src3 = po.t[:, 0:256].rearrange("p (e c) -> p e c", c=128)[:, :, 0:n]
                if 'post' in EVP:
                    post(src3, [po], n, h, c0)
                if 'supd' not in EVP:
                    continue
                pt = ps()
                ptb = pt.t[:, :].bitcast(BF16)
                P.op('pe', lambda e, ptb=ptb, n=n, h=h, c0=c0: e.transpose(ptb[0:n, 0:128], b1(4 + h, c0, n), CMB.t[:, CM_IDENT, :]),
                     [(BIG1, 4 + h), CMB], [pt])
                ts('dve', KD[0:n, 0:128], ptb[0:n, 0:128], DECK.t[0:n, di, h:h + 1], None, ALU.mult, None, [pt, DECK], [KDb])
                pu = ps()
                mm(pu.t[:, 0:256], KD[0:n, 0:128], vt(ci, n, h * 256, 256), True, True, [KDb, (BIG2, ci)], [pu])
                stt(RS[j].t[:, h, :], RS[j].t[:, h, :], float(GAMMA[h] ** n), pu.t[:, 0:256], ALU.mult, ALU.add, [RS[j], pu], [RS[j]])
                cp('act', RSB.t[:, h, :], RS[j].t[:, h, :], [RS[j]], [RSB])
        if seg == 1 and 'sample' in EVP:
            c0 = 1024
            KFb = scr()
            KF = KFb.t[:, 0:64].rearrange("p (h b) -> p h b", b=16)
            ts('dve', KF, QKS.t[:, 4:8, :], DK_SCALE, None, ALU.mult, None, [QKS], [KFb])
            POS = ps()
            state['reserved'].add(PS.index(POS))
            for b in range(16):
                RSSb = SMP[b % 2]
                RSS = RSSb.t[:, 0:1024].rearrange("p (h e) -> p h e", e=256)
                P.dma(RSS, s_ret[j, b].rearrange("h d e -> d h e"), writes=[RSSb])
                pbv = [ps(), ps()]
                for half in range(2):
                    for hh in range(2):
                        mm(pbv[half].t[:, 0:512], SELH.t[0:16, b, hh, :], BIG2.t[0:16, 8 * 1024 + half * 512:8 * 1024 + (half + 1) * 512], hh == 0, hh == 1,
                           [SELH, (BIG2, 8)], [pbv[half]])
                for h in range(4):
                    ts('dve', RSS[:, h, :], RSS[:, h, :], float(GAMMA[h]), None, ALU.mult, None, [RSSb], [RSSb])
                    stt(RSS[:, h, :], pbv[h // 2].t[:, (h % 2) * 256:(h % 2) * 256 + 256], KF[:, h, b:b + 1], RSS[:, h, :], ALU.mult, ALU.add,
                        [pbv[h // 2], KFb, RSSb], [RSSb])
                P.dma(o_rets[j, b].rearrange("h d e -> d h e"), RSS, reads=[RSSb], eng='pool')
                for h in range(4):
                    for ec in range(2):
                        col = (h * 2 + ec) * 16 + b
                        mm(POS.t[:, col:col + 1], RSS[:, h, ec * 128:(ec + 1) * 128], QKS.t[:, h, b:b + 1], True, True, [RSSb, QKS], [POS])
            for h in range(4):
                src3 = POS.t[:, h * 32:h * 32 + 32].rearrange("p (e c) -> p e c", c=16)
                post(src3, [POS], 16, h, c0)
            state['reserved'].discard(PS.index(POS))
        for wt in range(2 if 'wouta' in EVP else 0):
            bf, wv = wload([(w_out[j][0:1024, wt * 512:(wt + 1) * 512], 0, 512)], 8, 512)
            for q in range(4):
                proj_fm(bf, wv, 8, q * 128, lambda k, c0, n: b1(8 + k, c0, n), [BIG1], x_accum(wt * 4 + q))
        BIG1.fence()
        BIG2.fence()
        NP = TC if seg == 0 else 1024
        if seg == 1:
            P.dma(STS.t[:, 0:128].rearrange("p (g b) -> p g b", b=16), s_lru[j], writes=[STS])
            P.dma(STS.t[:, 128:512].rearrange("p (g t b) -> p g t b", t=3, b=16), s_lconv[j], writes=[STS])
        SLR = STS.t[:, 0:128].rearrange("p (g b) -> p g b", b=16)
        SLC = STS.t[:, 128:512].rearrange("p (g t b) -> p g t b", t=3, b=16)
        XB = HBUF
        XC, RR, II, AA, TT_, HS, GB = [(lambda c0, n, r=r: b1f(r, c0, n)) for r in range(7)]
        keys = [(BIG1, ('f', r)) for r in range(7)]
        kXC, kRR, kII, kAA, kTT, kHS, kGB = keys
        kXCB = (BIG1, ('f', 7))

        def XCB(c0, n):
            return BIG1.t[:, 7 * 2 * TC + c0:7 * 2 * TC + c0 + n]
        for gp in range(4 if 'lru' in EVP else 0):
            bf, wv = wload([(w_in[j][:, 3072 + gp * 256:3072 + (gp + 1) * 256], 0, 256),
                            (w_in[j][:, 4096 + gp * 256:4096 + (gp + 1) * 256], 256, 256)], 8, 512)
            for gi in range(2):
                g = gp * 2 + gi
                cw = [VEC8.t[:, V8['cw%d_%d' % (m, j)], g:g + 1] for m in range(4)]
                cb = VEC8.t[:, V8['cb%d' % j], g:g + 1]
                cp('pool', XB.t[:, 0:3], LCH[:, j, g, :], [CAR], [XB])

                def evx(pb, c0, n):
                    cp('act', XB.t[:, 3 + c0:3 + c0 + n], pb.t[:, 0:n], [pb], [XB])
                proj_fm(bf, wv, 8, gi * 128, hview, [H], evx)

                def evg(pb, c0, n):
                    act(GB(c0, n), pb.t[:, 0:n], AF.Gelu_apprx_tanh, [pb], [kGB])
                proj_fm(bf, wv, 8, 256 + gi * 128, hview, [H], evg)
                if seg == 0:
                    cp('pool', LCH[:, j, g, :], XB.t[:, 3 + TC - 3:3 + TC], [XB], [CAR])
                else:
                    cp('pool', LCO[:, j, g, :], XB.t[:, 3 + 1021:3 + 1024], [XB], [CAR])
                ts('dve', XC(0, TC), XB.t[:, 0:TC], cw[0], cb, ALU.mult, ALU.add, [XB, VEC8], [kXC])
                for m in range(1, 4):
                    stt(XC(0, TC), XB.t[:, m:m + TC], cw[m], XC(0, TC), ALU.mult, ALU.add, [XB, VEC8, kXC], [kXC])
                if seg == 1:
                    ts('dve', XC(1024, 16), SLC[:, g, 0, :], cw[0], cb, ALU.mult, ALU.add, [STS, VEC8], [kXC])
                    stt(XC(1024, 16), SLC[:, g, 1, :], cw[1], XC(1024, 16), ALU.mult, ALU.add, [STS, VEC8, kXC], [kXC])
                    stt(XC(1024, 16), SLC[:, g, 2, :], cw[2], XC(1024, 16), ALU.mult, ALU.add, [STS, VEC8, kXC], [kXC])
                    stt(XC(1024, 16), XB.t[:, 3 + 1024:3 + 1040], cw[3], XC(1024, 16), ALU.mult, ALU.add, [XB, VEC8, kXC], [kXC])
                    cp('pool', LCS.t[:, g, 0, :], SLC[:, g, 1, :], [STS], [LCS])
                    cp('pool', LCS.t[:, g, 1, :], SLC[:, g, 2, :], [STS], [LCS])
                    cp('pool', LCS.t[:, g, 2, :], XB.t[:, 3 + 1024:3 + 1040], [XB], [LCS])
                cp('pool', XCB(0, TC), XC(0, TC), [kXC], [kXCB])
                for which, dst, kd, bname in ((0, RR, kRR, 'ba%d' % j), (1, II, kII, 'bx%d' % j)):
                    for (c0, n) in TOK_TILES:
                        pb = ps()
                        mm(pb.t[:, 0:n], WAX.t[:, which, g, :], XCB(c0, n), True, True, [WAX, kXCB], [pb])
                        act(dst(c0, n), pb.t[:, 0:n], AF.Sigmoid, [pb, VEC8], [kd], bias=VEC8.t[:, V8[bname], g:g + 1])
                act(AA(0, TC), RR(0, TC), AF.Exp, [kRR, SP8], [kAA], scale=SP8.t[:, j, g:g + 1])
                tt('pool', TT_(0, TC), AA(0, TC), AA(0, TC), ALU.mult, [kAA], [kTT])
                ts('pool', TT_(0, TC), TT_(0, TC), -1.0, 1.0, ALU.mult, ALU.add, [kTT], [kTT])
                ts('pool', TT_(0, TC), TT_(0, TC), 0.0, None, ALU.max, None, [kTT], [kTT])
                act(TT_(0, TC), TT_(0, TC), AF.Sqrt, [kTT], [kTT])
                tt('pool', II(0, TC), II(0, TC), XC(0, TC), ALU.mult, [kII, kXC], [kII])
                tt('dve', II(0, TC), II(0, TC), TT_(0, TC), ALU.mult, [kII, kTT], [kII])
                init = 0.0 if seg == 0 else LHC[:, j, g:g + 1]
                P.op('dve', lambda e, init=init: e.tensor_tensor_scan(out=HS(0, NP), data0=AA(0, NP), data1=II(0, NP), initial=init,
                                                                      op0=ALU.mult, op1=ALU.add), [kAA, kII, CAR], [kHS])
                if seg == 0:
                    cp('pool', LHC[:, j, g:g + 1], HS(NP - 1, 1), [kHS], [CAR])
                else:
                    cp('pool', LRO[:, j, g:g + 1], HS(1023, 1), [kHS], [CAR])
                    tt('pool', HS(1024, 16), AA(1024, 16), SLR[:, g, :], ALU.mult, [kAA, STS], [kHS])
                    tt('pool', HS(1024, 16), HS(1024, 16), II(1024, 16), ALU.add, [kHS, kII], [kHS])
                    cp('pool', LCS.t[:, 0, 0, 0:1] if False else STS.t[:, 512 + g * 16:512 + g * 16 + 16], HS(1024, 16), [kHS], [STS])
                tt('dve', b2(g, 0, TC), HS(0, TC), GB(0, TC), ALU.mult, [kHS, kGB], [(BIG2, ('y', g))])
        if seg == 1:
            P.dma(o_lrus[j], STS.t[:, 512:640].rearrange("p (g b) -> p g b", b=16), reads=[STS], eng='pool')
            P.dma(o_lconvs[j], LCS.t[:], reads=[LCS], eng='pool')
        BIG2.fence()
        for wt in range(2 if 'woutb' in EVP else 0):
            bf, wv = wload([(w_out[j][1024:2048, wt * 512:(wt + 1) * 512], 0, 512)], 8, 512)
            for q in range(4):
                proj_fm(bf, wv, 8, q * 128, lambda k, c0, n: b2(k, c0, n), [BIG2], x_accum(wt * 4 + q))
        BIG1.fence()
        BIG2.fence()

    ctx['even_layer'] = even_layer


def build_rwkv(ctx):
    globals().update(ctx)
    W0 = WST[0]
    B0 = WBF[0]
    B1_ = WBF[1]

    def Q(idx):
        return W0.t[:, idx * 136:idx * 136 + 128].rearrange("p (k n) -> p k n", n=16)

    def kq(idx):
        return (W0, ('q', idx))

    def bc(name):
        return VEC8.t[:, V8[name], :].rearrange("p (k o) -> p k o", o=1).to_broadcast([128, 8, 16])

    def p3(pb):
        return pb.t[:, 0:128].rearrange("p (k n) -> p k n", n=16)

    WRv = BIG1.t[:, 0:8192].rearrange("p (k n) -> p k n", n=1024)
    WKv = BIG1.t[:, 8192:16384].rearrange("p (k n) -> p k n", n=1024)
    WVv = BIG2.t[:, 0:8192].rearrange("p (k n) -> p k n", n=1024)
    WOv = H.t[:, 0:8192].rearrange("p (k n) -> p k n", n=1024)
    o_ = 16384
    W1v = BIG1.t[:, o_:o_ + 512].rearrange("p (k n) -> p k n", n=64)
    A1v = BIG1.t[:, o_ + 512:o_ + 1024].rearrange("p (k n) -> p k n", n=64)
    V1v = BIG1.t[:, o_ + 1024:o_ + 1280].rearrange("p (k n) -> p k n", n=32)
    G1v = BIG1.t[:, o_ + 1280:o_ + 2304].rearrange("p (k n) -> p k n", n=128)
    W2v = BIG1.t[:, o_ + 2304:o_ + 3328]
    A2v = BIG1.t[:, o_ + 3328:o_ + 4352]
    V2v = BIG1.t[:, o_ + 4352:o_ + 5376]
    G2v = BIG1.t[:, o_ + 5376:o_ + 6400]

    def load_small(dst, src, rows):
        s = state['ws'] % 2
        state['ws'] += 1
        st = WST[s]
        P.dma(st.t[0:rows, 0:1024], src, writes=[st])
        cp('pool', dst[0:rows, :], st.t[0:rows, 0:1024], [st], [BIG1])

    def rwkv_layer(seg, li):
        j = li // 2
        BIG1.fence()
        BIG2.fence()
        H.fence()
        for half in range(2):
            cs = slice(half * 512, (half + 1) * 512)
            wload_to(BIG1, WRv[:, :, cs], w_r[j][:, cs], 8, 512)
            wload_to(BIG1, WKv[:, :, cs], w_k[j][:, cs], 8, 512)
            wload_to(BIG2, WVv[:, :, cs], w_v[j][:, cs], 8, 512)
            wload_to(H, WOv[:, :, cs], w_o[j][:, cs], 8, 512)
        wload_to(BIG1, W1v, d_w1[j], 8, 64)
        wload_to(BIG1, A1v, d_a1[j], 8, 64)
        if li == 3:
            wload_to(BIG1, V1v, d_v1[0], 8, 32)
            load_small(V2v, d_v2[0], 32)
        wload_to(BIG1, G1v, d_g1[j], 8, 128)
        load_small(W2v, d_w2[j], 64)
        load_small(A2v, d_a2[j], 64)
        load_small(G2v, d_g2[j], 128)
        CHUNK = DBG.get('chunk', True)
        if CHUNK:
            P.dma(HBUF.t[0:16, 0:768], d_cmask, writes=[HBUF])
        W0.fence()
        B0.fence()
        B1_.fence()
        if seg == 0:
            groups = [(16 * i, False) for i in range(65)]
        else:
            groups = [(16 * i, False) for i in range(64)] + [(1024, True)]
        if seg == 1:
            P.dma(STS.t[:, 0:128].rearrange("p (g b) -> p g b", b=16), s_shift[j], writes=[STS])
        SSH = STS.t[:, 0:128].rearrange("p (g b) -> p g b", b=16)
        MS = MST[j].t[:, :, :]
        XM = B0.t[:, 0:768].rearrange("p (m k n) -> p m k n", m=6, k=8)
        for gi_, (c0, sample) in enumerate(groups):
            HFb = W0.t[:, 29 * 136:30 * 136].rearrange("p (k n) -> p k n", n=17)
            kHF = (W0, ('q', 29))
            sqv = B1_.t[:, 3072:3200].rearrange("p (k n) -> p k n", n=16)
            kSQ = (B1_, 'sq')
            act(sqv, X.t[:, :, c0:c0 + 16], AF.Square, [X], [kSQ])
            pb = ps()
            for k in range(8):
                mm(pb.t[:, 0:16], CMB.t[:, CM_MEAN1024, :], sqv[:, k, :], k == 0, k == 7, [CMB, kSQ], [pb])
            rs = scr()
            act(rs.t[:, 0:16], pb.t[:, 0:16], AF.Sqrt, [pb, SMALL], [rs], bias=EPSC)
            P.op('dve', lambda e, rs=rs: e.reciprocal(out=rs.t[:, 0:16], in_=rs.t[:, 0:16]), [rs], [rs])
            if not sample:
                cp('pool', HFb[:, :, 0], SHC[:, j, :], [CAR], [kHF])
            tt('dve', HFb[:, :, 1:17], X.t[:, :, c0:c0 + 16], bc('nmix%d' % li), ALU.mult, [X, VEC8], [kHF])
            tt('dve', HFb[:, :, 1:17], HFb[:, :, 1:17], rs.t[:, 0:16].rearrange("p (o n) -> p o n", o=1).to_broadcast([128, 8, 16]), ALU.mult,
               [kHF, rs], [kHF])
            cur = HFb[:, :, 1:17]
            XX = Q(0)
            if sample:
                tt('dve', XX, SSH, cur, ALU.subtract, [STS, kHF], [kq(0)])
                P.dma(o_shifts[j], cur, reads=[kHF], eng='pool')
            else:
                tt('dve', XX, HFb[:, :, 0:16], cur, ALU.subtract, [kHF], [kq(0)])
                cp('pool', SHC[:, j, :], HFb[:, :, 16], [kHF], [CAR])
                if seg == 1 and c0 == 1008:
                    cp('pool', SHO[:, j, :], HFb[:, :, 16], [kHF], [CAR])
            for m in range(6):
                tt('dve', Q(1), XX, bc('mix%d_%d' % (j, m)), ALU.mult, [kq(0), VEC8], [kq(1)])
                tt('pool', XM[:, m], Q(1), cur, ALU.add, [kq(1), kHF], [(B0, ('xm', m))])

            def projN(wv, m, wbuf):
                pb = ps()
                for oc in range(8):
                    for k in range(8):
                        mm(pb.t[:, oc * 16:oc * 16 + 16], wv[:, k, oc * 128:oc * 128 + 128], XM[:, m, k, :], k == 0, k == 7,
                           [wbuf, (B0, ('xm', m))], [pb])
                return pb

            def lora(w1v, r, m, func, w2v):
                pl = ps()
                for k in range(8):
                    mm(pl.t[0:r, 0:16], w1v[:, k, 0:r], XM[:, m, k, :], k == 0, k == 7, [BIG1, (B0, ('xm', m))], [pl])
                tl = B1_.t[0:r, 3200 + m * 16:3200 + m * 16 + 16]
                kt = (B1_, ('tl', m))
                if func is None:
                    cp('act', tl, pl.t[0:r, 0:16], [pl], [kt])
                else:
                    act(tl, pl.t[0:r, 0:16], func, [pl], [kt])
                po2 = ps()
                for oc in range(8):
                    mm(po2.t[:, oc * 16:oc * 16 + 16], w2v[0:r, oc * 128:oc * 128 + 128], tl, True, True, [BIG1, kt], [po2])
                return po2
            pr = projN(WRv, 0, BIG1)
            RF = Q(2)
            cp('act', RF, p3(pr), [pr], [kq(2)])
            pk = projN(WKv, 2, BIG1)
            KF = Q(3)
            cp('act', KF, p3(pk), [pk], [kq(3)])
            pv = projN(WVv, 3, BIG2)
            VV = Q(4)
            cp('act', VV, p3(pv), [pv], [kq(4)])
            pw = lora(W1v, 64, 1, AF.Tanh, W2v)
            DEC = Q(5)
            tt('dve', DEC, p3(pw), bc('w0_%d' % j), ALU.add, [pw, VEC8], [kq(5)])
            act(DEC, DEC, AF.Sigmoid, [kq(5)], [kq(5)])
            use_chunk = CHUNK and not sample
            if not use_chunk:
                act(DEC, DEC, AF.Exp, [kq(5)], [kq(5)], scale=-WDEC)
            pa = lora(A1v, 64, 4, None, A2v)
            AS = Q(6)
            tt('dve', AS, p3(pa), bc('a0_%d' % j), ALU.add, [pa, VEC8], [kq(6)])
            act(AS, AS, AF.Sigmoid, [kq(6)], [kq(6)])
            pg = lora(G1v, 128, 5, AF.Sigmoid, G2v)
            GG = Q(7)
            cp('act', GG, p3(pg), [pg], [kq(7)])
            if li == 3:
                pvl = lora(V1v, 32, 3, None, V2v)
                VG = Q(8)
                tt('dve', VG, p3(pvl), bc('v0_0'), ALU.add, [pvl, VEC8], [kq(8)])
                act(VG, VG, AF.Sigmoid, [kq(8)], [kq(8)])
                P.dma(Q(23), vf_dram[:, :, seg * TC + c0:seg * TC + c0 + 16], reads=[(VFD, seg * TC + c0)], writes=[kq(23)])
                tt('dve', Q(9), Q(23), VV, ALU.subtract, [kq(23), kq(4)], [kq(9)])
                tt('dve', Q(9), Q(9), VG, ALU.mult, [kq(9), kq(8)], [kq(9)])
                tt('dve', VV, VV, Q(9), ALU.add, [kq(4), kq(9)], [kq(4)])
            else:
                P.dma(vf_dram[:, :, seg * TC + c0:seg * TC + c0 + 16], VV, reads=[kq(4)], writes=[(VFD, seg * TC + c0)], eng='pool')
            KK = Q(10)
            tt('dve', KK, KF, bc('kk%d' % j), ALU.mult, [kq(3), VEC8], [kq(10)])
            tt('pool', Q(11), KK, KK, ALU.mult, [kq(10)], [kq(11)])
            pss = ps()
            mm(pss.t[:, 0:128], CM.t[:, CM_BLKONES, :], W0.t[:, 11 * 136:11 * 136 + 128], True, True, [CM, kq(11)], [pss])
            INV = Q(12)
            ts('dve', INV, p3(pss), 1e-24, None, ALU.max, None, [pss], [kq(12)])
            act(INV, INV, AF.Sqrt, [kq(12)], [kq(12)])
            P.op('dve', lambda e, INV=INV: e.reciprocal(out=INV, in_=INV), [kq(12)], [kq(12)])
            tt('dve', KK, KK, INV, ALU.mult, [kq(10), kq(12)], [kq(10)])
            AT = Q(13)
            ts('pool', AT, KK, -1.0, None, ALU.mult, None, [kq(10)], [kq(13)])
            BT = Q(14)
            tt('pool', BT, KK, AS, ALU.mult, [kq(10), kq(6)], [kq(14)])
            K2 = Q(15)
            stt(K2, AS, -1.0, bc('ka%d' % j), ALU.add, ALU.mult, [kq(6), VEC8], [kq(15)])
            stt(K2, K2, 1.0, KF, ALU.add, ALU.mult, [kq(15), kq(3)], [kq(15)])
            RK = Q(16)
            tt('dve', RK, RF, K2, ALU.mult, [kq(2), kq(15)], [kq(16)])
            tt('pool', RK, RK, bc('rk%d' % j), ALU.mult, [kq(16), VEC8], [kq(16)])
            pbo = ps()
            mm(pbo.t[:, 0:128], CM.t[:, CM_BLKONES, :], W0.t[:, 16 * 136:16 * 136 + 128], True, True, [CM, kq(16)], [pbo])
            BON = Q(17)
            tt('dve', BON, p3(pbo), VV, ALU.mult, [pbo, kq(4)], [kq(17)])
            YS = Q(18)
            for _once in ([0] if use_chunk else []):
                SG = DEC
                CS = Q(24)
                ones16 = SMALL.t[:, 2:3].to_broadcast([128, 16])
                for k in range(8):
                    P.op('dve', lambda e, k=k: e.tensor_tensor_scan(out=CS[:, k, :], data0=ones16, data1=SG[:, k, :], initial=0.0,
                                                                op0=ALU.mult, op1=ALU.add), [kq(5), SMALL], [kq(24)])
                PEXC, PINC, PINV, BP, KP = Q(0), Q(1), Q(11), Q(27), Q(28)
                tt('dve', PEXC, CS, SG, ALU.subtract, [kq(24), kq(5)], [kq(0)])
                act(PEXC, PEXC, AF.Exp, [kq(0)], [kq(0)], scale=-WDEC)
                act(PINC, CS, AF.Exp, [kq(24)], [kq(1)], scale=-WDEC)
                act(PINV, CS, AF.Exp, [kq(24)], [kq(11)], scale=WDEC)
                PLb = PINC[:, :, 15:16].to_broadcast([128, 8, 16])
                ARv = B1_.t[:, 3328:3584].rearrange("p (k a n) -> p k a n", a=2, n=16)
                kAR = (B1_, ('qb', 0))
                BTq = B1_.t[:, 3584:3712].rearrange("p (k n) -> p k n", n=16)
                kBT = (B1_, ('qb', 2))
                KTq = B1_.t[:, 3712:3840].rearrange("p (k n) -> p k n", n=16)
                kKT = (B1_, ('qb', 3))
                BDq = B1_.t[:, 3840:3968].rearrange("p (k n) -> p k n", n=16)
                kBD = (B1_, ('qb', 4))
                KDq = B0.t[:, 768:896].rearrange("p (k n) -> p k n", n=16)
                kKD = (B0, ('qb', 5))
                VBq = B0.t[:, 896:1024].rearrange("p (k n) -> p k n", n=16)
                kVB = (B0, ('qb', 6))
                tt('dve', ARv[:, :, 0, :], AT, PEXC, ALU.mult, [kq(13), kq(0)], [kAR])
                tt('dve', ARv[:, :, 1, :], RF, PINC, ALU.mult, [kq(2), kq(1)], [kAR])
                tt('dve', BP, BT, PINV, ALU.mult, [kq(14), kq(11)], [kq(27)])
                tt('dve', KP, K2, PINV, ALU.mult, [kq(15), kq(11)], [kq(28)])
                cp('pool', BTq, BP, [kq(27)], [kBT])
                cp('pool', KTq, KP, [kq(28)], [kKT])
                tt('pool', BDq, BP, PLb, ALU.mult, [kq(27), kq(1)], [kBD])
                tt('pool', KDq, KP, PLb, ALU.mult, [kq(28), kq(1)], [kKD])
                cp('act', VBq, VV, [kq(4)], [kVB])
                ARm = [B0.t[:, 3584:3840].rearrange("p (k a n) -> p k a n", a=2, n=16),
                       B0.t[:, 3840:4096].rearrange("p (k a n) -> p k a n", a=2, n=16)]
                kARm = (B0, ('arm', 0))
                for hh_ in range(2):
                    ts('dve' if hh_ == 0 else 'pool', ARm[hh_], ARv, CM.t[:, CM_BLKONES, 64 * hh_:64 * hh_ + 1], None, ALU.mult, None, [kAR, CM], [kARm])
                if DBG.get('cstop', 99) < 2:
                    break
                VTM = B1_.t[0:16, 0:1024]
                BDTM = B1_.t[0:16, 1024:2048]
                KDTM = B1_.t[0:16, 2048:3072]
                kVTM, kBDTM, kKDTM = (B1_, ('tm', 0)), (B1_, ('tm', 1)), (B1_, ('tm', 2))
                for (qb, kqb, tm, ktm) in ((VBq, kVB, VTM, kVTM), (BDq, kBD, BDTM, kBDTM), (KDq, kKD, KDTM, kKDTM)):
                    pt = ps()
                    ptb = pt.t[:, :].bitcast(BF16)
                    for k in range(8):
                        P.op('pe', lambda e, ptb=ptb, qb=qb, k=k: e.transpose(ptb[0:16, k * 128:(k + 1) * 128], qb[:, k, :], CMB.t[:, CM_IDENT, :]),
                             [kqb, CMB], [pt])
                    evcp(tm, ptb[0:16, 0:1024], [pt], [ktm])
                MB = W0.t[:, 25 * 136:25 * 136 + 256].bitcast(BF16).rearrange("p (k n) -> p k n", n=64)
                kMB = (W0, ('q', 25))
                cp('act', MB, MS, [MST[j]], [kMB])
                if DBG.get('cstop', 99) < 3:
                    break
                G1 = B0.t[0:16, 1024:1536].rearrange("s (h a t) -> s h a t", a=2, t=16)
                G2 = B0.t[0:16, 1536:2048].rearrange("s (h a t) -> s h a t", a=2, t=16)
                kG1, kG2 = (B0, ('g', 1)), (B0, ('g', 2))
                ATn = B0.t[0:16, 3072:3328].rearrange("s (h t) -> s h t", t=16)
                An = B0.t[0:16, 3328:3584].rearrange("s (h t) -> s h t", t=16)
                kATn = [(B0, ('a', 0, 0)), (B0, ('a', 0, 1))]
                kAn = [(B0, ('a', 1, 0)), (B0, ('a', 1, 1))]
                ZB = B0.t[0:16, 2048:3072]
                kZB = [(B0, ('z', 0)), (B0, ('z', 1))]
                MG = HBUF.t[0:16, 0:512]
                MA = HBUF.t[0:16, 512:768]
                pg1, pg2, pga = ps(), ps(), ps()
                for h in range(16):
                    p_, hh = h // 2, h % 2
                    rows = slice(64 * hh, 64 * hh + 64)
                    arr = ARm[hh][:, p_, :, :]
                    mm(pg1.t[0:16, h * 32:h * 32 + 32], BTq[:, p_, :], arr, True, True, [kBT, kARm], [pg1])
                    mm(pg2.t[0:16, h * 32:h * 32 + 32], KTq[:, p_, :], arr, True, True, [kKT, kARm], [pg2])
                    mm(pga.t[0:16, h * 16:h * 16 + 16], ARm[hh][:, p_, 0, :], BTq[:, p_, :], True, True, [kARm, kBT], [pga])
                tt('dve', B0.t[0:16, 1024:1536], pg1.t[0:16, 0:512], MG, ALU.mult, [pg1, HBUF], [kG1])
                tt('dve', B0.t[0:16, 1536:2048], pg2.t[0:16, 0:512], MG, ALU.mult, [pg2, HBUF], [kG2])
                for hf in range(2):
                    tt('dve', B0.t[0:16, 3328 + hf * 128:3328 + hf * 128 + 128], pga.t[0:16, hf * 128:hf * 128 + 128], MA[:, hf * 128:hf * 128 + 128], ALU.mult,
                       [pga, HBUF], [kAn[hf]])
                    cp('pool', ATn[:, hf * 8:hf * 8 + 8, :], G1[:, hf * 8:hf * 8 + 8, 0, :], [kG1], [kATn[hf]])
                if DBG.get('cstop', 99) < 4:
                    break
                pw = [ps(), ps()]
                for h in range(16):
                    p_, hh = h // 2, h % 2
                    rows = slice(64 * hh, 64 * hh + 64)
                    o_ap = pw[h // 8].t[0:16, (h % 8) * 64:(h % 8) * 64 + 64]
                    mm(o_ap, ARm[hh][:, p_, 0, :], MB[:, p_, :], True, False, [kARm, kMB], [pw[h // 8]])
                    mm(o_ap, G2[:, h, 0, :], VTM[:, h * 64:h * 64 + 64], False, True, [kG2, kVTM], [pw[h // 8]])
                for half in range(2):
                    evcp(ZB[:, half * 512:(half + 1) * 512], pw[half].t[0:16, 0:512], [pw[half]], [kZB[half]])
                if DBG.get('cstop', 99) < 5:
                    break
                I16 = CMB.t[0:16, CM_IDENT, 0:16]
                for lvl in range(4):
                    pz = [ps(), ps()]
                    pa = [ps(), ps()] if lvl < 3 else None
                    for hf in range(2):
                        for h in range(hf * 8, hf * 8 + 8):
                            o_ap = pz[hf].t[0:16, (h % 8) * 64:(h % 8) * 64 + 64]
                            zsl = ZB[:, h * 64:h * 64 + 64]
                            mm(o_ap, ATn[:, h, :], zsl, True, True, [kATn[hf], kZB[hf]], [pz[hf]])
                        if lvl < 3:
                            for h in range(hf * 8, hf * 8 + 8):
                                hl = h % 8
                                mm(pa[hf].t[0:16, hl * 16:hl * 16 + 16], An[:, h, :], ATn[:, h, :], True, True, [kAn[hf], kATn[hf]], [pa[hf]])
                                mm(pa[hf].t[0:16, 128 + hl * 16:128 + hl * 16 + 16], ATn[:, h, :], An[:, h, :], True, True, [kAn[hf], kATn[hf]], [pa[hf]])
                    for hf in range(2):
                        tt('dve', ZB[:, hf * 512:(hf + 1) * 512], pz[hf].t[0:16, 0:512], ZB[:, hf * 512:(hf + 1) * 512], ALU.add,
                           [pz[hf], kZB[hf]], [kZB[hf]])
                        if lvl < 3:
                            cp('act', B0.t[0:16, 3072 + hf * 128:3072 + hf * 128 + 128], pa[hf].t[0:16, 0:128], [pa[hf]], [kATn[hf]])
                            cp('act', B0.t[0:16, 3328 + hf * 128:3328 + hf * 128 + 128], pa[hf].t[0:16, 128:256], [pa[hf]], [kAn[hf]])
                if DBG.get('cstop', 99) < 6:
                    break
                py = [ps(), ps()]
                for h in range(16):
                    p_, hh = h // 2, h % 2
                    rows = slice(64 * hh, 64 * hh + 64)
                    o_ap = py[h // 8].t[0:16, (h % 8) * 64:(h % 8) * 64 + 64]
                    mm(o_ap, ARm[hh][:, p_, 1, :], MB[:, p_, :], True, False, [kARm, kMB], [py[h // 8]])
                    mm(o_ap, G1[:, h, 1, :], ZB[:, h * 64:h * 64 + 64], False, False, [kG1, kZB[h // 8]], [py[h // 8]])
                    mm(o_ap, G2[:, h, 1, :], VTM[:, h * 64:h * 64 + 64], False, True, [kG2, kVTM], [py[h // 8]])
                YTB = B0.t[0:16, 3072:4096]
                kYTB = (B0, ('ytb', 0))
                for half in range(2):
                    evcp(YTB[:, half * 512:(half + 1) * 512], py[half].t[0:16, 0:512], [py[half]], [kYTB, kAn[0], kAn[1], kATn[0], kATn[1], kARm])
                pyt = ps()
                pytb = pyt.t[:, :].bitcast(BF16)
                for k in range(8):
                    P.op('pe', lambda e, pytb=pytb, k=k: e.transpose(pytb[:, k * 16:(k + 1) * 16], YTB[:, k * 128:(k + 1) * 128], CMB.t[0:16, CM_IDENT, 0:16]),
                         [kYTB, kAn[0], kAn[1], kATn[0], kATn[1], kARm, CMB], [pyt])
                cp('act', YS, pytb[:, 0:128].rearrange("p (k n) -> p k n", n=16), [pyt], [kq(18)])
                if DBG.get('cstop', 99) < 7:
                    break
                psu = [ps(), ps()]
                for p_ in range(8):
                    o_ap = psu[p_ // 4].t[:, (p_ % 4) * 128:(p_ % 4) * 128 + 128]
                    mm(o_ap, BDTM[:, p_ * 128:(p_ + 1) * 128], ZB[:, p_ * 128:(p_ + 1) * 128], True, False, [kBDTM, kZB[p_ // 4]], [psu[p_ // 4]])
                    mm(o_ap, KDTM[:, p_ * 128:(p_ + 1) * 128], VTM[:, p_ * 128:(p_ + 1) * 128], False, True, [kKDTM, kVTM], [psu[p_ // 4]])
                tt('dve', MS, MS, PINC[:, :, 15:16].to_broadcast([128, 8, 64]), ALU.mult, [MST[j], kq(1)], [MST[j]])
                for g2 in range(2):
                    for hh in range(2):
                        rows = slice(64 * hh, 64 * hh + 64)
                        src = psu[g2].t[rows, 0:512].rearrange("p (q c) -> p q c", c=128)[:, :, hh * 64:hh * 64 + 64]
                        tt('dve', MS[rows, 4 * g2:4 * g2 + 4, :], MS[rows, 4 * g2:4 * g2 + 4, :], src, ALU.add, [MST[j], psu[g2]], [MST[j]])
            if not use_chunk:
                TM = []
                qlist = [(AT, kq(13)), (DEC, kq(5)), (BT, kq(14)), (K2, kq(15)), (RF, kq(2)), (None, None)]
                qbs = []
                for qi, (src, ksrc) in enumerate(qlist):
                    if qi < 5:
                        qb = B1_.t[:, 3328 + qi * 128:3328 + qi * 128 + 128].rearrange("p (k n) -> p k n", n=16)
                        kqb = (B1_, ('qb', qi))
                    else:
                        qb = B0.t[:, 768:896].rearrange("p (k n) -> p k n", n=16)
                        kqb = (B0, ('qb', qi))
                    qbs.append((qb, kqb))
                    if qi == 5:
                        tt('dve', Q(24), DEC, qbs[1][0], ALU.subtract, [kq(5), qbs[1][1]], [kq(24)])
                        cp('pool', qb, Q(24), [kq(24)], [kqb])
                    else:
                        cp('pool' if qi % 2 else 'act', qb, src, [ksrc], [kqb])
                    pt = ps()
                    ptb = pt.t[:, :].bitcast(BF16)
                    for k in range(8):
                        P.op('pe', lambda e, ptb=ptb, qb=qb, k=k: e.transpose(ptb[0:16, k * 128:(k + 1) * 128], qb[:, k, :], CMB.t[:, CM_IDENT, :]),
                             [kqb, CMB], [pt])
                    if qi < 3:
                        tm = B1_.t[0:16, qi * 1024:(qi + 1) * 1024]
                        ktm = (B1_, ('tm', qi))
                    else:
                        tm = B0.t[0:16, 1024 + (qi - 3) * 1024:1024 + (qi - 2) * 1024]
                        ktm = (B0, ('tm', qi))
                    evcp(tm, ptb[0:16, 0:1024], [pt], [ktm])
                    TM.append((tm, ktm))
                for t in range(16):
                    if sample:
                        Sb = SMP[0]
                        S3 = Sb.t[:, (t % 2) * 512:(t % 2) * 512 + 512].rearrange("p (k n) -> p k n", n=64)
                        for hh in range(2):
                            P.dma(S3[hh * 64:(hh + 1) * 64], s_rwkv[j, t].rearrange("(p hh) i jj -> hh i p jj", hh=2)[hh], writes=[Sb])
                        kS = Sb
                    else:
                        S3 = MS
                        kS = MST[j]
                    pq = []
                    for qi in range(5):
                        pbq = ps()
                        srcs = [TM[qi]] + ([TM[5]] if qi == 1 else [])
                        nmm = 2 * len(srcs)
                        im = 0
                        for (tm, ktm) in srcs:
                            tm4 = tm.rearrange("s (p hh jj) -> s p hh jj", hh=2, jj=64)
                            for hh in range(2):
                                mm(pbq.t[:, 0:512], SELH.t[0:16, t, hh, :], tm4[:, :, hh, :], im == 0, im == nmm - 1, [SELH, ktm], [pbq])
                                im += 1
                        pq.append(pbq)

                    def q3(i):
                        return pq[i].t[:, 0:512].rearrange("p (k n) -> p k n", n=64)
                    T1b = scr()
                    T1 = T1b.t[:, 0:512].rearrange("p (k n) -> p k n", n=64)
                    SAb = scr()
                    SA = SAb.t[:, 0:8]
                    tt('dve', T1, S3, q3(0), ALU.mult, [kS, pq[0]], [T1b])
                    P.op('dve', lambda e, SA=SA, T1=T1: e.tensor_reduce(out=SA, in_=T1, axis=AX.X, op=ALU.add), [T1b], [SAb])
                    tt('dve', S3, S3, q3(1), ALU.mult, [kS, pq[1]], [kS])
                    T2b = scr()
                    T2 = T2b.t[:, 0:512].rearrange("p (k n) -> p k n", n=64)
                    tt('dve', T2, q3(2), SAb.t[:, 0:8].rearrange("p (k o) -> p k o", o=1).to_broadcast([128, 8, 64]), ALU.mult, [pq[2], SAb], [T2b])
                    tt('pool', S3, S3, T2, ALU.add, [kS, T2b], [kS])
                    T3b = scr()
                    T3 = T3b.t[:, 0:512].rearrange("p (k n) -> p k n", n=64)
                    tt('dve', T3, q3(3), VV[:, :, t:t + 1].to_broadcast([128, 8, 64]), ALU.mult, [pq[3], kq(4)], [T3b])
                    tt('pool', S3, S3, T3, ALU.add, [kS, T3b], [kS])
                    T4b = scr()
                    T4 = T4b.t[:, 0:512].rearrange("p (k n) -> p k n", n=64)
                    tt('dve', T4, S3, q3(4), ALU.mult, [kS, pq[4]], [T4b])
                    P.op('dve', lambda e, T4=T4, t=t: e.tensor_reduce(out=YS[:, :, t], in_=T4, axis=AX.X, op=ALU.add), [T4b], [kq(18)])
                    if sample:
                        for hh in range(2):
                            P.dma(o_rwkvs[j, t].rearrange("(p hh) i jj -> hh i p jj", hh=2)[hh], S3[hh * 64:(hh + 1) * 64], reads=[Sb], eng='pool')
            tt('pool', Q(19), YS, YS, ALU.mult, [kq(18)], [kq(19)])
            pm = ps()
            mm(pm.t[:, 0:128], CM.t[:, CM_BLKMEAN64, :], W0.t[:, 18 * 136:18 * 136 + 128], True, True, [CM, kq(18)], [pm])
            mm(pm.t[:, 128:256], CM.t[:, CM_BLKMEAN64, :], W0.t[:, 19 * 136:19 * 136 + 128], True, True, [CM, kq(19)], [pm])
            ME = Q(20)
            VA = Q(21)
            cp('act', ME, p3(pm), [pm], [kq(20)])
            tt('dve', VA, ME, ME, ALU.mult, [kq(20)], [kq(21)])
            tt('dve', VA, pm.t[:, 128:256].rearrange("p (k n) -> p k n", n=16), VA, ALU.subtract, [pm, kq(21)], [kq(21)])
            ts('dve', VA, VA, 0.0, None, ALU.max, None, [kq(21)], [kq(21)])
            act(VA, VA, AF.Sqrt, [kq(21), SMALL], [kq(21)], bias=GNEPS)
            P.op('dve', lambda e, VA=VA: e.reciprocal(out=VA, in_=VA), [kq(21)], [kq(21)])
            OO = Q(22)
            tt('dve', OO, YS, ME, ALU.subtract, [kq(18), kq(20)], [kq(22)])
            tt('dve', OO, OO, VA, ALU.mult, [kq(22), kq(21)], [kq(22)])
            tt('pool', OO, OO, bc('gng%d' % j), ALU.mult, [kq(22), VEC8], [kq(22)])
            tt('pool', OO, OO, bc('gnb%d' % j), ALU.add, [kq(22), VEC8], [kq(22)])
            tt('pool', OO, OO, BON, ALU.add, [kq(22), kq(17)], [kq(22)])
            OB = B1_.t[:, 3968:4096].rearrange("p (k n) -> p k n", n=16)
            kOB = (B1_, 'ob')
            tt('dve', OB, OO, GG, ALU.mult, [kq(22), kq(7)], [kOB])
            po_ = ps()
            for oc in range(8):
                for k in range(8):
                    mm(po_.t[:, oc * 16:oc * 16 + 16], WOv[:, k, oc * 128:oc * 128 + 128], OB[:, k, :], k == 0, k == 7, [H, kOB], [po_])
            tt('dve', X.t[:, :, c0:c0 + 16], X.t[:, :, c0:c0 + 16], p3(po_), ALU.add, [X, po_], [X])
        W0.fence()
        B0.fence()
        B1_.fence()
        BIG1.fence()
        BIG2.fence()
        H.fence()

    ctx['rwkv_layer'] = rwkv_layer


def _prep_shared(inp):
    c = build_consts()
    sh = dict(cm=c['cm'], rope=c['rope'], rmask=c['rmask'], decin=c['decin'], deck=c['deck'], mask4=c['mask4'],
              maska=c['maska'], reset=c['reset'], sel=c['sel'], small=c['small'])
    selh = np.zeros((16, 16, 2, 128), np.float32)
    for t in range(16):
        selh[t, t, 0, :64] = 1.0
        selh[t, t, 1, 64:] = 1.0
    sh['selh'] = selh
    i16 = np.arange(16)
    strictT = (i16[:, None] < i16[None, :]).astype(np.float32)
    inclT = (i16[:, None] <= i16[None, :]).astype(np.float32)
    cm_ = np.zeros((16, 768), np.float32)
    cm_[:, 0:512] = np.tile(np.concatenate([strictT, inclT], axis=1), (1, 16))
    cm_[:, 512:768] = np.tile((i16[:, None] > i16[None, :]).astype(np.float32), (1, 16))
    sh['cmask'] = cm_
    f = lambda a: np.ascontiguousarray(np.asarray(a, np.float32))
    vec8 = np.zeros((128, NV8, 8), np.float32)

    def put8(name, v):
        vec8[:, V8[name], :] = fm(v, 8)
    for li in range(4):
        put8('nmix%d' % li, inp['norm_mix_g'][li])
        put8('nffn%d' % li, inp['norm_ffn_g'][li])
    put8('nfinal', inp['norm_final_g'])
    for j in range(2):
        put8('gn%d' % j, inp['ev_ret_gn_g'][j])
        for m in range(4):
            put8('cw%d_%d' % (m, j), inp['ev_lru_conv_w'][j][m])
        put8('cb%d' % j, inp['ev_lru_conv_b'][j])
        put8('ba%d' % j, inp['ev_lru_ba'][j])
        put8('bx%d' % j, inp['ev_lru_bx'][j])
        put8('lam%d' % j, inp['ev_lru_lambda'][j])
        for m in range(6):
            put8('mix%d_%d' % (j, m), inp['od_mix'][j][m])
        put8('w0_%d' % j, inp['od_w0'][j])
        put8('a0_%d' % j, inp['od_a0'][j])
        put8('kk%d' % j, inp['od_k_k'][j])
        put8('ka%d' % j, inp['od_k_a'][j])
        put8('rk%d' % j, np.asarray(inp['od_r_k'][j]).reshape(-1))
        put8('gng%d' % j, inp['od_gn_g'][j])
        put8('gnb%d' % j, inp['od_gn_b'][j])
    put8('v0_0', inp['od_v0'][0])
    vec22 = np.zeros((128, NV22, 22), np.float32)
    for li in range(4):
        for m in range(3):
            vec22[:, V22['fw%d_%d' % (m, li)], :] = fm(inp['ff_conv_w'][li][m], 22)
        vec22[:, V22['fb%d' % li], :] = fm(inp['ff_conv_b'][li], 22)
    sh['vec8'] = vec8
    sh['vec22'] = vec22
    w_in = f(inp['ev_w_in'])
    perm = np.array([(c // 128) * 128 + ((c % 128) + 64) % 128 for c in range(1024)])
    sh['w_in'] = w_in
    sh['w_swap'] = np.ascontiguousarray(w_in[:, :, perm])
    sh['w_out'] = f(inp['ev_w_out'])
    wa = np.asarray(inp['ev_lru_wa'], np.float32).transpose(0, 2, 1, 3)
    wx = np.asarray(inp['ev_lru_wx'], np.float32).transpose(0, 2, 1, 3)
    sh['wax'] = np.ascontiguousarray(np.stack([wa, wx], axis=2))
    sh['w_r'] = f(inp['od_w_r'])
    sh['w_k'] = f(inp['od_w_k'])
    sh['w_v'] = f(inp['od_w_v'])
    sh['w_o'] = f(inp['od_w_o'])
    sh['w1'] = f(inp['od_w1'])
    sh['w2'] = f(inp['od_w2'])
    sh['a1'] = f(inp['od_a1'])
    sh['a2'] = f(inp['od_a2'])
    sh['v1'] = f(inp['od_v1'])
    sh['v2'] = f(inp['od_v2'])
    sh['g1'] = f(inp['od_g1'])
    sh['g2'] = f(inp['od_g2'])
    sh['w_up'] = f(inp['ff_w_up'])
    sh['w_down'] = np.ascontiguousarray(f(inp['ff_w_down']).reshape(4, NFC, 128, 8, 128).transpose(0, 3, 2, 1, 4).reshape(4, 8, 128, NFC * 128))
    return sh


def kernel(**inp):
    if 'nc' not in _CACHE:
        _CACHE['nc'] = build_program()
    nc = _CACHE['nc']
    sh = _prep_shared(inp)
    xp = np.asarray(inp['x_prompt'], np.float32)
    xs = np.asarray(inp['x_sample'], np.float32)
    meta = np.asarray(inp['meta_tokens'], np.float32)
    in_maps = []
    NCORES = DBG['ncores']
    for c in range(NCORES):
        sl = slice(16 * c, 16 * c + 16)
        m = dict(sh)
        xall = np.concatenate([meta, xp[c], xs[sl, 0, :]], axis=0)
        m['xT'] = np.ascontiguousarray(xall.T)
        m['s_ret'] = np.ascontiguousarray(np.asarray(inp['state_ret'], np.float32)[:, sl])
        m['s_lru'] = np.ascontiguousarray(np.asarray(inp['state_lru'], np.float32)[:, sl].reshape(2, 16, 8, 128).transpose(0, 3, 2, 1))
        m['s_lconv'] = np.ascontiguousarray(np.asarray(inp['state_lru_conv'], np.float32)[:, sl].reshape(2, 16, 3, 8, 128).transpose(0, 4, 3, 2, 1))
        m['s_rwkv'] = np.ascontiguousarray(np.asarray(inp['state_rwkv'], np.float32)[:, sl])
        m['s_shift'] = np.ascontiguousarray(np.asarray(inp['state_shift'], np.float32)[:, sl].reshape(2, 16, 8, 128).transpose(0, 3, 2, 1))
        m['s_ffn'] = np.ascontiguousarray(np.asarray(inp['state_ffn_conv'], np.float32)[:, sl].reshape(4, 16, 2, 22, 128).transpose(0, 4, 3, 2, 1))
        in_maps.append(m)
    res = run_bass_kernel_spmd(nc, in_maps, core_ids=list(range(NCORES)))
    R = res.results
    y_p = np.zeros((8, 2048, 1024), np.float32)
    y_s = np.zeros((128, 1, 1024), np.float32)
    ret_p = np.zeros((2, 8, 4, 128, 256), np.float32)
    lru_p = np.zeros((2, 8, 1024), np.float32)
    lconv_p = np.zeros((2, 8, 3, 1024), np.float32)
    rwkv_p = np.zeros((2, 8, 16, 64, 64), np.float32)
    shift_p = np.zeros((2, 8, 1024), np.float32)
    ffn_p = np.zeros((4, 8, 2, 2816), np.float32)
    ret_s = np.zeros((2, 128, 4, 128, 256), np.float32)
    lru_s = np.zeros((2, 128, 1024), np.float32)
    lconv_s = np.zeros((2, 128, 3, 1024), np.float32)
    rwkv_s = np.zeros((2, 128, 16, 64, 64), np.float32)
    shift_s = np.zeros((2, 128, 1024), np.float32)
    ffn_s = np.zeros((4, 128, 2, 2816), np.float32)
    for c in range(NCORES):
        r = R[c]
        sl = slice(16 * c, 16 * c + 16)
        yT = np.asarray(r['yT'])
        y_p[c] = yT[:, 16:2064].T
        y_s[sl, 0, :] = yT[:, 2064:].T
        ret_p[:, c] = np.asarray(r['o_retp'])
        lru_p[:, c] = np.asarray(r['o_lrup']).transpose(1, 2, 0).reshape(2, 1024)
        lconv_p[:, c] = np.asarray(r['o_lconvp']).transpose(1, 3, 2, 0).reshape(2, 3, 1024)
        if DBG.get('chunk', True):
            rwkv_p[:, c] = np.asarray(r['o_rwkvp']).reshape(2, 2, 64, 8, 64).transpose(0, 3, 1, 4, 2).reshape(2, 16, 64, 64)
        else:
            rwkv_p[:, c] = np.asarray(r['o_rwkvp']).reshape(2, 2, 64, 8, 64).transpose(0, 3, 1, 2, 4).reshape(2, 16, 64, 64)
        shift_p[:, c] = np.asarray(r['o_shiftp']).transpose(1, 2, 0).reshape(2, 1024)
        ffn_p[:, c] = np.asarray(r['o_ffnp']).transpose(1, 3, 2, 0).reshape(4, 2, 2816)
        ret_s[:, sl] = np.asarray(r['o_rets'])
        lru_s[:, sl] = np.asarray(r['o_lrus']).transpose(0, 3, 2, 1).reshape(2, 16, 1024)
        lconv_s[:, sl] = np.asarray(r['o_lconvs']).transpose(0, 4, 3, 2, 1).reshape(2, 16, 3, 1024)
        rwkv_s[:, sl] = np.asarray(r['o_rwkvs'])
        shift_s[:, sl] = np.asarray(r['o_shifts']).transpose(0, 3, 2, 1).reshape(2, 16, 1024)
        ffn_s[:, sl] = np.asarray(r['o_ffns']).transpose(0, 4, 3, 2, 1).reshape(4, 16, 2, 2816)
    return (y_p, y_s, ret_p, lru_p, lconv_p, rwkv_p, shift_p, ffn_p, ret_s, lru_s, lconv_s, rwkv_s, shift_s, ffn_s)
```

```python
import math
import numpy as np
from contextlib import ExitStack
import concourse.bass as bass
import concourse.mybir as mybir
from concourse.bass_utils import run_bass_kernel_spmd

F32 = mybir.dt.float32
BF16 = mybir.dt.bfloat16
AF = mybir.ActivationFunctionType
ALU = mybir.AluOpType
AX = mybir.AxisListType

SEM_CAP = 30000
N_DMA_SEMS = 16

D = 1024
TC = 1040
NTOK = 2080
DFF = 2816
NFC = 22
EPS = 1e-6
GN_EPS_C = 64e-5
WDEC = math.exp(-0.5)
DK_SCALE = 128 ** -0.5
GAMMA = [1.0 - 2.0 ** (-5 - h) for h in range(4)]
TOK_TILES = [(0, 512), (512, 512), (1024, 16)]


class Buf:
    def __init__(self, t, name):
        self.t = t
        self.name = name
        self.st = {'_all': [[], []]}

    def __getitem__(self, idx):
        return self.t[idx]

    def states(self, key):
        if key is None:
            return list(self.st.values())
        if key not in self.st:
            a = self.st['_all']
            self.st[key] = [list(a[0]), list(a[1])]
        return [self.st[key]]

    def fence(self):
        w = []
        r = []
        for st in self.st.values():
            w.extend(st[0])
            r.extend(st[1])
        self.st = {'_all': [w, r]}


class Prog:
    def __init__(self, nc, es):
        self.nc = nc
        self.es = es
        self.engs = ['pe', 'act', 'dve', 'pool', 'sp']
        self.q = {e: [] for e in self.engs}
        self.cur_sem = {}
        self.cnt = {}
        self.nsem = 0
        for e in ['pe', 'act', 'dve', 'pool']:
            self._new_sem(e)
        self.waited = {e: {} for e in self.engs}
        self.dma_sems = [es.enter_context(nc.semaphore("dq%d" % i)) for i in range(2 * N_DMA_SEMS)]
        self.dma_cnt = [0] * (2 * N_DMA_SEMS)
        self.dma_n = 0
        self.dma_ne = {'sp': 0, 'pool': 0}
        self.nbuf = 0
        self.n_ops = 0
        self.n_waits = 0
        self.psi = 0

    def _new_sem(self, e):
        s = self.es.enter_context(self.nc.semaphore("tl_%s_%d" % (e, self.nsem)))
        self.nsem += 1
        self.cur_sem[e] = s
        self.cnt[e] = 0

    def sbuf(self, shape, dt, name=None):
        self.nbuf += 1
        name = "%s_%d" % (name or "sb", self.nbuf)
        t = self.es.enter_context(self.nc.sbuf_tensor(name, list(shape), dt))
        return Buf(t, name)

    def psum(self, shape, dt, name=None):
        self.nbuf += 1
        name = "%s_%d" % (name or "ps", self.nbuf)
        t = self.es.enter_context(self.nc.psum_tensor(name, list(shape), dt))
        return Buf(t, name)

    def _deps(self, eng, reads, writes):
        deps = []
        for (b, k) in reads:
            for st in b.states(k):
                deps.extend(st[0])
        for (b, k) in writes:
            for st in b.states(k):
                deps.extend(st[0])
                deps.extend(st[1])
        best = {}
        for (sem, val, seng) in deps:
            if eng == 'pe' and seng == 'pe':
                continue
            kk = id(sem)
            if kk not in best or best[kk][1] < val:
                best[kk] = (sem, val)
        waits = []
        w = self.waited[eng]
        for kk, (sem, val) in best.items():
            if w.get(kk, 0) >= val:
                continue
            w[kk] = val
            waits.append((sem, val))
        return waits

    @staticmethod
    def _compact(lst):
        best = {}
        for t in lst:
            kk = id(t[0])
            if kk not in best or best[kk][1] < t[1]:
                best[kk] = t
        return list(best.values())

    def _record(self, tok, reads, writes):
        for (b, k) in reads:
            for st in b.states(k):
                st[1].append(tok)
                if len(st[1]) > 48:
                    st[1] = self._compact(st[1])
        for (b, k) in writes:
            for st in b.states(k):
                st[0] = [tok]
                st[1] = []

    @staticmethod
    def _norm(lst):
        out = []
        for x in lst:
            if isinstance(x, tuple):
                out.append(x)
            else:
                out.append((x, None))
        return out

    def op(self, eng, fn, reads=(), writes=()):
        reads = self._norm(reads)
        writes = self._norm(writes)
        waits = self._deps(eng, reads, writes)
        if self.cnt[eng] >= SEM_CAP:
            self._new_sem(eng)
        self.cnt[eng] += 1
        tok = (self.cur_sem[eng], self.cnt[eng], eng)
        self.q[eng].append((waits, fn, (tok[0], 1)))
        self._record(tok, reads, writes)
        self.n_ops += 1
        self.n_waits += len(waits)
        return tok

    def dma(self, out_ap, in_ap, reads=(), writes=(), eng='sp', **kw):
        reads = self._norm(reads)
        writes = self._norm(writes)
        waits = self._deps(eng, reads, writes)
        i = (self.dma_ne[eng] % N_DMA_SEMS) + (N_DMA_SEMS if eng == 'pool' else 0)
        self.dma_ne[eng] += 1
        self.dma_n += 1
        sem = self.dma_sems[i]
        prev = self.dma_cnt[i]
        w = self.waited[eng]
        if prev > 0 and w.get(id(sem), 0) < prev:
            waits.append((sem, prev))
            w[id(sem)] = prev
        self.dma_cnt[i] += 16
        tok = (sem, self.dma_cnt[i], 'dma')

        def fn(e, out_ap=out_ap, in_ap=in_ap, kw=kw):
            return e.dma_start(out=out_ap, in_=in_ap, **kw)
        self.q[eng].append((waits, fn, (sem, 16)))
        self._record(tok, reads, writes)
        self.n_ops += 1
        self.n_waits += len(waits)
        return tok

    def finish(self):
        nc = self.nc
        waits = []
        for i, s in enumerate(self.dma_sems):
            if self.dma_cnt[i] > 0:
                waits.append((s, self.dma_cnt[i]))
        for en in ['pe', 'act', 'dve', 'pool']:
            if self.cnt[en] > 0:
                waits.append((self.cur_sem[en], self.cnt[en]))
        self.q['sp'].append((waits, None, None))
        qs = self.q
        with nc.Block() as block:
            def run(engine, lst):
                for (waits, fn, inc) in lst:
                    for (sem, val) in waits:
                        engine.wait_ge(sem, val)
                    if fn is not None:
                        ins = fn(engine)
                        if inc is not None:
                            ins.then_inc(inc[0], inc[1])

            @block.tensor
            def _(t):
                run(t, qs['pe'])

            @block.scalar
            def _(t):
                run(t, qs['act'])

            @block.vector
            def _(t):
                run(t, qs['dve'])

            @block.gpsimd
            def _(t):
                run(t, qs['pool'])

            @block.sync
            def _(t):
                run(t, qs['sp'])


V8_NAMES = []
for _li in range(4):
    V8_NAMES += ['nmix%d' % _li, 'nffn%d' % _li]
V8_NAMES += ['nfinal']
for _j in range(2):
    V8_NAMES += ['gn%d' % _j, 'cw0_%d' % _j, 'cw1_%d' % _j, 'cw2_%d' % _j, 'cw3_%d' % _j, 'cb%d' % _j,
                 'ba%d' % _j, 'bx%d' % _j, 'lam%d' % _j]
for _j in range(2):
    V8_NAMES += ['mix%d_%d' % (_j, m) for m in range(6)]
    V8_NAMES += ['w0_%d' % _j, 'a0_%d' % _j, 'kk%d' % _j, 'ka%d' % _j, 'rk%d' % _j, 'gng%d' % _j, 'gnb%d' % _j]
V8_NAMES += ['v0_0']
V8 = {n: i for i, n in enumerate(V8_NAMES)}
NV8 = len(V8_NAMES)
V22_NAMES = []
for _li in range(4):
    V22_NAMES += ['fw0_%d' % _li, 'fw1_%d' % _li, 'fw2_%d' % _li, 'fb%d' % _li]
V22 = {n: i for i, n in enumerate(V22_NAMES)}
NV22 = len(V22_NAMES)

CM_MEAN1024, CM_MEAN256, CM_BLKMEAN64, CM_IDENT, CM_BLKONES = range(5)
NCM = 5


def fm(v, nch):
    return np.ascontiguousarray(np.asarray(v, np.float32).reshape(nch, 128).T)


def build_consts():
    cm = np.zeros((128, NCM, 128), np.float32)
    cm[:, CM_MEAN1024, :] = 1.0 / 1024
    cm[:, CM_MEAN256, :] = 1.0 / 256
    blk = np.zeros((128, 128), np.float32)
    blk[:64, :64] = 1.0
    blk[64:, 64:] = 1.0
    cm[:, CM_BLKMEAN64, :] = blk / 64.0
    cm[:, CM_IDENT, :] = np.eye(128, dtype=np.float32)
    cm[:, CM_BLKONES, :] = blk
    inv = (10000.0 ** (-np.linspace(0.0, 1.0, 64, dtype=np.float32))).astype(np.float32)
    pos = np.concatenate([np.arange(2064, dtype=np.float32), np.full(16, 16384.0, np.float32)])
    ang = (pos[None, :] * inv[:, None]).astype(np.float32)
    cos = np.cos(ang.astype(np.float64)).astype(np.float32)
    sin = np.sin(ang.astype(np.float64)).astype(np.float32)
    rope = np.zeros((128, 2, NTOK), np.float32)
    rope[:64, 0] = cos
    rope[64:, 0] = cos
    rope[:64, 1] = -sin
    rope[64:, 1] = sin
    idx = np.arange(128)
    diff = idx[None, :] - idx[:, None]
    rmask = np.zeros((128, 4, 128), np.float32)
    decin = np.zeros((128, 4, 128), np.float32)
    deck = np.zeros((128, 2, 4), np.float32)
    for h in range(4):
        lg = math.log1p(-2.0 ** (-5 - h))
        rmask[:, h, :] = np.where(diff >= 0, DK_SCALE * np.exp(lg * np.maximum(diff, 0)), 0.0)
        decin[:, h, :] = np.exp(lg * (idx + 1.0))[None, :]
        deck[:, 0, h] = DK_SCALE * np.exp(lg * (127.0 - idx))
        deck[:16, 1, h] = DK_SCALE * np.exp(lg * (15.0 - idx[:16]))
    i64 = np.arange(64)
    strictT = (i64[:, None] < i64[None, :]).astype(np.float32)
    inclT = (i64[:, None] <= i64[None, :]).astype(np.float32)
    mask4 = np.zeros((128, 4, 64), np.float32)
    mask4[:64, 0] = strictT
    mask4[:64, 1] = inclT
    mask4[:64, 2] = strictT
    mask4[:64, 3] = inclT
    maska = np.zeros((128, 64), np.float32)
    maska[:64] = (i64[:, None] > i64[None, :]).astype(np.float32)
    reset = np.ones((128, 8, 64), np.float32)
    reset[:, :, 0] = 0.0
    sel = np.zeros((128, 16, 128), np.float32)
    for b in range(16):
        sel[b, b, :] = 1.0
    small = np.zeros((128, 8), np.float32)
    small[:, 0] = EPS
    small[:, 1] = GN_EPS_C
    small[:, 2] = 1.0
    small[:, 3] = 0.0
    return dict(cm=cm, rope=rope, rmask=rmask, decin=decin, deck=deck, mask4=mask4, maska=maska,
                reset=reset, sel=sel, small=small)


_CACHE = {}
DBG = {'layers': [0, 1, 2, 3], 'even': True, 'rwkv': True, 'ffn': True, 'segs': [0, 1], 'ncores': 8,
       'evp': ['qk', 'v', 'ga', 'chunks', 'sample', 'wouta', 'lru', 'woutb', 'post', 'supd']}


def build_program():
    nc = bass.Bass("TRN2", target_bir_lowering=False)

    def din(name, shape):
        return nc.dram_tensor(name, list(shape), F32, kind="ExternalInput").ap()

    def dout(name, shape):
        return nc.dram_tensor(name, list(shape), F32, kind="ExternalOutput").ap()

    xT = din("xT", [D, NTOK])
    d_cm = din("cm", [128, NCM, 128])
    d_rope = din("rope", [128, 2, NTOK])
    d_rmask = din("rmask", [128, 4, 128])
    d_decin = din("decin", [128, 4, 128])
    d_deck = din("deck", [128, 2, 4])
    d_mask4 = din("mask4", [128, 4, 64])
    d_maska = din("maska", [128, 64])
    d_reset = din("reset", [128, 8, 64])
    d_sel = din("sel", [128, 16, 128])
    d_small = din("small", [128, 8])
    d_selh = din("selh", [16, 16, 2, 128])
    d_cmask = din("cmask", [16, 768])
    d_vec8 = din("vec8", [128, NV8, 8])
    d_vec22 = din("vec22", [128, NV22, 22])
    w_in = din("w_in", [2, D, 5120])
    w_swap = din("w_swap", [2, D, 1024])
    w_out = din("w_out", [2, 2048, D])
    d_wax = din("wax", [2, 128, 2, 8, 128])
    w_r = din("w_r", [2, D, D])
    w_k = din("w_k", [2, D, D])
    w_v = din("w_v", [2, D, D])
    w_o = din("w_o", [2, D, D])
    d_w1 = din("w1", [2, D, 64])
    d_w2 = din("w2", [2, 64, D])
    d_a1 = din("a1", [2, D, 64])
    d_a2 = din("a2", [2, 64, D])
    d_v1 = din("v1", [1, D, 32])
    d_v2 = din("v2", [1, 32, D])
    d_g1 = din("g1", [2, D, 128])
    d_g2 = din("g2", [2, 128, D])
    w_up = din("w_up", [4, D, 2 * DFF])
    w_down = din("w_down", [4, 8, 128, NFC * 128])
    s_ret = din("s_ret", [2, 16, 4, 128, 256])
    s_lru = din("s_lru", [2, 128, 8, 16])
    s_lconv = din("s_lconv", [2, 128, 8, 3, 16])
    s_rwkv = din("s_rwkv", [2, 16, 16, 64, 64])
    s_shift = din("s_shift", [2, 128, 8, 16])
    s_ffn = din("s_ffn", [4, 128, 22, 2, 16])

    yT = dout("yT", [D, NTOK])
    o_retp = dout("o_retp", [2, 4, 128, 256])
    o_lrup = dout("o_lrup", [128, 2, 8])
    o_lconvp = dout("o_lconvp", [128, 2, 8, 3])
    o_rwkvp = dout("o_rwkvp", [2, 128, 8, 64])
    o_shiftp = dout("o_shiftp", [128, 2, 8])
    o_ffnp = dout("o_ffnp", [128, 4, 22, 2])
    o_rets = dout("o_rets", [2, 16, 4, 128, 256])
    o_lrus = dout("o_lrus", [2, 128, 8, 16])
    o_lconvs = dout("o_lconvs", [2, 128, 8, 3, 16])
    o_rwkvs = dout("o_rwkvs", [2, 16, 16, 64, 64])
    o_shifts = dout("o_shifts", [2, 128, 8, 16])
    o_ffns = dout("o_ffns", [4, 128, 22, 2, 16])

    with ExitStack() as es:
        P = Prog(nc, es)
        X = P.sbuf([128, 8, TC], F32, "X")
        H = P.sbuf([128, 8 * TC], BF16, "H")
        VFD = Buf(None, "vfd")
        vf_dram = nc.dram_tensor("vf_scratch", [128, 8, NTOK], F32, kind="Internal").ap()
        WST0 = P.sbuf([128, 4096], F32, "WST0")
        WST = [WST0, WST0]
        WBF = [P.sbuf([128, 4096], BF16, "WBF%d" % i) for i in range(2)]
        BIG1 = P.sbuf([128, 22 * TC], BF16, "BIG1")
        BIG2 = P.sbuf([128, 9 * 1024], BF16, "BIG2")
        HBUF = P.sbuf([128, 1044], F32, "HBUF")
        SCR = [P.sbuf([128, 512], F32, "SCR%d" % i) for i in range(5)]
        SMP = [HBUF, HBUF]
        PS = [P.psum([128, 512], F32, "PS%d" % i) for i in range(8)]
        CM = P.sbuf([128, NCM, 128], F32, "CM")
        CMB = P.sbuf([128, NCM, 128], BF16, "CMB")
        ROPE = P.sbuf([128, 2, 512], F32, "ROPE")
        RMASK = P.sbuf([128, 4, 128], F32, "RMASK")
        DECIN = P.sbuf([128, 4, 128], F32, "DECIN")
        DECK = P.sbuf([128, 2, 4], F32, "DECK")
        SMALL = P.sbuf([128, 8], F32, "SMALL")
        VEC8 = P.sbuf([128, NV8, 8], F32, "VEC8")
        VEC22 = P.sbuf([128, NV22, 22], F32, "VEC22")
        RS = [P.sbuf([128, 4, 256], F32, "RS%d" % j) for j in range(2)]
        RSB = P.sbuf([128, 4, 256], BF16, "RSB")
        MST = [P.sbuf([128, 8, 64], F32, "MST%d" % j) for j in range(2)]
        CAR = P.sbuf([128, 160], F32, "CAR")
        FFH = P.sbuf([128, 4, 22, 2], F32, "FFH")
        FFO = P.sbuf([128, 4, 22, 2], F32, "FFO")
        SP8 = P.sbuf([128, 2, 8], F32, "SP8")
        QKS = P.sbuf([128, 8, 16], F32, "QKS")
        STS = P.sbuf([128, 22 * 2 * 16], F32, "STS")
        LCS = P.sbuf([128, 8, 3, 16], F32, "LCS")
        WAX = P.sbuf([128, 2, 8, 128], BF16, "WAX")
        SELH = P.sbuf([16, 16, 2, 128], BF16, "SELH")

        LHC = CAR.t[:, 0:16].rearrange("p (j g) -> p j g", g=8)
        LRO = CAR.t[:, 16:32].rearrange("p (j g) -> p j g", g=8)
        LCH = CAR.t[:, 32:80].rearrange("p (j g t) -> p j g t", g=8, t=3)
        LCO = CAR.t[:, 80:128].rearrange("p (j g t) -> p j g t", g=8, t=3)
        SHC = CAR.t[:, 128:144].rearrange("p (j g) -> p j g", g=8)
        SHO = CAR.t[:, 144:160].rearrange("p (j g) -> p j g", g=8)
        EPSC = SMALL.t[:, 0:1]
        GNEPS = SMALL.t[:, 1:2]

        state = {'ps': 0, 'ws': 0, 'scr': 0, 'ev': 0, 'reserved': set()}

        def ps():
            while True:
                i = state['ps'] % 8
                state['ps'] += 1
                if i not in state['reserved']:
                    return PS[i]

        def scr():
            i = state['scr'] % 5
            state['scr'] += 1
            return SCR[i]

        def mm(out_ap, lhsT, rhs, start, stop, reads, writes):
            P.op('pe', lambda e: e.matmul(out_ap, lhsT=lhsT, rhs=rhs, start=start, stop=stop), reads, writes)

        def cp(eng, out_ap, in_ap, reads, writes):
            if eng == 'act':
                P.op('act', lambda e: e.copy(out=out_ap, in_=in_ap), reads, writes)
            else:
                P.op(eng, lambda e: e.tensor_copy(out=out_ap, in_=in_ap), reads, writes)

        def evcp(out_ap, in_ap, reads, writes):
            state['ev'] += 1
            cp('act' if state['ev'] % 2 else 'dve', out_ap, in_ap, reads, writes)

        def tt(eng, out_ap, a, b, op, reads, writes):
            P.op(eng, lambda e: e.tensor_tensor(out=out_ap, in0=a, in1=b, op=op), reads, writes)

        def ts(eng, out_ap, a, s1, s2, op0, op1, reads, writes):
            if s2 is None:
                P.op(eng, lambda e: e.tensor_scalar(out=out_ap, in0=a, scalar1=s1, scalar2=None, op0=op0), reads, writes)
            else:
                P.op(eng, lambda e: e.tensor_scalar(out=out_ap, in0=a, scalar1=s1, scalar2=s2, op0=op0, op1=op1), reads, writes)

        def stt(out_ap, a, s, b, op0, op1, reads, writes):
            P.op('dve', lambda e: e.scalar_tensor_tensor(out=out_ap, in0=a, scalar=s, in1=b, op0=op0, op1=op1), reads, writes)

        def act(out_ap, in_ap, func, reads, writes, bias=None, scale=None):
            kw = {}
            if bias is not None:
                kw['bias'] = bias
            if scale is not None:
                kw['scale'] = scale
            P.op('act', lambda e: e.activation(out=out_ap, in_=in_ap, func=func, **kw), reads, writes)

        def v8(name):
            return VEC8.t[:, V8[name], :]

        def v8c(name, k):
            return VEC8.t[:, V8[name], k:k + 1]

        def v22c(name, k):
            return VEC22.t[:, V22[name], k:k + 1]

        def hview(k, c0, n):
            return H.t[:, k * TC + c0:k * TC + c0 + n]

        def b1(s, c0, n):
            return BIG1.t[:, s * TC + c0:s * TC + c0 + n]

        def b1f(r, c0, n):
            return BIG1.t[:, r * 2 * TC:(r + 1) * 2 * TC].bitcast(F32)[:, c0:c0 + n]

        def b2(s, c0, n):
            return BIG2.t[:, s * TC + c0:s * TC + c0 + n]

        def wload(parts, KC, NC):
            s = state['ws'] % 2
            state['ws'] += 1
            st, bf = WST[s], WBF[s]
            dstv = st.t[:, 0:KC * NC].rearrange("p (k n) -> p k n", n=NC)
            for (src, off, ncols) in parts:
                P.dma(dstv[:, :, off:off + ncols], src.rearrange("(k p) n -> p k n", p=128), writes=[st])
            P.op('pool', lambda e: e.tensor_copy(out=bf.t[:, 0:KC * NC], in_=st.t[:, 0:KC * NC]), reads=[st], writes=[bf])
            return bf, bf.t[:, 0:KC * NC].rearrange("p (k n) -> p k n", n=NC)

        def wload_to(dst_buf, dst_view, src, KC, NC, dkey=None):
            s = state['ws'] % 2
            state['ws'] += 1
            st = WST[s]
            dstv = st.t[:, 0:KC * NC].rearrange("p (k n) -> p k n", n=NC)
            P.dma(dstv, src.rearrange("(k p) n -> p k n", p=128), writes=[st])
            P.op('pool', lambda e: e.tensor_copy(out=dst_view, in_=dstv), reads=[st], writes=[(dst_buf, dkey)])

        P.dma(CM.t[:], d_cm, writes=[CM])
        cp('dve', CMB.t[:], CM.t[:], [CM], [CMB])
        for (b_, d_) in [(RMASK, d_rmask), (DECIN, d_decin), (DECK, d_deck), (SMALL, d_small), (VEC8, d_vec8), (VEC22, d_vec22)]:
            P.dma(b_.t[:], d_, writes=[b_])
        P.dma(WST[0].t[0:16, 0:4096].rearrange('p (a b c) -> p a b c', a=16, b=2), d_selh, writes=[WST[0]])
        cp('dve', SELH.t[:], WST[0].t[0:16, 0:4096].rearrange('p (a b c) -> p a b c', a=16, b=2), [WST[0]], [SELH])
        P.op('pool', lambda e: e.memset(CAR.t[:], 0.0), writes=[CAR])
        P.op('pool', lambda e: e.memset(FFH.t[:], 0.0), writes=[FFH])
        P.op('pool', lambda e: e.memset(FFO.t[:], 0.0), writes=[FFO])
        P.op('pool', lambda e: e.memset(STS.t[:], 0.0), writes=[STS])
        P.op('pool', lambda e: e.memset(LCS.t[:], 0.0), writes=[LCS])
        for j in range(2):
            P.op('pool', lambda e, j=j: e.memset(RS[j].t[:], 0.0), writes=[RS[j]])
            P.op('pool', lambda e, j=j: e.memset(MST[j].t[:], 0.0), writes=[MST[j]])
        for j in range(2):
            act(SP8.t[:, j, :], v8('lam%d' % j), AF.Exp, [VEC8], [SP8], scale=-1.0)
            act(SP8.t[:, j, :], SP8.t[:, j, :], AF.Ln, [SP8], [SP8], bias=1.0)
            ts('dve', SP8.t[:, j, :], SP8.t[:, j, :], -8.0, None, ALU.mult, None, [SP8], [SP8])

        def rmsnorm(gname, out_fn, out_buf, c_tiles=TOK_TILES):
            for (c0, n) in c_tiles:
                sq = WBF[state['ws'] % 2]
                sqv = sq.t[:, 0:8 * n].rearrange("p (k n) -> p k n", n=n)
                act(sqv, X.t[:, :, c0:c0 + n], AF.Square, [X], [sq])
                pb = ps()
                for k in range(8):
                    mm(pb.t[:, 0:n], CMB.t[:, CM_MEAN1024, :], sqv[:, k, :], k == 0, k == 7, [CMB, sq], [pb])
                rs = scr()
                act(rs.t[:, 0:n], pb.t[:, 0:n], AF.Sqrt, [pb, SMALL], [rs], bias=EPSC)
                P.op('dve', lambda e, rs=rs, n=n: e.reciprocal(out=rs.t[:, 0:n], in_=rs.t[:, 0:n]), [rs], [rs])
                for k in range(8):
                    stt(out_fn(k, c0, n), X.t[:, k, c0:c0 + n], v8c(gname, k), rs.t[:, 0:n], ALU.mult, ALU.mult,
                        [X, VEC8, rs], [out_buf])

        def proj_fm(wbuf, wview, KC, col0, src_fn, src_reads, evac):
            for (c0, n) in TOK_TILES:
                pb = ps()
                for k in range(KC):
                    mm(pb.t[:, 0:n], wview[:, k, col0:col0 + 128], src_fn(k, c0, n), k == 0, k == KC - 1,
                       [wbuf] + src_reads, [pb])
                evac(pb, c0, n)

        def x_accum(dc):
            def ev(pb, c0, n):
                tt('dve', X.t[:, dc, c0:c0 + n], X.t[:, dc, c0:c0 + n], pb.t[:, 0:n], ALU.add, [X, pb], [X])
            return ev

        def ffn(seg, li):
            rmsnorm('nffn%d' % li, hview, H)
            BIG1.fence()
            if seg == 1:
                P.dma(STS.t[:, 0:22 * 32].rearrange("p (f t b) -> p f t b", t=2, b=16), s_ffn[li], writes=[STS])
            stsv = STS.t[:, 0:22 * 32].rearrange("p (f t b) -> p f t b", t=2, b=16)
            NP = TC if seg == 0 else 1024
            for f2 in range(11):
                bf, wview = wload([(w_up[li][:, f2 * 256:(f2 + 1) * 256], 0, 256), (w_up[li][:, DFF + f2 * 256:DFF + (f2 + 1) * 256], 256, 256)], 8, 512)
                for fi in range(2):
                    fc = f2 * 2 + fi
                    UG = HBUF
                    cp('pool', UG.t[:, 0:2], FFH.t[:, li, fc, :], [FFH], [UG])
                    pbv = []
                    for (c0, n) in TOK_TILES:
                        pg = ps()
                        for k in range(8):
                            mm(pg.t[:, 0:n], wview[:, k, fi * 128:fi * 128 + 128], hview(k, c0, n), k == 0, k == 7, [bf, H], [pg])
                        cp('act', UG.t[:, 2 + c0:2 + c0 + n], pg.t[:, 0:n], [pg], [UG])
                        pv = ps()
                        for k in range(8):
                            mm(pv.t[:, 0:n], wview[:, k, 256 + fi * 128:256 + fi * 128 + 128], hview(k, c0, n), k == 0, k == 7, [bf, H], [pv])
                        pbv.append(pv)
                    if seg == 0:
                        cp('pool', FFH.t[:, li, fc, :], UG.t[:, 2 + TC - 2:2 + TC], [UG], [FFH])
                    else:
                        cp('pool', FFO.t[:, li, fc, :], UG.t[:, 2 + 1022:2 + 1024], [UG], [FFO])
                    T1 = (BIG2, None)
                    par = fc % 2
                    t1 = BIG2.t[:, par * 2 * TC:(par + 1) * 2 * TC].bitcast(F32)
                    kT1 = (BIG2, ('t1', par))
                    ts('dve', t1[:, 0:TC], UG.t[:, 0:TC], v22c('fw0_%d' % li, fc), v22c('fb%d' % li, fc), ALU.mult, ALU.add,
                       [UG, VEC22], [kT1])
                    stt(t1[:, 0:TC], UG.t[:, 1:TC + 1], v22c('fw1_%d' % li, fc), t1[:, 0:TC], ALU.mult, ALU.add, [UG, VEC22, kT1], [kT1])
                    stt(t1[:, 0:TC], UG.t[:, 2:TC + 2], v22c('fw2_%d' % li, fc), t1[:, 0:TC], ALU.mult, ALU.add, [UG, VEC22, kT1], [kT1])
                    if seg == 1:
                        sc = slice(1024, 1040)
                        ts('dve', t1[:, sc], stsv[:, fc, 0, :], v22c('fw0_%d' % li, fc), v22c('fb%d' % li, fc), ALU.mult, ALU.add,
                           [STS, VEC22], [kT1])
                        stt(t1[:, sc], stsv[:, fc, 1, :], v22c('fw1_%d' % li, fc), t1[:, sc], ALU.mult, ALU.add, [STS, VEC22, kT1], [kT1])
                        stt(t1[:, sc], UG.t[:, 2 + 1024:2 + 1040], v22c('fw2_%d' % li, fc), t1[:, sc], ALU.mult, ALU.add, [UG, VEC22, kT1], [kT1])
                        P.dma(o_ffns[li, :, fc, 0, :], stsv[:, fc, 1, :], reads=[STS], eng='pool')
                        P.dma(o_ffns[li, :, fc, 1, :], UG.t[:, 2 + 1024:2 + 1040], reads=[UG], eng='pool')
                    act(t1[:, 0:TC], t1[:, 0:TC], AF.Gelu_apprx_tanh, [kT1], [kT1])
                    for ti, (c0, n) in enumerate(TOK_TILES):
                        tt('dve', b1(fc, c0, n), t1[:, c0:c0 + n], pbv[ti].t[:, 0:n], ALU.mult, [kT1, pbv[ti]], [(BIG1, fc)])
            BIG1.fence()
            for dc in range(8):
                s = state['ws'] % 2
                state['ws'] += 1
                st, bf = WST[s], WBF[s]
                dstv = st.t[:, 0:22 * 128].rearrange("p (k n) -> p k n", n=128)
                P.dma(st.t[:, 0:22 * 128], w_down[li, dc], writes=[st])
                P.op('pool', lambda e, bf=bf, st=st: e.tensor_copy(out=bf.t[:, 0:2816], in_=st.t[:, 0:2816]), reads=[st], writes=[bf])
                wview = bf.t[:, 0:2816].rearrange("p (k n) -> p k n", n=128)
                proj_fm(bf, wview, 22, 0, lambda k, c0, n: b1(k, c0, n), [BIG1], x_accum(dc))

        ctx = dict(locals())
        build_even(ctx)
        build_rwkv(ctx)
        even_layer = ctx['even_layer']
        rwkv_layer = ctx['rwkv_layer']

        for seg in DBG['segs']:
            P.dma(X.t[:], xT[:, seg * TC:(seg + 1) * TC].rearrange("(k p) t -> p k t", p=128), writes=[X])
            for li in DBG['layers']:
                if li % 2 == 0:
                    if DBG['even']:
                        even_layer(seg, li)
                else:
                    if DBG['rwkv']:
                        rwkv_layer(seg, li)
                if DBG['ffn']:
                    ffn(seg, li)
            for (c0, n) in TOK_TILES:
                s = state['ws'] % 2
                state['ws'] += 1
                st = WST[s]
                ov = st.t[:, 0:8 * n].rearrange("p (k n) -> p k n", n=n)
                rmsnorm('nfinal', lambda k, c0_, n_, ov=ov: ov[:, k, :], st, c_tiles=[(c0, n)])
                P.dma(yT[:, seg * TC + c0:seg * TC + c0 + n].rearrange("(k p) t -> p k t", p=128), ov, reads=[st], eng='pool')
        for j in range(2):
            P.dma(o_retp[j].rearrange("h d e -> d h e"), RS[j].t[:], reads=[RS[j]], eng='pool')
            P.dma(o_rwkvp[j], MST[j].t[:, :, :], reads=[MST[j]], eng='pool')
        P.dma(o_lrup, LRO, reads=[CAR], eng='pool')
        P.dma(o_lconvp, LCO, reads=[CAR], eng='pool')
        P.dma(o_shiftp, SHO, reads=[CAR], eng='pool')
        P.dma(o_ffnp, FFO.t[:], reads=[FFO], eng='pool')
        P.finish()
    return nc


def build_even(ctx):
    globals().update(ctx)

    def seg_chunks(seg):
        if seg == 0:
            return [(0, 16)] + [(16 + 128 * i, 128) for i in range(8)]
        return [(128 * i, 128) for i in range(8)]

    def vt(ci, n, c0, w):
        return BIG2.t[0:n, ci * 1024 + c0:ci * 1024 + c0 + w]

    def post(src3, src_reads, n, h, c0):
        OFb, OBb, OSb = scr(), scr(), scr()
        OF = OFb.t[:, 0:2 * n].rearrange("p (e c) -> p e c", c=n)
        OB = OBb.t[:, 0:256].bitcast(BF16)[:, 0:2 * n].rearrange("p (e c) -> p e c", c=n)
        OS = OSb.t[:, 0:256].bitcast(BF16)[:, 0:2 * n].rearrange("p (e c) -> p e c", c=n)
        cp('act', OF, src3, src_reads, [OFb])
        act(OS, src3, AF.Square, src_reads, [OSb])
        cp('pool', OB, OF, [OFb], [OBb])
        pst = ps()
        for ec in range(2):
            mm(pst.t[:, 0:n], CMB.t[:, CM_MEAN256, :], OB[:, ec, :], ec == 0, ec == 1, [CMB, OBb], [pst])
        for ec in range(2):
            mm(pst.t[:, 128:128 + n], CMB.t[:, CM_MEAN256, :], OS[:, ec, :], ec == 0, ec == 1, [CMB, OSb], [pst])
        MEb, VAb = scr(), scr()
        ME = MEb.t[:, 0:n]
        VA = VAb.t[:, 0:n]
        cp('act', ME, pst.t[:, 0:n], [pst], [MEb])
        tt('dve', VA, ME, ME, ALU.mult, [MEb], [VAb])
        tt('dve', VA, pst.t[:, 128:128 + n], VA, ALU.subtract, [pst, VAb], [VAb])
        ts('dve', VA, VA, 0.0, None, ALU.max, None, [VAb], [VAb])
        act(VA, VA, AF.Sqrt, [VAb, SMALL], [VAb], bias=EPSC)
        P.op('dve', lambda e: e.reciprocal(out=VA, in_=VA), [VAb], [VAb])
        for ec in range(2):
            tt('dve', OF[:, ec, :], OF[:, ec, :], ME, ALU.subtract, [OFb, MEb], [OFb])
            stt(OF[:, ec, :], OF[:, ec, :], VEC8.t[:, V8['gn%d' % cur['j']], 2 * h + ec:2 * h + ec + 1], VA, ALU.mult, ALU.mult,
                [OFb, VEC8, VAb], [OFb])
            tt('pool', b1(8 + 2 * h + ec, c0, n), OF[:, ec, :], b1(8 + 2 * h + ec, c0, n), ALU.mult,
               [OFb, (BIG1, 8 + 2 * h + ec)], [(BIG1, 8 + 2 * h + ec)])

    cur = {'j': 0}

    def even_layer(seg, li):
        j = li // 2
        cur['j'] = j
        rmsnorm('nmix%d' % li, hview, H)
        BIG1.fence()
        BIG2.fence()
        s = state['ws'] % 2
        state['ws'] += 1
        P.dma(WST[s].t[:, 0:2048].rearrange("p (a g d) -> p a g d", a=2, g=8), d_wax[j], writes=[WST[s]])
        cp('pool', WAX.t[:], WST[s].t[:, 0:2048].rearrange("p (a g d) -> p a g d", a=2, g=8), [WST[s]], [WAX])
        EVP = DBG['evp']
        ropes = {0: (ROPE, ROPE.t[:, :, 0:512]),
                 512: (HBUF, HBUF.t[:, 0:1024].rearrange("p (a n) -> p a n", a=2)),
                 1024: (LCS, LCS.t[:, 0:2, 0, :])}
        for c0_, (rb_, rv_) in ropes.items():
            n_ = 16 if c0_ == 1024 else 512
            P.dma(rv_, d_rope[:, :, seg * TC + c0_:seg * TC + c0_ + n_], writes=[rb_])
        for t in range(4 if 'qk' in EVP else 0):
            bf, wv = wload([(w_in[j][:, t * 256:(t + 1) * 256], 0, 256), (w_swap[j][:, t * 256:(t + 1) * 256], 256, 256)], 8, 512)
            for hi in range(2):
                hc = 2 * t + hi
                for (c0, n) in TOK_TILES:
                    po_, ps_ = ps(), ps()
                    for k in range(8):
                        mm(po_.t[:, 0:n], wv[:, k, hi * 128:hi * 128 + 128], hview(k, c0, n), k == 0, k == 7, [bf, H], [po_])
                    for k in range(8):
                        mm(ps_.t[:, 0:n], wv[:, k, 256 + hi * 128:256 + hi * 128 + 128], hview(k, c0, n), k == 0, k == 7, [bf, H], [ps_])
                    T1, T2 = scr(), scr()
                    rb, rv = ropes[c0]
                    tt('dve', T1.t[:, 0:n], po_.t[:, 0:n], rv[:, 0, :], ALU.mult, [po_, rb], [T1])
                    tt('dve', T2.t[:, 0:n], ps_.t[:, 0:n], rv[:, 1, :], ALU.mult, [ps_, rb], [T2])
                    tt('pool', b1(hc, c0, n), T1.t[:, 0:n], T2.t[:, 0:n], ALU.add, [T1, T2], [(BIG1, hc)])
                    if seg == 1 and c0 == 1024:
                        tt('pool', QKS.t[:, hc, :], T1.t[:, 0:n], T2.t[:, 0:n], ALU.add, [T1, T2], [QKS])
        chunks = seg_chunks(seg)
        allch = chunks + ([(1024, 16)] if seg == 1 else [])
        for wt in range(2 if 'v' in EVP else 0):
            bf, wv = wload([(w_in[j][:, 1024 + wt * 512:1024 + (wt + 1) * 512], 0, 512)], 8, 512)
            for ci, (c0, n) in enumerate(allch):
                pb = ps()
                for k in range(8):
                    mm(pb.t[0:n, 0:512], hview(k, c0, n), wv[:, k, 0:512], k == 0, k == 7, [bf, H], [pb])
                vm = DBG.get('vmode', '')
                if 'dveonly' in vm:
                    cp('dve', vt(ci, n, wt * 512, 512), pb.t[0:n, 0:512], [pb], [(BIG2, ci)])
                elif 'actonly' in vm:
                    cp('act', vt(ci, n, wt * 512, 512), pb.t[0:n, 0:512], [pb], [(BIG2, ci)])
                elif 'noevac' in vm:
                    pass
                else:
                    evcp(vt(ci, n, wt * 512, 512), pb.t[0:n, 0:512], [pb], [(BIG2, ci)])
        for wt in range(2 if 'ga' in EVP else 0):
            bf, wv = wload([(w_in[j][:, 2048 + wt * 512:2048 + (wt + 1) * 512], 0, 512)], 8, 512)
            for q in range(4):
                oc = wt * 4 + q

                def ev(pb, c0, n, oc=oc):
                    act(b1(8 + oc, c0, n), pb.t[:, 0:n], AF.Silu, [pb], [(BIG1, 8 + oc)])
                proj_fm(bf, wv, 8, q * 128, hview, [H], ev)
        cp('act', RSB.t[:], RS[j].t[:], [RS[j]], [RSB])
        for ci, (c0, n) in enumerate(chunks if 'chunks' in EVP else []):
            di = 0 if n == 128 else 1
            for h in range(4):
                pb = ps()
                mm(pb.t[0:n, 0:n], b1(4 + h, c0, n), b1(h, c0, n), True, True, [(BIG1, 4 + h), (BIG1, h)], [pb])
                SCb, QDb, KDb = scr(), scr(), scr()
                SC = SCb.t[:, 0:64].bitcast(BF16)
                QD = QDb.t[:, 0:64].bitcast(BF16)
                KD = KDb.t[:, 0:64].bitcast(BF16)
                tt('dve', SC[0:n, 0:n], pb.t[0:n, 0:n], RMASK.t[0:n, h, 0:n], ALU.mult, [pb, RMASK], [SCb])
                tt('pool', QD[:, 0:n], b1(h, c0, n), DECIN.t[:, h, 0:n], ALU.mult, [(BIG1, h), DECIN], [QDb])
                po = ps()
                for ec in range(2):
                    mm(po.t[:, ec * 128:ec * 128 + n], vt(ci, n, h * 256 + ec * 128, 128), SC[0:n, 0:n], True, False,
                       [(BIG2, ci), SCb], [po])
                    mm(po.t[:, ec * 128:ec * 128 + n], RSB.t[:, h, ec * 128:ec * 128 + 128], QD[:, 0:n], False, True,
                       [RSB, QDb], [po])
                src3 = po.t[:, 0:256].rearrange("p (e c) -> p e c", c=128)[:, :, 0:n]
                if 'post' in EVP:
                    post(src3, [po], n, h, c0)
                if 'supd' not in EVP:
                    continue
                pt = ps()
                ptb = pt.t[:, :].bitcast(BF16)
                P.op('pe', lambda e, ptb=ptb, n=n, h=h, c0=c0: e.transpose(ptb[0:n, 0:128], b1(4 + h, c0, n), CMB.t[:, CM_IDENT, :]),
                     [(BIG1, 4 + h), CMB], [pt])
                ts('dve', KD[0:n, 0:128], ptb[0:n, 0:128], DECK.t[0:n, di, h:h + 1], None, ALU.mult, None, [pt, DECK], [KDb])
                pu = ps()
                mm(pu.t[:, 0:256], KD[0:n, 0:128], vt(ci, n, h * 256, 256), True, True, [KDb, (BIG2, ci)], [pu])
                stt(RS[j].t[:, h, :], RS[j].t[:, h, :], float(GAMMA[h] ** n), pu.t[:, 0:256], ALU.mult, ALU.add, [RS[j], pu], [RS[j]])
                cp('act', RSB.t[:, h, :], RS[j].t[:, h, :], [RS[j]], [RSB])
        if seg == 1 and 'sample' in EVP:
            c0 = 1024
            KFb = scr()
            KF = KFb.t[:, 0:64].rearrange("p (h b) -> p h b", b=16)
            ts('dve', KF, QKS.t[:, 4:8, :], DK_SCALE, None, ALU.mult, None, [QKS], [KFb])
            POS = ps()
            state['reserved'].add(PS.index(POS))
            for b in range(16):
                RSSb = SMP[b % 2]
                RSS = RSSb.t[:, 0:1024].rearrange("p (h e) -> p h e", e=256)
                P.dma(RSS, s_ret[j, b].rearrange("h d e -> d h e"), writes=[RSSb])
                pbv = [ps(), ps()]
                for half in range(2):
                    for hh in range(2):
                        mm(pbv[half].t[:, 0:512], SELH.t[0:16, b, hh, :], BIG2.t[0:16, 8 * 1024 + half * 512:8 * 1024 + (half + 1) * 512], hh == 0, hh == 1,
                           [SELH, (BIG2, 8)], [pbv[half]])
                for h in range(4):
                    ts('dve', RSS[:, h, :], RSS[:, h, :], float(GAMMA[h]), None, ALU.mult, None, [RSSb], [RSSb])
                    stt(RSS[:, h, :], pbv[h // 2].t[:, (h % 2) * 256:(h % 2) * 256 + 256], KF[:, h, b:b + 1], RSS[:, h, :], ALU.mult, ALU.add,
                        [pbv[h // 2], KFb, RSSb], [RSSb])
                P.dma(o_rets[j, b].rearrange("h d e -> d h e"), RSS, reads=[RSSb], eng='pool')
                for h in range(4):
                    for ec in range(2):
                        col = (h * 2 + ec) * 16 + b
                        mm(POS.t[:, col:col + 1], RSS[:, h, ec * 128:(ec + 1) * 128], QKS.t[:, h, b:b + 1], True, True, [RSSb, QKS], [POS])
            for h in range(4):
                src3 = POS.t[:, h * 32:h * 32 + 32].rearrange("p (e c) -> p e c", c=16)
                post(src3, [POS], 16, h, c0)
            state['reserved'].discard(PS.index(POS))
        for wt in range(2 if 'wouta' in EVP else 0):
            bf, wv = wload([(w_out[j][0:1024, wt * 512:(wt + 1) * 512], 0, 512)], 8, 512)
            for q in range(4):
                proj_fm(bf, wv, 8, q * 128, lambda k, c0, n: b1(8 + k, c0, n), [BIG1], x_accum(wt * 4 + q))
        BIG1.fence()
        BIG2.fence()
        NP = TC if seg == 0 else 1024
        if seg == 1:
            P.dma(STS.t[:, 0:128].rearrange("p (g b) -> p g b", b=16), s_lru[j], writes=[STS])
            P.dma(STS.t[:, 128:512].rearrange("p (g t b) -> p g t b", t=3, b=16), s_lconv[j], writes=[STS])
        SLR = STS.t[:, 0:128].rearrange("p (g b) -> p g b", b=16)
        SLC = STS.t[:, 128:512].rearrange("p (g t b) -> p g t b", t=3, b=16)
        XB = HBUF
        XC, RR, II, AA, TT_, HS, GB = [(lambda c0, n, r=r: b1f(r, c0, n)) for r in range(7)]
        keys = [(BIG1, ('f', r)) for r in range(7)]
        kXC, kRR, kII, kAA, kTT, kHS, kGB = keys
        kXCB = (BIG1, ('f', 7))

        def XCB(c0, n):
            return BIG1.t[:, 7 * 2 * TC + c0:7 * 2 * TC + c0 + n]
        for gp in range(4 if 'lru' in EVP else 0):
            bf, wv = wload([(w_in[j][:, 3072 + gp * 256:3072 + (gp + 1) * 256], 0, 256),
                            (w_in[j][:, 4096 + gp * 256:4096 + (gp + 1) * 256], 256, 256)], 8, 512)
            for gi in range(2):
                g = gp * 2 + gi
                cw = [VEC8.t[:, V8['cw%d_%d' % (m, j)], g:g + 1] for m in range(4)]
                cb = VEC8.t[:, V8['cb%d' % j], g:g + 1]
                cp('pool', XB.t[:, 0:3], LCH[:, j, g, :], [CAR], [XB])

                def evx(pb, c0, n):
                    cp('act', XB.t[:, 3 + c0:3 + c0 + n], pb.t[:, 0:n], [pb], [XB])
                proj_fm(bf, wv, 8, gi * 128, hview, [H], evx)

                def evg(pb, c0, n):
                    act(GB(c0, n), pb.t[:, 0:n], AF.Gelu_apprx_tanh, [pb], [kGB])
                proj_fm(bf, wv, 8, 256 + gi * 128, hview, [H], evg)
                if seg == 0:
                    cp('pool', LCH[:, j, g, :], XB.t[:, 3 + TC - 3:3 + TC], [XB], [CAR])
                else:
                    cp('pool', LCO[:, j, g, :], XB.t[:, 3 + 1021:3 + 1024], [XB], [CAR])
                ts('dve', XC(0, TC), XB.t[:, 0:TC], cw[0], cb, ALU.mult, ALU.add, [XB, VEC8], [kXC])
                for m in range(1, 4):
                    stt(XC(0, TC), XB.t[:, m:m + TC], cw[m], XC(0, TC), ALU.mult, ALU.add, [XB, VEC8, kXC], [kXC])
                if seg == 1:
                    ts('dve', XC(1024, 16), SLC[:, g, 0, :], cw[0], cb, ALU.mult, ALU.add, [STS, VEC8], [kXC])
                    stt(XC(1024, 16), SLC[:, g, 1, :], cw[1], XC(1024, 16), ALU.mult, ALU.add, [STS, VEC8, kXC], [kXC])
                    stt(XC(1024, 16), SLC[:, g, 2, :], cw[2], XC(1024, 16), ALU.mult, ALU.add, [STS, VEC8, kXC], [kXC])
                    stt(XC(1024, 16), XB.t[:, 3 + 1024:3 + 1040], cw[3], XC(1024, 16), ALU.mult, ALU.add, [XB, VEC8, kXC], [kXC])
                    cp('pool', LCS.t[:, g, 0, :], SLC[:, g, 1, :], [STS], [LCS])
                    cp('pool', LCS.t[:, g, 1, :], SLC[:, g, 2, :], [STS], [LCS])
                    cp('pool', LCS.t[:, g, 2, :], XB.t[:, 3 + 1024:3 + 1040], [XB], [LCS])
                cp('pool', XCB(0, TC), XC(0, TC), [kXC], [kXCB])
                for which, dst, kd, bname in ((0, RR, kRR, 'ba%d' % j), (1, II, kII, 'bx%d' % j)):
                    for (c0, n) in TOK_TILES:
                        pb = ps()
                        mm(pb.t[:, 0:n], WAX.t[:, which, g, :], XCB(c0, n), True, True, [WAX, kXCB], [pb])
                        act(dst(c0, n), pb.t[:, 0:n], AF.Sigmoid, [pb, VEC8], [kd], bias=VEC8.t[:, V8[bname], g:g + 1])
                act(AA(0, TC), RR(0, TC), AF.Exp, [kRR, SP8], [kAA], scale=SP8.t[:, j, g:g + 1])
                tt('pool', TT_(0, TC), AA(0, TC), AA(0, TC), ALU.mult, [kAA], [kTT])
                ts('pool', TT_(0, TC), TT_(0, TC), -1.0, 1.0, ALU.mult, ALU.add, [kTT], [kTT])
                ts('pool', TT_(0, TC), TT_(0, TC), 0.0, None, ALU.max, None, [kTT], [kTT])
                act(TT_(0, TC), TT_(0, TC), AF.Sqrt, [kTT], [kTT])
                tt('pool', II(0, TC), II(0, TC), XC(0, TC), ALU.mult, [kII, kXC], [kII])
                tt('dve', II(0, TC), II(0, TC), TT_(0, TC), ALU.mult, [kII, kTT], [kII])
                init = 0.0 if seg == 0 else LHC[:, j, g:g + 1]
                P.op('dve', lambda e, init=init: e.tensor_tensor_scan(out=HS(0, NP), data0=AA(0, NP), data1=II(0, NP), initial=init,
                                                                      op0=ALU.mult, op1=ALU.add), [kAA, kII, CAR], [kHS])
                if seg == 0:
                    cp('pool', LHC[:, j, g:g + 1], HS(NP - 1, 1), [kHS], [CAR])
                else:
                    cp('pool', LRO[:, j, g:g + 1], HS(1023, 1), [kHS], [CAR])
                    tt('pool', HS(1024, 16), AA(1024, 16), SLR[:, g, :], ALU.mult, [kAA, STS], [kHS])
                    tt('pool', HS(1024, 16), HS(1024, 16), II(1024, 16), ALU.add, [kHS, kII], [kHS])
                    cp('pool', LCS.t[:, 0, 0, 0:1] if False else STS.t[:, 512 + g * 16:512 + g * 16 + 16], HS(1024, 16), [kHS], [STS])
                tt('dve', b2(g, 0, TC), HS(0, TC), GB(0, TC), ALU.mult, [kHS, kGB], [(BIG2, ('y', g))])
        if seg == 1:
            P.dma(o_lrus[j], STS.t[:, 512:640].rearrange("p (g b) -> p g b", b=16), reads=[STS], eng='pool')
            P.dma(o_lconvs[j], LCS.t[:], reads=[LCS], eng='pool')
        BIG2.fence()
        for wt in range(2 if 'woutb' in EVP else 0):
            bf, wv = wload([(w_out[j][1024:2048, wt * 512:(wt + 1) * 512], 0, 512)], 8, 512)
            for q in range(4):
                proj_fm(bf, wv, 8, q * 128, lambda k, c0, n: b2(k, c0, n), [BIG2], x_accum(wt * 4 + q))
        BIG1.fence()
        BIG2.fence()

    ctx['even_layer'] = even_layer


def build_rwkv(ctx):
    globals().update(ctx)
    W0 = WST[0]
    B0 = WBF[0]
    B1_ = WBF[1]

    def Q(idx):
        return W0.t[:, idx * 136:idx * 136 + 128].rearrange("p (k n) -> p k n", n=16)

    def kq(idx):
        return (W0, ('q', idx))

    def bc(name):
        return VEC8.t[:, V8[name], :].rearrange("p (k o) -> p k o", o=1).to_broadcast([128, 8, 16])

    def p3(pb):
        return pb.t[:, 0:128].rearrange("p (k n) -> p k n", n=16)

    WRv = BIG1.t[:, 0:8192].rearrange("p (k n) -> p k n", n=1024)
    WKv = BIG1.t[:, 8192:16384].rearrange("p (k n) -> p k n", n=1024)
    WVv = BIG2.t[:, 0:8192].rearrange("p (k n) -> p k n", n=1024)
    WOv = H.t[:, 0:8192].rearrange("p (k n) -> p k n", n=1024)
    o_ = 16384
    W1v = BIG1.t[:, o_:o_ + 512].rearrange("p (k n) -> p k n", n=64)
    A1v = BIG1.t[:, o_ + 512:o_ + 1024].rearrange("p (k n) -> p k n", n=64)
    V1v = BIG1.t[:, o_ + 1024:o_ + 1280].rearrange("p (k n) -> p k n", n=32)
    G1v = BIG1.t[:, o_ + 1280:o_ + 2304].rearrange("p (k n) -> p k n", n=128)
    W2v = BIG1.t[:, o_ + 2304:o_ + 3328]
    A2v = BIG1.t[:, o_ + 3328:o_ + 4352]
    V2v = BIG1.t[:, o_ + 4352:o_ + 5376]
    G2v = BIG1.t[:, o_ + 5376:o_ + 6400]

    def load_small(dst, src, rows):
        s = state['ws'] % 2
        state['ws'] += 1
        st = WST[s]
        P.dma(st.t[0:rows, 0:1024], src, writes=[st])
        cp('pool', dst[0:rows, :], st.t[0:rows, 0:1024], [st], [BIG1])

    def rwkv_layer(seg, li):
        j = li // 2
        BIG1.fence()
        BIG2.fence()
        H.fence()
        for half in range(2):
            cs = slice(half * 512, (half + 1) * 512)
            wload_to(BIG1, WRv[:, :, cs], w_r[j][:, cs], 8, 512)
            wload_to(BIG1, WKv[:, :, cs], w_k[j][:, cs], 8, 512)
            wload_to(BIG2, WVv[:, :, cs], w_v[j][:, cs], 8, 512)
            wload_to(H, WOv[:, :, cs], w_o[j][:, cs], 8, 512)
        wload_to(BIG1, W1v, d_w1[j], 8, 64)
        wload_to(BIG1, A1v, d_a1[j], 8, 64)
        if li == 3:
            wload_to(BIG1, V1v, d_v1[0], 8, 32)
            load_small(V2v, d_v2[0], 32)
        wload_to(BIG1, G1v, d_g1[j], 8, 128)
        load_small(W2v, d_w2[j], 64)
        load_small(A2v, d_a2[j], 64)
        load_small(G2v, d_g2[j], 128)
        CHUNK = DBG.get('chunk', True)
        if CHUNK:
            P.dma(HBUF.t[0:16, 0:768], d_cmask, writes=[HBUF])
            P.dma(HBUF.t[:, 768:896].rearrange('p (k n) -> p k n', n=16), d_reset[:, :, 0:16], writes=[HBUF])
        W0.fence()
        B0.fence()
        B1_.fence()
        if seg == 0:
            groups = [(16 * i, False) for i in range(65)]
        else:
            groups = [(16 * i, False) for i in range(64)] + [(1024, True)]
        if seg == 1:
            P.dma(STS.t[:, 0:128].rearrange("p (g b) -> p g b", b=16), s_shift[j], writes=[STS])
        SSH = STS.t[:, 0:128].rearrange("p (g b) -> p g b", b=16)
        MS = MST[j].t[:, :, :]
        XM = B0.t[:, 0:768].rearrange("p (m k n) -> p m k n", m=6, k=8)
        for gi_, (c0, sample) in enumerate(groups):
            HFb = W0.t[:, 29 * 136:30 * 136].rearrange("p (k n) -> p k n", n=17)
            kHF = (W0, ('q', 29))
            sqv = B1_.t[:, 3072:3200].rearrange("p (k n) -> p k n", n=16)
            kSQ = (B1_, 'sq')
            act(sqv, X.t[:, :, c0:c0 + 16], AF.Square, [X], [kSQ])
            pb = ps()
            for k in range(8):
                mm(pb.t[:, 0:16], CMB.t[:, CM_MEAN1024, :], sqv[:, k, :], k == 0, k == 7, [CMB, kSQ], [pb])
            rs = scr()
            act(rs.t[:, 0:16], pb.t[:, 0:16], AF.Sqrt, [pb, SMALL], [rs], bias=EPSC)
            P.op('dve', lambda e, rs=rs: e.reciprocal(out=rs.t[:, 0:16], in_=rs.t[:, 0:16]), [rs], [rs])
            if not sample:
                cp('pool', HFb[:, :, 0], SHC[:, j, :], [CAR], [kHF])
            tt('dve', HFb[:, :, 1:17], X.t[:, :, c0:c0 + 16], bc('nmix%d' % li), ALU.mult, [X, VEC8], [kHF])
            tt('dve', HFb[:, :, 1:17], HFb[:, :, 1:17], rs.t[:, 0:16].rearrange("p (o n) -> p o n", o=1).to_broadcast([128, 8, 16]), ALU.mult,
               [kHF, rs], [kHF])
            cur = HFb[:, :, 1:17]
            XX = Q(0)
            if sample:
                tt('dve', XX, SSH, cur, ALU.subtract, [STS, kHF], [kq(0)])
                P.dma(o_shifts[j], cur, reads=[kHF], eng='pool')
            else:
                tt('dve', XX, HFb[:, :, 0:16], cur, ALU.subtract, [kHF], [kq(0)])
                cp('pool', SHC[:, j, :], HFb[:, :, 16], [kHF], [CAR])
                if seg == 1 and c0 == 1008:
                    cp('pool', SHO[:, j, :], HFb[:, :, 16], [kHF], [CAR])
            for m in range(6):
                qt = 1 if m % 2 == 0 else 9
                tt('dve', Q(qt), XX, bc('mix%d_%d' % (j, m)), ALU.mult, [kq(0), VEC8], [kq(qt)])
                tt('pool', XM[:, m], Q(qt), cur, ALU.add, [kq(qt), kHF], [(B0, ('xm', m))])

            def projN(wv, m, wbuf):
                pb = ps()
                for oc in range(8):
                    for k in range(8):
                        mm(pb.t[:, oc * 16:oc * 16 + 16], wv[:, k, oc * 128:oc * 128 + 128], XM[:, m, k, :], k == 0, k == 7,
                           [wbuf, (B0, ('xm', m))], [pb])
                return pb

            def lora(w1v, r, m, func, w2v):
                pl = ps()
                for k in range(8):
                    mm(pl.t[0:r, 0:16], w1v[:, k, 0:r], XM[:, m, k, :], k == 0, k == 7, [BIG1, (B0, ('xm', m))], [pl])
                tl = B1_.t[0:r, 3200 + m * 16:3200 + m * 16 + 16]
                kt = (B1_, ('tl', m))
                if func is None:
                    cp('act', tl, pl.t[0:r, 0:16], [pl], [kt])
                else:
                    act(tl, pl.t[0:r, 0:16], func, [pl], [kt])
                po2 = ps()
                for oc in range(8):
                    mm(po2.t[:, oc * 16:oc * 16 + 16], w2v[0:r, oc * 128:oc * 128 + 128], tl, True, True, [BIG1, kt], [po2])
                return po2
            pr = projN(WRv, 0, BIG1)
            RF = Q(2)
            cp('act', RF, p3(pr), [pr], [kq(2)])
            pk = projN(WKv, 2, BIG1)
            KF = Q(3)
            cp('act', KF, p3(pk), [pk], [kq(3)])
            pv = projN(WVv, 3, BIG2)
            VV = Q(4)
            cp('act', VV, p3(pv), [pv], [kq(4)])
            pw = lora(W1v, 64, 1, AF.Tanh, W2v)
            DEC = Q(5)
            tt('dve', DEC, p3(pw), bc('w0_%d' % j), ALU.add, [pw, VEC8], [kq(5)])
            act(DEC, DEC, AF.Sigmoid, [kq(5)], [kq(5)])
            use_chunk = CHUNK and not sample
            if not use_chunk:
                act(DEC, DEC, AF.Exp, [kq(5)], [kq(5)], scale=-WDEC)
            pa = lora(A1v, 64, 4, None, A2v)
            AS = Q(6)
            tt('dve', AS, p3(pa), bc('a0_%d' % j), ALU.add, [pa, VEC8], [kq(6)])
            act(AS, AS, AF.Sigmoid, [kq(6)], [kq(6)])
            pg = lora(G1v, 128, 5, AF.Sigmoid, G2v)
            GG = Q(7)
            cp('act', GG, p3(pg), [pg], [kq(7)])
            if li == 3:
                pvl = lora(V1v, 32, 3, None, V2v)
                VG = Q(8)
                tt('dve', VG, p3(pvl), bc('v0_0'), ALU.add, [pvl, VEC8], [kq(8)])
                act(VG, VG, AF.Sigmoid, [kq(8)], [kq(8)])
                P.dma(Q(23), vf_dram[:, :, seg * TC + c0:seg * TC + c0 + 16], reads=[(VFD, seg * TC + c0)], writes=[kq(23)])
                tt('dve', Q(9), Q(23), VV, ALU.subtract, [kq(23), kq(4)], [kq(9)])
                tt('dve', Q(9), Q(9), VG, ALU.mult, [kq(9), kq(8)], [kq(9)])
                tt('dve', VV, VV, Q(9), ALU.add, [kq(4), kq(9)], [kq(4)])
            else:
                P.dma(vf_dram[:, :, seg * TC + c0:seg * TC + c0 + 16], VV, reads=[kq(4)], writes=[(VFD, seg * TC + c0)], eng='pool')
            KK = Q(10)
            tt('dve', KK, KF, bc('kk%d' % j), ALU.mult, [kq(3), VEC8], [kq(10)])
            tt('pool', Q(11), KK, KK, ALU.mult, [kq(10)], [kq(11)])
            pss = ps()
            mm(pss.t[:, 0:128], CM.t[:, CM_BLKONES, :], W0.t[:, 11 * 136:11 * 136 + 128], True, True, [CM, kq(11)], [pss])
            INV = Q(12)
            ts('dve', INV, p3(pss), 1e-24, None, ALU.max, None, [pss], [kq(12)])
            act(INV, INV, AF.Sqrt, [kq(12)], [kq(12)])
            P.op('dve', lambda e, INV=INV: e.reciprocal(out=INV, in_=INV), [kq(12)], [kq(12)])
            tt('dve', KK, KK, INV, ALU.mult, [kq(10), kq(12)], [kq(10)])
            AT = Q(13)
            ts('pool', AT, KK, -1.0, None, ALU.mult, None, [kq(10)], [kq(13)])
            BT = Q(14)
            tt('pool', BT, KK, AS, ALU.mult, [kq(10), kq(6)], [kq(14)])
            K2 = Q(15)
            stt(K2, AS, -1.0, bc('ka%d' % j), ALU.add, ALU.mult, [kq(6), VEC8], [kq(15)])
            stt(K2, K2, 1.0, KF, ALU.add, ALU.mult, [kq(15), kq(3)], [kq(15)])
            RK = Q(16)
            tt('dve', RK, RF, K2, ALU.mult, [kq(2), kq(15)], [kq(16)])
            tt('pool', RK, RK, bc('rk%d' % j), ALU.mult, [kq(16), VEC8], [kq(16)])
            pbo = ps()
            mm(pbo.t[:, 0:128], CM.t[:, CM_BLKONES, :], W0.t[:, 16 * 136:16 * 136 + 128], True, True, [CM, kq(16)], [pbo])
            BON = Q(17)
            tt('dve', BON, p3(pbo), VV, ALU.mult, [pbo, kq(4)], [kq(17)])
            YS = Q(18)
            for _once in ([0] if use_chunk else []):
                SG = DEC
                CS = Q(24)
                ones16 = SMALL.t[:, 2:3].to_broadcast([128, 16])
                P.op('dve', lambda e: e.tensor_tensor_scan(out=W0.t[:, 24 * 136:24 * 136 + 128], data0=HBUF.t[:, 768:896],
                                                           data1=W0.t[:, 5 * 136:5 * 136 + 128], initial=0.0,
                                                           op0=ALU.mult, op1=ALU.add), [kq(5), HBUF], [kq(24)])
                PEXC, PINC, PINV, BP, KP = Q(0), Q(1), Q(11), Q(27), Q(28)
                tt('dve', PEXC, CS, SG, ALU.subtract, [kq(24), kq(5)], [kq(0)])
                act(PEXC, PEXC, AF.Exp, [kq(0)], [kq(0)], scale=-WDEC)
                act(PINC, CS, AF.Exp, [kq(24)], [kq(1)], scale=-WDEC)
                act(PINV, CS, AF.Exp, [kq(24)], [kq(11)], scale=WDEC)
                PLb = PINC[:, :, 15:16].to_broadcast([128, 8, 16])
                ARv = B1_.t[:, 3328:3584].rearrange("p (k a n) -> p k a n", a=2, n=16)
                kAR = (B1_, ('qb', 0))
                BTq = B1_.t[:, 3584:3712].rearrange("p (k n) -> p k n", n=16)
                kBT = (B1_, ('qb', 2))
                KTq = B1_.t[:, 3712:3840].rearrange("p (k n) -> p k n", n=16)
                kKT = (B1_, ('qb', 3))
                BDq = B1_.t[:, 3840:3968].rearrange("p (k n) -> p k n", n=16)
                kBD = (B1_, ('qb', 4))
                KDq = B0.t[:, 768:896].rearrange("p (k n) -> p k n", n=16)
                kKD = (B0, ('qb', 5))
                VBq = B0.t[:, 896:1024].rearrange("p (k n) -> p k n", n=16)
                kVB = (B0, ('qb', 6))
                tt('dve', ARv[:, :, 0, :], AT, PEXC, ALU.mult, [kq(13), kq(0)], [kAR])
                tt('dve', ARv[:, :, 1, :], RF, PINC, ALU.mult, [kq(2), kq(1)], [kAR])
                tt('dve', BP, BT, PINV, ALU.mult, [kq(14), kq(11)], [kq(27)])
                tt('dve', KP, K2, PINV, ALU.mult, [kq(15), kq(11)], [kq(28)])
                cp('pool', BTq, BP, [kq(27)], [kBT])
                cp('pool', KTq, KP, [kq(28)], [kKT])
                tt('pool', BDq, BP, PLb, ALU.mult, [kq(27), kq(1)], [kBD])
                tt('pool', KDq, KP, PLb, ALU.mult, [kq(28), kq(1)], [kKD])
                cp('act', VBq, VV, [kq(4)], [kVB])
                ARm = [B0.t[:, 3584:3840].rearrange("p (k a n) -> p k a n", a=2, n=16),
                       B0.t[:, 3840:4096].rearrange("p (k a n) -> p k a n", a=2, n=16)]
                kARm = (B0, ('arm', 0))
                for hh_ in range(2):
                    ts('dve' if hh_ == 0 else 'pool', ARm[hh_], ARv, CM.t[:, CM_BLKONES, 64 * hh_:64 * hh_ + 1], None, ALU.mult, None, [kAR, CM], [kARm])
                if DBG.get('cstop', 99) < 2:
                    break
                VTM = B1_.t[0:16, 0:1024]
                BDTM = B1_.t[0:16, 1024:2048]
                KDTM = B1_.t[0:16, 2048:3072]
                kVTM, kBDTM, kKDTM = (B1_, ('tm', 0)), (B1_, ('tm', 1)), (B1_, ('tm', 2))
                for (qb, kqb, tm, ktm) in ((VBq, kVB, VTM, kVTM), (BDq, kBD, BDTM, kBDTM), (KDq, kKD, KDTM, kKDTM)):
                    pt = ps()
                    ptb = pt.t[:, :].bitcast(BF16)
                    for k in range(8):
                        P.op('pe', lambda e, ptb=ptb, qb=qb, k=k: e.transpose(ptb[0:16, k * 128:(k + 1) * 128], qb[:, k, :], CMB.t[:, CM_IDENT, :]),
                             [kqb, CMB], [pt])
                    evcp(tm, ptb[0:16, 0:1024], [pt], [ktm])
                MB = W0.t[:, 25 * 136:25 * 136 + 256].bitcast(BF16).rearrange("p (k n) -> p k n", n=64)
                kMB = (W0, ('q', 25))
                cp('act', MB, MS, [MST[j]], [kMB])
                if DBG.get('cstop', 99) < 3:
                    break
                G1 = B0.t[0:16, 1024:1536].rearrange("s (h a t) -> s h a t", a=2, t=16)
                G2 = B0.t[0:16, 1536:2048].rearrange("s (h a t) -> s h a t", a=2, t=16)
                kG1, kG2 = (B0, ('g', 1)), (B0, ('g', 2))
                ATn = B0.t[0:16, 3072:3328].rearrange("s (h t) -> s h t", t=16)
                An = B0.t[0:16, 3328:3584].rearrange("s (h t) -> s h t", t=16)
                kATn = [(B0, ('a', 0, 0)), (B0, ('a', 0, 1))]
                kAn = [(B0, ('a', 1, 0)), (B0, ('a', 1, 1))]
                ZB = B0.t[0:16, 2048:3072]
                kZB = [(B0, ('z', 0)), (B0, ('z', 1))]
                MG = HBUF.t[0:16, 0:512]
                MA = HBUF.t[0:16, 512:768]
                pg1, pg2, pga = ps(), ps(), ps()
                for h in range(16):
                    p_, hh = h // 2, h % 2
                    rows = slice(64 * hh, 64 * hh + 64)
                    arr = ARm[hh][:, p_, :, :]
                    mm(pg1.t[0:16, h * 32:h * 32 + 32], BTq[:, p_, :], arr, True, True, [kBT, kARm], [pg1])
                    mm(pg2.t[0:16, h * 32:h * 32 + 32], KTq[:, p_, :], arr, True, True, [kKT, kARm], [pg2])
                    mm(pga.t[0:16, h * 16:h * 16 + 16], ARm[hh][:, p_, 0, :], BTq[:, p_, :], True, True, [kARm, kBT], [pga])
                tt('dve', B0.t[0:16, 1024:1536], pg1.t[0:16, 0:512], MG, ALU.mult, [pg1, HBUF], [kG1])
                tt('dve', B0.t[0:16, 1536:2048], pg2.t[0:16, 0:512], MG, ALU.mult, [pg2, HBUF], [kG2])
                for hf in range(2):
                    tt('dve', B0.t[0:16, 3328 + hf * 128:3328 + hf * 128 + 128], pga.t[0:16, hf * 128:hf * 128 + 128], MA[:, hf * 128:hf * 128 + 128], ALU.mult,
                       [pga, HBUF], [kAn[hf]])
                    cp('pool', ATn[:, hf * 8:hf * 8 + 8, :], G1[:, hf * 8:hf * 8 + 8, 0, :], [kG1], [kATn[hf]])
                if DBG.get('cstop', 99) < 4:
                    break
                pw = [ps(), ps()]
                for h in range(16):
                    p_, hh = h // 2, h % 2
                    rows = slice(64 * hh, 64 * hh + 64)
                    o_ap = pw[h // 8].t[0:16, (h % 8) * 64:(h % 8) * 64 + 64]
                    mm(o_ap, ARm[hh][:, p_, 0, :], MB[:, p_, :], True, False, [kARm, kMB], [pw[h // 8]])
                    mm(o_ap, G2[:, h, 0, :], VTM[:, h * 64:h * 64 + 64], False, True, [kG2, kVTM], [pw[h // 8]])
                for half in range(2):
                    evcp(ZB[:, half * 512:(half + 1) * 512], pw[half].t[0:16, 0:512], [pw[half]], [kZB[half]])
                if DBG.get('cstop', 99) < 5:
                    break
                I16 = CMB.t[0:16, CM_IDENT, 0:16]
                for lvl in range(4):
                    pz = [ps(), ps()]
                    pa = [ps(), ps()] if lvl < 3 else None
                    for hf in range(2):
                        for h in range(hf * 8, hf * 8 + 8):
                            o_ap = pz[hf].t[0:16, (h % 8) * 64:(h % 8) * 64 + 64]
                            zsl = ZB[:, h * 64:h * 64 + 64]
                            mm(o_ap, ATn[:, h, :], zsl, True, True, [kATn[hf], kZB[hf]], [pz[hf]])
                        if lvl < 3:
                            for h in range(hf * 8, hf * 8 + 8):
                                hl = h % 8
                                mm(pa[hf].t[0:16, hl * 16:hl * 16 + 16], An[:, h, :], ATn[:, h, :], True, True, [kAn[hf], kATn[hf]], [pa[hf]])
                                mm(pa[hf].t[0:16, 128 + hl * 16:128 + hl * 16 + 16], ATn[:, h, :], An[:, h, :], True, True, [kAn[hf], kATn[hf]], [pa[hf]])
                    for hf in range(2):
                        tt('dve', ZB[:, hf * 512:(hf + 1) * 512], pz[hf].t[0:16, 0:512], ZB[:, hf * 512:(hf + 1) * 512], ALU.add,
                           [pz[hf], kZB[hf]], [kZB[hf]])
                        if lvl < 3:
                            cp('act', B0.t[0:16, 3072 + hf * 128:3072 + hf * 128 + 128], pa[hf].t[0:16, 0:128], [pa[hf]], [kATn[hf]])
                            cp('act', B0.t[0:16, 3328 + hf * 128:3328 + hf * 128 + 128], pa[hf].t[0:16, 128:256], [pa[hf]], [kAn[hf]])
                if DBG.get('cstop', 99) < 6:
                    break
                py = [ps(), ps()]
                for h in range(16):
                    p_, hh = h // 2, h % 2
                    rows = slice(64 * hh, 64 * hh + 64)
                    o_ap = py[h // 8].t[0:16, (h % 8) * 64:(h % 8) * 64 + 64]
                    mm(o_ap, ARm[hh][:, p_, 1, :], MB[:, p_, :], True, False, [kARm, kMB], [py[h // 8]])
                    mm(o_ap, G1[:, h, 1, :], ZB[:, h * 64:h * 64 + 64], False, False, [kG1, kZB[h // 8]], [py[h // 8]])
                    mm(o_ap, G2[:, h, 1, :], VTM[:, h * 64:h * 64 + 64], False, True, [kG2, kVTM], [py[h // 8]])
                YTB = B0.t[0:16, 3072:4096]
                kYTB = (B0, ('ytb', 0))
                for half in range(2):
                    evcp(YTB[:, half * 512:(half + 1) * 512], py[half].t[0:16, 0:512], [py[half]], [kYTB, kAn[0], kAn[1], kATn[0], kATn[1], kARm])
                pyt = ps()
                pytb = pyt.t[:, :].bitcast(BF16)
                for k in range(8):
                    P.op('pe', lambda e, pytb=pytb, k=k: e.transpose(pytb[:, k * 16:(k + 1) * 16], YTB[:, k * 128:(k + 1) * 128], CMB.t[0:16, CM_IDENT, 0:16]),
                         [kYTB, kAn[0], kAn[1], kATn[0], kATn[1], kARm, CMB], [pyt])
                cp('act', YS, pytb[:, 0:128].rearrange("p (k n) -> p k n", n=16), [pyt], [kq(18)])
                if DBG.get('cstop', 99) < 7:
                    break
                psu = [ps(), ps()]
                for p_ in range(8):
                    o_ap = psu[p_ // 4].t[:, (p_ % 4) * 128:(p_ % 4) * 128 + 128]
                    mm(o_ap, BDTM[:, p_ * 128:(p_ + 1) * 128], ZB[:, p_ * 128:(p_ + 1) * 128], True, False, [kBDTM, kZB[p_ // 4]], [psu[p_ // 4]])
                    mm(o_ap, KDTM[:, p_ * 128:(p_ + 1) * 128], VTM[:, p_ * 128:(p_ + 1) * 128], False, True, [kKDTM, kVTM], [psu[p_ // 4]])
                tt('dve', MS, MS, PINC[:, :, 15:16].to_broadcast([128, 8, 64]), ALU.mult, [MST[j], kq(1)], [MST[j]])
                for g2 in range(2):
                    for hh in range(2):
                        rows = slice(64 * hh, 64 * hh + 64)
                        src = psu[g2].t[rows, 0:512].rearrange("p (q c) -> p q c", c=128)[:, :, hh * 64:hh * 64 + 64]
                        tt('dve', MS[rows, 4 * g2:4 * g2 + 4, :], MS[rows, 4 * g2:4 * g2 + 4, :], src, ALU.add, [MST[j], psu[g2]], [MST[j]])
            if not use_chunk:
                TM = []
                qlist = [(AT, kq(13)), (DEC, kq(5)), (BT, kq(14)), (K2, kq(15)), (RF, kq(2)), (None, None)]
                qbs = []
                for qi, (src, ksrc) in enumerate(qlist):
                    if qi < 5:
                        qb = B1_.t[:, 3328 + qi * 128:3328 + qi * 128 + 128].rearrange("p (k n) -> p k n", n=16)
                        kqb = (B1_, ('qb', qi))
                    else:
                        qb = B0.t[:, 768:896].rearrange("p (k n) -> p k n", n=16)
                        kqb = (B0, ('qb', qi))
                    qbs.append((qb, kqb))
                    if qi == 5:
                        tt('dve', Q(24), DEC, qbs[1][0], ALU.subtract, [kq(5), qbs[1][1]], [kq(24)])
                        cp('pool', qb, Q(24), [kq(24)], [kqb])
                    else:
                        cp('pool' if qi % 2 else 'act', qb, src, [ksrc], [kqb])
                    pt = ps()
                    ptb = pt.t[:, :].bitcast(BF16)
                    for k in range(8):
                        P.op('pe', lambda e, ptb=ptb, qb=qb, k=k: e.transpose(ptb[0:16, k * 128:(k + 1) * 128], qb[:, k, :], CMB.t[:, CM_IDENT, :]),
                             [kqb, CMB], [pt])
                    if qi < 3:
                        tm = B1_.t[0:16, qi * 1024:(qi + 1) * 1024]
                        ktm = (B1_, ('tm', qi))
                    else:
                        tm = B0.t[0:16, 1024 + (qi - 3) * 1024:1024 + (qi - 2) * 1024]
                        ktm = (B0, ('tm', qi))
                    evcp(tm, ptb[0:16, 0:1024], [pt], [ktm])
                    TM.append((tm, ktm))
                for t in range(16):
                    if sample:
                        Sb = SMP[0]
                        S3 = Sb.t[:, (t % 2) * 512:(t % 2) * 512 + 512].rearrange("p (k n) -> p k n", n=64)
                        for hh in range(2):
                            P.dma(S3[hh * 64:(hh + 1) * 64], s_rwkv[j, t].rearrange("(p hh) i jj -> hh i p jj", hh=2)[hh], writes=[Sb])
                        kS = Sb
                    else:
                        S3 = MS
                        kS = MST[j]
                    pq = []
                    for qi in range(5):
                        pbq = ps()
                        srcs = [TM[qi]] + ([TM[5]] if qi == 1 else [])
                        nmm = 2 * len(srcs)
                        im = 0
                        for (tm, ktm) in srcs:
                            tm4 = tm.rearrange("s (p hh jj) -> s p hh jj", hh=2, jj=64)
                            for hh in range(2):
                                mm(pbq.t[:, 0:512], SELH.t[0:16, t, hh, :], tm4[:, :, hh, :], im == 0, im == nmm - 1, [SELH, ktm], [pbq])
                                im += 1
                        pq.append(pbq)

                    def q3(i):
                        return pq[i].t[:, 0:512].rearrange("p (k n) -> p k n", n=64)
                    T1b = scr()
                    T1 = T1b.t[:, 0:512].rearrange("p (k n) -> p k n", n=64)
                    SAb = scr()
                    SA = SAb.t[:, 0:8]
                    tt('dve', T1, S3, q3(0), ALU.mult, [kS, pq[0]], [T1b])
                    P.op('dve', lambda e, SA=SA, T1=T1: e.tensor_reduce(out=SA, in_=T1, axis=AX.X, op=ALU.add), [T1b], [SAb])
                    tt('dve', S3, S3, q3(1), ALU.mult, [kS, pq[1]], [kS])
                    T2b = scr()
                    T2 = T2b.t[:, 0:512].rearrange("p (k n) -> p k n", n=64)
                    tt('dve', T2, q3(2), SAb.t[:, 0:8].rearrange("p (k o) -> p k o", o=1).to_broadcast([128, 8, 64]), ALU.mult, [pq[2], SAb], [T2b])
                    tt('pool', S3, S3, T2, ALU.add, [kS, T2b], [kS])
                    T3b = scr()
                    T3 = T3b.t[:, 0:512].rearrange("p (k n) -> p k n", n=64)
                    tt('dve', T3, q3(3), VV[:, :, t:t + 1].to_broadcast([128, 8, 64]), ALU.mult, [pq[3], kq(4)], [T3b])
                    tt('pool', S3, S3, T3, ALU.add, [kS, T3b], [kS])
                    T4b = scr()
                    T4 = T4b.t[:, 0:512].rearrange("p (k n) -> p k n", n=64)
                    tt('dve', T4, S3, q3(4), ALU.mult, [kS, pq[4]], [T4b])
                    P.op('dve', lambda e, T4=T4, t=t: e.tensor_reduce(out=YS[:, :, t], in_=T4, axis=AX.X, op=ALU.add), [T4b], [kq(18)])
                    if sample:
                        for hh in range(2):
                            P.dma(o_rwkvs[j, t].rearrange("(p hh) i jj -> hh i p jj", hh=2)[hh], S3[hh * 64:(hh + 1) * 64], reads=[Sb], eng='pool')
            tt('pool', Q(19), YS, YS, ALU.mult, [kq(18)], [kq(19)])
            pm = ps()
            mm(pm.t[:, 0:128], CM.t[:, CM_BLKMEAN64, :], W0.t[:, 18 * 136:18 * 136 + 128], True, True, [CM, kq(18)], [pm])
            mm(pm.t[:, 128:256], CM.t[:, CM_BLKMEAN64, :], W0.t[:, 19 * 136:19 * 136 + 128], True, True, [CM, kq(19)], [pm])
            ME = Q(20)
            VA = Q(21)
            cp('act', ME, p3(pm), [pm], [kq(20)])
            tt('dve', VA, ME, ME, ALU.mult, [kq(20)], [kq(21)])
            tt('dve', VA, pm.t[:, 128:256].rearrange("p (k n) -> p k n", n=16), VA, ALU.subtract, [pm, kq(21)], [kq(21)])
            ts('dve', VA, VA, 0.0, None, ALU.max, None, [kq(21)], [kq(21)])
            act(VA, VA, AF.Sqrt, [kq(21), SMALL], [kq(21)], bias=GNEPS)
            P.op('dve', lambda e, VA=VA: e.reciprocal(out=VA, in_=VA), [kq(21)], [kq(21)])
            OO = Q(22)
            tt('dve', OO, YS, ME, ALU.subtract, [kq(18), kq(20)], [kq(22)])
            tt('dve', OO, OO, VA, ALU.mult, [kq(22), kq(21)], [kq(22)])
            tt('pool', OO, OO, bc('gng%d' % j), ALU.mult, [kq(22), VEC8], [kq(22)])
            tt('pool', OO, OO, bc('gnb%d' % j), ALU.add, [kq(22), VEC8], [kq(22)])
            tt('pool', OO, OO, BON, ALU.add, [kq(22), kq(17)], [kq(22)])
            OB = B1_.t[:, 3968:4096].rearrange("p (k n) -> p k n", n=16)
            kOB = (B1_, 'ob')
            tt('dve', OB, OO, GG, ALU.mult, [kq(22), kq(7)], [kOB])
            po_ = ps()
            for oc in range(8):
                for k in range(8):
                    mm(po_.t[:, oc * 16:oc * 16 + 16], WOv[:, k, oc * 128:oc * 128 + 128], OB[:, k, :], k == 0, k == 7, [H, kOB], [po_])
            tt('dve', X.t[:, :, c0:c0 + 16], X.t[:, :, c0:c0 + 16], p3(po_), ALU.add, [X, po_], [X])
        W0.fence()
        B0.fence()
        B1_.fence()
        BIG1.fence()
        BIG2.fence()
        H.fence()

    ctx['rwkv_layer'] = rwkv_layer


def _prep_shared(inp):
    c = build_consts()
    sh = dict(cm=c['cm'], rope=c['rope'], rmask=c['rmask'], decin=c['decin'], deck=c['deck'], mask4=c['mask4'],
              maska=c['maska'], reset=c['reset'], sel=c['sel'], small=c['small'])
    selh = np.zeros((16, 16, 2, 128), np.float32)
    for t in range(16):
        selh[t, t, 0, :64] = 1.0
        selh[t, t, 1, 64:] = 1.0
    sh['selh'] = selh
    i16 = np.arange(16)
    strictT = (i16[:, None] < i16[None, :]).astype(np.float32)
    inclT = (i16[:, None] <= i16[None, :]).astype(np.float32)
    cm_ = np.zeros((16, 768), np.float32)
    cm_[:, 0:512] = np.tile(np.concatenate([strictT, inclT], axis=1), (1, 16))
    cm_[:, 512:768] = np.tile((i16[:, None] > i16[None, :]).astype(np.float32), (1, 16))
    sh['cmask'] = cm_
    f = lambda a: np.ascontiguousarray(np.asarray(a, np.float32))
    vec8 = np.zeros((128, NV8, 8), np.float32)

    def put8(name, v):
        vec8[:, V8[name], :] = fm(v, 8)
    for li in range(4):
        put8('nmix%d' % li, inp['norm_mix_g'][li])
        put8('nffn%d' % li, inp['norm_ffn_g'][li])
    put8('nfinal', inp['norm_final_g'])
    for j in range(2):
        put8('gn%d' % j, inp['ev_ret_gn_g'][j])
        for m in range(4):
            put8('cw%d_%d' % (m, j), inp['ev_lru_conv_w'][j][m])
        put8('cb%d' % j, inp['ev_lru_conv_b'][j])
        put8('ba%d' % j, inp['ev_lru_ba'][j])
        put8('bx%d' % j, inp['ev_lru_bx'][j])
        put8('lam%d' % j, inp['ev_lru_lambda'][j])
        for m in range(6):
            put8('mix%d_%d' % (j, m), inp['od_mix'][j][m])
        put8('w0_%d' % j, inp['od_w0'][j])
        put8('a0_%d' % j, inp['od_a0'][j])
        put8('kk%d' % j, inp['od_k_k'][j])
        put8('ka%d' % j, inp['od_k_a'][j])
        put8('rk%d' % j, np.asarray(inp['od_r_k'][j]).reshape(-1))
        put8('gng%d' % j, inp['od_gn_g'][j])
        put8('gnb%d' % j, inp['od_gn_b'][j])
    put8('v0_0', inp['od_v0'][0])
    vec22 = np.zeros((128, NV22, 22), np.float32)
    for li in range(4):
        for m in range(3):
            vec22[:, V22['fw%d_%d' % (m, li)], :] = fm(inp['ff_conv_w'][li][m], 22)
        vec22[:, V22['fb%d' % li], :] = fm(inp['ff_conv_b'][li], 22)
    sh['vec8'] = vec8
    sh['vec22'] = vec22
    w_in = f(inp['ev_w_in'])
    perm = np.array([(c // 128) * 128 + ((c % 128) + 64) % 128 for c in range(1024)])
    sh['w_in'] = w_in
    sh['w_swap'] = np.ascontiguousarray(w_in[:, :, perm])
    sh['w_out'] = f(inp['ev_w_out'])
    wa = np.asarray(inp['ev_lru_wa'], np.float32).transpose(0, 2, 1, 3)
    wx = np.asarray(inp['ev_lru_wx'], np.float32).transpose(0, 2, 1, 3)
    sh['wax'] = np.ascontiguousarray(np.stack([wa, wx], axis=2))
    sh['w_r'] = f(inp['od_w_r'])
    sh['w_k'] = f(inp['od_w_k'])
    sh['w_v'] = f(inp['od_w_v'])
    sh['w_o'] = f(inp['od_w_o'])
    sh['w1'] = f(inp['od_w1'])
    sh['w2'] = f(inp['od_w2'])
    sh['a1'] = f(inp['od_a1'])
    sh['a2'] = f(inp['od_a2'])
    sh['v1'] = f(inp['od_v1'])
    sh['v2'] = f(inp['od_v2'])
    sh['g1'] = f(inp['od_g1'])
    sh['g2'] = f(inp['od_g2'])
    sh['w_up'] = f(inp['ff_w_up'])
    sh['w_down'] = np.ascontiguousarray(f(inp['ff_w_down']).reshape(4, NFC, 128, 8, 128).transpose(0, 3, 2, 1, 4).reshape(4, 8, 128, NFC * 128))
    return sh


def kernel(**inp):
    if 'nc' not in _CACHE:
        _CACHE['nc'] = build_program()
    nc = _CACHE['nc']
    sh = _prep_shared(inp)
    xp = np.asarray(inp['x_prompt'], np.float32)
    xs = np.asarray(inp['x_sample'], np.float32)
    meta = np.asarray(inp['meta_tokens'], np.float32)
    in_maps = []
    NCORES = DBG['ncores']
    for c in range(NCORES):
        sl = slice(16 * c, 16 * c + 16)
        m = dict(sh)
        xall = np.concatenate([meta, xp[c], xs[sl, 0, :]], axis=0)
        m['xT'] = np.ascontiguousarray(xall.T)
        m['s_ret'] = np.ascontiguousarray(np.asarray(inp['state_ret'], np.float32)[:, sl])
        m['s_lru'] = np.ascontiguousarray(np.asarray(inp['state_lru'], np.float32)[:, sl].reshape(2, 16, 8, 128).transpose(0, 3, 2, 1))
        m['s_lconv'] = np.ascontiguousarray(np.asarray(inp['state_lru_conv'], np.float32)[:, sl].reshape(2, 16, 3, 8, 128).transpose(0, 4, 3, 2, 1))
        m['s_rwkv'] = np.ascontiguousarray(np.asarray(inp['state_rwkv'], np.float32)[:, sl])
        m['s_shift'] = np.ascontiguousarray(np.asarray(inp['state_shift'], np.float32)[:, sl].reshape(2, 16, 8, 128).transpose(0, 3, 2, 1))
        m['s_ffn'] = np.ascontiguousarray(np.asarray(inp['state_ffn_conv'], np.float32)[:, sl].reshape(4, 16, 2, 22, 128).transpose(0, 4, 3, 2, 1))
        in_maps.append(m)
    res = run_bass_kernel_spmd(nc, in_maps, core_ids=list(range(NCORES)))
    R = res.results
    y_p = np.zeros((8, 2048, 1024), np.float32)
    y_s = np.zeros((128, 1, 1024), np.float32)
    ret_p = np.zeros((2, 8, 4, 128, 256), np.float32)
    lru_p = np.zeros((2, 8, 1024), np.float32)
    lconv_p = np.zeros((2, 8, 3, 1024), np.float32)
    rwkv_p = np.zeros((2, 8, 16, 64, 64), np.float32)
    shift_p = np.zeros((2, 8, 1024), np.float32)
    ffn_p = np.zeros((4, 8, 2, 2816), np.float32)
    ret_s = np.zeros((2, 128, 4, 128, 256), np.float32)
    lru_s = np.zeros((2, 128, 1024), np.float32)
    lconv_s = np.zeros((2, 128, 3, 1024), np.float32)
    rwkv_s = np.zeros((2, 128, 16, 64, 64), np.float32)
    shift_s = np.zeros((2, 128, 1024), np.float32)
    ffn_s = np.zeros((4, 128, 2, 2816), np.float32)
    for c in range(NCORES):
        r = R[c]
        sl = slice(16 * c, 16 * c + 16)
        yT = np.asarray(r['yT'])
        y_p[c] = yT[:, 16:2064].T
        y_s[sl, 0, :] = yT[:, 2064:].T
        ret_p[:, c] = np.asarray(r['o_retp'])
        lru_p[:, c] = np.asarray(r['o_lrup']).transpose(1, 2, 0).reshape(2, 1024)
        lconv_p[:, c] = np.asarray(r['o_lconvp']).transpose(1, 3, 2, 0).reshape(2, 3, 1024)
        if DBG.get('chunk', True):
            rwkv_p[:, c] = np.asarray(r['o_rwkvp']).reshape(2, 2, 64, 8, 64).transpose(0, 3, 1, 4, 2).reshape(2, 16, 64, 64)
        else:
            rwkv_p[:, c] = np.asarray(r['o_rwkvp']).reshape(2, 2, 64, 8, 64).transpose(0, 3, 1, 2, 4).reshape(2, 16, 64, 64)
        shift_p[:, c] = np.asarray(r['o_shiftp']).transpose(1, 2, 0).reshape(2, 1024)
        ffn_p[:, c] = np.asarray(r['o_ffnp']).transpose(1, 3, 2, 0).reshape(4, 2, 2816)
        ret_s[:, sl] = np.asarray(r['o_rets'])
        lru_s[:, sl] = np.asarray(r['o_lrus']).transpose(0, 3, 2, 1).reshape(2, 16, 1024)
        lconv_s[:, sl] = np.asarray(r['o_lconvs']).transpose(0, 4, 3, 2, 1).reshape(2, 16, 3, 1024)
        rwkv_s[:, sl] = np.asarray(r['o_rwkvs'])
        shift_s[:, sl] = np.asarray(r['o_shifts']).transpose(0, 3, 2, 1).reshape(2, 16, 1024)
        ffn_s[:, sl] = np.asarray(r['o_ffns']).transpose(0, 4, 3, 2, 1).reshape(4, 16, 2, 2816)
    return (y_p, y_s, ret_p, lru_p, lconv_p, rwkv_p, shift_p, ffn_p, ret_s, lru_s, lconv_s, rwkv_s, shift_s, ffn_s)
```

```python
import math
import numpy as np
from contextlib import ExitStack
import concourse.bass as bass
import concourse.mybir as mybir
from concourse.bass_utils import run_bass_kernel_spmd

F32 = mybir.dt.float32
BF16 = mybir.dt.bfloat16
AF = mybir.ActivationFunctionType
ALU = mybir.AluOpType
AX = mybir.AxisListType

SEM_CAP = 30000
N_DMA_SEMS = 16

D = 1024
TC = 1040
NTOK = 2080
DFF = 2816
NFC = 22
EPS = 1e-6
GN_EPS_C = 64e-5
WDEC = math.exp(-0.5)
DK_SCALE = 128 ** -0.5
GAMMA = [1.0 - 2.0 ** (-5 - h) for h in range(4)]
TOK_TILES = [(0, 512), (512, 512), (1024, 16)]


class Buf:
    def __init__(self, t, name):
        self.t = t
        self.name = name
        self.st = {'_all': [[], []]}

    def __getitem__(self, idx):
        return self.t[idx]

    def states(self, key):
        if key is None:
            return list(self.st.values())
        if key not in self.st:
            a = self.st['_all']
            self.st[key] = [list(a[0]), list(a[1])]
        return [self.st[key]]

    def fence(self):
        w = []
        r = []
        for st in self.st.values():
            w.extend(st[0])
            r.extend(st[1])
        self.st = {'_all': [w, r]}


class Prog:
    def __init__(self, nc, es):
        self.nc = nc
        self.es = es
        self.engs = ['pe', 'act', 'dve', 'pool', 'sp']
        self.q = {e: [] for e in self.engs}
        self.cur_sem = {}
        self.cnt = {}
        self.nsem = 0
        for e in ['pe', 'act', 'dve', 'pool']:
            self._new_sem(e)
        self.waited = {e: {} for e in self.engs}
        self.dma_sems = [es.enter_context(nc.semaphore("dq%d" % i)) for i in range(2 * N_DMA_SEMS)]
        self.dma_cnt = [0] * (2 * N_DMA_SEMS)
        self.dma_n = 0
        self.dma_ne = {'sp': 0, 'pool': 0}
        self.nbuf = 0
        self.n_ops = 0
        self.n_waits = 0
        self.psi = 0

    def _new_sem(self, e):
        s = self.es.enter_context(self.nc.semaphore("tl_%s_%d" % (e, self.nsem)))
        self.nsem += 1
        self.cur_sem[e] = s
        self.cnt[e] = 0

    def sbuf(self, shape, dt, name=None):
        self.nbuf += 1
        name = "%s_%d" % (name or "sb", self.nbuf)
        t = self.es.enter_context(self.nc.sbuf_tensor(name, list(shape), dt))
        return Buf(t, name)

    def psum(self, shape, dt, name=None):
        self.nbuf += 1
        name = "%s_%d" % (name or "ps", self.nbuf)
        t = self.es.enter_context(self.nc.psum_tensor(name, list(shape), dt))
        return Buf(t, name)

    def _deps(self, eng, reads, writes):
        deps = []
        for (b, k) in reads:
            for st in b.states(k):
                deps.extend(st[0])
        for (b, k) in writes:
            for st in b.states(k):
                deps.extend(st[0])
                deps.extend(st[1])
        best = {}
        for (sem, val, seng) in deps:
            if eng == 'pe' and seng == 'pe':
                continue
            kk = id(sem)
            if kk not in best or best[kk][1] < val:
                best[kk] = (sem, val)
        waits = []
        w = self.waited[eng]
        for kk, (sem, val) in best.items():
            if w.get(kk, 0) >= val:
                continue
            w[kk] = val
            waits.append((sem, val))
        return waits

    @staticmethod
    def _compact(lst):
        best = {}
        for t in lst:
            kk = id(t[0])
            if kk not in best or best[kk][1] < t[1]:
                best[kk] = t
        return list(best.values())

    def _record(self, tok, reads, writes):
        for (b, k) in reads:
            for st in b.states(k):
                st[1].append(tok)
                if len(st[1]) > 48:
                    st[1] = self._compact(st[1])
        for (b, k) in writes:
            for st in b.states(k):
                st[0] = [tok]
                st[1] = []

    @staticmethod
    def _norm(lst):
        out = []
        for x in lst:
            if isinstance(x, tuple):
                out.append(x)
            else:
                out.append((x, None))
        return out

    def op(self, eng, fn, reads=(), writes=()):
        reads = self._norm(reads)
        writes = self._norm(writes)
        waits = self._deps(eng, reads, writes)
        if self.cnt[eng] >= SEM_CAP:
            self._new_sem(eng)
        self.cnt[eng] += 1
        tok = (self.cur_sem[eng], self.cnt[eng], eng)
        self.q[eng].append((waits, fn, (tok[0], 1), tok[1]))
        self._record(tok, reads, writes)
        self.n_ops += 1
        self.n_waits += len(waits)
        return tok

    def dma(self, out_ap, in_ap, reads=(), writes=(), eng='sp', **kw):
        reads = self._norm(reads)
        writes = self._norm(writes)
        waits = self._deps(eng, reads, writes)
        i = (self.dma_ne[eng] % N_DMA_SEMS) + (N_DMA_SEMS if eng == 'pool' else 0)
        self.dma_ne[eng] += 1
        self.dma_n += 1
        sem = self.dma_sems[i]
        prev = self.dma_cnt[i]
        w = self.waited[eng]
        if prev > 0 and w.get(id(sem), 0) < prev:
            waits.append((sem, prev))
            w[id(sem)] = prev
        self.dma_cnt[i] += 16
        tok = (sem, self.dma_cnt[i], 'dma')

        def fn(e, out_ap=out_ap, in_ap=in_ap, kw=kw):
            return e.dma_start(out=out_ap, in_=in_ap, **kw)
        self.q[eng].append((waits, fn, (sem, 16), None))
        self._record(tok, reads, writes)
        self.n_ops += 1
        self.n_waits += len(waits)
        return tok

    def finish(self):
        nc = self.nc
        waits = []
        for i, s in enumerate(self.dma_sems):
            if self.dma_cnt[i] > 0:
                waits.append((s, self.dma_cnt[i]))
        for en in ['pe', 'act', 'dve', 'pool']:
            if self.cnt[en] > 0:
                waits.append((self.cur_sem[en], self.cnt[en]))
        self.q['sp'].append((waits, None, None, None))
        qs = self.q
        dma_ids = set(id(x) for x in self.dma_sems)
        need = {}
        for e_ in self.engs:
            for (waits_, fn_, inc_, val_) in qs[e_]:
                for (sem_, v_) in waits_:
                    if id(sem_) not in dma_ids:
                        need.setdefault(id(sem_), set()).add(v_)
        remap = {k: {v: i + 1 for i, v in enumerate(sorted(vs))} for k, vs in need.items()}
        with nc.Block() as block:
            def run(engine, lst):
                for (waits, fn, inc, myval) in lst:
                    for (sem, val) in waits:
                        rm = remap.get(id(sem))
                        engine.wait_ge(sem, rm[val] if rm is not None else val)
                    if fn is not None:
                        ins = fn(engine)
                        if inc is not None:
                            if myval is None:
                                ins.then_inc(inc[0], inc[1])
                            elif myval in remap.get(id(inc[0]), ()):
                                ins.then_inc(inc[0], 1)

            @block.tensor
            def _(t):
                run(t, qs['pe'])

            @block.scalar
            def _(t):
                run(t, qs['act'])

            @block.vector
            def _(t):
                run(t, qs['dve'])

            @block.gpsimd
            def _(t):
                run(t, qs['pool'])

            @block.sync
            def _(t):
                run(t, qs['sp'])


V8_NAMES = []
for _li in range(4):
    V8_NAMES += ['nmix%d' % _li, 'nffn%d' % _li]
V8_NAMES += ['nfinal']
for _j in range(2):
    V8_NAMES += ['gn%d' % _j, 'cw0_%d' % _j, 'cw1_%d' % _j, 'cw2_%d' % _j, 'cw3_%d' % _j, 'cb%d' % _j,
                 'ba%d' % _j, 'bx%d' % _j, 'lam%d' % _j]
for _j in range(2):
    V8_NAMES += ['mix%d_%d' % (_j, m) for m in range(6)]
    V8_NAMES += ['w0_%d' % _j, 'a0_%d' % _j, 'kk%d' % _j, 'ka%d' % _j, 'rk%d' % _j, 'gng%d' % _j, 'gnb%d' % _j]
V8_NAMES += ['v0_0']
V8 = {n: i for i, n in enumerate(V8_NAMES)}
NV8 = len(V8_NAMES)
V22_NAMES = []
for _li in range(4):
    V22_NAMES += ['fw0_%d' % _li, 'fw1_%d' % _li, 'fw2_%d' % _li, 'fb%d' % _li]
V22 = {n: i for i, n in enumerate(V22_NAMES)}
NV22 = len(V22_NAMES)

CM_MEAN1024, CM_MEAN256, CM_BLKMEAN64, CM_IDENT, CM_BLKONES = range(5)
NCM = 5


def fm(v, nch):
    return np.ascontiguousarray(np.asarray(v, np.float32).reshape(nch, 128).T)


def build_consts():
    cm = np.zeros((128, NCM, 128), np.float32)
    cm[:, CM_MEAN1024, :] = 1.0 / 1024
    cm[:, CM_MEAN256, :] = 1.0 / 256
    blk = np.zeros((128, 128), np.float32)
    blk[:64, :64] = 1.0
    blk[64:, 64:] = 1.0
    cm[:, CM_BLKMEAN64, :] = blk / 64.0
    cm[:, CM_IDENT, :] = np.eye(128, dtype=np.float32)
    cm[:, CM_BLKONES, :] = blk
    inv = (10000.0 ** (-np.linspace(0.0, 1.0, 64, dtype=np.float32))).astype(np.float32)
    pos = np.concatenate([np.arange(2064, dtype=np.float32), np.full(16, 16384.0, np.float32)])
    ang = (pos[None, :] * inv[:, None]).astype(np.float32)
    cos = np.cos(ang.astype(np.float64)).astype(np.float32)
    sin = np.sin(ang.astype(np.float64)).astype(np.float32)
    rope = np.zeros((128, 2, NTOK), np.float32)
    rope[:64, 0] = cos
    rope[64:, 0] = cos
    rope[:64, 1] = -sin
    rope[64:, 1] = sin
    idx = np.arange(128)
    diff = idx[None, :] - idx[:, None]
    rmask = np.zeros((128, 4, 128), np.float32)
    decin = np.zeros((128, 4, 128), np.float32)
    deck = np.zeros((128, 2, 4), np.float32)
    for h in range(4):
        lg = math.log1p(-2.0 ** (-5 - h))
        rmask[:, h, :] = np.where(diff >= 0, DK_SCALE * np.exp(lg * np.maximum(diff, 0)), 0.0)
        decin[:, h, :] = np.exp(lg * (idx + 1.0))[None, :]
        deck[:, 0, h] = DK_SCALE * np.exp(lg * (127.0 - idx))
        deck[:16, 1, h] = DK_SCALE * np.exp(lg * (15.0 - idx[:16]))
    i64 = np.arange(64)
    strictT = (i64[:, None] < i64[None, :]).astype(np.float32)
    inclT = (i64[:, None] <= i64[None, :]).astype(np.float32)
    mask4 = np.zeros((128, 4, 64), np.float32)
    mask4[:64, 0] = strictT
    mask4[:64, 1] = inclT
    mask4[:64, 2] = strictT
    mask4[:64, 3] = inclT
    maska = np.zeros((128, 64), np.float32)
    maska[:64] = (i64[:, None] > i64[None, :]).astype(np.float32)
    reset = np.ones((128, 8, 64), np.float32)
    reset[:, :, 0] = 0.0
    sel = np.zeros((128, 16, 128), np.float32)
    for b in range(16):
        sel[b, b, :] = 1.0
    small = np.zeros((128, 8), np.float32)
    small[:, 0] = EPS
    small[:, 1] = GN_EPS_C
    small[:, 2] = 1.0
    small[:, 3] = 0.0
    return dict(cm=cm, rope=rope, rmask=rmask, decin=decin, deck=deck, mask4=mask4, maska=maska,
                reset=reset, sel=sel, small=small)


_CACHE = {}
DBG = {'layers': [0, 1, 2, 3], 'even': True, 'rwkv': True, 'ffn': True, 'segs': [0, 1], 'ncores': 8,
       'evp': ['qk', 'v', 'ga', 'chunks', 'sample', 'wouta', 'lru', 'woutb', 'post', 'supd']}


def build_program():
    nc = bass.Bass("TRN2", target_bir_lowering=False)

    def din(name, shape):
        return nc.dram_tensor(name, list(shape), F32, kind="ExternalInput").ap()

    def dout(name, shape):
        return nc.dram_tensor(name, list(shape), F32, kind="ExternalOutput").ap()

    xT = din("xT", [D, NTOK])
    d_cm = din("cm", [128, NCM, 128])
    d_rope = din("rope", [128, 2, NTOK])
    d_rmask = din("rmask", [128, 4, 128])
    d_decin = din("decin", [128, 4, 128])
    d_deck = din("deck", [128, 2, 4])
    d_mask4 = din("mask4", [128, 4, 64])
    d_maska = din("maska", [128, 64])
    d_reset = din("reset", [128, 8, 64])
    d_sel = din("sel", [128, 16, 128])
    d_small = din("small", [128, 8])
    d_selh = din("selh", [16, 16, 2, 128])
    d_cmask = din("cmask", [16, 768])
    d_vec8 = din("vec8", [128, NV8, 8])
    d_vec22 = din("vec22", [128, NV22, 22])
    w_in = din("w_in", [2, D, 5120])
    w_swap = din("w_swap", [2, D, 1024])
    w_out = din("w_out", [2, 2048, D])
    d_wax = din("wax", [2, 128, 2, 8, 128])
    w_r = din("w_r", [2, D, D])
    w_k = din("w_k", [2, D, D])
    w_v = din("w_v", [2, D, D])
    w_o = din("w_o", [2, D, D])
    d_w1 = din("w1", [2, D, 64])
    d_w2 = din("w2", [2, 64, D])
    d_a1 = din("a1", [2, D, 64])
    d_a2 = din("a2", [2, 64, D])
    d_v1 = din("v1", [1, D, 32])
    d_v2 = din("v2", [1, 32, D])
    d_g1 = din("g1", [2, D, 128])
    d_g2 = din("g2", [2, 128, D])
    w_up = din("w_up", [4, D, 2 * DFF])
    w_down = din("w_down", [4, 8, 128, NFC * 128])
    s_ret = din("s_ret", [2, 16, 4, 128, 256])
    s_lru = din("s_lru", [2, 128, 8, 16])
    s_lconv = din("s_lconv", [2, 128, 8, 3, 16])
    s_rwkv = din("s_rwkv", [2, 16, 16, 64, 64])
    s_shift = din("s_shift", [2, 128, 8, 16])
    s_ffn = din("s_ffn", [4, 128, 22, 2, 16])

    yT = dout("yT", [D, NTOK])
    o_retp = dout("o_retp", [2, 4, 128, 256])
    o_lrup = dout("o_lrup", [128, 2, 8])
    o_lconvp = dout("o_lconvp", [128, 2, 8, 3])
    o_rwkvp = dout("o_rwkvp", [2, 128, 8, 64])
    o_shiftp = dout("o_shiftp", [128, 2, 8])
    o_ffnp = dout("o_ffnp", [128, 4, 22, 2])
    o_rets = dout("o_rets", [2, 16, 4, 128, 256])
    o_lrus = dout("o_lrus", [2, 128, 8, 16])
    o_lconvs = dout("o_lconvs", [2, 128, 8, 3, 16])
    o_rwkvs = dout("o_rwkvs", [2, 16, 16, 64, 64])
    o_shifts = dout("o_shifts", [2, 128, 8, 16])
    o_ffns = dout("o_ffns", [4, 128, 22, 2, 16])

    with ExitStack() as es:
        P = Prog(nc, es)
        X = P.sbuf([128, 8, TC], F32, "X")
        H = P.sbuf([128, 8 * TC], BF16, "H")
        VFD = Buf(None, "vfd")
        vf_dram = nc.dram_tensor("vf_scratch", [128, 8, NTOK], F32, kind="Internal").ap()
        WST0 = P.sbuf([128, 4096], F32, "WST0")
        WST = [WST0, WST0]
        WBF = [P.sbuf([128, 4096], BF16, "WBF%d" % i) for i in range(2)]
        BIG1 = P.sbuf([128, 22 * TC], BF16, "BIG1")
        BIG2 = P.sbuf([128, 9 * 1024], BF16, "BIG2")
        HBUF = P.sbuf([128, 1044], F32, "HBUF")
        SCR = [P.sbuf([128, 512], F32, "SCR%d" % i) for i in range(5)]
        SMP = [HBUF, HBUF]
        PS = [P.psum([128, 512], F32, "PS%d" % i) for i in range(8)]
        CM = P.sbuf([128, NCM, 128], F32, "CM")
        CMB = P.sbuf([128, NCM, 128], BF16, "CMB")
        ROPE = P.sbuf([128, 2, 512], F32, "ROPE")
        RMASK = P.sbuf([128, 4, 128], F32, "RMASK")
        DECIN = P.sbuf([128, 4, 128], F32, "DECIN")
        DECK = P.sbuf([128, 2, 4], F32, "DECK")
        SMALL = P.sbuf([128, 8], F32, "SMALL")
        VEC8 = P.sbuf([128, NV8, 8], F32, "VEC8")
        VEC22 = P.sbuf([128, NV22, 22], F32, "VEC22")
        RS = [P.sbuf([128, 4, 256], F32, "RS%d" % j) for j in range(2)]
        RSB = P.sbuf([128, 4, 256], BF16, "RSB")
        MST = [P.sbuf([128, 8, 64], F32, "MST%d" % j) for j in range(2)]
        CAR = P.sbuf([128, 160], F32, "CAR")
        FFH = P.sbuf([128, 4, 22, 2], F32, "FFH")
        FFO = P.sbuf([128, 4, 22, 2], F32, "FFO")
        SP8 = P.sbuf([128, 2, 8], F32, "SP8")
        QKS = P.sbuf([128, 8, 16], F32, "QKS")
        STS = P.sbuf([128, 22 * 2 * 16], F32, "STS")
        LCS = P.sbuf([128, 8, 3, 16], F32, "LCS")
        WAX = P.sbuf([128, 2, 8, 128], BF16, "WAX")
        SELH = P.sbuf([16, 16, 2, 128], BF16, "SELH")

        LHC = CAR.t[:, 0:16].rearrange("p (j g) -> p j g", g=8)
        LRO = CAR.t[:, 16:32].rearrange("p (j g) -> p j g", g=8)
        LCH = CAR.t[:, 32:80].rearrange("p (j g t) -> p j g t", g=8, t=3)
        LCO = CAR.t[:, 80:128].rearrange("p (j g t) -> p j g t", g=8, t=3)
        SHC = CAR.t[:, 128:144].rearrange("p (j g) -> p j g", g=8)
        SHO = CAR.t[:, 144:160].rearrange("p (j g) -> p j g", g=8)
        EPSC = SMALL.t[:, 0:1]
        GNEPS = SMALL.t[:, 1:2]

        state = {'ps': 0, 'ws': 0, 'scr': 0, 'ev': 0, 'reserved': set()}

        def ps():
            while True:
                i = state['ps'] % 8
                state['ps'] += 1
                if i not in state['reserved']:
                    return PS[i]

        def scr():
            i = state['scr'] % 5
            state['scr'] += 1
            return SCR[i]

        def mm(out_ap, lhsT, rhs, start, stop, reads, writes):
            P.op('pe', lambda e: e.matmul(out_ap, lhsT=lhsT, rhs=rhs, start=start, stop=stop), reads, writes)

        def cp(eng, out_ap, in_ap, reads, writes):
            if eng == 'act':
                P.op('act', lambda e: e.copy(out=out_ap, in_=in_ap), reads, writes)
            else:
                P.op(eng, lambda e: e.tensor_copy(out=out_ap, in_=in_ap), reads, writes)

        def evcp(out_ap, in_ap, reads, writes):
            state['ev'] += 1
            cp('act' if state['ev'] % 2 else 'dve', out_ap, in_ap, reads, writes)

        def tt(eng, out_ap, a, b, op, reads, writes):
            P.op(eng, lambda e: e.tensor_tensor(out=out_ap, in0=a, in1=b, op=op), reads, writes)

        def ts(eng, out_ap, a, s1, s2, op0, op1, reads, writes):
            if s2 is None:
                P.op(eng, lambda e: e.tensor_scalar(out=out_ap, in0=a, scalar1=s1, scalar2=None, op0=op0), reads, writes)
            else:
                P.op(eng, lambda e: e.tensor_scalar(out=out_ap, in0=a, scalar1=s1, scalar2=s2, op0=op0, op1=op1), reads, writes)

        def stt(out_ap, a, s, b, op0, op1, reads, writes):
            P.op('dve', lambda e: e.scalar_tensor_tensor(out=out_ap, in0=a, scalar=s, in1=b, op0=op0, op1=op1), reads, writes)

        def act(out_ap, in_ap, func, reads, writes, bias=None, scale=None):
            kw = {}
            if bias is not None:
                kw['bias'] = bias
            if scale is not None:
                kw['scale'] = scale
            P.op('act', lambda e: e.activation(out=out_ap, in_=in_ap, func=func, **kw), reads, writes)

        def v8(name):
            return VEC8.t[:, V8[name], :]

        def v8c(name, k):
            return VEC8.t[:, V8[name], k:k + 1]

        def v22c(name, k):
            return VEC22.t[:, V22[name], k:k + 1]

        def hview(k, c0, n):
            return H.t[:, k * TC + c0:k * TC + c0 + n]

        def b1(s, c0, n):
            return BIG1.t[:, s * TC + c0:s * TC + c0 + n]

        def b1f(r, c0, n):
            return BIG1.t[:, r * 2 * TC:(r + 1) * 2 * TC].bitcast(F32)[:, c0:c0 + n]

        def b2(s, c0, n):
            return BIG2.t[:, s * TC + c0:s * TC + c0 + n]

        def wload(parts, KC, NC):
            s = state['ws'] % 2
            state['ws'] += 1
            st, bf = WST[s], WBF[s]
            dstv = st.t[:, 0:KC * NC].rearrange("p (k n) -> p k n", n=NC)
            for (src, off, ncols) in parts:
                P.dma(dstv[:, :, off:off + ncols], src.rearrange("(k p) n -> p k n", p=128), writes=[st])
            P.op('pool', lambda e: e.tensor_copy(out=bf.t[:, 0:KC * NC], in_=st.t[:, 0:KC * NC]), reads=[st], writes=[bf])
            return bf, bf.t[:, 0:KC * NC].rearrange("p (k n) -> p k n", n=NC)

        def wload_to(dst_buf, dst_view, src, KC, NC, dkey=None):
            s = state['ws'] % 2
            state['ws'] += 1
            st = WST[s]
            dstv = st.t[:, 0:KC * NC].rearrange("p (k n) -> p k n", n=NC)
            P.dma(dstv, src.rearrange("(k p) n -> p k n", p=128), writes=[st])
            P.op('pool', lambda e: e.tensor_copy(out=dst_view, in_=dstv), reads=[st], writes=[(dst_buf, dkey)])

        P.dma(CM.t[:], d_cm, writes=[CM])
        cp('dve', CMB.t[:], CM.t[:], [CM], [CMB])
        for (b_, d_) in [(RMASK, d_rmask), (DECIN, d_decin), (DECK, d_deck), (SMALL, d_small), (VEC8, d_vec8), (VEC22, d_vec22)]:
            P.dma(b_.t[:], d_, writes=[b_])
        P.dma(WST[0].t[0:16, 0:4096].rearrange('p (a b c) -> p a b c', a=16, b=2), d_selh, writes=[WST[0]])
        cp('dve', SELH.t[:], WST[0].t[0:16, 0:4096].rearrange('p (a b c) -> p a b c', a=16, b=2), [WST[0]], [SELH])
        P.op('pool', lambda e: e.memset(CAR.t[:], 0.0), writes=[CAR])
        P.op('pool', lambda e: e.memset(FFH.t[:], 0.0), writes=[FFH])
        P.op('pool', lambda e: e.memset(FFO.t[:], 0.0), writes=[FFO])
        P.op('pool', lambda e: e.memset(STS.t[:], 0.0), writes=[STS])
        P.op('pool', lambda e: e.memset(LCS.t[:], 0.0), writes=[LCS])
        for j in range(2):
            P.op('pool', lambda e, j=j: e.memset(RS[j].t[:], 0.0), writes=[RS[j]])
            P.op('pool', lambda e, j=j: e.memset(MST[j].t[:], 0.0), writes=[MST[j]])
        for j in range(2):
            act(SP8.t[:, j, :], v8('lam%d' % j), AF.Exp, [VEC8], [SP8], scale=-1.0)
            act(SP8.t[:, j, :], SP8.t[:, j, :], AF.Ln, [SP8], [SP8], bias=1.0)
            ts('dve', SP8.t[:, j, :], SP8.t[:, j, :], -8.0, None, ALU.mult, None, [SP8], [SP8])

        def rmsnorm(gname, out_fn, out_buf, c_tiles=TOK_TILES):
            for (c0, n) in c_tiles:
                sq = WBF[state['ws'] % 2]
                sqv = sq.t[:, 0:8 * n].rearrange("p (k n) -> p k n", n=n)
                act(sqv, X.t[:, :, c0:c0 + n], AF.Square, [X], [sq])
                pb = ps()
                for k in range(8):
                    mm(pb.t[:, 0:n], CMB.t[:, CM_MEAN1024, :], sqv[:, k, :], k == 0, k == 7, [CMB, sq], [pb])
                rs = scr()
                act(rs.t[:, 0:n], pb.t[:, 0:n], AF.Sqrt, [pb, SMALL], [rs], bias=EPSC)
                P.op('dve', lambda e, rs=rs, n=n: e.reciprocal(out=rs.t[:, 0:n], in_=rs.t[:, 0:n]), [rs], [rs])
                for k in range(8):
                    stt(out_fn(k, c0, n), X.t[:, k, c0:c0 + n], v8c(gname, k), rs.t[:, 0:n], ALU.mult, ALU.mult,
                        [X, VEC8, rs], [out_buf])

        def proj_fm(wbuf, wview, KC, col0, src_fn, src_reads, evac):
            for (c0, n) in TOK_TILES:
                pb = ps()
                for k in range(KC):
                    mm(pb.t[:, 0:n], wview[:, k, col0:col0 + 128], src_fn(k, c0, n), k == 0, k == KC - 1,
                       [wbuf] + src_reads, [pb])
                evac(pb, c0, n)

        def x_accum(dc):
            def ev(pb, c0, n):
                tt('dve', X.t[:, dc, c0:c0 + n], X.t[:, dc, c0:c0 + n], pb.t[:, 0:n], ALU.add, [X, pb], [X])
            return ev

        def ffn(seg, li):
            rmsnorm('nffn%d' % li, hview, H)
            BIG1.fence()
            if seg == 1:
                P.dma(STS.t[:, 0:22 * 32].rearrange("p (f t b) -> p f t b", t=2, b=16), s_ffn[li], writes=[STS])
            stsv = STS.t[:, 0:22 * 32].rearrange("p (f t b) -> p f t b", t=2, b=16)
            NP = TC if seg == 0 else 1024
            for f2 in range(11):
                bf, wview = wload([(w_up[li][:, f2 * 256:(f2 + 1) * 256], 0, 256), (w_up[li][:, DFF + f2 * 256:DFF + (f2 + 1) * 256], 256, 256)], 8, 512)
                for fi in range(2):
                    fc = f2 * 2 + fi
                    UG = HBUF
                    cp('pool', UG.t[:, 0:2], FFH.t[:, li, fc, :], [FFH], [UG])
                    pbv = []
                    for (c0, n) in TOK_TILES:
                        pg = ps()
                        for k in range(8):
                            mm(pg.t[:, 0:n], wview[:, k, fi * 128:fi * 128 + 128], hview(k, c0, n), k == 0, k == 7, [bf, H], [pg])
                        cp('act', UG.t[:, 2 + c0:2 + c0 + n], pg.t[:, 0:n], [pg], [UG])
                        pv = ps()
                        for k in range(8):
                            mm(pv.t[:, 0:n], wview[:, k, 256 + fi * 128:256 + fi * 128 + 128], hview(k, c0, n), k == 0, k == 7, [bf, H], [pv])
                        pbv.append(pv)
                    if seg == 0:
                        cp('pool', FFH.t[:, li, fc, :], UG.t[:, 2 + TC - 2:2 + TC], [UG], [FFH])
                    else:
                        cp('pool', FFO.t[:, li, fc, :], UG.t[:, 2 + 1022:2 + 1024], [UG], [FFO])
                    T1 = (BIG2, None)
                    par = fc % 2
                    t1 = BIG2.t[:, par * 2 * TC:(par + 1) * 2 * TC].bitcast(F32)
                    kT1 = (BIG2, ('t1', par))
                    ts('dve', t1[:, 0:TC], UG.t[:, 0:TC], v22c('fw0_%d' % li, fc), v22c('fb%d' % li, fc), ALU.mult, ALU.add,
                       [UG, VEC22], [kT1])
                    stt(t1[:, 0:TC], UG.t[:, 1:TC + 1], v22c('fw1_%d' % li, fc), t1[:, 0:TC], ALU.mult, ALU.add, [UG, VEC22, kT1], [kT1])
                    stt(t1[:, 0:TC], UG.t[:, 2:TC + 2], v22c('fw2_%d' % li, fc), t1[:, 0:TC], ALU.mult, ALU.add, [UG, VEC22, kT1], [kT1])
                    if seg == 1:
                        sc = slice(1024, 1040)
                        ts('dve', t1[:, sc], stsv[:, fc, 0, :], v22c('fw0_%d' % li, fc), v22c('fb%d' % li, fc), ALU.mult, ALU.add,
                           [STS, VEC22], [kT1])
                        stt(t1[:, sc], stsv[:, fc, 1, :], v22c('fw1_%d' % li, fc), t1[:, sc], ALU.mult, ALU.add, [STS, VEC22, kT1], [kT1])
                        stt(t1[:, sc], UG.t[:, 2 + 1024:2 + 1040], v22c('fw2_%d' % li, fc), t1[:, sc], ALU.mult, ALU.add, [UG, VEC22, kT1], [kT1])
                        P.dma(o_ffns[li, :, fc, 0, :], stsv[:, fc, 1, :], reads=[STS], eng='pool')
                        P.dma(o_ffns[li, :, fc, 1, :], UG.t[:, 2 + 1024:2 + 1040], reads=[UG], eng='pool')
                    act(t1[:, 0:TC], t1[:, 0:TC], AF.Gelu_apprx_tanh, [kT1], [kT1])
                    for ti, (c0, n) in enumerate(TOK_TILES):
                        tt('dve', b1(fc, c0, n), t1[:, c0:c0 + n], pbv[ti].t[:, 0:n], ALU.mult, [kT1, pbv[ti]], [(BIG1, fc)])
            BIG1.fence()
            for dc in range(8):
                s = state['ws'] % 2
                state['ws'] += 1
                st, bf = WST[s], WBF[s]
                dstv = st.t[:, 0:22 * 128].rearrange("p (k n) -> p k n", n=128)
                P.dma(st.t[:, 0:22 * 128], w_down[li, dc], writes=[st])
                P.op('pool', lambda e, bf=bf, st=st: e.tensor_copy(out=bf.t[:, 0:2816], in_=st.t[:, 0:2816]), reads=[st], writes=[bf])
                wview = bf.t[:, 0:2816].rearrange("p (k n) -> p k n", n=128)
                proj_fm(bf, wview, 22, 0, lambda k, c0, n: b1(k, c0, n), [BIG1], x_accum(dc))

        ctx = dict(locals())
        build_even(ctx)
        build_rwkv(ctx)
        even_layer = ctx['even_layer']
        rwkv_layer = ctx['rwkv_layer']

        for seg in DBG['segs']:
            P.dma(X.t[:], xT[:, seg * TC:(seg + 1) * TC].rearrange("(k p) t -> p k t", p=128), writes=[X])
            for li in DBG['layers']:
                if li % 2 == 0:
                    if DBG['even']:
                        even_layer(seg, li)
                else:
                    if DBG['rwkv']:
                        rwkv_layer(seg, li)
                if DBG['ffn']:
                    ffn(seg, li)
            for (c0, n) in TOK_TILES:
                s = state['ws'] % 2
                state['ws'] += 1
                st = WST[s]
                ov = st.t[:, 0:8 * n].rearrange("p (k n) -> p k n", n=n)
                rmsnorm('nfinal', lambda k, c0_, n_, ov=ov: ov[:, k, :], st, c_tiles=[(c0, n)])
                P.dma(yT[:, seg * TC + c0:seg * TC + c0 + n].rearrange("(k p) t -> p k t", p=128), ov, reads=[st], eng='pool')
        for j in range(2):
            P.dma(o_retp[j].rearrange("h d e -> d h e"), RS[j].t[:], reads=[RS[j]], eng='pool')
            P.dma(o_rwkvp[j], MST[j].t[:, :, :], reads=[MST[j]], eng='pool')
        P.dma(o_lrup, LRO, reads=[CAR], eng='pool')
        P.dma(o_lconvp, LCO, reads=[CAR], eng='pool')
        P.dma(o_shiftp, SHO, reads=[CAR], eng='pool')
        P.dma(o_ffnp, FFO.t[:], reads=[FFO], eng='pool')
        P.finish()
    return nc


def build_even(ctx):
    globals().update(ctx)

    def seg_chunks(seg):
        if seg == 0:
            return [(0, 16)] + [(16 + 128 * i, 128) for i in range(8)]
        return [(128 * i, 128) for i in range(8)]

    def vt(ci, n, c0, w):
        return BIG2.t[0:n, ci * 1024 + c0:ci * 1024 + c0 + w]

    def post(src3, src_reads, n, h, c0):
        OFb, OBb, OSb = scr(), scr(), scr()
        OF = OFb.t[:, 0:2 * n].rearrange("p (e c) -> p e c", c=n)
        OB = OBb.t[:, 0:256].bitcast(BF16)[:, 0:2 * n].rearrange("p (e c) -> p e c", c=n)
        OS = OSb.t[:, 0:256].bitcast(BF16)[:, 0:2 * n].rearrange("p (e c) -> p e c", c=n)
        cp('act', OF, src3, src_reads, [OFb])
        act(OS, src3, AF.Square, src_reads, [OSb])
        cp('pool', OB, OF, [OFb], [OBb])
        pst = ps()
        for ec in range(2):
            mm(pst.t[:, 0:n], CMB.t[:, CM_MEAN256, :], OB[:, ec, :], ec == 0, ec == 1, [CMB, OBb], [pst])
        for ec in range(2):
            mm(pst.t[:, 128:128 + n], CMB.t[:, CM_MEAN256, :], OS[:, ec, :], ec == 0, ec == 1, [CMB, OSb], [pst])
        MEb, VAb = scr(), scr()
        ME = MEb.t[:, 0:n]
        VA = VAb.t[:, 0:n]
        cp('act', ME, pst.t[:, 0:n], [pst], [MEb])
        tt('dve', VA, ME, ME, ALU.mult, [MEb], [VAb])
        tt('dve', VA, pst.t[:, 128:128 + n], VA, ALU.subtract, [pst, VAb], [VAb])
        ts('dve', VA, VA, 0.0, None, ALU.max, None, [VAb], [VAb])
        act(VA, VA, AF.Sqrt, [VAb, SMALL], [VAb], bias=EPSC)
        P.op('dve', lambda e: e.reciprocal(out=VA, in_=VA), [VAb], [VAb])
        for ec in range(2):
            tt('dve', OF[:, ec, :], OF[:, ec, :], ME, ALU.subtract, [OFb, MEb], [OFb])
            stt(OF[:, ec, :], OF[:, ec, :], VEC8.t[:, V8['gn%d' % cur['j']], 2 * h + ec:2 * h + ec + 1], VA, ALU.mult, ALU.mult,
                [OFb, VEC8, VAb], [OFb])
            tt('pool', b1(8 + 2 * h + ec, c0, n), OF[:, ec, :], b1(8 + 2 * h + ec, c0, n), ALU.mult,
               [OFb, (BIG1, 8 + 2 * h + ec)], [(BIG1, 8 + 2 * h + ec)])

    cur = {'j': 0}

    def even_layer(seg, li):
        j = li // 2
        cur['j'] = j
        rmsnorm('nmix%d' % li, hview, H)
        BIG1.fence()
        BIG2.fence()
        s = state['ws'] % 2
        state['ws'] += 1
        P.dma(WST[s].t[:, 0:2048].rearrange("p (a g d) -> p a g d", a=2, g=8), d_wax[j], writes=[WST[s]])
        cp('pool', WAX.t[:], WST[s].t[:, 0:2048].rearrange("p (a g d) -> p a g d", a=2, g=8), [WST[s]], [WAX])
        EVP = DBG['evp']
        ropes = {0: (ROPE, ROPE.t[:, :, 0:512]),
                 512: (HBUF, HBUF.t[:, 0:1024].rearrange("p (a n) -> p a n", a=2)),
                 1024: (LCS, LCS.t[:, 0:2, 0, :])}
        for c0_, (rb_, rv_) in ropes.items():
            n_ = 16 if c0_ == 1024 else 512
            P.dma(rv_, d_rope[:, :, seg * TC + c0_:seg * TC + c0_ + n_], writes=[rb_])
        for t in range(4 if 'qk' in EVP else 0):
            bf, wv = wload([(w_in[j][:, t * 256:(t + 1) * 256], 0, 256), (w_swap[j][:, t * 256:(t + 1) * 256], 256, 256)], 8, 512)
            for hi in range(2):
                hc = 2 * t + hi
                for (c0, n) in TOK_TILES:
                    po_, ps_ = ps(), ps()
                    for k in range(8):
                        mm(po_.t[:, 0:n], wv[:, k, hi * 128:hi * 128 + 128], hview(k, c0, n), k == 0, k == 7, [bf, H], [po_])
                    for k in range(8):
                        mm(ps_.t[:, 0:n], wv[:, k, 256 + hi * 128:256 + hi * 128 + 128], hview(k, c0, n), k == 0, k == 7, [bf, H], [ps_])
                    T1, T2 = scr(), scr()
                    rb, rv = ropes[c0]
                    tt('dve', T1.t[:, 0:n], po_.t[:, 0:n], rv[:, 0, :], ALU.mult, [po_, rb], [T1])
                    tt('dve', T2.t[:, 0:n], ps_.t[:, 0:n], rv[:, 1, :], ALU.mult, [ps_, rb], [T2])
                    tt('pool', b1(hc, c0, n), T1.t[:, 0:n], T2.t[:, 0:n], ALU.add, [T1, T2], [(BIG1, hc)])
                    if seg == 1 and c0 == 1024:
                        tt('pool', QKS.t[:, hc, :], T1.t[:, 0:n], T2.t[:, 0:n], ALU.add, [T1, T2], [QKS])
        chunks = seg_chunks(seg)
        allch = chunks + ([(1024, 16)] if seg == 1 else [])
        for wt in range(2 if 'v' in EVP else 0):
            bf, wv = wload([(w_in[j][:, 1024 + wt * 512:1024 + (wt + 1) * 512], 0, 512)], 8, 512)
            for ci, (c0, n) in enumerate(allch):
                pb = ps()
                for k in range(8):
                    mm(pb.t[0:n, 0:512], hview(k, c0, n), wv[:, k, 0:512], k == 0, k == 7, [bf, H], [pb])
                vm = DBG.get('vmode', '')
                if 'dveonly' in vm:
                    cp('dve', vt(ci, n, wt * 512, 512), pb.t[0:n, 0:512], [pb], [(BIG2, ci)])
                elif 'actonly' in vm:
                    cp('act', vt(ci, n, wt * 512, 512), pb.t[0:n, 0:512], [pb], [(BIG2, ci)])
                elif 'noevac' in vm:
                    pass
                else:
                    evcp(vt(ci, n, wt * 512, 512), pb.t[0:n, 0:512], [pb], [(BIG2, ci)])
        for wt in range(2 if 'ga' in EVP else 0):
            bf, wv = wload([(w_in[j][:, 2048 + wt * 512:2048 + (wt + 1) * 512], 0, 512)], 8, 512)
            for q in range(4):
                oc = wt * 4 + q

                def ev(pb, c0, n, oc=oc):
                    act(b1(8 + oc, c0, n), pb.t[:, 0:n], AF.Silu, [pb], [(BIG1, 8 + oc)])
                proj_fm(bf, wv, 8, q * 128, hview, [H], ev)
        cp('act', RSB.t[:], RS[j].t[:], [RS[j]], [RSB])
        for ci, (c0, n) in enumerate(chunks if 'chunks' in EVP else []):
            di = 0 if n == 128 else 1
            for h in range(4):
                pb = ps()
                mm(pb.t[0:n, 0:n], b1(4 + h, c0, n), b1(h, c0, n), True, True, [(BIG1, 4 + h), (BIG1, h)], [pb])
                SCb, QDb, KDb = scr(), scr(), scr()
                SC = SCb.t[:, 0:64].bitcast(BF16)
                QD = QDb.t[:, 0:64].bitcast(BF16)
                KD = KDb.t[:, 0:64].bitcast(BF16)
                tt('dve', SC[0:n, 0:n], pb.t[0:n, 0:n], RMASK.t[0:n, h, 0:n], ALU.mult, [pb, RMASK], [SCb])
                tt('pool', QD[:, 0:n], b1(h, c0, n), DECIN.t[:, h, 0:n], ALU.mult, [(BIG1, h), DECIN], [QDb])
                po = ps()
                for ec in range(2):
                    mm(po.t[:, ec * 128:ec * 128 + n], vt(ci, n, h * 256 + ec * 128, 128), SC[0:n, 0:n], True, False,
                       [(BIG2, ci), SCb], [po])
                    mm(po.t[:, ec * 128:ec * 128 + n], RSB.t[:, h, ec * 128:ec * 128 + 128], QD[:, 0:n], False, True,
                       [RSB, QDb], [po])
                src3 = po.t[:, 0:256].rearrange("p (e c) -> p e c", c=128)[:, :, 0:n]
                if 'post' in EVP:
                    post(src3, [po], n, h, c0)
                if 'supd' not in EVP:
                    continue
                pt = ps()
                ptb = pt.t[:, :].bitcast(BF16)
                P.op('pe', lambda e, ptb=ptb, n=n, h=h, c0=c0: e.transpose(ptb[0:n, 0:128], b1(4 + h, c0, n), CMB.t[:, CM_IDENT, :]),
                     [(BIG1, 4 + h), CMB], [pt])
                ts('dve', KD[0:n, 0:128], ptb[0:n, 0:128], DECK.t[0:n, di, h:h + 1], None, ALU.mult, None, [pt, DECK], [KDb])
                pu = ps()
                mm(pu.t[:, 0:256], KD[0:n, 0:128], vt(ci, n, h * 256, 256), True, True, [KDb, (BIG2, ci)], [pu])
                stt(RS[j].t[:, h, :], RS[j].t[:, h, :], float(GAMMA[h] ** n), pu.t[:, 0:256], ALU.mult, ALU.add, [RS[j], pu], [RS[j]])
                cp('act', RSB.t[:, h, :], RS[j].t[:, h, :], [RS[j]], [RSB])
        if seg == 1 and 'sample' in EVP:
            c0 = 1024
            KFb = scr()
            KF = KFb.t[:, 0:64].rearrange("p (h b) -> p h b", b=16)
            ts('dve', KF, QKS.t[:, 4:8, :], DK_SCALE, None, ALU.mult, None, [QKS], [KFb])
            POS = ps()
            state['reserved'].add(PS.index(POS))
            for b in range(16):
                RSSb = SMP[b % 2]
                RSS = RSSb.t[:, 0:1024].rearrange("p (h e) -> p h e", e=256)
                P.dma(RSS, s_ret[j, b].rearrange("h d e -> d h e"), writes=[RSSb])
                pbv = [ps(), ps()]
                for half in range(2):
                    for hh in range(2):
                        mm(pbv[half].t[:, 0:512], SELH.t[0:16, b, hh, :], BIG2.t[0:16, 8 * 1024 + half * 512:8 * 1024 + (half + 1) * 512], hh == 0, hh == 1,
                           [SELH, (BIG2, 8)], [pbv[half]])
                for h in range(4):
                    ts('dve', RSS[:, h, :], RSS[:, h, :], float(GAMMA[h]), None, ALU.mult, None, [RSSb], [RSSb])
                    stt(RSS[:, h, :], pbv[h // 2].t[:, (h % 2) * 256:(h % 2) * 256 + 256], KF[:, h, b:b + 1], RSS[:, h, :], ALU.mult, ALU.add,
                        [pbv[h // 2], KFb, RSSb], [RSSb])
                P.dma(o_rets[j, b].rearrange("h d e -> d h e"), RSS, reads=[RSSb], eng='pool')
                for h in range(4):
                    for ec in range(2):
                        col = (h * 2 + ec) * 16 + b
                        mm(POS.t[:, col:col + 1], RSS[:, h, ec * 128:(ec + 1) * 128], QKS.t[:, h, b:b + 1], True, True, [RSSb, QKS], [POS])
            for h in range(4):
                src3 = POS.t[:, h * 32:h * 32 + 32].rearrange("p (e c) -> p e c", c=16)
                post(src3, [POS], 16, h, c0)
            state['reserved'].discard(PS.index(POS))
        for wt in range(2 if 'wouta' in EVP else 0):
            bf, wv = wload([(w_out[j][0:1024, wt * 512:(wt + 1) * 512], 0, 512)], 8, 512)
            for q in range(4):
                proj_fm(bf, wv, 8, q * 128, lambda k, c0, n: b1(8 + k, c0, n), [BIG1], x_accum(wt * 4 + q))
        BIG1.fence()
        BIG2.fence()
        NP = TC if seg == 0 else 1024
        if seg == 1:
            P.dma(STS.t[:, 0:128].rearrange("p (g b) -> p g b", b=16), s_lru[j], writes=[STS])
            P.dma(STS.t[:, 128:512].rearrange("p (g t b) -> p g t b", t=3, b=16), s_lconv[j], writes=[STS])
        SLR = STS.t[:, 0:128].rearrange("p (g b) -> p g b", b=16)
        SLC = STS.t[:, 128:512].rearrange("p (g t b) -> p g t b", t=3, b=16)
        XB = HBUF
        XC, RR, II, AA, TT_, HS, GB = [(lambda c0, n, r=r: b1f(r, c0, n)) for r in range(7)]
        keys = [(BIG1, ('f', r)) for r in range(7)]
        kXC, kRR, kII, kAA, kTT, kHS, kGB = keys
        kXCB = (BIG1, ('f', 7))

        def XCB(c0, n):
            return BIG1.t[:, 7 * 2 * TC + c0:7 * 2 * TC + c0 + n]
        for gp in range(4 if 'lru' in EVP else 0):
            bf, wv = wload([(w_in[j][:, 3072 + gp * 256:3072 + (gp + 1) * 256], 0, 256),
                            (w_in[j][:, 4096 + gp * 256:4096 + (gp + 1) * 256], 256, 256)], 8, 512)
            for gi in range(2):
                g = gp * 2 + gi
                cw = [VEC8.t[:, V8['cw%d_%d' % (m, j)], g:g + 1] for m in range(4)]
                cb = VEC8.t[:, V8['cb%d' % j], g:g + 1]
                cp('pool', XB.t[:, 0:3], LCH[:, j, g, :], [CAR], [XB])

                def evx(pb, c0, n):
                    cp('act', XB.t[:, 3 + c0:3 + c0 + n], pb.t[:, 0:n], [pb], [XB])
                proj_fm(bf, wv, 8, gi * 128, hview, [H], evx)

                def evg(pb, c0, n):
                    act(GB(c0, n), pb.t[:, 0:n], AF.Gelu_apprx_tanh, [pb], [kGB])
                proj_fm(bf, wv, 8, 256 + gi * 128, hview, [H], evg)
                if seg == 0:
                    cp('pool', LCH[:, j, g, :], XB.t[:, 3 + TC - 3:3 + TC], [XB], [CAR])
                else:
                    cp('pool', LCO[:, j, g, :], XB.t[:, 3 + 1021:3 + 1024], [XB], [CAR])
                ts('dve', XC(0, TC), XB.t[:, 0:TC], cw[0], cb, ALU.mult, ALU.add, [XB, VEC8], [kXC])
                for m in range(1, 4):
                    stt(XC(0, TC), XB.t[:, m:m + TC], cw[m], XC(0, TC), ALU.mult, ALU.add, [XB, VEC8, kXC], [kXC])
                if seg == 1:
                    ts('dve', XC(1024, 16), SLC[:, g, 0, :], cw[0], cb, ALU.mult, ALU.add, [STS, VEC8], [kXC])
                    stt(XC(1024, 16), SLC[:, g, 1, :], cw[1], XC(1024, 16), ALU.mult, ALU.add, [STS, VEC8, kXC], [kXC])
                    stt(XC(1024, 16), SLC[:, g, 2, :], cw[2], XC(1024, 16), ALU.mult, ALU.add, [STS, VEC8, kXC], [kXC])
                    stt(XC(1024, 16), XB.t[:, 3 + 1024:3 + 1040], cw[3], XC(1024, 16), ALU.mult, ALU.add, [XB, VEC8, kXC], [kXC])
                    cp('pool', LCS.t[:, g, 0, :], SLC[:, g, 1, :], [STS], [LCS])
                    cp('pool', LCS.t[:, g, 1, :], SLC[:, g, 2, :], [STS], [LCS])
                    cp('pool', LCS.t[:, g, 2, :], XB.t[:, 3 + 1024:3 + 1040], [XB], [LCS])
                cp('pool', XCB(0, TC), XC(0, TC), [kXC], [kXCB])
                for which, dst, kd, bname in ((0, RR, kRR, 'ba%d' % j), (1, II, kII, 'bx%d' % j)):
                    for (c0, n) in TOK_TILES:
                        pb = ps()
                        mm(pb.t[:, 0:n], WAX.t[:, which, g, :], XCB(c0, n), True, True, [WAX, kXCB], [pb])
                        act(dst(c0, n), pb.t[:, 0:n], AF.Sigmoid, [pb, VEC8], [kd], bias=VEC8.t[:, V8[bname], g:g + 1])
                act(AA(0, TC), RR(0, TC), AF.Exp, [kRR, SP8], [kAA], scale=SP8.t[:, j, g:g + 1])
                tt('pool', TT_(0, TC), AA(0, TC), AA(0, TC), ALU.mult, [kAA], [kTT])
                ts('pool', TT_(0, TC), TT_(0, TC), -1.0, 1.0, ALU.mult, ALU.add, [kTT], [kTT])
                ts('pool', TT_(0, TC), TT_(0, TC), 0.0, None, ALU.max, None, [kTT], [kTT])
                act(TT_(0, TC), TT_(0, TC), AF.Sqrt, [kTT], [kTT])
                tt('pool', II(0, TC), II(0, TC), XC(0, TC), ALU.mult, [kII, kXC], [kII])
                tt('dve', II(0, TC), II(0, TC), TT_(0, TC), ALU.mult, [kII, kTT], [kII])
                init = 0.0 if seg == 0 else LHC[:, j, g:g + 1]
                P.op('dve', lambda e, init=init: e.tensor_tensor_scan(out=HS(0, NP), data0=AA(0, NP), data1=II(0, NP), initial=init,
                                                                      op0=ALU.mult, op1=ALU.add), [kAA, kII, CAR], [kHS])
                if seg == 0:
                    cp('pool', LHC[:, j, g:g + 1], HS(NP - 1, 1), [kHS], [CAR])
                else:
                    cp('pool', LRO[:, j, g:g + 1], HS(1023, 1), [kHS], [CAR])
                    tt('pool', HS(1024, 16), AA(1024, 16), SLR[:, g, :], ALU.mult, [kAA, STS], [kHS])
                    tt('pool', HS(1024, 16), HS(1024, 16), II(1024, 16), ALU.add, [kHS, kII], [kHS])
                    cp('pool', LCS.t[:, 0, 0, 0:1] if False else STS.t[:, 512 + g * 16:512 + g * 16 + 16], HS(1024, 16), [kHS], [STS])
                tt('dve', b2(g, 0, TC), HS(0, TC), GB(0, TC), ALU.mult, [kHS, kGB], [(BIG2, ('y', g))])
        if seg == 1:
            P.dma(o_lrus[j], STS.t[:, 512:640].rearrange("p (g b) -> p g b", b=16), reads=[STS], eng='pool')
            P.dma(o_lconvs[j], LCS.t[:], reads=[LCS], eng='pool')
        BIG2.fence()
        for wt in range(2 if 'woutb' in EVP else 0):
            bf, wv = wload([(w_out[j][1024:2048, wt * 512:(wt + 1) * 512], 0, 512)], 8, 512)
            for q in range(4):
                proj_fm(bf, wv, 8, q * 128, lambda k, c0, n: b2(k, c0, n), [BIG2], x_accum(wt * 4 + q))
        BIG1.fence()
        BIG2.fence()

    ctx['even_layer'] = even_layer


def build_rwkv(ctx):
    globals().update(ctx)
    W0 = WST[0]
    B0 = WBF[0]
    B1_ = WBF[1]

    def Q(idx):
        return W0.t[:, idx * 136:idx * 136 + 128].rearrange("p (k n) -> p k n", n=16)

    def kq(idx):
        return (W0, ('q', idx))

    def bc(name):
        return VEC8.t[:, V8[name], :].rearrange("p (k o) -> p k o", o=1).to_broadcast([128, 8, 16])

    def p3(pb):
        return pb.t[:, 0:128].rearrange("p (k n) -> p k n", n=16)

    WRv = BIG1.t[:, 0:8192].rearrange("p (k n) -> p k n", n=1024)
    WKv = BIG1.t[:, 8192:16384].rearrange("p (k n) -> p k n", n=1024)
    WVv = BIG2.t[:, 0:8192].rearrange("p (k n) -> p k n", n=1024)
    WOv = H.t[:, 0:8192].rearrange("p (k n) -> p k n", n=1024)
    o_ = 16384
    W1v = BIG1.t[:, o_:o_ + 512].rearrange("p (k n) -> p k n", n=64)
    A1v = BIG1.t[:, o_ + 512:o_ + 1024].rearrange("p (k n) -> p k n", n=64)
    V1v = BIG1.t[:, o_ + 1024:o_ + 1280].rearrange("p (k n) -> p k n", n=32)
    G1v = BIG1.t[:, o_ + 1280:o_ + 2304].rearrange("p (k n) -> p k n", n=128)
    W2v = BIG1.t[:, o_ + 2304:o_ + 3328]
    A2v = BIG1.t[:, o_ + 3328:o_ + 4352]
    V2v = BIG1.t[:, o_ + 4352:o_ + 5376]
    G2v = BIG1.t[:, o_ + 5376:o_ + 6400]

    def load_small(dst, src, rows):
        s = state['ws'] % 2
        state['ws'] += 1
        st = WST[s]
        P.dma(st.t[0:rows, 0:1024], src, writes=[st])
        cp('pool', dst[0:rows, :], st.t[0:rows, 0:1024], [st], [BIG1])

    def rwkv_layer(seg, li):
        j = li // 2
        BIG1.fence()
        BIG2.fence()
        H.fence()
        for half in range(2):
            cs = slice(half * 512, (half + 1) * 512)
            wload_to(BIG1, WRv[:, :, cs], w_r[j][:, cs], 8, 512)
            wload_to(BIG1, WKv[:, :, cs], w_k[j][:, cs], 8, 512)
            wload_to(BIG2, WVv[:, :, cs], w_v[j][:, cs], 8, 512)
            wload_to(H, WOv[:, :, cs], w_o[j][:, cs], 8, 512)
        wload_to(BIG1, W1v, d_w1[j], 8, 64)
        wload_to(BIG1, A1v, d_a1[j], 8, 64)
        if li == 3:
            wload_to(BIG1, V1v, d_v1[0], 8, 32)
            load_small(V2v, d_v2[0], 32)
        wload_to(BIG1, G1v, d_g1[j], 8, 128)
        load_small(W2v, d_w2[j], 64)
        load_small(A2v, d_a2[j], 64)
        load_small(G2v, d_g2[j], 128)
        CHUNK = DBG.get('chunk', True)
        if CHUNK:
            P.dma(HBUF.t[0:16, 0:768], d_cmask, writes=[HBUF])
            P.dma(HBUF.t[:, 768:896].rearrange('p (k n) -> p k n', n=16), d_reset[:, :, 0:16], writes=[HBUF])
        W0.fence()
        B0.fence()
        B1_.fence()
        if seg == 0:
            groups = [(16 * i, False) for i in range(65)]
        else:
            groups = [(16 * i, False) for i in range(64)] + [(1024, True)]
        if seg == 1:
            P.dma(STS.t[:, 0:128].rearrange("p (g b) -> p g b", b=16), s_shift[j], writes=[STS])
        SSH = STS.t[:, 0:128].rearrange("p (g b) -> p g b", b=16)
        MS = MST[j].t[:, :, :]
        XM = B0.t[:, 0:768].rearrange("p (m k n) -> p m k n", m=6, k=8)
        for gi_, (c0, sample) in enumerate(groups):
            HFb = W0.t[:, 29 * 136:30 * 136].rearrange("p (k n) -> p k n", n=17)
            kHF = (W0, ('q', 29))
            sqv = B1_.t[:, 3072:3200].rearrange("p (k n) -> p k n", n=16)
            kSQ = (B1_, 'sq')
            act(sqv, X.t[:, :, c0:c0 + 16], AF.Square, [X], [kSQ])
            pb = ps()
            for k in range(8):
                mm(pb.t[:, 0:16], CMB.t[:, CM_MEAN1024, :], sqv[:, k, :], k == 0, k == 7, [CMB, kSQ], [pb])
            rs = scr()
            act(rs.t[:, 0:16], pb.t[:, 0:16], AF.Sqrt, [pb, SMALL], [rs], bias=EPSC)
            P.op('dve', lambda e, rs=rs: e.reciprocal(out=rs.t[:, 0:16], in_=rs.t[:, 0:16]), [rs], [rs])
            if not sample:
                cp('pool', HFb[:, :, 0], SHC[:, j, :], [CAR], [kHF])
            tt('dve', HFb[:, :, 1:17], X.t[:, :, c0:c0 + 16], bc('nmix%d' % li), ALU.mult, [X, VEC8], [kHF])
            tt('dve', HFb[:, :, 1:17], HFb[:, :, 1:17], rs.t[:, 0:16].rearrange("p (o n) -> p o n", o=1).to_broadcast([128, 8, 16]), ALU.mult,
               [kHF, rs], [kHF])
            cur = HFb[:, :, 1:17]
            XX = Q(0)
            if sample:
                tt('dve', XX, SSH, cur, ALU.subtract, [STS, kHF], [kq(0)])
                P.dma(o_shifts[j], cur, reads=[kHF], eng='pool')
            else:
                tt('dve', XX, HFb[:, :, 0:16], cur, ALU.subtract, [kHF], [kq(0)])
                cp('pool', SHC[:, j, :], HFb[:, :, 16], [kHF], [CAR])
                if seg == 1 and c0 == 1008:
                    cp('pool', SHO[:, j, :], HFb[:, :, 16], [kHF], [CAR])
            for m in range(6):
                qt = 1 if m % 2 == 0 else 9
                tt('dve', Q(qt), XX, bc('mix%d_%d' % (j, m)), ALU.mult, [kq(0), VEC8], [kq(qt)])
                tt('pool', XM[:, m], Q(qt), cur, ALU.add, [kq(qt), kHF], [(B0, ('xm', m))])

            def projN(wv, m, wbuf):
                pb = ps()
                for oc in range(8):
                    for k in range(8):
                        mm(pb.t[:, oc * 16:oc * 16 + 16], wv[:, k, oc * 128:oc * 128 + 128], XM[:, m, k, :], k == 0, k == 7,
                           [wbuf, (B0, ('xm', m))], [pb])
                return pb

            def lora(w1v, r, m, func, w2v):
                pl = ps()
                for k in range(8):
                    mm(pl.t[0:r, 0:16], w1v[:, k, 0:r], XM[:, m, k, :], k == 0, k == 7, [BIG1, (B0, ('xm', m))], [pl])
                tl = B1_.t[0:r, 3200 + m * 16:3200 + m * 16 + 16]
                kt = (B1_, ('tl', m))
                if func is None:
                    cp('act', tl, pl.t[0:r, 0:16], [pl], [kt])
                else:
                    act(tl, pl.t[0:r, 0:16], func, [pl], [kt])
                po2 = ps()
                for oc in range(8):
                    mm(po2.t[:, oc * 16:oc * 16 + 16], w2v[0:r, oc * 128:oc * 128 + 128], tl, True, True, [BIG1, kt], [po2])
                return po2
            pr = projN(WRv, 0, BIG1)
            RF = Q(2)
            cp('act', RF, p3(pr), [pr], [kq(2)])
            pk = projN(WKv, 2, BIG1)
            KF = Q(3)
            cp('act', KF, p3(pk), [pk], [kq(3)])
            pv = projN(WVv, 3, BIG2)
            VV = Q(4)
            cp('act', VV, p3(pv), [pv], [kq(4)])
            pw = lora(W1v, 64, 1, AF.Tanh, W2v)
            DEC = Q(5)
            tt('dve', DEC, p3(pw), bc('w0_%d' % j), ALU.add, [pw, VEC8], [kq(5)])
            act(DEC, DEC, AF.Sigmoid, [kq(5)], [kq(5)])
            use_chunk = CHUNK and not sample
            if not use_chunk:
                act(DEC, DEC, AF.Exp, [kq(5)], [kq(5)], scale=-WDEC)
            pa = lora(A1v, 64, 4, None, A2v)
            AS = Q(6)
            tt('dve', AS, p3(pa), bc('a0_%d' % j), ALU.add, [pa, VEC8], [kq(6)])
            act(AS, AS, AF.Sigmoid, [kq(6)], [kq(6)])
            pg = lora(G1v, 128, 5, AF.Sigmoid, G2v)
            GG = Q(7)
            cp('act', GG, p3(pg), [pg], [kq(7)])
            if li == 3:
                pvl = lora(V1v, 32, 3, None, V2v)
                VG = Q(8)
                tt('dve', VG, p3(pvl), bc('v0_0'), ALU.add, [pvl, VEC8], [kq(8)])
                act(VG, VG, AF.Sigmoid, [kq(8)], [kq(8)])
                P.dma(Q(23), vf_dram[:, :, seg * TC + c0:seg * TC + c0 + 16], reads=[(VFD, seg * TC + c0)], writes=[kq(23)])
                tt('dve', Q(9), Q(23), VV, ALU.subtract, [kq(23), kq(4)], [kq(9)])
                tt('dve', Q(9), Q(9), VG, ALU.mult, [kq(9), kq(8)], [kq(9)])
                tt('dve', VV, VV, Q(9), ALU.add, [kq(4), kq(9)], [kq(4)])
            else:
                P.dma(vf_dram[:, :, seg * TC + c0:seg * TC + c0 + 16], VV, reads=[kq(4)], writes=[(VFD, seg * TC + c0)], eng='pool')
            KK = Q(10)
            tt('dve', KK, KF, bc('kk%d' % j), ALU.mult, [kq(3), VEC8], [kq(10)])
            tt('pool', Q(11), KK, KK, ALU.mult, [kq(10)], [kq(11)])
            pss = ps()
            mm(pss.t[:, 0:128], CM.t[:, CM_BLKONES, :], W0.t[:, 11 * 136:11 * 136 + 128], True, True, [CM, kq(11)], [pss])
            INV = Q(12)
            ts('dve', INV, p3(pss), 1e-24, None, ALU.max, None, [pss], [kq(12)])
            act(INV, INV, AF.Sqrt, [kq(12)], [kq(12)])
            P.op('dve', lambda e, INV=INV: e.reciprocal(out=INV, in_=INV), [kq(12)], [kq(12)])
            tt('dve', KK, KK, INV, ALU.mult, [kq(10), kq(12)], [kq(10)])
            AT = Q(13)
            ts('pool', AT, KK, -1.0, None, ALU.mult, None, [kq(10)], [kq(13)])
            BT = Q(14)
            tt('pool', BT, KK, AS, ALU.mult, [kq(10), kq(6)], [kq(14)])
            K2 = Q(15)
            stt(K2, AS, -1.0, bc('ka%d' % j), ALU.add, ALU.mult, [kq(6), VEC8], [kq(15)])
            stt(K2, K2, 1.0, KF, ALU.add, ALU.mult, [kq(15), kq(3)], [kq(15)])
            RK = Q(16)
            tt('dve', RK, RF, K2, ALU.mult, [kq(2), kq(15)], [kq(16)])
            tt('pool', RK, RK, bc('rk%d' % j), ALU.mult, [kq(16), VEC8], [kq(16)])
            pbo = ps()
            mm(pbo.t[:, 0:128], CM.t[:, CM_BLKONES, :], W0.t[:, 16 * 136:16 * 136 + 128], True, True, [CM, kq(16)], [pbo])
            BON = Q(17)
            tt('dve', BON, p3(pbo), VV, ALU.mult, [pbo, kq(4)], [kq(17)])
            YS = Q(18)
            for _once in ([0] if use_chunk else []):
                SG = DEC
                CS = Q(24)
                ones16 = SMALL.t[:, 2:3].to_broadcast([128, 16])
                P.op('dve', lambda e: e.tensor_tensor_scan(out=W0.t[:, 24 * 136:24 * 136 + 128], data0=HBUF.t[:, 768:896],
                                                           data1=W0.t[:, 5 * 136:5 * 136 + 128], initial=0.0,
                                                           op0=ALU.mult, op1=ALU.add), [kq(5), HBUF], [kq(24)])
                PEXC, PINC, PINV, BP, KP = Q(0), Q(1), Q(11), Q(27), Q(28)
                tt('dve', PEXC, CS, SG, ALU.subtract, [kq(24), kq(5)], [kq(0)])
                act(PEXC, PEXC, AF.Exp, [kq(0)], [kq(0)], scale=-WDEC)
                act(PINC, CS, AF.Exp, [kq(24)], [kq(1)], scale=-WDEC)
                act(PINV, CS, AF.Exp, [kq(24)], [kq(11)], scale=WDEC)
                PLb = PINC[:, :, 15:16].to_broadcast([128, 8, 16])
                ARv = B1_.t[:, 3328:3584].rearrange("p (k a n) -> p k a n", a=2, n=16)
                kAR = (B1_, ('qb', 0))
                BTq = B1_.t[:, 3584:3712].rearrange("p (k n) -> p k n", n=16)
                kBT = (B1_, ('qb', 2))
                KTq = B1_.t[:, 3712:3840].rearrange("p (k n) -> p k n", n=16)
                kKT = (B1_, ('qb', 3))
                BDq = B1_.t[:, 3840:3968].rearrange("p (k n) -> p k n", n=16)
                kBD = (B1_, ('qb', 4))
                KDq = B0.t[:, 768:896].rearrange("p (k n) -> p k n", n=16)
                kKD = (B0, ('qb', 5))
                VBq = B0.t[:, 896:1024].rearrange("p (k n) -> p k n", n=16)
                kVB = (B0, ('qb', 6))
                tt('dve', ARv[:, :, 0, :], AT, PEXC, ALU.mult, [kq(13), kq(0)], [kAR])
                tt('dve', ARv[:, :, 1, :], RF, PINC, ALU.mult, [kq(2), kq(1)], [kAR])
                tt('dve', BP, BT, PINV, ALU.mult, [kq(14), kq(11)], [kq(27)])
                tt('dve', KP, K2, PINV, ALU.mult, [kq(15), kq(11)], [kq(28)])
                cp('pool', BTq, BP, [kq(27)], [kBT])
                cp('pool', KTq, KP, [kq(28)], [kKT])
                tt('pool', BDq, BP, PLb, ALU.mult, [kq(27), kq(1)], [kBD])
                tt('pool', KDq, KP, PLb, ALU.mult, [kq(28), kq(1)], [kKD])
                cp('act', VBq, VV, [kq(4)], [kVB])
                ARm = [B0.t[:, 3584:3840].rearrange("p (k a n) -> p k a n", a=2, n=16),
                       B0.t[:, 3840:4096].rearrange("p (k a n) -> p k a n", a=2, n=16)]
                kARm = (B0, ('arm', 0))
                for hh_ in range(2):
                    ts('dve' if hh_ == 0 else 'pool', ARm[hh_], ARv, CM.t[:, CM_BLKONES, 64 * hh_:64 * hh_ + 1], None, ALU.mult, None, [kAR, CM], [kARm])
                if DBG.get('cstop', 99) < 2:
                    break
                VTM = B1_.t[0:16, 0:1024]
                BDTM = B1_.t[0:16, 1024:2048]
                KDTM = B1_.t[0:16, 2048:3072]
                kVTM, kBDTM, kKDTM = (B1_, ('tm', 0)), (B1_, ('tm', 1)), (B1_, ('tm', 2))
                for (qb, kqb, tm, ktm) in ((VBq, kVB, VTM, kVTM), (BDq, kBD, BDTM, kBDTM), (KDq, kKD, KDTM, kKDTM)):
                    pt = ps()
                    ptb = pt.t[:, :].bitcast(BF16)
                    for k in range(8):
                        P.op('pe', lambda e, ptb=ptb, qb=qb, k=k: e.transpose(ptb[0:16, k * 128:(k + 1) * 128], qb[:, k, :], CMB.t[:, CM_IDENT, :]),
                             [kqb, CMB], [pt])
                    evcp(tm, ptb[0:16, 0:1024], [pt], [ktm])
                MB = W0.t[:, 25 * 136:25 * 136 + 256].bitcast(BF16).rearrange("p (k n) -> p k n", n=64)
                kMB = (W0, ('q', 25))
                cp('act', MB, MS, [MST[j]], [kMB])
                if DBG.get('cstop', 99) < 3:
                    break
                G1 = B0.t[0:16, 1024:1536].rearrange("s (h a t) -> s h a t", a=2, t=16)
                G2 = B0.t[0:16, 1536:2048].rearrange("s (h a t) -> s h a t", a=2, t=16)
                kG1, kG2 = (B0, ('g', 1)), (B0, ('g', 2))
                ATn = B0.t[0:16, 3072:3328].rearrange("s (h t) -> s h t", t=16)
                An = B0.t[0:16, 3328:3584].rearrange("s (h t) -> s h t", t=16)
                kATn = [(B0, ('a', 0, 0)), (B0, ('a', 0, 1))]
                kAn = [(B0, ('a', 1, 0)), (B0, ('a', 1, 1))]
                ZB = B0.t[0:16, 2048:3072]
                kZB = [(B0, ('z', 0)), (B0, ('z', 1))]
                MG = HBUF.t[0:16, 0:512]
                MA = HBUF.t[0:16, 512:768]
                pg1, pg2, pga = ps(), ps(), ps()
                for h in range(16):
                    p_, hh = h // 2, h % 2
                    rows = slice(64 * hh, 64 * hh + 64)
                    arr = ARm[hh][:, p_, :, :]
                    mm(pg1.t[0:16, h * 32:h * 32 + 32], BTq[:, p_, :], arr, True, True, [kBT, kARm], [pg1])
                    mm(pg2.t[0:16, h * 32:h * 32 + 32], KTq[:, p_, :], arr, True, True, [kKT, kARm], [pg2])
                    mm(pga.t[0:16, h * 16:h * 16 + 16], ARm[hh][:, p_, 0, :], BTq[:, p_, :], True, True, [kARm, kBT], [pga])
                tt('dve', B0.t[0:16, 1024:1536], pg1.t[0:16, 0:512], MG, ALU.mult, [pg1, HBUF], [kG1])
                tt('dve', B0.t[0:16, 1536:2048], pg2.t[0:16, 0:512], MG, ALU.mult, [pg2, HBUF], [kG2])
                for hf in range(2):
                    tt('dve', B0.t[0:16, 3328 + hf * 128:3328 + hf * 128 + 128], pga.t[0:16, hf * 128:hf * 128 + 128], MA[:, hf * 128:hf * 128 + 128], ALU.mult,
                       [pga, HBUF], [kAn[hf]])
                    cp('pool', ATn[:, hf * 8:hf * 8 + 8, :], G1[:, hf * 8:hf * 8 + 8, 0, :], [kG1], [kATn[hf]])
                if DBG.get('cstop', 99) < 4:
                    break
                pw = [ps(), ps()]
                for h in range(16):
                    p_, hh = h // 2, h % 2
                    rows = slice(64 * hh, 64 * hh + 64)
                    o_ap = pw[h // 8].t[0:16, (h % 8) * 64:(h % 8) * 64 + 64]
                    mm(o_ap, ARm[hh][:, p_, 0, :], MB[:, p_, :], True, False, [kARm, kMB], [pw[h // 8]])
                    mm(o_ap, G2[:, h, 0, :], VTM[:, h * 64:h * 64 + 64], False, True, [kG2, kVTM], [pw[h // 8]])
                for half in range(2):
                    evcp(ZB[:, half * 512:(half + 1) * 512], pw[half].t[0:16, 0:512], [pw[half]], [kZB[half]])
                if DBG.get('cstop', 99) < 5:
                    break
                I16 = CMB.t[0:16, CM_IDENT, 0:16]
                for lvl in range(4):
                    pz = [ps(), ps()]
                    pa = [ps(), ps()] if lvl < 3 else None
                    for hf in range(2):
                        for h in range(hf * 8, hf * 8 + 8):
                            o_ap = pz[hf].t[0:16, (h % 8) * 64:(h % 8) * 64 + 64]
                            zsl = ZB[:, h * 64:h * 64 + 64]
                            mm(o_ap, ATn[:, h, :], zsl, True, True, [kATn[hf], kZB[hf]], [pz[hf]])
                        if lvl < 3:
                            for h in range(hf * 8, hf * 8 + 8):
                                hl = h % 8
                                mm(pa[hf].t[0:16, hl * 16:hl * 16 + 16], An[:, h, :], ATn[:, h, :], True, True, [kAn[hf], kATn[hf]], [pa[hf]])
                                mm(pa[hf].t[0:16, 128 + hl * 16:128 + hl * 16 + 16], ATn[:, h, :], An[:, h, :], True, True, [kAn[hf], kATn[hf]], [pa[hf]])
                    for hf in range(2):
                        tt('dve', ZB[:, hf * 512:(hf + 1) * 512], pz[hf].t[0:16, 0:512], ZB[:, hf * 512:(hf + 1) * 512], ALU.add,
                           [pz[hf], kZB[hf]], [kZB[hf]])
                        if lvl < 3:
                            cp('act', B0.t[0:16, 3072 + hf * 128:3072 + hf * 128 + 128], pa[hf].t[0:16, 0:128], [pa[hf]], [kATn[hf]])
                            cp('act', B0.t[0:16, 3328 + hf * 128:3328 + hf * 128 + 128], pa[hf].t[0:16, 128:256], [pa[hf]], [kAn[hf]])
                if DBG.get('cstop', 99) < 6:
                    break
                py = [ps(), ps()]
                for h in range(16):
                    p_, hh = h // 2, h % 2
                    rows = slice(64 * hh, 64 * hh + 64)
                    o_ap = py[h // 8].t[0:16, (h % 8) * 64:(h % 8) * 64 + 64]
                    mm(o_ap, ARm[hh][:, p_, 1, :], MB[:, p_, :], True, False, [kARm, kMB], [py[h // 8]])
                    mm(o_ap, G1[:, h, 1, :], ZB[:, h * 64:h * 64 + 64], False, False, [kG1, kZB[h // 8]], [py[h // 8]])
                    mm(o_ap, G2[:, h, 1, :], VTM[:, h * 64:h * 64 + 64], False, True, [kG2, kVTM], [py[h // 8]])
                YTB = B0.t[0:16, 3072:4096]
                kYTB = (B0, ('ytb', 0))
                for half in range(2):
                    evcp(YTB[:, half * 512:(half + 1) * 512], py[half].t[0:16, 0:512], [py[half]], [kYTB, kAn[0], kAn[1], kATn[0], kATn[1], kARm])
                pyt = ps()
                pytb = pyt.t[:, :].bitcast(BF16)
                for k in range(8):
                    P.op('pe', lambda e, pytb=pytb, k=k: e.transpose(pytb[:, k * 16:(k + 1) * 16], YTB[:, k * 128:(k + 1) * 128], CMB.t[0:16, CM_IDENT, 0:16]),
                         [kYTB, kAn[0], kAn[1], kATn[0], kATn[1], kARm, CMB], [pyt])
                cp('act', YS, pytb[:, 0:128].rearrange("p (k n) -> p k n", n=16), [pyt], [kq(18)])
                if DBG.get('cstop', 99) < 7:
                    break
                psu = [ps(), ps()]
                for p_ in range(8):
                    o_ap = psu[p_ // 4].t[:, (p_ % 4) * 128:(p_ % 4) * 128 + 128]
                    mm(o_ap, BDTM[:, p_ * 128:(p_ + 1) * 128], ZB[:, p_ * 128:(p_ + 1) * 128], True, False, [kBDTM, kZB[p_ // 4]], [psu[p_ // 4]])
                    mm(o_ap, KDTM[:, p_ * 128:(p_ + 1) * 128], VTM[:, p_ * 128:(p_ + 1) * 128], False, True, [kKDTM, kVTM], [psu[p_ // 4]])
                tt('dve', MS, MS, PINC[:, :, 15:16].to_broadcast([128, 8, 64]), ALU.mult, [MST[j], kq(1)], [MST[j]])
                for g2 in range(2):
                    for hh in range(2):
                        rows = slice(64 * hh, 64 * hh + 64)
                        src = psu[g2].t[rows, 0:512].rearrange("p (q c) -> p q c", c=128)[:, :, hh * 64:hh * 64 + 64]
                        tt('dve', MS[rows, 4 * g2:4 * g2 + 4, :], MS[rows, 4 * g2:4 * g2 + 4, :], src, ALU.add, [MST[j], psu[g2]], [MST[j]])
            if not use_chunk:
                TM = []
                qlist = [(AT, kq(13)), (DEC, kq(5)), (BT, kq(14)), (K2, kq(15)), (RF, kq(2)), (None, None)]
                qbs = []
                for qi, (src, ksrc) in enumerate(qlist):
                    if qi < 5:
                        qb = B1_.t[:, 3328 + qi * 128:3328 + qi * 128 + 128].rearrange("p (k n) -> p k n", n=16)
                        kqb = (B1_, ('qb', qi))
                    else:
                        qb = B0.t[:, 768:896].rearrange("p (k n) -> p k n", n=16)
                        kqb = (B0, ('qb', qi))
                    qbs.append((qb, kqb))
                    if qi == 5:
                        tt('dve', Q(24), DEC, qbs[1][0], ALU.subtract, [kq(5), qbs[1][1]], [kq(24)])
                        cp('pool', qb, Q(24), [kq(24)], [kqb])
                    else:
                        cp('pool' if qi % 2 else 'act', qb, src, [ksrc], [kqb])
                    pt = ps()
                    ptb = pt.t[:, :].bitcast(BF16)
                    for k in range(8):
                        P.op('pe', lambda e, ptb=ptb, qb=qb, k=k: e.transpose(ptb[0:16, k * 128:(k + 1) * 128], qb[:, k, :], CMB.t[:, CM_IDENT, :]),
                             [kqb, CMB], [pt])
                    if qi < 3:
                        tm = B1_.t[0:16, qi * 1024:(qi + 1) * 1024]
                        ktm = (B1_, ('tm', qi))
                    else:
                        tm = B0.t[0:16, 1024 + (qi - 3) * 1024:1024 + (qi - 2) * 1024]
                        ktm = (B0, ('tm', qi))
                    evcp(tm, ptb[0:16, 0:1024], [pt], [ktm])
                    TM.append((tm, ktm))
                for t in range(16):
                    if sample:
                        Sb = SMP[0]
                        S3 = Sb.t[:, (t % 2) * 512:(t % 2) * 512 + 512].rearrange("p (k n) -> p k n", n=64)
                        for hh in range(2):
                            P.dma(S3[hh * 64:(hh + 1) * 64], s_rwkv[j, t].rearrange("(p hh) i jj -> hh i p jj", hh=2)[hh], writes=[Sb])
                        kS = Sb
                    else:
                        S3 = MS
                        kS = MST[j]
                    pq = []
                    for qi in range(5):
                        pbq = ps()
                        srcs = [TM[qi]] + ([TM[5]] if qi == 1 else [])
                        nmm = 2 * len(srcs)
                        im = 0
                        for (tm, ktm) in srcs:
                            tm4 = tm.rearrange("s (p hh jj) -> s p hh jj", hh=2, jj=64)
                            for hh in range(2):
                                mm(pbq.t[:, 0:512], SELH.t[0:16, t, hh, :], tm4[:, :, hh, :], im == 0, im == nmm - 1, [SELH, ktm], [pbq])
                                im += 1
                        pq.append(pbq)

                    def q3(i):
                        return pq[i].t[:, 0:512].rearrange("p (k n) -> p k n", n=64)
                    T1b = scr()
                    T1 = T1b.t[:, 0:512].rearrange("p (k n) -> p k n", n=64)
                    SAb = scr()
                    SA = SAb.t[:, 0:8]
                    tt('dve', T1, S3, q3(0), ALU.mult, [kS, pq[0]], [T1b])
                    P.op('dve', lambda e, SA=SA, T1=T1: e.tensor_reduce(out=SA, in_=T1, axis=AX.X, op=ALU.add), [T1b], [SAb])
                    tt('dve', S3, S3, q3(1), ALU.mult, [kS, pq[1]], [kS])
                    T2b = scr()
                    T2 = T2b.t[:, 0:512].rearrange("p (k n) -> p k n", n=64)
                    tt('dve', T2, q3(2), SAb.t[:, 0:8].rearrange("p (k o) -> p k o", o=1).to_broadcast([128, 8, 64]), ALU.mult, [pq[2], SAb], [T2b])
                    tt('pool', S3, S3, T2, ALU.add, [kS, T2b], [kS])
                    T3b = scr()
                    T3 = T3b.t[:, 0:512].rearrange("p (k n) -> p k n", n=64)
                    tt('dve', T3, q3(3), VV[:, :, t:t + 1].to_broadcast([128, 8, 64]), ALU.mult, [pq[3], kq(4)], [T3b])
                    tt('pool', S3, S3, T3, ALU.add, [kS, T3b], [kS])
                    T4b = scr()
                    T4 = T4b.t[:, 0:512].rearrange("p (k n) -> p k n", n=64)
                    tt('dve', T4, S3, q3(4), ALU.mult, [kS, pq[4]], [T4b])
                    P.op('dve', lambda e, T4=T4, t=t: e.tensor_reduce(out=YS[:, :, t], in_=T4, axis=AX.X, op=ALU.add), [T4b], [kq(18)])
                    if sample:
                        for hh in range(2):
                            P.dma(o_rwkvs[j, t].rearrange("(p hh) i jj -> hh i p jj", hh=2)[hh], S3[hh * 64:(hh + 1) * 64], reads=[Sb], eng='pool')
            tt('pool', Q(19), YS, YS, ALU.mult, [kq(18)], [kq(19)])
            pm = ps()
            mm(pm.t[:, 0:128], CM.t[:, CM_BLKMEAN64, :], W0.t[:, 18 * 136:18 * 136 + 128], True, True, [CM, kq(18)], [pm])
            mm(pm.t[:, 128:256], CM.t[:, CM_BLKMEAN64, :], W0.t[:, 19 * 136:19 * 136 + 128], True, True, [CM, kq(19)], [pm])
            ME = Q(20)
            VA = Q(21)
            cp('act', ME, p3(pm), [pm], [kq(20)])
            tt('dve', VA, ME, ME, ALU.mult, [kq(20)], [kq(21)])
            tt('dve', VA, pm.t[:, 128:256].rearrange("p (k n) -> p k n", n=16), VA, ALU.subtract, [pm, kq(21)], [kq(21)])
            ts('dve', VA, VA, 0.0, None, ALU.max, None, [kq(21)], [kq(21)])
            act(VA, VA, AF.Sqrt, [kq(21), SMALL], [kq(21)], bias=GNEPS)
            P.op('dve', lambda e, VA=VA: e.reciprocal(out=VA, in_=VA), [kq(21)], [kq(21)])
            OO = Q(22)
            tt('dve', OO, YS, ME, ALU.subtract, [kq(18), kq(20)], [kq(22)])
            tt('dve', OO, OO, VA, ALU.mult, [kq(22), kq(21)], [kq(22)])
            tt('pool', OO, OO, bc('gng%d' % j), ALU.mult, [kq(22), VEC8], [kq(22)])
            tt('pool', OO, OO, bc('gnb%d' % j), ALU.add, [kq(22), VEC8], [kq(22)])
            tt('pool', OO, OO, BON, ALU.add, [kq(22), kq(17)], [kq(22)])
            OB = B1_.t[:, 3968:4096].rearrange("p (k n) -> p k n", n=16)
            kOB = (B1_, 'ob')
            tt('dve', OB, OO, GG, ALU.mult, [kq(22), kq(7)], [kOB])
            po_ = ps()
            for oc in range(8):
                for k in range(8):
                    mm(po_.t[:, oc * 16:oc * 16 + 16], WOv[:, k, oc * 128:oc * 128 + 128], OB[:, k, :], k == 0, k == 7, [H, kOB], [po_])
            tt('dve', X.t[:, :, c0:c0 + 16], X.t[:, :, c0:c0 + 16], p3(po_), ALU.add, [X, po_], [X])
        W0.fence()
        B0.fence()
        B1_.fence()
        BIG1.fence()
        BIG2.fence()
        H.fence()

    ctx['rwkv_layer'] = rwkv_layer


def _prep_shared(inp):
    c = build_consts()
    sh = dict(cm=c['cm'], rope=c['rope'], rmask=c['rmask'], decin=c['decin'], deck=c['deck'], mask4=c['mask4'],
              maska=c['maska'], reset=c['reset'], sel=c['sel'], small=c['small'])
    selh = np.zeros((16, 16, 2, 128), np.float32)
    for t in range(16):
        selh[t, t, 0, :64] = 1.0
        selh[t, t, 1, 64:] = 1.0
    sh['selh'] = selh
    i16 = np.arange(16)
    strictT = (i16[:, None] < i16[None, :]).astype(np.float32)
    inclT = (i16[:, None] <= i16[None, :]).astype(np.float32)
    cm_ = np.zeros((16, 768), np.float32)
    cm_[:, 0:512] = np.tile(np.concatenate([strictT, inclT], axis=1), (1, 16))
    cm_[:, 512:768] = np.tile((i16[:, None] > i16[None, :]).astype(np.float32), (1, 16))
    sh['cmask'] = cm_
    f = lambda a: np.ascontiguousarray(np.asarray(a, np.float32))
    vec8 = np.zeros((128, NV8, 8), np.float32)

    def put8(name, v):
        vec8[:, V8[name], :] = fm(v, 8)
    for li in range(4):
        put8('nmix%d' % li, inp['norm_mix_g'][li])
        put8('nffn%d' % li, inp['norm_ffn_g'][li])
    put8('nfinal', inp['norm_final_g'])
    for j in range(2):
        put8('gn%d' % j, inp['ev_ret_gn_g'][j])
        for m in range(4):
            put8('cw%d_%d' % (m, j), inp['ev_lru_conv_w'][j][m])
        put8('cb%d' % j, inp['ev_lru_conv_b'][j])
        put8('ba%d' % j, inp['ev_lru_ba'][j])
        put8('bx%d' % j, inp['ev_lru_bx'][j])
        put8('lam%d' % j, inp['ev_lru_lambda'][j])
        for m in range(6):
            put8('mix%d_%d' % (j, m), inp['od_mix'][j][m])
        put8('w0_%d' % j, inp['od_w0'][j])
        put8('a0_%d' % j, inp['od_a0'][j])
        put8('kk%d' % j, inp['od_k_k'][j])
        put8('ka%d' % j, inp['od_k_a'][j])
        put8('rk%d' % j, np.asarray(inp['od_r_k'][j]).reshape(-1))
        put8('gng%d' % j, inp['od_gn_g'][j])
        put8('gnb%d' % j, inp['od_gn_b'][j])
    put8('v0_0', inp['od_v0'][0])
    vec22 = np.zeros((128, NV22, 22), np.float32)
    for li in range(4):
        for m in range(3):
            vec22[:, V22['fw%d_%d' % (m, li)], :] = fm(inp['ff_conv_w'][li][m], 22)
        vec22[:, V22['fb%d' % li], :] = fm(inp['ff_conv_b'][li], 22)
    sh['vec8'] = vec8
    sh['vec22'] = vec22
    w_in = f(inp['ev_w_in'])
    perm = np.array([(c // 128) * 128 + ((c % 128) + 64) % 128 for c in range(1024)])
    sh['w_in'] = w_in
    sh['w_swap'] = np.ascontiguousarray(w_in[:, :, perm])
    sh['w_out'] = f(inp['ev_w_out'])
    wa = np.asarray(inp['ev_lru_wa'], np.float32).transpose(0, 2, 1, 3)
    wx = np.asarray(inp['ev_lru_wx'], np.float32).transpose(0, 2, 1, 3)
    sh['wax'] = np.ascontiguousarray(np.stack([wa, wx], axis=2))
    sh['w_r'] = f(inp['od_w_r'])
    sh['w_k'] = f(inp['od_w_k'])
    sh['w_v'] = f(inp['od_w_v'])
    sh['w_o'] = f(inp['od_w_o'])
    sh['w1'] = f(inp['od_w1'])
    sh['w2'] = f(inp['od_w2'])
    sh['a1'] = f(inp['od_a1'])
    sh['a2'] = f(inp['od_a2'])
    sh['v1'] = f(inp['od_v1'])
    sh['v2'] = f(inp['od_v2'])
    sh['g1'] = f(inp['od_g1'])
    sh['g2'] = f(inp['od_g2'])
    sh['w_up'] = f(inp['ff_w_up'])
    sh['w_down'] = np.ascontiguousarray(f(inp['ff_w_down']).reshape(4, NFC, 128, 8, 128).transpose(0, 3, 2, 1, 4).reshape(4, 8, 128, NFC * 128))
    return sh


def kernel(**inp):
    if 'nc' not in _CACHE:
        _CACHE['nc'] = build_program()
    nc = _CACHE['nc']
    sh = _prep_shared(inp)
    xp = np.asarray(inp['x_prompt'], np.float32)
    xs = np.asarray(inp['x_sample'], np.float32)
    meta = np.asarray(inp['meta_tokens'], np.float32)
    in_maps = []
    NCORES = DBG['ncores']
    for c in range(NCORES):
        sl = slice(16 * c, 16 * c + 16)
        m = dict(sh)
        xall = np.concatenate([meta, xp[c], xs[sl, 0, :]], axis=0)
        m['xT'] = np.ascontiguousarray(xall.T)
        m['s_ret'] = np.ascontiguousarray(np.asarray(inp['state_ret'], np.float32)[:, sl])
        m['s_lru'] = np.ascontiguousarray(np.asarray(inp['state_lru'], np.float32)[:, sl].reshape(2, 16, 8, 128).transpose(0, 3, 2, 1))
        m['s_lconv'] = np.ascontiguousarray(np.asarray(inp['state_lru_conv'], np.float32)[:, sl].reshape(2, 16, 3, 8, 128).transpose(0, 4, 3, 2, 1))
        m['s_rwkv'] = np.ascontiguousarray(np.asarray(inp['state_rwkv'], np.float32)[:, sl])
        m['s_shift'] = np.ascontiguousarray(np.asarray(inp['state_shift'], np.float32)[:, sl].reshape(2, 16, 8, 128).transpose(0, 3, 2, 1))
        m['s_ffn'] = np.ascontiguousarray(np.asarray(inp['state_ffn_conv'], np.float32)[:, sl].reshape(4, 16, 2, 22, 128).transpose(0, 4, 3, 2, 1))
        in_maps.append(m)
    res = run_bass_kernel_spmd(nc, in_maps, core_ids=list(range(NCORES)))
    R = res.results
    y_p = np.zeros((8, 2048, 1024), np.float32)
    y_s = np.zeros((128, 1, 1024), np.float32)
    ret_p = np.zeros((2, 8, 4, 128, 256), np.float32)
    lru_p = np.zeros((2, 8, 1024), np.float32)
    lconv_p = np.zeros((2, 8, 3, 1024), np.float32)
    rwkv_p = np.zeros((2, 8, 16, 64, 64), np.float32)
    shift_p = np.zeros((2, 8, 1024), np.float32)
    ffn_p = np.zeros((4, 8, 2, 2816), np.float32)
    ret_s = np.zeros((2, 128, 4, 128, 256), np.float32)
    lru_s = np.zeros((2, 128, 1024), np.float32)
    lconv_s = np.zeros((2, 128, 3, 1024), np.float32)
    rwkv_s = np.zeros((2, 128, 16, 64, 64), np.float32)
    shift_s = np.zeros((2, 128, 1024), np.float32)
    ffn_s = np.zeros((4, 128, 2, 2816), np.float32)
    for c in range(NCORES):
        r = R[c]
        sl = slice(16 * c, 16 * c + 16)
        yT = np.asarray(r['yT'])
        y_p[c] = yT[:, 16:2064].T
        y_s[sl, 0, :] = yT[:, 2064:].T
        ret_p[:, c] = np.asarray(r['o_retp'])
        lru_p[:, c] = np.asarray(r['o_lrup']).transpose(1, 2, 0).reshape(2, 1024)
        lconv_p[:, c] = np.asarray(r['o_lconvp']).transpose(1, 3, 2, 0).reshape(2, 3, 1024)
        if DBG.get('chunk', True):
            rwkv_p[:, c] = np.asarray(r['o_rwkvp']).reshape(2, 2, 64, 8, 64).transpose(0, 3, 1, 4, 2).reshape(2, 16, 64, 64)
        else:
            rwkv_p[:, c] = np.asarray(r['o_rwkvp']).reshape(2, 2, 64, 8, 64).transpose(0, 3, 1, 2, 4).reshape(2, 16, 64, 64)
        shift_p[:, c] = np.asarray(r['o_shiftp']).transpose(1, 2, 0).reshape(2, 1024)
        ffn_p[:, c] = np.asarray(r['o_ffnp']).transpose(1, 3, 2, 0).reshape(4, 2, 2816)
        ret_s[:, sl] = np.asarray(r['o_rets'])
        lru_s[:, sl] = np.asarray(r['o_lrus']).transpose(0, 3, 2, 1).reshape(2, 16, 1024)
        lconv_s[:, sl] = np.asarray(r['o_lconvs']).transpose(0, 4, 3, 2, 1).reshape(2, 16, 3, 1024)
        rwkv_s[:, sl] = np.asarray(r['o_rwkvs'])
        shift_s[:, sl] = np.asarray(r['o_shifts']).transpose(0, 3, 2, 1).reshape(2, 16, 1024)
        ffn_s[:, sl] = np.asarray(r['o_ffns']).transpose(0, 4, 3, 2, 1).reshape(4, 16, 2, 2816)
    return (y_p, y_s, ret_p, lru_p, lconv_p, rwkv_p, shift_p, ffn_p, ret_s, lru_s, lconv_s, rwkv_s, shift_s, ffn_s)
```
